# Optimizing a Trainium2 kernel written in Bass

```python
import math
import jax
import jax.numpy as jnp
from jax import lax
import numpy as np

D_MODEL = 1024
BATCH = 2
SEQ = 8192
DEPTH = 4

GRID_W = 64
CTX_LEN = 256
N_MIXERS = 4
CTX_READING_MIXERS = (0, 1)
RMS_EPS = 1e-6
MLP_HIDDEN = 4 * D_MODEL
N_MOD = 6

SSD_EXPAND = 2
SSD_INNER = SSD_EXPAND * D_MODEL
SSD_HEADDIM = 64
SSD_HEADS = SSD_INNER // SSD_HEADDIM
SSD_GROUPS = 4
SSD_HPG = SSD_HEADS // SSD_GROUPS
SSD_STATE = 128
SSD_CONV = 3
SSD_CHUNK = 128
SSD_BC = 2 * 2 * SSD_GROUPS * SSD_STATE
SSD_CONV_CH = SSD_INNER + SSD_BC
SSD_IN = SSD_INNER + SSD_CONV_CH + 2 * SSD_HEADS

NA_HEADS = 16
NA_HEADDIM = D_MODEL // NA_HEADS
NA_WIN_ROWS = 8
NA_WIN_COLS = 16

SC_CONV = 3

FN_GROUPS = 8
FN_GROUP_CH = D_MODEL // FN_GROUPS

kernel_name = 'hybrid_interleaved_diffusion_trunk'


def rmsnorm(t, g):
    tf = t.astype(jnp.float32)
    tf = tf * lax.rsqrt(jnp.mean(tf * tf, axis=-1, keepdims=True) + RMS_EPS)
    return tf.astype(t.dtype) * g


def modulate(t, shift, scale):
    return t * (1 + scale) + shift


def dwconv_centred(u, w):
    k, ch = w.shape
    return lax.conv_general_dilated(u, w[:, None, :], window_strides=(1,), padding=[(k // 2, k // 2)],
                                    dimension_numbers=('NWC', 'WIO', 'NWC'), feature_group_count=ch)


def sq_relu_mlp(h, w1, w2):
    return jnp.square(jax.nn.relu(h @ w1)) @ w2


def finish_layer(t, y, m, g, w1, w2):
    t = t + m[2] * rmsnorm(y, g[1])
    h2 = modulate(rmsnorm(t, g[2]), m[3], m[4])
    return t + m[5] * rmsnorm(sq_relu_mlp(h2, w1, w2), g[3])


def ssd_scan(x, dt, a, bm, cm, state0):
    bsz, seq = x.shape[:2]
    nc = seq // SSD_CHUNK

    def chunkify(t):
        return jnp.moveaxis(t.reshape(bsz, nc, SSD_CHUNK, *t.shape[2:]), 1, 0)

    mask = jnp.tril(jnp.ones((SSD_CHUNK, SSD_CHUNK), bool))[:, :, None, None]

    def step(state, inp):
        xc, dtc, bc, cc = inp
        cum = jnp.cumsum(dtc * a, axis=1)
        seg = cum[:, :, None] - cum[:, None, :]
        decay = jnp.exp(jnp.where(mask, seg, -jnp.inf))
        cb = jnp.einsum('bign,bjgn->bijg', cc, bc)
        scores = cb[..., None] * decay * dtc[:, None]
        y = jnp.einsum('bijge,bjgep->bigep', scores, xc)
        y = y + jnp.einsum('bign,bgepn->bigep', cc, state) * jnp.exp(cum)[..., None]
        w_end = jnp.exp(cum[:, -1:] - cum) * dtc
        state = state * jnp.exp(cum[:, -1])[..., None, None] + jnp.einsum('bjgn,bjge,bjgep->bgepn', bc, w_end, xc)
        return state, y

    state, ys = lax.scan(step, state0, (chunkify(x), chunkify(dt), chunkify(bm), chunkify(cm)))
    return jnp.moveaxis(ys, 0, 1).reshape(x.shape), state


def ssd_mixer(h, w_in, conv_w, conv_b, dt_bias, a_log, d_skip, norm_g, state0):
    bsz, seq, _ = h.shape
    proj = h @ w_in
    z = proj[..., :SSD_INNER]
    xbc = jax.nn.silu(dwconv_centred(proj[..., SSD_INNER:SSD_INNER + SSD_CONV_CH], conv_w) + conv_b)
    dt_raw = proj[..., SSD_INNER + SSD_CONV_CH:].reshape(bsz, seq, 2, SSD_HEADS)
    xs = xbc[..., :SSD_INNER].reshape(bsz, seq, SSD_GROUPS, SSD_HPG, SSD_HEADDIM).astype(jnp.float32)
    bc = xbc[..., SSD_INNER:].reshape(bsz, seq, 2, 2, SSD_GROUPS, SSD_STATE).astype(jnp.float32)
    dt = jax.nn.softplus(dt_raw.astype(jnp.float32) + dt_bias.astype(jnp.float32))
    dt = dt.reshape(bsz, seq, 2, SSD_GROUPS, SSD_HPG)
    a = -jnp.exp(a_log.astype(jnp.float32)).reshape(2, SSD_GROUPS, SSD_HPG)
    rev = lambda t: jnp.flip(t, axis=1)
    y_f, s_f = ssd_scan(xs, dt[:, :, 0], a[0], bc[:, :, 0, 0], bc[:, :, 0, 1], state0[0])
    y_b, s_b = ssd_scan(rev(xs), rev(dt[:, :, 1]), a[1], rev(bc[:, :, 1, 0]), rev(bc[:, :, 1, 1]), state0[1])
    d_tot = d_skip.astype(jnp.float32).sum(0).reshape(SSD_GROUPS, SSD_HPG)
    y = y_f + rev(y_b) + d_tot[..., None] * xs
    y = y.reshape(bsz, seq, SSD_INNER).astype(h.dtype) * jax.nn.silu(z)
    return rmsnorm(y, norm_g), jnp.stack([s_f, s_b])


def split_qkv(h, w_qkv):
    bsz, seq, _ = h.shape
    qkv = (h @ w_qkv).reshape(bsz, seq, 3, NA_HEADS, NA_HEADDIM)
    return qkv[:, :, 0], qkv[:, :, 1], qkv[:, :, 2]


def context_attention(qc, kc, vc):
    bsz, n, _, _ = qc.shape
    s = jnp.einsum('bqhd,bkhd->bhqk', qc, kc) * NA_HEADDIM ** -0.5
    p = jax.nn.softmax(s.astype(jnp.float32), axis=-1).astype(vc.dtype)
    return jnp.einsum('bhqk,bkhd->bqhd', p, vc).reshape(bsz, n, D_MODEL)


def neighbourhood_attention(h, kc, vc, w_qkv, rpb):
    bsz, seq, _ = h.shape
    rows = seq // GRID_W
    wr = min(NA_WIN_ROWS, rows)
    wc = NA_WIN_COLS
    n_loc = wr * wc
    q, k, v = split_qkv(h, w_qkv)
    q, k, v = (t.reshape(bsz, rows, GRID_W, NA_HEADS, NA_HEADDIM) for t in (q, k, v))
    cols = jnp.arange(GRID_W)
    col_idx = jnp.clip(cols - wc // 2, 0, GRID_W - wc)[:, None] + jnp.arange(wc)
    rpb_col = jnp.take(rpb, col_idx - cols[:, None] + NA_WIN_COLS - 1, axis=2)
    scale = NA_HEADDIM ** -0.5

    def row_block(r):
        r0 = jnp.clip(r - wr // 2, 0, rows - wr)
        q_r = lax.dynamic_index_in_dim(q, r, axis=1, keepdims=False)
        k_w = jnp.take(lax.dynamic_slice_in_dim(k, r0, wr, axis=1), col_idx, axis=2)
        v_w = jnp.take(lax.dynamic_slice_in_dim(v, r0, wr, axis=1), col_idx, axis=2)
        bias = jnp.take(rpb_col, r0 + jnp.arange(wr) - r + NA_WIN_ROWS - 1, axis=1)
        s_loc = jnp.einsum('bqhd,bwqchd->bqhwc', q_r, k_w) * scale + jnp.transpose(bias, (2, 0, 1, 3))
        s_ctx = jnp.einsum('bqhd,bkhd->bqhk', q_r, kc) * scale
        s = jnp.concatenate([s_loc.reshape(bsz, GRID_W, NA_HEADS, n_loc), s_ctx], axis=-1)
        p = jax.nn.softmax(s.astype(jnp.float32), axis=-1).astype(v.dtype)
        p_loc = p[..., :n_loc].reshape(bsz, GRID_W, NA_HEADS, wr, wc)
        return (jnp.einsum('bqhwc,bwqchd->bqhd', p_loc, v_w)
                + jnp.einsum('bqhk,bkhd->bqhd', p[..., n_loc:], vc))

    out = lax.map(row_block, jnp.arange(rows))
    return jnp.transpose(out, (1, 0, 2, 3, 4)).reshape(bsz, seq, D_MODEL)


def short_conv_mixer(h, w_in, conv_w, w_out):
    b_gate, c_gate, u = jnp.split(h @ w_in, 3, axis=-1)
    return (b_gate * dwconv_centred(c_gate * u, conv_w)) @ w_out


def fourier_mixer(h, w_out):
    bsz, seq, _ = h.shape
    hg = h.astype(jnp.float32).reshape(bsz, seq, FN_GROUPS, FN_GROUP_CH)
    f = jnp.fft.fft2(hg, axes=(1, 3)).real.reshape(bsz, seq, D_MODEL)
    return f.astype(h.dtype) @ w_out


def setup_inputs(seed: int = 0) -> dict:
    key = jax.random.key(seed)
    kit = iter(list(jax.random.split(key, 32)))
    f32 = jnp.float32

    def nrm(shape, scale):
        return jax.random.normal(next(kit), shape, f32) * scale

    n_a, n_b, n_c, n_d = (len(range(k, DEPTH, N_MIXERS)) for k in range(N_MIXERS))
    dt0 = jnp.exp(jax.random.uniform(next(kit), (n_a, 2, SSD_HEADS), f32, math.log(1e-3), math.log(1e-1)))
    return {
        'x': nrm((BATCH, SEQ, D_MODEL), 1.0),
        'c': nrm((BATCH, D_MODEL), 1.0),
        'ctx': nrm((BATCH, CTX_LEN, D_MODEL), 1.0),
        'c_ctx': nrm((D_MODEL,), 1.0),
        'mod_w': nrm((DEPTH, D_MODEL, N_MOD * D_MODEL), 0.5 * D_MODEL ** -0.5),
        'mod_b': nrm((DEPTH, N_MOD * D_MODEL), 0.02),
        'norm_g': 1.0 + nrm((DEPTH, 4, D_MODEL), 0.05),
        'mlp_w1': nrm((DEPTH, D_MODEL, MLP_HIDDEN), D_MODEL ** -0.5),
        'mlp_w2': nrm((DEPTH, MLP_HIDDEN, D_MODEL), MLP_HIDDEN ** -0.5),
        'ssd_w_in': nrm((n_a, D_MODEL, SSD_IN), D_MODEL ** -0.5),
        'ssd_conv_w': nrm((n_a, SSD_CONV, SSD_CONV_CH), SSD_CONV ** -0.5),
        'ssd_conv_b': nrm((n_a, SSD_CONV_CH), 0.02),
        'ssd_dt_bias': dt0 + jnp.log(-jnp.expm1(-dt0)),
        'ssd_a_log': jnp.log(jax.random.uniform(next(kit), (n_a, 2, SSD_HEADS), f32, 1.0, 16.0)),
        'ssd_d': 1.0 + nrm((n_a, 2, SSD_HEADS), 0.1),
        'ssd_norm_g': 1.0 + nrm((n_a, SSD_INNER), 0.05),
        'ssd_w_out': nrm((n_a, SSD_INNER, D_MODEL), SSD_INNER ** -0.5),
        'na_w_qkv': nrm((n_b, D_MODEL, 3 * D_MODEL), D_MODEL ** -0.5),
        'na_rpb': nrm((n_b, NA_HEADS, 2 * NA_WIN_ROWS - 1, 2 * NA_WIN_COLS - 1), 0.1),
        'na_w_out': nrm((n_b, D_MODEL, D_MODEL), D_MODEL ** -0.5),
        'sc_w_in': nrm((n_c, D_MODEL, 3 * D_MODEL), D_MODEL ** -0.5),
        'sc_conv_w': nrm((n_c, SC_CONV, D_MODEL), SC_CONV ** -0.5),
        'sc_w_out': nrm((n_c, D_MODEL, D_MODEL), D_MODEL ** -0.5),
        'fn_w_out': nrm((n_d, D_MODEL, D_MODEL), D_MODEL ** -0.5),
    }


def reference(x, c, ctx, c_ctx, mod_w, mod_b, norm_g, mlp_w1, mlp_w2, ssd_w_in, ssd_conv_w, ssd_conv_b,
              ssd_dt_bias, ssd_a_log, ssd_d, ssd_norm_g, ssd_w_out, na_w_qkv, na_rpb, na_w_out,
              sc_w_in, sc_conv_w, sc_w_out, fn_w_out):
    c_act = jax.nn.silu(c)
    c_ctx_act = jax.nn.silu(c_ctx)
    for i in range(DEPTH):
        kind, j = i % N_MIXERS, i // N_MIXERS
        ctx_needed_later = any((l % N_MIXERS) in CTX_READING_MIXERS for l in range(i + 1, DEPTH))
        m_lat = [m[:, None, :] for m in jnp.split(c_act @ mod_w[i] + mod_b[i], N_MOD, axis=-1)]
        m_ctx = jnp.split(c_ctx_act @ mod_w[i] + mod_b[i], N_MOD, axis=-1)
        g = norm_g[i]
        h = modulate(rmsnorm(x, g[0]), m_lat[0], m_lat[1])
        hc = None
        if kind in CTX_READING_MIXERS or ctx_needed_later:
            hc = modulate(rmsnorm(ctx, g[0]), m_ctx[0], m_ctx[1])
        y_ctx = None
        if kind == 0:
            prm = (ssd_w_in[j], ssd_conv_w[j], ssd_conv_b[j], ssd_dt_bias[j], ssd_a_log[j], ssd_d[j], ssd_norm_g[j])
            zero_state = jnp.zeros((2, hc.shape[0], SSD_GROUPS, SSD_HPG, SSD_HEADDIM, SSD_STATE), jnp.float32)
            yc_inner, ctx_states = ssd_mixer(hc, *prm, zero_state)
            y_inner, _ = ssd_mixer(h, *prm, ctx_states)
            y = y_inner @ ssd_w_out[j]
            if ctx_needed_later:
                y_ctx = yc_inner @ ssd_w_out[j]
        elif kind == 1:
            qc, kc, vc = split_qkv(hc, na_w_qkv[j])
            y = neighbourhood_attention(h, kc, vc, na_w_qkv[j], na_rpb[j]) @ na_w_out[j]
            if ctx_needed_later:
                y_ctx = context_attention(qc, kc, vc) @ na_w_out[j]
        elif kind == 2:
            y = short_conv_mixer(h, sc_w_in[j], sc_conv_w[j], sc_w_out[j])
            if ctx_needed_later:
                y_ctx = short_conv_mixer(hc, sc_w_in[j], sc_conv_w[j], sc_w_out[j])
        else:
            y = fourier_mixer(h, fn_w_out[j])
            if ctx_needed_later:
                y_ctx = fourier_mixer(hc, fn_w_out[j])
        x = finish_layer(x, y, m_lat, g, mlp_w1[i], mlp_w2[i])
        if ctx_needed_later:
            ctx = finish_layer(ctx, y_ctx, m_ctx, g, mlp_w1[i], mlp_w2[i])
    return x
```

```python
import numpy as np
from contextlib import ExitStack
import concourse.bass as bass
import concourse.mybir as mybir
from concourse.bass_utils import run_bass_kernel_spmd

F32 = mybir.dt.float32
BF16 = mybir.dt.bfloat16
AF = mybir.ActivationFunctionType
ALU = mybir.AluOpType
AX = mybir.AxisListType

ENGS = ("pe", "dve", "act", "pool", "sp")
N_DMA_SEMS = 40
NCORES = 8

D = 1024
SEQ = 8192
BATCH = 2
CTX = 256
HID = 4096
EPS = 1e-6
ARENA_WORDS = 52736
CAST_ENGS = ("dve", "pool", "act", "dve", "pool")


class T:
    __slots__ = ("ap", "w", "r", "name")

    def __init__(self, ap, name=""):
        self.ap = ap
        self.w = None
        self.r = []
        self.name = name

    def __getitem__(self, k):
        return self.ap[k]


class Sched:
    def __init__(self, nc, same_engine_sync=True):
        self.nc = nc
        self.es = ExitStack()
        self.q = {e: [] for e in ENGS}
        self.cnt = {e: 0 for e in ENGS}
        self.prog = {e: self.es.enter_context(nc.semaphore("prog_" + e)) for e in ENGS}
        self.dsem = [self.es.enter_context(nc.semaphore("dma%d" % i)) for i in range(N_DMA_SEMS)]
        self.dval = [0] * N_DMA_SEMS
        self.dnext = 0
        self.known = {e: {} for e in ENGS}
        self.same_engine_sync = same_engine_sync
        self.sem_owner = {id(self.prog[e]): e for e in ENGS}
        self.out_events = []
        self.n_sb = 0
        self.arena = None
        self.aoff = 0
        self.amax = 0
        self.swsem = {}
        self.swused = {}

    def tile(self, shape, dtype, name=None):
        if self.arena is None:
            self.arena = self.es.enter_context(self.nc.sbuf_tensor("arena", [128, ARENA_WORDS], F32))
            self.aoff = 0
        esz = 2 if dtype == BF16 else 4
        n = 1
        for d in shape[1:]:
            n *= d
        words = (n * esz + 3) // 4
        words = (words + 7) // 8 * 8
        if self.aoff + words > ARENA_WORDS:
            raise RuntimeError("SBUF arena overflow: need %d words at %d" % (words, self.aoff))
        ap = self.arena[:, self.aoff:self.aoff + (n * esz + 3) // 4]
        self.aoff += words
        self.amax = max(self.amax, self.aoff)
        if dtype != F32:
            ap = ap.bitcast(dtype)
        ap = ap[0:shape[0], 0:n]
        if len(shape) >= 3:
            names = ["d%d" % i for i in range(len(shape) - 1)]
            kw = {names[i]: shape[1 + i] for i in range(len(shape) - 2)}
            ap = ap.rearrange("p (%s) -> p %s" % (" ".join(names), " ".join(names)), **kw)
        return T(ap, name or "")

    def ptile(self, shape=(128, 512), dtype=F32, name=None):
        self.n_sb += 1
        return T(self.es.enter_context(self.nc.psum_tensor(name or ("ps%d" % self.n_sb), list(shape), dtype)), name or "")

    def _deps(self, eng, reads, writes):
        evs = []
        for t in reads:
            if t.w is not None:
                evs.append(t.w)
        for t in writes:
            if t.w is not None:
                evs.append(t.w)
            evs.extend(t.r)
        need = {}
        for (sem, val) in evs:
            owner = self.sem_owner.get(id(sem))
            if owner == eng and (eng == "pe" or not self.same_engine_sync):
                continue
            k = id(sem)
            if self.known[eng].get(k, 0) >= val:
                continue
            if k not in need or need[k][1] < val:
                need[k] = (sem, val)
        for k, (sem, val) in need.items():
            self.known[eng][k] = val
        return list(need.values())

    def _commit(self, ev, reads, writes):
        for t in reads:
            t.r.append(ev)
            if len(t.r) > 64:
                best = {}
                for (sem, val) in t.r:
                    if id(sem) not in best or best[id(sem)][1] < val:
                        best[id(sem)] = (sem, val)
                t.r = list(best.values())
        for t in writes:
            t.w = ev
            t.r = []

    def op(self, eng, fn, reads=(), writes=()):
        waits = self._deps(eng, reads, writes)
        self.cnt[eng] += 1
        ev = (self.prog[eng], self.cnt[eng])
        self.q[eng].append((waits, fn, (self.prog[eng], 1)))
        self._commit(ev, reads, writes)
        return ev

    def dma(self, eng, out_ap, in_ap, reads=(), writes=(), is_output=False, sub=0, **kw):
        if eng == "pool":
            return self._dma_sw(out_ap, in_ap, reads, writes, is_output, sub, kw)
        i = self.dnext
        self.dnext = (self.dnext + 1) % N_DMA_SEMS
        sem = self.dsem[i]
        waits = self._deps(eng, reads, writes)
        if self.dval[i] > 0 and self.known[eng].get(id(sem), 0) < self.dval[i]:
            waits.append((sem, self.dval[i]))
            self.known[eng][id(sem)] = self.dval[i]
        self.dval[i] += 16
        ev = (sem, self.dval[i])

        def fn(e, out_ap=out_ap, in_ap=in_ap, kw=kw):
            return e.dma_start(out=out_ap, in_=in_ap, **kw)
        self.q[eng].append((waits, fn, (sem, 16)))
        self._commit(ev, reads, writes)
        if is_output:
            self.out_events.append(ev)
        return ev

    def _dma_sw(self, out_ap, in_ap, reads, writes, is_output, sub, kw):
        eng = "pool"
        slot = writes[0]
        key = (id(slot), sub)
        if key not in self.swsem:
            self.swsem[key] = self.es.enter_context(self.nc.semaphore("sw%d" % len(self.swsem)))
            self.swused[key] = False
        sem = self.swsem[key]
        waits = self._deps(eng, reads, writes)
        reuse = self.swused[key]
        if reuse and self.known[eng].get(id(sem), 0) < 16:
            waits.append((sem, 16))
        for e in ENGS:
            self.known[e].pop(id(sem), None)
        self.swused[key] = True
        ev = (sem, 16)

        def fn(e, out_ap=out_ap, in_ap=in_ap, kw=kw, sem=sem, reuse=reuse):
            if reuse:
                e.sem_clear(sem)
            return e.dma_start(out=out_ap, in_=in_ap, **kw)
        self.q[eng].append((waits, fn, (sem, 16)))
        self._commit(ev, reads, writes)
        if is_output:
            self.out_events.append(ev)
        return ev

    def barrier(self):
        waits = []
        for key, sem in self.swsem.items():
            if self.swused[key] and self.known["pool"].get(id(sem), 0) < 16:
                waits.append((sem, 16))
                self.known["pool"][id(sem)] = 16
        assert not waits
        for e in ENGS:
            waits = []
            for f in ENGS:
                if f != e and self.cnt[f] > self.known[e].get(id(self.prog[f]), 0):
                    waits.append((self.prog[f], self.cnt[f]))
                    self.known[e][id(self.prog[f])] = self.cnt[f]
            for i in range(N_DMA_SEMS):
                if self.dval[i] > self.known[e].get(id(self.dsem[i]), 0):
                    waits.append((self.dsem[i], self.dval[i]))
                    self.known[e][id(self.dsem[i])] = self.dval[i]
            if waits:
                self.q[e].append((waits, None, None))

    def scope(self):
        return _Scope(self)

    def emit(self):
        nc = self.nc
        seen = {}
        for (sem, val) in self.out_events:
            if seen.get(id(sem), (None, 0))[1] < val:
                seen[id(sem)] = (sem, val)
        fin = list(seen.values())
        engmap = {"pe": "tensor", "dve": "vector", "act": "scalar", "pool": "gpsimd", "sp": "sync"}
        with nc.Block() as block:
            for e in ENGS:
                q = self.q[e]
                is_sp = (e == "sp")

                def body(eng, q=q, is_sp=is_sp):
                    for waits, fn, inc in q:
                        for (sem, val) in waits:
                            eng.wait_ge(sem, val)
                        if fn is None:
                            continue
                        ins = fn(eng)
                        ins.then_inc(inc[0], inc[1])
                    if is_sp:
                        for (sem, val) in fin:
                            eng.wait_ge(sem, val)
                getattr(block, engmap[e])(body)
        self.es.close()


class _Scope:
    def __init__(self, s):
        self.s = s

    def __enter__(self):
        self.saved = self.s.aoff
        return self

    def __exit__(self, *a):
        self.s.barrier()
        self.s.aoff = self.saved
        return False


class KX:
    def __init__(self, nc, npsum=8, nwbuf=3, wbuf_elems=4096):
        self.nc = nc
        self.s = Sched(nc)
        s = self.s
        self.ones = s.tile([128, 128], BF16, "ones")
        self.eps = s.tile([128, 1], F32, "eps")
        s.op("dve", lambda e: e.memset(self.ones[:], 1.0), writes=[self.ones])
        s.op("dve", lambda e: e.memset(self.eps[:], EPS), writes=[self.eps])
        self.ps = [s.ptile(name="psb%d" % i) for i in range(npsum)]
        self.psi = 0
        self.wb = [s.tile([128, wbuf_elems], BF16, "wbuf%d" % i) for i in range(nwbuf)]
        self.wbi = 0
        self.sqs = [s.tile([128, 512], BF16, "sqbuf%d" % i) for i in range(2)]
        self.stg = [s.tile([128, 2048], F32, "stage%d" % i) for i in range(2)]
        self.stgi = 0
        self.dmaq = ("sp", "act")
        self.dqi = 0
        self.cast_engs = CAST_ENGS
        self.cei = 0
        self.rs = [s.tile([128, 512], F32, "rstd%d" % i) for i in range(2)]
        self.rsi = 0
        self.tc = [s.tile([128, 512], F32, "tmpc%d" % i) for i in range(3)]
        self.tci = 0

    def nps(self):
        t = self.ps[self.psi]
        self.psi = (self.psi + 1) % len(self.ps)
        return t

    def ntc(self):
        t = self.tc[self.tci]
        self.tci = (self.tci + 1) % len(self.tc)
        return t

    def nwb(self):
        t = self.wb[self.wbi]
        self.wbi = (self.wbi + 1) % len(self.wb)
        return t

    def stage_cast(self, dst_T, dst_ap, src_ap, n):
        s = self.s
        st = self.stg[self.stgi]
        self.stgi = (self.stgi + 1) % len(self.stg)
        q = self.dmaq[self.dqi]
        self.dqi = (self.dqi + 1) % len(self.dmaq)
        shp = list(src_ap.shape)
        sv = st.ap[:, 0:n]
        if len(shp) == 3:
            sv = sv.rearrange("p (a b) -> p a b", a=shp[1])
        s.dma(q, sv, src_ap, writes=[st])
        ce = self.cast_engs[self.cei]
        self.cei = (self.cei + 1) % len(self.cast_engs)
        if ce == "act":
            s.op("act", lambda e: e.activation(out=dst_ap, in_=sv, func=AF.Identity), reads=[st], writes=[dst_T])
        else:
            s.op(ce, lambda e: e.tensor_copy(out=dst_ap, in_=sv), reads=[st], writes=[dst_T])

    def wload(self, w_ap, kc, ncols):
        t = self.nwb()
        view = t.ap[:, 0:kc * ncols].rearrange("p (c n) -> p c n", c=kc)
        src = w_ap.rearrange("(c p) n -> p c n", p=128)
        per = max(1, 2048 // ncols)
        for c0 in range(0, kc, per):
            c1 = min(kc, c0 + per)
            self.stage_cast(t, view[:, c0:c1, :], src[:, c0:c1, :], (c1 - c0) * ncols)
        return t, view

    def rstd(self, src_T, src_ap, n, nchunks=8, dim=D):
        s = self.s
        ps = self.nps()
        for c in range(nchunks):
            sq = self.sqs[c % 2]
            s.op("act", lambda e, c=c, sq=sq: e.activation(out=sq[:, 0:n], in_=src_ap[:, c, :], func=AF.Square), reads=[src_T], writes=[sq])
            s.op("pe", lambda e, c=c, sq=sq: e.matmul(ps[:, 0:n], lhsT=self.ones[:], rhs=sq[:, 0:n], start=(c == 0), stop=(c == nchunks - 1)),
                 reads=[self.ones, sq], writes=[ps])
        r = self.rs[self.rsi]
        self.rsi = (self.rsi + 1) % len(self.rs)
        s.op("act", lambda e: e.activation(out=r[:, 0:n], in_=ps[:, 0:n], func=AF.Ln, bias=self.eps[:], scale=1.0 / dim),
             reads=[ps, self.eps], writes=[r])
        s.op("act", lambda e: e.activation(out=r[:, 0:n], in_=r[:, 0:n], func=AF.Exp, scale=-0.5), reads=[r], writes=[r])
        return r

    def norm_mod(self, src_T, src_ap, n, ms, which, dst_T, dst_ap, tmp_T):
        s = self.s
        A, S = (ms.A0, ms.S0) if which == 0 else (ms.A2, ms.S2)
        r = self.rstd(src_T, src_ap, n)
        for c in range(8):
            tc = self.ntc()
            s.op("dve", lambda e, c=c, tc=tc: e.tensor_tensor(out=tc[:, 0:n], in0=src_ap[:, c, :], in1=r[:, 0:n], op=ALU.mult),
                 reads=[src_T, r], writes=[tc])
            s.op("act", lambda e, c=c, tc=tc: e.activation(out=dst_ap[:, c, :], in_=tc[:, 0:n], func=AF.Identity,
                                                     bias=S[:, c:c + 1], scale=A[:, c:c + 1]),
                 reads=[tc, ms.mv], writes=[dst_T])

    def resid_add(self, y_T, y_ap, n, ms, which, x_T, x_ap, tmp_T):
        s = self.s
        G = ms.G1 if which == 1 else ms.G2
        r = self.rstd(y_T, y_ap, n)
        for c in range(8):
            tc = self.ntc()
            s.op("dve", lambda e, c=c, tc=tc: e.tensor_tensor(out=tc[:, 0:n], in0=y_ap[:, c, :], in1=r[:, 0:n], op=ALU.mult),
                 reads=[y_T, r], writes=[tc])
            s.op("dve", lambda e, c=c, tc=tc: e.scalar_tensor_tensor(out=x_ap[:, c, :], in0=tc[:, 0:n], scalar=G[:, c:c + 1],
                                                               in1=x_ap[:, c, :], op0=ALU.mult, op1=ALU.add),
                 reads=[tc, x_T, ms.mv], writes=[x_T])

    def prep_mod(self, mvec_ap, gvec_ap, name="mv"):
        s = self.s
        mv = s.tile([128, 8, 16], F32, name)
        s.dma("sp", mv[:, :, 0:6], mvec_ap.rearrange("(c p) n -> p c n", p=128), writes=[mv])
        s.dma("sp", mv[:, :, 6:10], gvec_ap.rearrange("(c p) n -> p c n", p=128), writes=[mv])
        s.op("dve", lambda e: e.scalar_tensor_tensor(out=mv[:, :, 10], in0=mv[:, :, 1], scalar=1.0, in1=mv[:, :, 6], op0=ALU.add, op1=ALU.mult),
             reads=[mv], writes=[mv])
        s.op("dve", lambda e: e.tensor_tensor(out=mv[:, :, 11], in0=mv[:, :, 2], in1=mv[:, :, 7], op=ALU.mult), reads=[mv], writes=[mv])
        s.op("dve", lambda e: e.scalar_tensor_tensor(out=mv[:, :, 12], in0=mv[:, :, 4], scalar=1.0, in1=mv[:, :, 8], op0=ALU.add, op1=ALU.mult),
             reads=[mv], writes=[mv])
        s.op("dve", lambda e: e.tensor_tensor(out=mv[:, :, 13], in0=mv[:, :, 5], in1=mv[:, :, 9], op=ALU.mult), reads=[mv], writes=[mv])
        ms = ModSet()
        ms.mv = mv
        ms.A0, ms.S0, ms.G1, ms.A2, ms.S2, ms.G2 = mv[:, :, 10], mv[:, :, 0], mv[:, :, 11], mv[:, :, 12], mv[:, :, 3], mv[:, :, 13]
        return ms

    def mlp(self, blocks, h2_tiles, w1_ap, w2_ap, hid_T, out_fn):
        s = self.s
        offs = [0]
        for b in blocks:
            offs.append(offs[-1] + b.w)
        for jb in range(HID // 512):
            wt, wv = self.wload(w1_ap[:, jb * 512:(jb + 1) * 512], 8, 512)
            for sub in range(4):
                hc = jb * 4 + sub
                for i, b in enumerate(blocks):
                    ps = self.nps()
                    for c in range(8):
                        s.op("pe", lambda e, c=c, ps=ps, i=i, b=b, sub=sub, wv=wv: e.matmul(
                            ps[:, 0:b.w], lhsT=wv[:, c, sub * 128:(sub + 1) * 128], rhs=h2_tiles[i][:, c, 0:b.w], start=(c == 0), stop=(c == 7)),
                            reads=[wt, h2_tiles[i]], writes=[ps])
                    dst = hid_T[:, hc, offs[i]:offs[i + 1]]
                    s.op("act", lambda e, ps=ps, dst=dst, b=b: e.activation(out=dst, in_=ps[:, 0:b.w], func=AF.Relu), reads=[ps], writes=[hid_T])
                    s.op("dve", lambda e, dst=dst: e.tensor_tensor(out=dst, in0=dst, in1=dst, op=ALU.mult), reads=[hid_T], writes=[hid_T])
        for db in range(D // 128):
            wt, wv = self.wload(w2_ap[:, db * 128:(db + 1) * 128], 32, 128)
            for i, b in enumerate(blocks):
                ps = self.nps()
                for c in range(32):
                    s.op("pe", lambda e, c=c, ps=ps, i=i, b=b, wv=wv: e.matmul(
                        ps[:, 0:b.w], lhsT=wv[:, c, :], rhs=hid_T[:, c, offs[i]:offs[i + 1]], start=(c == 0), stop=(c == 31)),
                        reads=[wt, hid_T], writes=[ps])
                out_fn(i, db, ps)


class ModSet:
    pass


class Blk:
    def __init__(self, w, ms, x_T, x_view, y_T):
        self.w, self.ms, self.x_T, self.x_view, self.y_T = w, ms, x_T, x_view, y_T


def _mk(nc, name, shape, dtype=F32, kind="ExternalInput"):
    return nc.dram_tensor(name, list(shape), dtype, kind=kind).ap()


def emit_finish(kx, blocks, w1_ap, w2_ap, tmp_T):
    s = kx.s
    for b in blocks:
        kx.resid_add(b.y_T, b.y_T[:, :, 0:b.w], b.w, b.ms, 1, b.x_T, b.x_view, tmp_T)
    with s.scope():
        h2_tiles = [s.tile([128, 8, b.w], BF16, "h2_%d" % i) for i, b in enumerate(blocks)]
        hid_T = s.tile([128, 32, sum(b.w for b in blocks)], BF16, "hid")
        for i, b in enumerate(blocks):
            kx.norm_mod(b.x_T, b.x_view, b.w, b.ms, 2, h2_tiles[i], h2_tiles[i][:, :, 0:b.w], tmp_T)

        def out_fn(i, dc, ps):
            b = blocks[i]
            s.op("act", lambda e: e.activation(out=b.y_T[:, dc, 0:b.w], in_=ps[:, 0:b.w], func=AF.Identity), reads=[ps], writes=[b.y_T])
        kx.mlp(blocks, h2_tiles, w1_ap, w2_ap, hid_T, out_fn)
    for b in blocks:
        kx.resid_add(b.y_T, b.y_T[:, :, 0:b.w], b.w, b.ms, 2, b.x_T, b.x_view, tmp_T)


def build_sc(NT=2048):
    nc = bass.Bass("TRN2", target_bir_lowering=False)
    xT = _mk(nc, "xT", [D, NT + 2])
    mvec = _mk(nc, "mvec", [D, 6])
    gvec = _mk(nc, "gvec", [D, 4])
    hmask = _mk(nc, "hmask", [128, 2])
    w_in = _mk(nc, "w_in", [D, 3 * D])
    convw = _mk(nc, "convw", [D, 3])
    w_out = _mk(nc, "w_out", [D, D])
    w1 = _mk(nc, "w1", [D, HID])
    w2 = _mk(nc, "w2", [HID, D])
    mvec3 = _mk(nc, "mvec3", [D, 6])
    gvec3 = _mk(nc, "gvec3", [D, 4])
    outT = _mk(nc, "outT", [D, NT], kind="ExternalOutput")
    h3T = _mk(nc, "h3T", [D, NT], kind="ExternalOutput")
    kx = KX(nc)
    s = kx.s
    ms = kx.prep_mod(mvec, gvec)
    ms3 = kx.prep_mod(mvec3, gvec3, "mv3")
    HWD = NT // 2
    bw = 512
    nblk = HWD // bw
    hm = s.tile([128, 2], F32, "hm")
    s.dma("sp", hm[:], hmask, writes=[hm])
    cw = s.tile([128, 8, 3], F32, "cw")
    s.dma("sp", cw[:], convw.rearrange("(c p) k -> p c k", p=128), writes=[cw])
    xh = s.tile([128, 8, HWD + 2], F32, "xh")
    tmp_T = None
    y_tiles = [s.tile([128, 8, bw], F32, "y%d" % i) for i in range(nblk)]
    w_in4 = w_in.rearrange("(c p) (t j n) -> p c t j n", p=128, t=3, j=8)

    class XV:
        pass
    for half in range(2):
        c0 = half * HWD
        s.dma("sp", xh[:], xT[:, c0:c0 + HWD + 2].rearrange("(c p) n -> p c n", p=128), writes=[xh])
        with s.scope():
            hT = [s.tile([128, 8, 342], BF16, "hT%d" % i) for i in range(3)]
            bcu = [s.tile([128, 3, HWD + 2], F32, "bcu%d" % i) for i in range(2)]
            acc = [s.tile([128, HWD], F32, "acc%d" % i) for i in range(2)]
            gT = [s.tile([128, 8, bw], BF16, "gT%d" % i) for i in range(nblk)]
            for i in range(3):
                kx.norm_mod(xh, xh[:, :, i * 342:(i + 1) * 342], 342, ms, 0, hT[i], hT[i][:, :, 0:342], tmp_T)
            for j in range(8):
                wt = kx.nwb()
                wv = wt.ap[:, 0:8 * 3 * 128].rearrange("p (c t n) -> p c t n", c=8, t=3)
                for t in range(3):
                    kx.stage_cast(wt, wv[:, :, t, :], w_in4[:, :, t, j, :], 1024)
                bc = bcu[j % 2]
                ac = acc[j % 2]
                for t in range(3):
                    for i in range(3):
                        ps = kx.nps()
                        for c in range(8):
                            s.op("pe", lambda e, c=c, ps=ps, i=i, t=t, wv=wv: e.matmul(
                                ps[:, 0:342], lhsT=wv[:, c, t, :], rhs=hT[i][:, c, :], start=(c == 0), stop=(c == 7)),
                                reads=[wt, hT[i]], writes=[ps])
                        s.op("act", lambda e, ps=ps, t=t, i=i, bc=bc: e.activation(out=bc[:, t, i * 342:(i + 1) * 342], in_=ps[:, 0:342], func=AF.Identity),
                             reads=[ps], writes=[bc])
                s.op("dve", lambda e, bc=bc: e.tensor_tensor(out=bc[:, 1, :], in0=bc[:, 1, :], in1=bc[:, 2, :], op=ALU.mult), reads=[bc], writes=[bc])
                hc_ = 0 if half == 0 else HWD + 1
                s.op("dve", lambda e, bc=bc, hc_=hc_, half=half: e.tensor_scalar(out=bc[:, 1, hc_:hc_ + 1], in0=bc[:, 1, hc_:hc_ + 1], scalar1=hm[:, half:half + 1],
                                                                    scalar2=None, op0=ALU.mult), reads=[bc, hm], writes=[bc])
                s.op("dve", lambda e, bc=bc, ac=ac, j=j: e.tensor_scalar(out=ac[:, :], in0=bc[:, 1, 0:HWD], scalar1=cw[:, j, 0:1], scalar2=None, op0=ALU.mult),
                     reads=[bc, cw], writes=[ac])
                for k in (1, 2):
                    s.op("dve", lambda e, bc=bc, ac=ac, j=j, k=k: e.scalar_tensor_tensor(out=ac[:, :], in0=bc[:, 1, k:k + HWD], scalar=cw[:, j, k:k + 1],
                                                                                      in1=ac[:, :], op0=ALU.mult, op1=ALU.add),
                         reads=[bc, cw, ac], writes=[ac])
                for tb in range(nblk):
                    s.op("dve", lambda e, bc=bc, ac=ac, j=j, tb=tb: e.tensor_tensor(out=gT[tb][:, j, :], in0=ac[:, tb * bw:(tb + 1) * bw],
                                                                                 in1=bc[:, 0, 1 + tb * bw:1 + (tb + 1) * bw], op=ALU.mult),
                         reads=[ac, bc], writes=[gT[tb]])
            for db in range(2):
                wt, wv = kx.wload(w_out[:, db * 512:(db + 1) * 512], 8, 512)
                for sub in range(4):
                    dc = db * 4 + sub
                    for tb in range(nblk):
                        ps = kx.nps()
                        for c in range(8):
                            s.op("pe", lambda e, c=c, ps=ps, tb=tb, sub=sub, wv=wv: e.matmul(
                                ps[:, 0:bw], lhsT=wv[:, c, sub * 128:(sub + 1) * 128], rhs=gT[tb][:, c, :], start=(c == 0), stop=(c == 7)),
                                reads=[wt, gT[tb]], writes=[ps])
                        s.op("act", lambda e, ps=ps, dc=dc, tb=tb: e.activation(out=y_tiles[tb][:, dc, :], in_=ps[:, 0:bw], func=AF.Identity),
                             reads=[ps], writes=[y_tiles[tb]])
        x_views = [xh[:, :, 1 + tb * bw:1 + (tb + 1) * bw] for tb in range(nblk)]
        blocks = [Blk(bw, ms, xh, x_views[tb], y_tiles[tb]) for tb in range(nblk)]
        emit_finish(kx, blocks, w1, w2, tmp_T)
        for tb in range(nblk):
            s.dma("sp", outT[:, c0 + tb * bw:c0 + (tb + 1) * bw].rearrange("(c p) n -> p c n", p=128), x_views[tb], reads=[xh], is_output=True)
            kx.norm_mod(xh, x_views[tb], bw, ms3, 0, y_tiles[tb], y_tiles[tb][:, :, 0:bw], None)
            s.dma("act", h3T[:, c0 + tb * bw:c0 + (tb + 1) * bw].rearrange("(c p) n -> p c n", p=128), y_tiles[tb][:, :, 0:bw], reads=[y_tiles[tb]], is_output=True)
    s.emit()
    return nc


def _halo_T(xb, t0, n, lo=1, hi=1):
    L = xb.shape[0]
    out = np.zeros((D, lo + n + hi), np.float32)
    a = max(t0 - lo, 0)
    b = min(t0 + n + hi, L)
    out[:, a - (t0 - lo):b - (t0 - lo)] = xb[a:b].T
    return out


def run_sc(x, m_lat, g, w_in, convw, w_out, w1, w2, m_lat3, g3):
    NT = 2048
    nc = build_sc(NT)
    ins = []
    for core in range(NCORES):
        b, k = divmod(core, 4)
        t0 = k * NT
        hm = np.ones((128, 2), np.float32)
        if k == 0:
            hm[:, 0] = 0
        if k == 3:
            hm[:, 1] = 0
        ins.append({
            "xT": _halo_T(x[b], t0, NT), "mvec": np.ascontiguousarray(m_lat[b].reshape(6, D).T), "gvec": np.ascontiguousarray(g.T),
            "hmask": hm, "w_in": w_in, "convw": np.ascontiguousarray(convw.T), "w_out": w_out, "w1": w1, "w2": w2,
            "mvec3": np.ascontiguousarray(m_lat3[b].reshape(6, D).T), "gvec3": np.ascontiguousarray(g3.T)})
    res = run_bass_kernel_spmd(nc, ins, core_ids=list(range(NCORES)))
    out = np.empty_like(x)
    h3 = np.empty_like(x)
    for core in range(NCORES):
        b, k = divmod(core, 4)
        out[b, k * NT:(k + 1) * NT] = res.results[core]["outT"].T
        h3[b, k * NT:(k + 1) * NT] = res.results[core]["h3T"].T
    return out, h3


def build_mod():
    nc = bass.Bass("TRN2", target_bir_lowering=False)
    ccT = _mk(nc, "ccT", [D, 3])
    w = _mk(nc, "w", [D, 3072])
    bvec = _mk(nc, "bvec", [128, 24])
    outT = _mk(nc, "outT", [3072, 3], kind="ExternalOutput")
    s = Sched(nc)
    cc = s.tile([128, 8, 3], F32, "cc")
    bt = s.tile([128, 24], F32, "bt")
    ot = s.tile([128, 24, 3], F32, "ot")
    ps = [s.ptile(name="ps%d" % i) for i in range(4)]
    wb = [s.tile([128, 8, 512], F32, "wb%d" % i) for i in range(3)]
    s.dma("sp", cc[:], ccT.rearrange("(c p) n -> p c n", p=128), writes=[cc])
    s.dma("sp", bt[:], bvec, writes=[bt])
    s.op("act", lambda e: e.activation(out=cc[:], in_=cc[:], func=AF.Silu), reads=[cc], writes=[cc])
    for jb in range(6):
        wt = wb[jb % 3]
        s.dma("sp" if jb % 2 == 0 else "act", wt[:], w[:, jb * 512:(jb + 1) * 512].rearrange("(c p) n -> p c n", p=128), writes=[wt])
        for sub in range(4):
            oc = jb * 4 + sub
            p = ps[oc % 4]
            for c in range(8):
                s.op("pe", lambda e, c=c, p=p, sub=sub, wt=wt: e.matmul(p[:, 0:3], lhsT=wt[:, c, sub * 128:(sub + 1) * 128], rhs=cc[:, c, :],
                                                                    start=(c == 0), stop=(c == 7)), reads=[wt, cc], writes=[p])
            s.op("act", lambda e, p=p, oc=oc: e.activation(out=ot[:, oc, :], in_=p[:, 0:3], func=AF.Identity, bias=bt[:, oc:oc + 1], scale=1.0),
                 reads=[p, bt], writes=[ot])
    s.dma("sp", outT.rearrange("(c p) n -> p c n", p=128), ot[:], reads=[ot], is_output=True)
    s.emit()
    return nc


def run_mod(c, c_ctx, mod_w, mod_b):
    nc = build_mod()
    ccT = np.ascontiguousarray(np.concatenate([c, c_ctx[None]], 0).T)
    ins = []
    for core in range(NCORES):
        i, hf = divmod(core, 2)
        ins.append({"ccT": ccT, "w": np.ascontiguousarray(mod_w[i][:, hf * 3072:(hf + 1) * 3072]),
                    "bvec": np.ascontiguousarray(mod_b[i][hf * 3072:(hf + 1) * 3072].reshape(24, 128).T)})
    res = run_bass_kernel_spmd(nc, ins, core_ids=list(range(NCORES)))
    m = np.zeros((4, 3, 6144), np.float32)
    for core in range(NCORES):
        i, hf = divmod(core, 2)
        m[i, :, hf * 3072:(hf + 1) * 3072] = res.results[core]["outT"].T
    return m[:, 0:2], m[:, 2]


def _fft_consts():
    c = np.arange(128)
    ang = 2 * np.pi * np.outer(c, c) / 128.0
    fc_cos, fc_sin = np.cos(ang), np.sin(ang)
    f1 = np.concatenate([fc_cos, -fc_sin], 1).astype(np.float32)
    f2 = np.concatenate([fc_sin, fc_cos], 1).astype(np.float32)
    t2 = np.arange(64)[:, None, None]
    k1 = np.arange(128)[None, :, None]
    k2 = np.arange(64)[None, None, :]
    ang3 = 2 * np.pi * (k1 * t2 / 8192.0 + k2 * t2 / 64.0)
    g = np.stack([np.cos(ang3), np.sin(ang3)], 2).astype(np.float32)
    return f1, f2, np.ascontiguousarray(g.reshape(64, 128 * 2 * 64))


def build_fft():
    nc = bass.Bass("TRN2", target_bir_lowering=False)
    hg = _mk(nc, "hg", [2, 128, 8192])
    f1 = _mk(nc, "f1", [128, 256])
    f2 = _mk(nc, "f2", [128, 256])
    g3 = _mk(nc, "g3", [64, 128 * 128])
    fo = _mk(nc, "fo", [2, 64, 128 * 128], kind="ExternalOutput")
    s = Sched(nc)
    f1t = s.tile([128, 256], BF16, "f1t")
    f2t = s.tile([128, 256], BF16, "f2t")
    g3t = s.tile([64, 128, 2, 64], BF16, "g3t")
    stg = [s.tile([128, 2048], F32, "stage%d" % i) for i in range(2)]
    stgi = [0]

    def stage_cast(dst_T, dst_ap, src_ap, np_, n):
        st = stg[stgi[0] % 2]
        s.dma("sp" if stgi[0] % 2 == 0 else "act", st[0:np_, 0:n], src_ap, writes=[st])
        stgi[0] += 1
        s.op("pool", lambda e: e.tensor_copy(out=dst_ap, in_=st[0:np_, 0:n]), reads=[st], writes=[dst_T])
    stage_cast(f1t, f1t[:], f1, 128, 256)
    stage_cast(f2t, f2t[:], f2, 128, 256)
    for q in range(8):
        stage_cast(g3t, g3t[:, q * 16:(q + 1) * 16, :, :].rearrange("p a r k -> p (a r k)"), g3[:, q * 2048:(q + 1) * 2048], 64, 2048)
    ps = [s.ptile(name="ps%d" % i) for i in range(8)]
    psi = [0]

    def nps():
        t = ps[psi[0]]
        psi[0] = (psi[0] + 1) % 8
        return t
    xin = s.tile([128, 64, 128], BF16, "xin")
    B = s.tile([128, 64, 2, 128], BF16, "B")
    A = s.tile([64, 2, 128, 128], BF16, "A")
    ob = [s.tile([64, 16, 128], F32, "ob%d" % i) for i in range(2)]
    for g in range(2):
        for q in range(4):
            stage_cast(xin, xin[:, q * 16:(q + 1) * 16, :].rearrange("p a b -> p (a b)"), hg[g][:, q * 2048:(q + 1) * 2048], 128, 2048)
        for tp in range(32):
            p = nps()
            for u in range(2):
                t2 = tp * 2 + u
                s.op("pe", lambda e, p=p, u=u, t2=t2: e.matmul(p[:, u * 256:(u + 1) * 256], lhsT=xin[:, t2, :], rhs=f1t[:], start=True, stop=True),
                     reads=[xin, f1t], writes=[p])
            eng = "act" if tp % 2 == 0 else "dve"
            dst = B[:, tp * 2:tp * 2 + 2, :, :].rearrange("p a r m -> p (a r m)")
            if eng == "act":
                s.op("act", lambda e, p=p, dst=dst: e.activation(out=dst, in_=p[:], func=AF.Identity), reads=[p], writes=[B])
            else:
                s.op("dve", lambda e, p=p, dst=dst: e.tensor_copy(out=dst, in_=p[:]), reads=[p], writes=[B])
        for mp in range(64):
            p = nps()
            for u in range(2):
                m = mp * 2 + u
                s.op("pe", lambda e, p=p, u=u, m=m: e.matmul(p[0:64, u * 256:(u + 1) * 256], lhsT=B[:, :, 0, m], rhs=f1t[:], start=True, stop=False),
                     reads=[B, f1t], writes=[p])
                s.op("pe", lambda e, p=p, u=u, m=m: e.matmul(p[0:64, u * 256:(u + 1) * 256], lhsT=B[:, :, 1, m], rhs=f2t[:], start=False, stop=True),
                     reads=[B, f2t], writes=[p])
            for u in range(2):
                m = mp * 2 + u
                src = p[0:64, u * 256:(u + 1) * 256].rearrange("p (r k) -> p r k", r=2)
                if u == 0:
                    s.op("act", lambda e, src=src, m=m: e.activation(out=A[:, :, :, m], in_=src, func=AF.Identity), reads=[p], writes=[A])
                else:
                    s.op("dve", lambda e, src=src, m=m: e.tensor_copy(out=A[:, :, :, m], in_=src), reads=[p], writes=[A])
        for kb in range(8):
            o = ob[kb % 2]
            for kq in range(4):
                p = nps()
                for u in range(4):
                    k1 = kb * 16 + kq * 4 + u
                    s.op("pe", lambda e, p=p, u=u, k1=k1: e.matmul(p[0:64, u * 128:(u + 1) * 128], lhsT=g3t[:, k1, 0, :], rhs=A[:, 0, k1, :], start=True, stop=False),
                         reads=[g3t, A], writes=[p])
                    s.op("pe", lambda e, p=p, u=u, k1=k1: e.matmul(p[0:64, u * 128:(u + 1) * 128], lhsT=g3t[:, k1, 1, :], rhs=A[:, 1, k1, :], start=False, stop=True),
                         reads=[g3t, A], writes=[p])
                dst = o[:, kq * 4:(kq + 1) * 4, :].rearrange("p a m -> p (a m)")
                if kq % 2 == 0:
                    s.op("act", lambda e, p=p, dst=dst: e.activation(out=dst, in_=p[0:64, :], func=AF.Identity), reads=[p], writes=[o])
                else:
                    s.op("dve", lambda e, p=p, dst=dst: e.tensor_copy(out=dst, in_=p[0:64, :]), reads=[p], writes=[o])
            s.dma("sp", fo[g][:, kb * 16 * 128:(kb + 1) * 16 * 128], o[:].rearrange("p a m -> p (a m)"), reads=[o], is_output=True)
    s.emit()
    return nc


def run_fft(h):
    nc = build_fft()
    f1, f2, g3 = _fft_consts()
    ins = []
    for core in range(NCORES):
        b, gp = divmod(core, 4)
        hb = np.asarray(h[b][:, gp * 256:(gp + 1) * 256]).reshape(128, 64, 2, 128)
        hgc = np.ascontiguousarray(hb.transpose(2, 3, 1, 0)).reshape(2, 128, 8192)
        ins.append({"hg": hgc.astype(np.float32), "f1": f1, "f2": f2, "g3": g3})
    res = run_bass_kernel_spmd(nc, ins, core_ids=list(range(NCORES)))
    out = np.empty((BATCH, SEQ, D), np.float32)
    for core in range(NCORES):
        b, gp = divmod(core, 4)
        fo = res.results[core]["fo"].reshape(2, 64, 128, 128)
        out[b, :, gp * 256:(gp + 1) * 256] = fo.transpose(1, 2, 0, 3).reshape(8192, 256)
    return out


NA_ROWS_LOCAL = 39


def build_na(NT=2048):
    nc = bass.Bass("TRN2", target_bir_lowering=False)
    xhT = _mk(nc, "xhT", [D, NA_ROWS_LOCAL * 64])
    ctxT = _mk(nc, "ctxT", [D, CTX])
    mvec = _mk(nc, "mvec", [D, 6])
    mvec_c = _mk(nc, "mvec_c", [D, 6])
    gvec = _mk(nc, "gvec", [D, 4])
    w_qkv = _mk(nc, "w_qkv", [D, 3 * D])
    w_out = _mk(nc, "w_out", [D, D])
    w1 = _mk(nc, "w1", [D, HID])
    w2 = _mk(nc, "w2", [HID, D])
    btab = _mk(nc, "btab", [8, 2, 128, 3840])
    identd = _mk(nc, "ident", [128, 128])
    outT = _mk(nc, "outT", [D, NT], kind="ExternalOutput")
    kx = KX(nc)
    s = kx.s
    ms = kx.prep_mod(mvec, gvec)
    msc = kx.prep_mod(mvec_c, gvec, "mvc")
    ident = s.tile([128, 128], BF16, "ident")
    kx.stage_cast(ident, ident[:], identd, 128 * 128 // 128)
    HWD, bw, nblk = 1024, 512, 2
    WIN = 23 * 64
    xres = s.tile([128, 8, HWD], F32, "xres")
    y_tiles = [s.tile([128, 8, bw], F32, "y%d" % i) for i in range(nblk)]
    hcT = s.tile([128, 8, CTX], BF16, "hcT")
    s.dma("sp", y_tiles[0][:, :, 0:CTX], ctxT.rearrange("(c p) n -> p c n", p=128), writes=[y_tiles[0]])
    kx.norm_mod(y_tiles[0], y_tiles[0][:, :, 0:CTX], CTX, msc, 0, hcT, hcT[:, :, :], None)
    w_qkv4 = w_qkv.rearrange("(c p) (t j n) -> p c t j n", p=128, t=3, j=8)
    for half in range(2):
        wc0 = half * HWD
        s.dma("sp", xres[:], xhT[:, 256 + wc0:256 + wc0 + HWD].rearrange("(c p) n -> p c n", p=128), writes=[xres])
        with s.scope():
            hT = s.tile([128, 8, WIN], BF16, "hT")
            attT = [s.tile([128, 8, bw], BF16, "attT%d" % i) for i in range(nblk)]
            qT = s.tile([128, HWD], BF16, "qT")
            kT = s.tile([128, WIN], BF16, "kT")
            kcT = s.tile([128, CTX], BF16, "kcT")
            vt = s.tile([128, 12, 2, 65], BF16, "vt")
            vct = s.tile([128, 2, 2, 65], BF16, "vct")
            att_hp = s.tile([128, 8, 128], BF16, "att_hp")
            bt = s.tile([128, 3, 2, 5, 128], F32, "bt")
            stA = [s.tile([128, 512], F32, "stA%d" % i) for i in range(2)]
            stB = [s.tile([128, 128], F32, "stB%d" % i) for i in range(2)]
            pA = [s.tile([128, 512], BF16, "pA%d" % i) for i in range(2)]
            pB = [s.tile([128, 128], BF16, "pB%d" % i) for i in range(2)]
            pC = [s.tile([128, 256], BF16, "pC%d" % i) for i in range(2)]
            rc = [s.tile([128, 1], F32, "rc%d" % i) for i in range(2)]
            s.op("pool", lambda e: e.memset(vt[:, :, :, 64:65], 1.0), writes=[vt])
            s.op("pool", lambda e: e.memset(vct[:, :, :, 64:65], 1.0), writes=[vct])
            for j, (a, w) in enumerate(((0, 512), (512, 512), (1024, WIN - 1024))):
                yt = y_tiles[j % 2]
                s.dma("sp", yt[:, :, 0:w], xhT[:, wc0 + a:wc0 + a + w].rearrange("(c p) n -> p c n", p=128), writes=[yt])
                kx.norm_mod(yt, yt[:, :, 0:w], w, ms, 0, hT, hT[:, :, a:a + w], None)
            it = 0
            for hp in range(8):
                wt = kx.nwb()
                wv = wt.ap[:, 0:8 * 3 * 128].rearrange("p (c t n) -> p c t n", c=8, t=3)
                for t in range(3):
                    kx.stage_cast(wt, wv[:, :, t, :], w_qkv4[:, :, t, hp, :], 1024)
                s.dma("act", bt[:].rearrange("p a b c q -> p (a b c q)"), btab[hp, half], writes=[bt])
                for blk in range(2):
                    ps = kx.nps()
                    for c in range(8):
                        s.op("pe", lambda e, c=c, ps=ps, blk=blk, wv=wv: e.matmul(
                            ps[:, 0:512], lhsT=wv[:, c, 0, :], rhs=hT[:, c, 256 + blk * 512:256 + (blk + 1) * 512], start=(c == 0), stop=(c == 7)),
                            reads=[wt, hT], writes=[ps])
                    s.op("act", lambda e, ps=ps, blk=blk: e.activation(out=qT[:, blk * 512:(blk + 1) * 512], in_=ps[:, 0:512], func=AF.Identity),
                         reads=[ps], writes=[qT])
                for (a, w) in ((0, 512), (512, 512), (1024, WIN - 1024)):
                    ps = kx.nps()
                    for c in range(8):
                        s.op("pe", lambda e, c=c, ps=ps, a=a, w=w, wv=wv: e.matmul(
                            ps[:, 0:w], lhsT=wv[:, c, 1, :], rhs=hT[:, c, a:a + w], start=(c == 0), stop=(c == 7)),
                            reads=[wt, hT], writes=[ps])
                    s.op("dve", lambda e, ps=ps, a=a, w=w: e.tensor_copy(out=kT[:, a:a + w], in_=ps[:, 0:w]), reads=[ps], writes=[kT])
                ps = kx.nps()
                for c in range(8):
                    s.op("pe", lambda e, c=c, ps=ps, wv=wv: e.matmul(ps[:, 0:CTX], lhsT=wv[:, c, 1, :], rhs=hcT[:, c, :], start=(c == 0), stop=(c == 7)),
                         reads=[wt, hcT], writes=[ps])
                s.op("act", lambda e, ps=ps: e.activation(out=kcT[:, :], in_=ps[:, 0:CTX], func=AF.Identity), reads=[ps], writes=[kcT])
                for tg in range(3):
                    ps = kx.nps()
                    for u in range(4):
                        tcn = tg * 4 + u
                        ntok = 64 if tcn == 11 else 128
                        for c in range(8):
                            s.op("pe", lambda e, c=c, ps=ps, u=u, tcn=tcn, ntok=ntok, wv=wv: e.matmul(
                                ps[0:ntok, u * 128:(u + 1) * 128], lhsT=hT[:, c, tcn * 128:tcn * 128 + ntok], rhs=wv[:, c, 2, :], start=(c == 0), stop=(c == 7)),
                                reads=[wt, hT], writes=[ps])
                    nfull = 4 if tg < 2 else 3
                    s.op("act", lambda e, ps=ps, tg=tg, nfull=nfull: e.activation(
                        out=vt[:, tg * 4:tg * 4 + nfull, :, 0:64], in_=ps[:, 0:nfull * 128].rearrange("p (a b d) -> p a b d", a=nfull, b=2), func=AF.Identity),
                        reads=[ps], writes=[vt])
                    if tg == 2:
                        s.op("act", lambda e, ps=ps: e.activation(out=vt[0:64, 11, :, 0:64], in_=ps[0:64, 384:512].rearrange("p (b d) -> p b d", b=2), func=AF.Identity),
                             reads=[ps], writes=[vt])
                ps = kx.nps()
                for u in range(2):
                    for c in range(8):
                        s.op("pe", lambda e, c=c, ps=ps, u=u, wv=wv: e.matmul(
                            ps[:, u * 128:(u + 1) * 128], lhsT=hcT[:, c, u * 128:(u + 1) * 128], rhs=wv[:, c, 2, :], start=(c == 0), stop=(c == 7)),
                            reads=[wt, hcT], writes=[ps])
                s.op("dve", lambda e, ps=ps: e.tensor_copy(out=vct[:, :, :, 0:64], in_=ps[:, 0:256].rearrange("p (a b d) -> p a b d", a=2, b=2)),
                     reads=[ps], writes=[vct])
                for h2 in range(2):
                    pb = 64 * h2
                    for i in range(8):
                        P = half * 8 + i
                        cls = (0 if i == 0 else 1 if i == 1 else 2) if half == 0 else (1 if i == 6 else 2 if i == 7 else 0)
                        a_, b_, c_ = stA[it % 2], stB[it % 2], rc[it % 2]
                        pa, pb_, pc = pA[it % 2], pB[it % 2], pC[it % 2]
                        it += 1
                        psA = kx.nps()
                        for ch in range(4):
                            s.op("pe", lambda e, psA=psA, ch=ch, i=i, pb=pb: e.matmul(
                                psA[:, ch * 128:(ch + 1) * 128], lhsT=kT[pb:pb + 64, 128 * (i + ch):128 * (i + ch + 1)], rhs=qT[pb:pb + 64, 128 * i:128 * (i + 1)],
                                start=True, stop=True), reads=[kT, qT], writes=[psA])
                        psB = kx.nps()
                        s.op("pe", lambda e, psB=psB, i=i, pb=pb: e.matmul(
                            psB[0:64, 0:128], lhsT=kT[pb:pb + 64, 128 * (i + 4):128 * (i + 4) + 64], rhs=qT[pb:pb + 64, 128 * i:128 * (i + 1)],
                            start=True, stop=True), reads=[kT, qT], writes=[psB])
                        for cc in range(2):
                            s.op("pe", lambda e, psB=psB, cc=cc, i=i, pb=pb: e.matmul(
                                psB[:, 128 + cc * 128:256 + cc * 128], lhsT=kcT[pb:pb + 64, cc * 128:(cc + 1) * 128], rhs=qT[pb:pb + 64, 128 * i:128 * (i + 1)],
                                start=True, stop=True), reads=[kcT, qT], writes=[psB])
                        s.op("dve", lambda e, psA=psA, a_=a_, cls=cls, h2=h2: e.scalar_tensor_tensor(
                            out=a_[:, :], in0=psA[:, :], scalar=0.125, in1=bt[:, cls, h2, 0:4, :].rearrange("p a q -> p (a q)"), op0=ALU.mult, op1=ALU.add),
                            reads=[psA, bt], writes=[a_])
                        s.op("act", lambda e, a_=a_, pa=pa: e.activation(out=pa[:, :], in_=a_[:, :], func=AF.Exp), reads=[a_], writes=[pa])
                        s.op("dve", lambda e, psB=psB, b_=b_, cls=cls, h2=h2: e.scalar_tensor_tensor(
                            out=b_[0:64, :], in0=psB[0:64, 0:128], scalar=0.125, in1=bt[0:64, cls, h2, 4, :], op0=ALU.mult, op1=ALU.add),
                            reads=[psB, bt], writes=[b_])
                        s.op("act", lambda e, b_=b_, pb_=pb_: e.activation(out=pb_[0:64, :], in_=b_[0:64, :], func=AF.Exp), reads=[b_], writes=[pb_])
                        s.op("act", lambda e, psB=psB, pc=pc: e.activation(out=pc[:, :], in_=psB[:, 128:384], func=AF.Exp, scale=0.125), reads=[psB], writes=[pc])
                        psO = kx.nps()
                        for ch in range(4):
                            s.op("pe", lambda e, psO=psO, ch=ch, i=i, h2=h2, pa=pa: e.matmul(
                                psO[:, 0:65], lhsT=pa[:, ch * 128:(ch + 1) * 128], rhs=vt[:, i + ch, h2, :], start=(ch == 0), stop=False),
                                reads=[pa, vt], writes=[psO])
                        s.op("pe", lambda e, psO=psO, i=i, h2=h2, pb_=pb_: e.matmul(
                            psO[:, 0:65], lhsT=pb_[0:64, :], rhs=vt[0:64, i + 4, h2, :], start=False, stop=False), reads=[pb_, vt], writes=[psO])
                        for cc in range(2):
                            s.op("pe", lambda e, psO=psO, cc=cc, h2=h2, pc=pc: e.matmul(
                                psO[:, 0:65], lhsT=pc[:, cc * 128:(cc + 1) * 128], rhs=vct[:, cc, h2, :], start=False, stop=(cc == 1)),
                                reads=[pc, vct], writes=[psO])
                        s.op("dve", lambda e, psO=psO, c_=c_: e.reciprocal(out=c_[:, :], in_=psO[:, 64:65]), reads=[psO], writes=[c_])
                        s.op("dve", lambda e, psO=psO, c_=c_, i=i, h2=h2: e.tensor_scalar(
                            out=att_hp[:, i, h2 * 64:(h2 + 1) * 64], in0=psO[:, 0:64], scalar1=c_[:, 0:1], scalar2=None, op0=ALU.mult),
                            reads=[psO, c_], writes=[att_hp])
                for ig in range(2):
                    pst = kx.nps()
                    pv = pst.ap.bitcast(BF16)
                    for u in range(4):
                        i = ig * 4 + u
                        s.op("pe", lambda e, pv=pv, u=u, i=i: e.transpose(out=pv[:, u * 128:(u + 1) * 128], in_=att_hp[:, i, :], identity=ident[:]),
                             reads=[att_hp, ident], writes=[pst])
                    s.op("dve", lambda e, pv=pv, ig=ig, hp=hp: e.tensor_copy(out=attT[ig][:, hp, :], in_=pv[:, 0:512]), reads=[pst], writes=[attT[ig]])
            for db in range(2):
                wt, wv = kx.wload(w_out[:, db * 512:(db + 1) * 512], 8, 512)
                for sub in range(4):
                    dc = db * 4 + sub
                    for tb in range(nblk):
                        ps = kx.nps()
                        for c in range(8):
                            s.op("pe", lambda e, c=c, ps=ps, tb=tb, sub=sub, wv=wv: e.matmul(
                                ps[:, 0:bw], lhsT=wv[:, c, sub * 128:(sub + 1) * 128], rhs=attT[tb][:, c, :], start=(c == 0), stop=(c == 7)),
                                reads=[wt, attT[tb]], writes=[ps])
                        s.op("act", lambda e, ps=ps, dc=dc, tb=tb: e.activation(out=y_tiles[tb][:, dc, :], in_=ps[:, 0:bw], func=AF.Identity),
                             reads=[ps], writes=[y_tiles[tb]])
        x_views = [xres[:, :, tb * bw:(tb + 1) * bw] for tb in range(nblk)]
        blocks = [Blk(bw, ms, xres, x_views[tb], y_tiles[tb]) for tb in range(nblk)]
        emit_finish(kx, blocks, w1, w2, None)
        for tb in range(nblk):
            s.dma("sp", outT[:, half * HWD + tb * bw:half * HWD + (tb + 1) * bw].rearrange("(c p) n -> p c n", p=128), x_views[tb], reads=[xres], is_output=True)
    s.emit()
    return nc


def _na_rowmap(kq):
    if kq == 0:
        return [5, 6, 7, -1] + list(range(0, 35))
    if kq == 3:
        return [92 + j for j in range(36)] + [120, 121, -1]
    return [32 * kq - 4 + j for j in range(NA_ROWS_LOCAL)]


def _na_tables(rpb, kq):
    NEG = -30000.0
    rm = _na_rowmap(kq)
    tabs = {}
    qc = np.arange(64)
    kc = np.arange(64)
    c0 = np.clip(qc - 8, 0, 48)
    colok = (kc[:, None] >= c0[None, :]) & (kc[:, None] < c0[None, :] + 16)
    dc = kc[:, None] - qc[None, :] + 15
    dcc = np.clip(dc, 0, 30)
    for P in (0, 1, 2, 14, 15):
        tab = np.full((16, 640, 128), NEG, np.float32)
        seen = set()
        for j in range(9):
            g = rm[2 * P + j]
            if g < 0 or g in seen:
                continue
            seen.add(g)
            for u in range(2):
                qr = 32 * kq + 2 * P + u
                r0 = min(max(qr - 4, 0), 120)
                if not (r0 <= g < r0 + 8):
                    continue
                dr = g - qr + 7
                vals = rpb[:, dr, :][:, dcc]
                blk = np.where(colok[None], vals, NEG)
                tab[:, j * 64:(j + 1) * 64, u * 64:(u + 1) * 64] = blk
        tabs[P] = tab.reshape(16, 5, 128, 128)
    out = np.empty((8, 2, 128, 3, 2, 5, 128), np.float32)
    for half, Ps in ((0, (0, 1, 2)), (1, (2, 14, 15))):
        for ci, P in enumerate(Ps):
            t = tabs[P].reshape(8, 2, 5, 128, 128)
            out[:, half, :, ci] = t.transpose(0, 3, 1, 2, 4)
    return out.reshape(8, 2, 128, 3840)


def run_na(x, ctx, m_lat, m_ctx, g, w_qkv, rpb, w_out, w1, w2):
    NT = 2048
    nc = build_na(NT)
    ident = np.eye(128, dtype=np.float32)
    tabs = [_na_tables(rpb, kq) for kq in range(4)]
    ins = []
    for core in range(NCORES):
        b, kq = divmod(core, 4)
        rm = _na_rowmap(kq)
        xg = x[b].reshape(128, 64, D)
        xh = np.zeros((NA_ROWS_LOCAL, 64, D), np.float32)
        for j, gr in enumerate(rm):
            if gr >= 0:
                xh[j] = xg[gr]
        ins.append({
            "xhT": np.ascontiguousarray(xh.reshape(-1, D).T), "ctxT": np.ascontiguousarray(ctx[b].T),
            "mvec": np.ascontiguousarray(m_lat[b].reshape(6, D).T), "mvec_c": np.ascontiguousarray(m_ctx.reshape(6, D).T),
            "gvec": np.ascontiguousarray(g.T), "w_qkv": w_qkv, "w_out": w_out, "w1": w1, "w2": w2, "btab": tabs[kq], "ident": ident})
    res = run_bass_kernel_spmd(nc, ins, core_ids=list(range(NCORES)))
    out = np.empty_like(x)
    for core in range(NCORES):
        b, kq = divmod(core, 4)
        out[b, kq * NT:(kq + 1) * NT] = res.results[core]["outT"].T
    return out


SSD_IN = 6208


def build_ssd_a(NT=2048, NC_=64):
    nc = bass.Bass("TRN2", target_bir_lowering=False)
    xT = _mk(nc, "xT", [D, NT + 2])
    cT = _mk(nc, "cT", [D, NC_ + 2])
    mvec = _mk(nc, "mvec", [D, 6])
    mvec_c = _mk(nc, "mvec_c", [D, 6])
    gvec = _mk(nc, "gvec", [D, 4])
    hmask = _mk(nc, "hmask", [128, 4])
    w_in = _mk(nc, "w_in", [D, SSD_IN])
    convw = _mk(nc, "convw", [128, 32, 4])
    dtb = _mk(nc, "dtb", [64, 1])
    NTOT = NT + NC_
    zT = _mk(nc, "zT", [2048, NTOT], kind="ExternalOutput")
    xbcT = _mk(nc, "xbcT", [4096, NTOT], kind="ExternalOutput")
    dtT = _mk(nc, "dtT", [64, NTOT], kind="ExternalOutput")
    kx = KX(nc)
    s = kx.s
    ms = kx.prep_mod(mvec, gvec)
    msc = kx.prep_mod(mvec_c, gvec, "mvc")
    hm = s.tile([128, 4], F32, "hm")
    s.dma("sp", hm[:], hmask, writes=[hm])
    cw = s.tile([128, 32, 4], F32, "cw")
    s.dma("sp", cw[:], convw, writes=[cw])
    db_ = s.tile([64, 1], F32, "dtb")
    s.dma("sp", db_[:], dtb, writes=[db_])
    HWD = NT // 2
    xin = [s.tile([128, 8, 342], F32, "xin%d" % i) for i in range(2)]
    for grp in range(2):
        with s.scope():
            blks = []
            c0 = grp * HWD
            srcs = [(xT[:, c0 + i * 342:c0 + (i + 1) * 342], 342, ms) for i in range(3)]
            if grp == 1:
                srcs.append((cT[:, :], NC_ + 2, msc))
            W = sum(w for _, w, _ in srcs)
            for i, (src, w, m) in enumerate(srcs):
                xt = xin[i % 2]
                s.dma("sp", xt[:, :, 0:w], src.rearrange("(c p) n -> p c n", p=128), writes=[xt])
                ht = s.tile([128, 8, w], BF16, "hT%d" % i)
                kx.norm_mod(xt, xt[:, :, 0:w], w, m, 0, ht, ht[:, :, 0:w], None)
                blks.append((ht, w))
            pre = [s.tile([128, W], F32, "pre%d" % i) for i in range(2)]
            acc = [s.tile([128, W], F32, "acc%d" % i) for i in range(2)]
            segs = [(0, HWD, 0 if grp == 0 else None, 1 if grp == 1 else None, c0)]
            if grp == 1:
                segs.append((HWD + 2, NC_, 2, 3, NT))
            nchunks = 49
            for jb in range(13):
                ncols = 512 if jb < 12 else 64
                wt, wv = kx.wload(w_in[:, jb * 512:jb * 512 + ncols], 8, ncols)
                for sub in range(ncols // 128 if ncols >= 128 else 1):
                    j = jb * 4 + sub
                    mrows = 128 if j < 48 else 64
                    pr = pre[j % 2]
                    ac = acc[j % 2]
                    col = 0
                    for (ht, w) in blks:
                        ps = kx.nps()
                        for c in range(8):
                            s.op("pe", lambda e, c=c, ps=ps, ht=ht, w=w, sub=sub, wv=wv, mrows=mrows: e.matmul(
                                ps[0:mrows, 0:w], lhsT=wv[:, c, sub * 128:sub * 128 + mrows], rhs=ht[:, c, 0:w], start=(c == 0), stop=(c == 7)),
                                reads=[wt, ht], writes=[ps])
                        if j < 16:
                            s.op("act", lambda e, ps=ps, pr=pr, col=col, w=w: e.activation(out=pr[:, col:col + w], in_=ps[:, 0:w], func=AF.Identity),
                                 reads=[ps], writes=[pr])
                        elif j < 48:
                            s.op("act", lambda e, ps=ps, pr=pr, col=col, w=w: e.activation(out=pr[:, col:col + w], in_=ps[:, 0:w], func=AF.Identity),
                                 reads=[ps], writes=[pr])
                        else:
                            s.op("act", lambda e, ps=ps, pr=pr, col=col, w=w: e.activation(out=pr[0:64, col:col + w], in_=ps[0:64, 0:w], func=AF.Exp, bias=db_[:, 0:1], scale=1.0),
                                 reads=[ps, db_], writes=[pr])
                        col += w
                    for (st, ow, lm, rm, oc) in segs:
                        if j < 16:
                            s.dma("sp", zT[j * 128:(j + 1) * 128, oc:oc + ow], pr[:, st + 1:st + 1 + ow], reads=[pr], is_output=True)
                        elif j < 48:
                            jc = j - 16
                            if lm is not None:
                                s.op("dve", lambda e, pr=pr, st=st, lm=lm: e.tensor_scalar(out=pr[:, st:st + 1], in0=pr[:, st:st + 1], scalar1=hm[:, lm:lm + 1], scalar2=None, op0=ALU.mult),
                                     reads=[pr, hm], writes=[pr])
                            if rm is not None:
                                s.op("dve", lambda e, pr=pr, st=st, ow=ow, rm=rm: e.tensor_scalar(out=pr[:, st + ow + 1:st + ow + 2], in0=pr[:, st + ow + 1:st + ow + 2],
                                                                                                scalar1=hm[:, rm:rm + 1], scalar2=None, op0=ALU.mult),
                                     reads=[pr, hm], writes=[pr])
                            s.op("dve", lambda e, pr=pr, ac=ac, st=st, ow=ow, jc=jc: e.tensor_scalar(out=ac[:, st:st + ow], in0=pr[:, st:st + ow], scalar1=cw[:, jc, 0:1], scalar2=None, op0=ALU.mult),
                                 reads=[pr, cw], writes=[ac])
                            for k in (1, 2):
                                s.op("dve", lambda e, pr=pr, ac=ac, st=st, ow=ow, jc=jc, k=k: e.scalar_tensor_tensor(
                                    out=ac[:, st:st + ow], in0=pr[:, st + k:st + k + ow], scalar=cw[:, jc, k:k + 1], in1=ac[:, st:st + ow], op0=ALU.mult, op1=ALU.add),
                                    reads=[pr, cw, ac], writes=[ac])
                            s.op("act", lambda e, ac=ac, st=st, ow=ow, jc=jc: e.activation(out=ac[:, st:st + ow], in_=ac[:, st:st + ow], func=AF.Silu, bias=cw[:, jc, 3:4], scale=1.0),
                                 reads=[ac, cw], writes=[ac])
                            s.dma("sp", xbcT[jc * 128:(jc + 1) * 128, oc:oc + ow], ac[:, st:st + ow], reads=[ac], is_output=True)
                        else:
                            s.op("act", lambda e, pr=pr, ac=ac, st=st, ow=ow: e.activation(out=ac[0:64, st:st + ow], in_=pr[0:64, st + 1:st + 1 + ow], func=AF.Ln, bias=1.0, scale=1.0),
                                 reads=[pr], writes=[ac])
                            s.dma("sp", dtT[:, oc:oc + ow], ac[0:64, st:st + ow], reads=[ac], is_output=True)
    s.emit()
    return nc


def run_ssd_a(x, ctx, m_lat, m_ctx, g, w_in, conv_w, conv_b, dt_bias):
    NT, NC_ = 2048, 64
    nc = build_ssd_a(NT, NC_)
    cw = np.concatenate([conv_w.T, conv_b[:, None]], 1).astype(np.float32)
    cw = np.ascontiguousarray(cw.reshape(32, 128, 4).transpose(1, 0, 2))
    dtb = np.ascontiguousarray(dt_bias.reshape(64, 1).astype(np.float32))
    ins = []
    for core in range(NCORES):
        b, k = divmod(core, 4)
        hm = np.ones((128, 4), np.float32)
        if k == 0:
            hm[:, 0] = 0
            hm[:, 2] = 0
        if k == 3:
            hm[:, 1] = 0
            hm[:, 3] = 0
        ins.append({"xT": _halo_T(x[b], k * NT, NT), "cT": _halo_T(ctx[b], k * NC_, NC_),
                    "mvec": np.ascontiguousarray(m_lat[b].reshape(6, D).T), "mvec_c": np.ascontiguousarray(m_ctx.reshape(6, D).T),
                    "gvec": np.ascontiguousarray(g.T), "hmask": hm, "w_in": w_in, "convw": cw, "dtb": dtb})
    res = run_bass_kernel_spmd(nc, ins, core_ids=list(range(NCORES)))
    z = np.empty((BATCH, SEQ + CTX, 2048), np.float32)
    xbc = np.empty((BATCH, SEQ + CTX, 4096), np.float32)
    dt = np.empty((BATCH, SEQ + CTX, 64), np.float32)
    for core in range(NCORES):
        b, k = divmod(core, 4)
        r = res.results[core]
        for arr, name in ((z, "zT"), (xbc, "xbcT"), (dt, "dtT")):
            arr[b, k * NT:(k + 1) * NT] = r[name][:, 0:NT].T
            arr[b, SEQ + k * NC_:SEQ + (k + 1) * NC_] = r[name][:, NT:NT + NC_].T
    return z, xbc, dt


NCH = 66


def build_ssd_b(nch=NCH, dbg=False):
    nc = bass.Bass("TRN2", target_bir_lowering=False)
    X = _mk(nc, "X", [nch, 128, 512])
    DT = _mk(nc, "DT", [nch, 128, 16])
    BCT = _mk(nc, "BCT", [nch, 128, 4, 128])
    BTOK = _mk(nc, "BTOK", [nch, 128, 2, 128])
    alog = _mk(nc, "alog", [128, 16])
    masks = _mk(nc, "masks", [128, 4, 128])
    Y = _mk(nc, "Y", [nch, 128, 512], kind="ExternalOutput")
    s = Sched(nc)
    ps = [s.ptile(name="ps%d" % i) for i in range(8)]
    psi = [0]

    def nps():
        t = ps[psi[0]]
        psi[0] = (psi[0] + 1) % 8
        return t
    mk = s.tile([128, 4, 128], F32, "mk")
    s.dma("sp", mk[:], masks, writes=[mk])
    onesf = s.tile([128, 128], F32, "onesf")
    s.op("dve", lambda e: e.memset(onesf[:], 1.0), writes=[onesf])
    a_bc = s.tile([128, 16], F32, "a_bc")
    s.dma("sp", a_bc[:], alog, writes=[a_bc])
    s.op("act", lambda e: e.activation(out=a_bc[:], in_=a_bc[:], func=AF.Exp), reads=[a_bc], writes=[a_bc])
    s.op("dve", lambda e: e.tensor_scalar(out=a_bc[:], in0=a_bc[:], scalar1=-1.0, scalar2=None, op0=ALU.mult), reads=[a_bc], writes=[a_bc])
    state = s.tile([128, 8, 64], F32, "state")
    state_bf = s.tile([128, 512], BF16, "state_bf")
    NB = 2
    xt = [s.tile([128, 8, 64], F32, "xt%d" % i) for i in range(NB)]
    dtt = [s.tile([128, 16], F32, "dtt%d" % i) for i in range(NB)]
    bct = [s.tile([128, 2, 128], F32, "bct%d" % i) for i in range(NB)]
    btk = [s.tile([128, 128], F32, "btk%d" % i) for i in range(NB)]
    bcb = [s.tile([128, 2, 128], BF16, "bcb%d" % i) for i in range(NB)]
    btb = [s.tile([128, 128], BF16, "btb%d" % i) for i in range(NB)]
    yin = [s.tile([128, 512], F32, "yin%d" % i) for i in range(NB)]
    dtA = [s.tile([128, 8], F32, "dtA%d" % i) for i in range(NB)]
    ct = [s.tile([128, 16], F32, "ct%d" % i) for i in range(NB)]
    ee = [s.tile([128, 3, 8], F32, "ee%d" % i) for i in range(NB)]
    dtw = [s.tile([128, 8], F32, "dtw%d" % i) for i in range(NB)]
    xdt = [s.tile([128, 8, 64], BF16, "xdt%d" % i) for i in range(NB)]
    xw = [s.tile([128, 8, 64], BF16, "xw%d" % i) for i in range(NB)]
    Lall = [s.tile([128, 8, 128], F32, "Lall%d" % i) for i in range(NB)]
    dec = [s.tile([128, 8, 128], F32, "dec%d" % i) for i in range(NB)]
    cbm = [s.tile([128, 128], F32, "cbm%d" % i) for i in range(NB)]
    sc = [s.tile([128, 8, 128], BF16, "sc%d" % i) for i in range(NB)]
    tt = [s.tile([128, 8, 64], F32, "tt%d" % i) for i in range(NB)]
    yo = [s.tile([128, 8, 64], F32, "yo%d" % i) for i in range(NB)]
    t2 = s.tile([128, 8, 64], F32, "t2")
    Yc = [T(Y[c], "Y%d" % c) for c in range(nch)]
    nctx = 2
    it = 0
    def sweep(d, it):
        order = list(range(nch)) if d == 0 else ([1, 0] + list(range(nch - 1, nctx - 1, -1)))
        m_incl = mk[:, d, :]
        m_str = mk[:, 2 + d, :]
        s.op("dve", lambda e: e.memset(state[:], 0.0), writes=[state])
        s.op("dve", lambda e: e.memset(state_bf[:], 0.0), writes=[state_bf])
        for c in order:
            k = it % NB
            it += 1
            x_, dt_, bc_, bk_, bcb_, btb_, yin_ = xt[k], dtt[k], bct[k], btk[k], bcb[k], btb[k], yin[k]
            dA, ct_, ee_, dtw_, xdt_, xw_, L_, dec_, cbm_, sc_, tt_, yo_ = dtA[k], ct[k], ee[k], dtw[k], xdt[k], xw[k], Lall[k], dec[k], cbm[k], sc[k], tt[k], yo[k]
            s.dma("sp", x_[:].rearrange("p a b -> p (a b)"), X[c], writes=[x_])
            s.dma("act", dt_[:], DT[c], writes=[dt_])
            s.dma("sp", bc_[:], BCT[c][:, 2 * d:2 * d + 2, :], writes=[bc_])
            s.dma("act", bk_[:], BTOK[c][:, d, :], writes=[bk_])
            if d == 1:
                s.dma("sp", yin_[:], Y[c], reads=[Yc[c]], writes=[yin_])
            s.op("pool", lambda e, bc_=bc_, bcb_=bcb_: e.tensor_copy(out=bcb_[:], in_=bc_[:]), reads=[bc_], writes=[bcb_])
            s.op("pool", lambda e, bk_=bk_, btb_=btb_: e.tensor_copy(out=btb_[:], in_=bk_[:]), reads=[bk_], writes=[btb_])
            dts = dt_[:, d * 8:(d + 1) * 8]
            s.op("dve", lambda e, dA=dA, dts=dts: e.tensor_tensor(out=dA[:], in0=dts, in1=a_bc[:, d * 8:(d + 1) * 8], op=ALU.mult), reads=[dt_, a_bc], writes=[dA])
            psc = nps()
            s.op("pe", lambda e, psc=psc, dA=dA: e.matmul(psc[:, 0:8], lhsT=m_incl, rhs=dA[:], start=True, stop=True), reads=[mk, dA], writes=[psc])
            s.op("pe", lambda e, psc=psc, dA=dA: e.matmul(psc[:, 8:16], lhsT=onesf[:], rhs=dA[:], start=True, stop=True), reads=[onesf, dA], writes=[psc])
            s.op("dve", lambda e, psc=psc, ct_=ct_: e.tensor_copy(out=ct_[:], in_=psc[:, 0:16]), reads=[psc], writes=[ct_])
            s.op("dve", lambda e, ct_=ct_, ee_=ee_: e.tensor_tensor(out=ee_[:, 2, :], in0=ct_[:, 8:16], in1=ct_[:, 0:8], op=ALU.subtract), reads=[ct_], writes=[ee_])
            s.op("act", lambda e, ct_=ct_, ee_=ee_: e.activation(out=ee_[:, 0:2, :].rearrange("p a b -> p (a b)"), in_=ct_[:, 0:16], func=AF.Exp), reads=[ct_, ee_], writes=[ee_])
            s.op("act", lambda e, ee_=ee_: e.activation(out=ee_[:, 2, :], in_=ee_[:, 2, :], func=AF.Exp), reads=[ee_], writes=[ee_])
            s.op("dve", lambda e, dtw_=dtw_, dts=dts, ee_=ee_: e.tensor_tensor(out=dtw_[:], in0=dts, in1=ee_[:, 2, :], op=ALU.mult), reads=[dt_, ee_], writes=[dtw_])
            s.op("dve", lambda e, x_=x_, xdt_=xdt_, dts=dts: e.tensor_tensor(out=xdt_[:], in0=x_[:], in1=dts.unsqueeze(2).to_broadcast([128, 8, 64]), op=ALU.mult),
                 reads=[x_, dt_], writes=[xdt_])
            s.op("dve", lambda e, x_=x_, xw_=xw_, dtw_=dtw_: e.tensor_tensor(out=xw_[:], in0=x_[:], in1=dtw_[:].unsqueeze(2).to_broadcast([128, 8, 64]), op=ALU.mult),
                 reads=[x_, dtw_], writes=[xw_])
            s.op("pool", lambda e, L_=L_, dA=dA: e.tensor_tensor(out=L_[:], in0=m_str.unsqueeze(1).to_broadcast([128, 8, 128]),
                                                                in1=dA[:].unsqueeze(2).to_broadcast([128, 8, 128]), op=ALU.mult), reads=[mk, dA], writes=[L_])
            pss = [nps(), nps()]
            for e_ in range(8):
                p_ = pss[e_ // 4]
                s.op("pe", lambda e, p_=p_, e_=e_, L_=L_: e.matmul(p_[:, (e_ % 4) * 128:(e_ % 4 + 1) * 128], lhsT=L_[:, e_, :], rhs=m_incl, start=True, stop=True),
                     reads=[L_, mk], writes=[p_])
            for hh in range(2):
                s.op("act", lambda e, hh=hh, dec_=dec_, p_=pss[hh]: e.activation(out=dec_[:, hh * 4:(hh + 1) * 4, :].rearrange("p a b -> p (a b)"), in_=p_[:, :], func=AF.Exp),
                     reads=[pss[hh]], writes=[dec_])
            pcb = nps()
            s.op("pe", lambda e, pcb=pcb, bcb_=bcb_: e.matmul(pcb[:, 0:128], lhsT=bcb_[:, 0, :], rhs=bcb_[:, 1, :], start=True, stop=True), reads=[bcb_], writes=[pcb])
            s.op("dve", lambda e, pcb=pcb, cbm_=cbm_: e.tensor_tensor(out=cbm_[:], in0=pcb[:, 0:128], in1=m_incl, op=ALU.mult), reads=[pcb, mk], writes=[cbm_])
            s.op("dve", lambda e, sc_=sc_, dec_=dec_, cbm_=cbm_: e.tensor_tensor(out=sc_[:], in0=dec_[:], in1=cbm_[:].unsqueeze(1).to_broadcast([128, 8, 128]), op=ALU.mult),
                 reads=[dec_, cbm_], writes=[sc_])
            if dbg and d == 0 and c == 0:
                for nm, tl, n in (("d_dec", dec_, 1024), ("d_cbm", cbm_, 128), ("d_ct", ct_, 16), ("d_ee", ee_, 24), ("d_L", L_, 1024), ("d_dA", dA, 8)):
                    o = _mk(nc, nm, [128, n], kind="ExternalOutput")
                    v = tl[:] if len(tl.ap.shape) == 2 else tl[:].rearrange("p a b -> p (a b)")
                    s.dma("sp", o, v, reads=[tl], is_output=True)
            psY = nps()
            for e_ in range(8):
                s.op("pe", lambda e, psY=psY, e_=e_, sc_=sc_, xdt_=xdt_: e.matmul(psY[:, e_ * 64:(e_ + 1) * 64], lhsT=sc_[:, e_, :], rhs=xdt_[:, e_, :], start=True, stop=True),
                     reads=[sc_, xdt_], writes=[psY])
            psS = nps()
            s.op("pe", lambda e, psS=psS, bcb_=bcb_: e.matmul(psS[:, 0:512], lhsT=bcb_[:, 1, :], rhs=state_bf[:], start=True, stop=True), reads=[bcb_, state_bf], writes=[psS])
            s.op("dve", lambda e, psS=psS, tt_=tt_, ee_=ee_: e.tensor_tensor(out=tt_[:], in0=psS[:, 0:512].rearrange("p (a b) -> p a b", a=8),
                                                                         in1=ee_[:, 0, :].unsqueeze(2).to_broadcast([128, 8, 64]), op=ALU.mult), reads=[psS, ee_], writes=[tt_])
            s.op("dve", lambda e, psY=psY, tt_=tt_, yo_=yo_: e.tensor_tensor(out=yo_[:], in0=tt_[:], in1=psY[:, 0:512].rearrange("p (a b) -> p a b", a=8), op=ALU.add),
                 reads=[psY, tt_], writes=[yo_])
            if d == 1:
                s.op("pool", lambda e, yo_=yo_, yin_=yin_: e.tensor_tensor(out=yo_[:].rearrange("p a b -> p (a b)"), in0=yo_[:].rearrange("p a b -> p (a b)"), in1=yin_[:], op=ALU.add),
                     reads=[yo_, yin_], writes=[yo_])
            s.dma("sp", Y[c], yo_[:].rearrange("p a b -> p (a b)"), reads=[yo_], writes=[Yc[c]], is_output=True)
            psU = nps()
            s.op("pe", lambda e, psU=psU, btb_=btb_, xw_=xw_: e.matmul(psU[:, 0:512], lhsT=btb_[:], rhs=xw_[:].rearrange("p a b -> p (a b)"), start=True, stop=True),
                 reads=[btb_, xw_], writes=[psU])
            s.op("dve", lambda e, ee_=ee_: e.tensor_tensor(out=t2[:], in0=state[:], in1=ee_[:, 1, :].unsqueeze(2).to_broadcast([128, 8, 64]), op=ALU.mult),
                 reads=[state, ee_], writes=[t2])
            s.op("dve", lambda e, psU=psU: e.tensor_tensor(out=state[:], in0=t2[:], in1=psU[:, 0:512].rearrange("p (a b) -> p a b", a=8), op=ALU.add),
                 reads=[t2, psU], writes=[state])
            s.op("act", lambda e: e.activation(out=state_bf[:], in_=state[:].rearrange("p a b -> p (a b)"), func=AF.Identity), reads=[state], writes=[state_bf])
        return it
    it = sweep(0, it)
    it = sweep(1, it)
    s.emit()
    return nc


def _ssd_masks():
    k = np.arange(128)
    m = np.zeros((128, 4, 128), np.float32)
    m[:, 0, :] = (k[:, None] <= k[None, :])
    m[:, 1, :] = (k[:, None] >= k[None, :])
    m[:, 2, :] = (k[:, None] > k[None, :])
    m[:, 3, :] = (k[:, None] < k[None, :])
    return m


def run_ssd_b(xbc, dt, a_log):
    nc = build_ssd_b()
    masks = _ssd_masks()
    ins = []
    for core in range(NCORES):
        b, g = divmod(core, 4)
        def chunks(a):
            return np.concatenate([a[SEQ:].reshape(2, 128, -1), a[:SEQ].reshape(64, 128, -1)], 0)
        xg = chunks(xbc[b][:, g * 512:(g + 1) * 512])
        bc = xbc[b][:, 2048:].reshape(-1, 2, 2, 4, 128)[:, :, :, g, :]
        bcc = chunks(bc.reshape(-1, 4 * 128)).reshape(NCH, 128, 4, 128)
        bct = np.ascontiguousarray(bcc.transpose(0, 3, 2, 1))
        btok = np.ascontiguousarray(bcc[:, :, [0, 2], :])
        dtg = dt[b].reshape(-1, 2, 4, 8)[:, :, g, :].reshape(-1, 16)
        al = np.ascontiguousarray(np.broadcast_to(a_log.reshape(2, 4, 8)[:, g, :].reshape(1, 16), (128, 16))).astype(np.float32)
        ins.append({"X": np.ascontiguousarray(xg), "DT": np.ascontiguousarray(chunks(dtg)), "BCT": bct, "BTOK": btok, "alog": al, "masks": masks})
    res = run_bass_kernel_spmd(nc, ins, core_ids=list(range(NCORES)))
    y = np.empty((BATCH, SEQ + CTX, 2048), np.float32)
    for core in range(NCORES):
        b, g = divmod(core, 4)
        Yc = res.results[core]["Y"]
        y[b, SEQ:, g * 512:(g + 1) * 512] = Yc[0:2].reshape(256, 512)
        y[b, :SEQ, g * 512:(g + 1) * 512] = Yc[2:].reshape(SEQ, 512)
    return y


def build_ssd_c(NT=2048, NC_=64):
    nc = bass.Bass("TRN2", target_bir_lowering=False)
    NTOT = NT + NC_
    yT = _mk(nc, "yT", [2048, NTOT])
    xsT = _mk(nc, "xsT", [2048, NTOT])
    zT = _mk(nc, "zT", [2048, NTOT])
    xT = _mk(nc, "xT", [D, NTOT])
    mvec = _mk(nc, "mvec", [D, 6])
    mvec_c = _mk(nc, "mvec_c", [D, 6])
    gvec = _mk(nc, "gvec", [D, 4])
    dcol = _mk(nc, "dcol", [128, 16, 2])
    ngd = _mk(nc, "ng", [128, 16])
    w_out = _mk(nc, "w_out", [2048, D])
    w1 = _mk(nc, "w1", [D, HID])
    w2 = _mk(nc, "w2", [HID, D])
    outT = _mk(nc, "outT", [D, NTOT], kind="ExternalOutput")
    kx = KX(nc, nwbuf=2)
    s = kx.s
    ms = kx.prep_mod(mvec, gvec)
    msc = kx.prep_mod(mvec_c, gvec, "mvc")
    dc_ = s.tile([128, 16, 2], F32, "dcol")
    s.dma("sp", dc_[:], dcol, writes=[dc_])
    dsum = s.tile([128, 16], F32, "dsum")
    s.op("dve", lambda e: e.tensor_tensor(out=dsum[:], in0=dc_[:, :, 0], in1=dc_[:, :, 1], op=ALU.add), reads=[dc_], writes=[dsum])
    ng = s.tile([128, 16], F32, "ng")
    s.dma("sp", ng[:], ngd, writes=[ng])
    HWD = NT // 2
    xres = s.tile([128, 8, HWD + NC_], F32, "xres")
    y_tiles = [s.tile([128, 8, 512], F32, "y0"), s.tile([128, 8, 512], F32, "y1"), s.tile([128, 8, NC_], F32, "y2")]
    for half in range(2):
        c0 = half * HWD
        cols = [(c0, 512, ms, 0), (c0 + 512, 512, ms, 512)]
        s.dma("sp", xres[:, :, 0:HWD], xT[:, c0:c0 + HWD].rearrange("(c p) n -> p c n", p=128), writes=[xres])
        if half == 1:
            cols.append((NT, NC_, msc, HWD))
            s.dma("sp", xres[:, :, HWD:HWD + NC_], xT[:, NT:NT + NC_].rearrange("(c p) n -> p c n", p=128), writes=[xres])
        with s.scope():
            gz = s.tile([128, 16, 512], F32, "gz")
            ynT = [s.tile([128, 16, w], BF16, "ynT%d" % i) for i, (_, w, _, _) in enumerate(cols)]
            ld = [[s.tile([128, 512], F32, "ld%d_%d" % (a, b)) for b in range(2)] for a in range(3)]
            cnt = 0
            for bi, (co, w, m, xo) in enumerate(cols):
                for ch in range(16):
                    ly, lx, lz = ld[0][cnt % 2], ld[1][cnt % 2], ld[2][cnt % 2]
                    cnt += 1
                    s.dma("sp", ly[:, 0:w], yT[ch * 128:(ch + 1) * 128, co:co + w], writes=[ly])
                    s.dma("act", lx[:, 0:w], xsT[ch * 128:(ch + 1) * 128, co:co + w], writes=[lx])
                    s.dma("sp", lz[:, 0:w], zT[ch * 128:(ch + 1) * 128, co:co + w], writes=[lz])
                    s.op("dve", lambda e, ly=ly, lx=lx, ch=ch, w=w: e.scalar_tensor_tensor(out=ly[:, 0:w], in0=lx[:, 0:w], scalar=dsum[:, ch:ch + 1], in1=ly[:, 0:w],
                                                                                      op0=ALU.mult, op1=ALU.add), reads=[lx, ly, dsum], writes=[ly])
                    s.op("act", lambda e, lz=lz, w=w: e.activation(out=lz[:, 0:w], in_=lz[:, 0:w], func=AF.Silu), reads=[lz], writes=[lz])
                    s.op("dve", lambda e, ly=ly, lz=lz, ch=ch, w=w: e.tensor_tensor(out=gz[:, ch, 0:w], in0=ly[:, 0:w], in1=lz[:, 0:w], op=ALU.mult),
                         reads=[ly, lz], writes=[gz])
                r = kx.rstd(gz, gz[:, :, 0:w], w, nchunks=16, dim=2048)
                for ch in range(16):
                    tc = kx.ntc()
                    s.op("dve", lambda e, ch=ch, tc=tc, r=r, w=w: e.tensor_tensor(out=tc[:, 0:w], in0=gz[:, ch, 0:w], in1=r[:, 0:w], op=ALU.mult),
                         reads=[gz, r], writes=[tc])
                    s.op("act", lambda e, ch=ch, tc=tc, w=w, yn=ynT[bi]: e.activation(out=yn[:, ch, :], in_=tc[:, 0:w], func=AF.Identity, scale=ng[:, ch:ch + 1]),
                         reads=[tc, ng], writes=[ynT[bi]])
            for dc in range(8):
                wt, wv = kx.wload(w_out[:, dc * 128:(dc + 1) * 128], 16, 128)
                for bi, (co, w, m, xo) in enumerate(cols):
                    ps = kx.nps()
                    for c in range(16):
                        s.op("pe", lambda e, c=c, ps=ps, bi=bi, w=w, wv=wv: e.matmul(ps[:, 0:w], lhsT=wv[:, c, :], rhs=ynT[bi][:, c, :], start=(c == 0), stop=(c == 15)),
                             reads=[wt, ynT[bi]], writes=[ps])
                    s.op("act", lambda e, ps=ps, dc=dc, bi=bi, w=w: e.activation(out=y_tiles[bi][:, dc, 0:w], in_=ps[:, 0:w], func=AF.Identity),
                         reads=[ps], writes=[y_tiles[bi]])
        blocks = [Blk(w, m, xres, xres[:, :, xo:xo + w], y_tiles[bi]) for bi, (co, w, m, xo) in enumerate(cols)]
        emit_finish(kx, blocks, w1, w2, None)
        for bi, (co, w, m, xo) in enumerate(cols):
            s.dma("sp", outT[:, co:co + w].rearrange("(c p) n -> p c n", p=128), xres[:, :, xo:xo + w], reads=[xres], is_output=True)
    s.emit()
    return nc


def run_ssd_c(x, ctx, y, xbc, z, m_lat, m_ctx, g, ssd_d, ssd_norm_g, w_out, w1, w2):
    NT, NC_ = 2048, 64
    nc = build_ssd_c(NT, NC_)
    dcol = np.repeat(ssd_d.reshape(2, 32).T, 64, axis=0).astype(np.float32)
    dcol = np.ascontiguousarray(dcol.reshape(16, 128, 2).transpose(1, 0, 2))
    ng = np.ascontiguousarray(ssd_norm_g.reshape(16, 128).T.astype(np.float32))
    ins = []
    for core in range(NCORES):
        b, k = divmod(core, 4)
        def cat(a_main, a_ctx):
            return np.ascontiguousarray(np.concatenate([a_main[k * NT:(k + 1) * NT], a_ctx[k * NC_:(k + 1) * NC_]], 0).T)
        ins.append({"yT": cat(y[b][:SEQ], y[b][SEQ:]), "xsT": cat(xbc[b][:SEQ, :2048], xbc[b][SEQ:, :2048]), "zT": cat(z[b][:SEQ], z[b][SEQ:]),
                    "xT": cat(x[b], ctx[b]), "mvec": np.ascontiguousarray(m_lat[b].reshape(6, D).T), "mvec_c": np.ascontiguousarray(m_ctx.reshape(6, D).T),
                    "gvec": np.ascontiguousarray(g.T), "dcol": dcol, "ng": ng, "w_out": w_out, "w1": w1, "w2": w2})
    res = run_bass_kernel_spmd(nc, ins, core_ids=list(range(NCORES)))
    xo = np.empty_like(x)
    co = np.empty_like(ctx)
    for core in range(NCORES):
        b, k = divmod(core, 4)
        o = res.results[core]["outT"]
        xo[b, k * NT:(k + 1) * NT] = o[:, :NT].T
        co[b, k * NC_:(k + 1) * NC_] = o[:, NT:].T
    return xo, co


def build_projfin(NT=2048):
    nc = bass.Bass("TRN2", target_bir_lowering=False)
    fT = _mk(nc, "fT", [D, NT])
    xT = _mk(nc, "xT", [D, NT])
    mvec = _mk(nc, "mvec", [D, 6])
    gvec = _mk(nc, "gvec", [D, 4])
    w_out = _mk(nc, "w_out", [D, D])
    w1 = _mk(nc, "w1", [D, HID])
    w2 = _mk(nc, "w2", [HID, D])
    outT = _mk(nc, "outT", [D, NT], kind="ExternalOutput")
    kx = KX(nc)
    s = kx.s
    ms = kx.prep_mod(mvec, gvec)
    HWD, bw, nblk = NT // 2, 512, 2
    xres = s.tile([128, 8, HWD], F32, "xres")
    y_tiles = [s.tile([128, 8, bw], F32, "y%d" % i) for i in range(nblk)]
    for half in range(2):
        c0 = half * HWD
        s.dma("sp", xres[:], xT[:, c0:c0 + HWD].rearrange("(c p) n -> p c n", p=128), writes=[xres])
        with s.scope():
            fb = [s.tile([128, 8, bw], BF16, "fb%d" % i) for i in range(nblk)]
            for tb in range(nblk):
                for c in range(8):
                    kx.stage_cast(fb[tb], fb[tb][:, c, :], fT[c * 128:(c + 1) * 128, c0 + tb * bw:c0 + (tb + 1) * bw], bw)
            for db in range(2):
                wt, wv = kx.wload(w_out[:, db * 512:(db + 1) * 512], 8, 512)
                for sub in range(4):
                    dc = db * 4 + sub
                    for tb in range(nblk):
                        ps = kx.nps()
                        for c in range(8):
                            s.op("pe", lambda e, c=c, ps=ps, tb=tb, sub=sub, wv=wv: e.matmul(
                                ps[:, 0:bw], lhsT=wv[:, c, sub * 128:(sub + 1) * 128], rhs=fb[tb][:, c, :], start=(c == 0), stop=(c == 7)),
                                reads=[wt, fb[tb]], writes=[ps])
                        s.op("act", lambda e, ps=ps, dc=dc, tb=tb: e.activation(out=y_tiles[tb][:, dc, :], in_=ps[:, 0:bw], func=AF.Identity),
                             reads=[ps], writes=[y_tiles[tb]])
        x_views = [xres[:, :, tb * bw:(tb + 1) * bw] for tb in range(nblk)]
        blocks = [Blk(bw, ms, xres, x_views[tb], y_tiles[tb]) for tb in range(nblk)]
        emit_finish(kx, blocks, w1, w2, None)
        for tb in range(nblk):
            s.dma("sp", outT[:, c0 + tb * bw:c0 + (tb + 1) * bw].rearrange("(c p) n -> p c n", p=128), x_views[tb], reads=[xres], is_output=True)
    s.emit()
    return nc


def run_projfin(x, f, m_lat, g, w_out, w1, w2):
    NT = 2048
    nc = build_projfin(NT)
    ins = []
    for core in range(NCORES):
        b, k = divmod(core, 4)
        ins.append({"fT": np.ascontiguousarray(f[b, k * NT:(k + 1) * NT].T), "xT": np.ascontiguousarray(x[b, k * NT:(k + 1) * NT].T),
                    "mvec": np.ascontiguousarray(m_lat[b].reshape(6, D).T), "gvec": np.ascontiguousarray(g.T), "w_out": w_out, "w1": w1, "w2": w2})
    res = run_bass_kernel_spmd(nc, ins, core_ids=list(range(NCORES)))
    out = np.empty_like(x)
    for core in range(NCORES):
        b, k = divmod(core, 4)
        out[b, k * NT:(k + 1) * NT] = res.results[core]["outT"].T
    return out


def kernel(x, c, ctx, c_ctx, mod_w, mod_b, norm_g, mlp_w1, mlp_w2, ssd_w_in, ssd_conv_w, ssd_conv_b,
           ssd_dt_bias, ssd_a_log, ssd_d, ssd_norm_g, ssd_w_out, na_w_qkv, na_rpb, na_w_out,
           sc_w_in, sc_conv_w, sc_w_out, fn_w_out):
    f32 = lambda a: np.ascontiguousarray(np.asarray(a), dtype=np.float32)
    x, c, ctx, c_ctx, mod_w, mod_b, norm_g, mlp_w1, mlp_w2 = map(f32, (x, c, ctx, c_ctx, mod_w, mod_b, norm_g, mlp_w1, mlp_w2))
    m_lat, m_ctx = run_mod(c, c_ctx, mod_w, mod_b)
    z, xbc, dt = run_ssd_a(x, ctx, m_lat[0], m_ctx[0], norm_g[0], f32(ssd_w_in)[0], f32(ssd_conv_w)[0], f32(ssd_conv_b)[0], f32(ssd_dt_bias)[0])
    y = run_ssd_b(xbc, dt, f32(ssd_a_log)[0])
    x, ctx = run_ssd_c(x, ctx, y, xbc, z, m_lat[0], m_ctx[0], norm_g[0], f32(ssd_d)[0], f32(ssd_norm_g)[0], f32(ssd_w_out)[0], mlp_w1[0], mlp_w2[0])
    x = run_na(x, ctx, m_lat[1], m_ctx[1], norm_g[1], f32(na_w_qkv)[0], f32(na_rpb)[0], f32(na_w_out)[0], mlp_w1[1], mlp_w2[1])
    x, h3 = run_sc(x, m_lat[2], norm_g[2], f32(sc_w_in)[0], f32(sc_conv_w)[0], f32(sc_w_out)[0], mlp_w1[2], mlp_w2[2], m_lat[3], norm_g[3])
    f = run_fft(h3)
    x = run_projfin(x, f, m_lat[3], norm_g[3], f32(fn_w_out)[0], mlp_w1[3], mlp_w2[3])
    return x.astype(np.float32)
```

```python
import numpy as np
from contextlib import ExitStack
import concourse.bass as bass
import concourse.mybir as mybir
from concourse.bass_utils import run_bass_kernel_spmd

F32 = mybir.dt.float32
BF16 = mybir.dt.bfloat16
AF = mybir.ActivationFunctionType
ALU = mybir.AluOpType
AX = mybir.AxisListType

ENGS = ("pe", "dve", "act", "pool", "sp")
N_DMA_SEMS = 40
NCORES = 8

D = 1024
SEQ = 8192
BATCH = 2
CTX = 256
HID = 4096
EPS = 1e-6
ARENA_WORDS = 52736
CAST_ENGS = ("pool",)


class T:
    __slots__ = ("ap", "w", "r", "name")

    def __init__(self, ap, name=""):
        self.ap = ap
        self.w = None
        self.r = []
        self.name = name

    def __getitem__(self, k):
        return self.ap[k]


class Sched:
    def __init__(self, nc, same_engine_sync=True):
        self.nc = nc
        self.es = ExitStack()
        self.q = {e: [] for e in ENGS}
        self.cnt = {e: 0 for e in ENGS}
        self.prog = {e: self.es.enter_context(nc.semaphore("prog_" + e)) for e in ENGS}
        self.dsem = [self.es.enter_context(nc.semaphore("dma%d" % i)) for i in range(N_DMA_SEMS)]
        self.dval = [0] * N_DMA_SEMS
        self.dnext = 0
        self.known = {e: {} for e in ENGS}
        self.same_engine_sync = same_engine_sync
        self.sem_owner = {id(self.prog[e]): e for e in ENGS}
        self.out_events = []
        self.n_sb = 0
        self.arena = None
        self.aoff = 0
        self.amax = 0
        self.swsem = {}
        self.swused = {}

    def tile(self, shape, dtype, name=None):
        if self.arena is None:
            self.arena = self.es.enter_context(self.nc.sbuf_tensor("arena", [128, ARENA_WORDS], F32))
            self.aoff = 0
        esz = 2 if dtype == BF16 else 4
        n = 1
        for d in shape[1:]:
            n *= d
        words = (n * esz + 3) // 4
        words = (words + 7) // 8 * 8
        if self.aoff + words > ARENA_WORDS:
            raise RuntimeError("SBUF arena overflow: need %d words at %d" % (words, self.aoff))
        ap = self.arena[:, self.aoff:self.aoff + (n * esz + 3) // 4]
        self.aoff += words
        self.amax = max(self.amax, self.aoff)
        if dtype != F32:
            ap = ap.bitcast(dtype)
        ap = ap[0:shape[0], 0:n]
        if len(shape) >= 3:
            names = ["d%d" % i for i in range(len(shape) - 1)]
            kw = {names[i]: shape[1 + i] for i in range(len(shape) - 2)}
            ap = ap.rearrange("p (%s) -> p %s" % (" ".join(names), " ".join(names)), **kw)
        return T(ap, name or "")

    def ptile(self, shape=(128, 512), dtype=F32, name=None):
        self.n_sb += 1
        return T(self.es.enter_context(self.nc.psum_tensor(name or ("ps%d" % self.n_sb), list(shape), dtype)), name or "")

    def _deps(self, eng, reads, writes):
        evs = []
        for t in reads:
            if t.w is not None:
                evs.append(t.w)
        for t in writes:
            if t.w is not None:
                evs.append(t.w)
            evs.extend(t.r)
        need = {}
        for (sem, val) in evs:
            owner = self.sem_owner.get(id(sem))
            if owner == eng and (eng == "pe" or not self.same_engine_sync):
                continue
            k = id(sem)
            if self.known[eng].get(k, 0) >= val:
                continue
            if k not in need or need[k][1] < val:
                need[k] = (sem, val)
        for k, (sem, val) in need.items():
            self.known[eng][k] = val
        return list(need.values())

    def _commit(self, ev, reads, writes):
        for t in reads:
            t.r.append(ev)
            if len(t.r) > 64:
                best = {}
                for (sem, val) in t.r:
                    if id(sem) not in best or best[id(sem)][1] < val:
                        best[id(sem)] = (sem, val)
                t.r = list(best.values())
        for t in writes:
            t.w = ev
            t.r = []

    def op(self, eng, fn, reads=(), writes=()):
        waits = self._deps(eng, reads, writes)
        self.cnt[eng] += 1
        ev = (self.prog[eng], self.cnt[eng])
        self.q[eng].append((waits, fn, (self.prog[eng], 1)))
        self._commit(ev, reads, writes)
        return ev

    def dma(self, eng, out_ap, in_ap, reads=(), writes=(), is_output=False, sub=0, **kw):
        if eng == "pool":
            return self._dma_sw(out_ap, in_ap, reads, writes, is_output, sub, kw)
        i = self.dnext
        self.dnext = (self.dnext + 1) % N_DMA_SEMS
        sem = self.dsem[i]
        waits = self._deps(eng, reads, writes)
        if self.dval[i] > 0 and self.known[eng].get(id(sem), 0) < self.dval[i]:
            waits.append((sem, self.dval[i]))
            self.known[eng][id(sem)] = self.dval[i]
        self.dval[i] += 16
        ev = (sem, self.dval[i])

        def fn(e, out_ap=out_ap, in_ap=in_ap, kw=kw):
            return e.dma_start(out=out_ap, in_=in_ap, **kw)
        self.q[eng].append((waits, fn, (sem, 16)))
        self._commit(ev, reads, writes)
        if is_output:
            self.out_events.append(ev)
        return ev

    def _dma_sw(self, out_ap, in_ap, reads, writes, is_output, sub, kw):
        eng = "pool"
        slot = writes[0]
        key = (id(slot), sub)
        if key not in self.swsem:
            self.swsem[key] = self.es.enter_context(self.nc.semaphore("sw%d" % len(self.swsem)))
            self.swused[key] = False
        sem = self.swsem[key]
        waits = self._deps(eng, reads, writes)
        reuse = self.swused[key]
        if reuse and self.known[eng].get(id(sem), 0) < 16:
            waits.append((sem, 16))
        for e in ENGS:
            self.known[e].pop(id(sem), None)
        self.swused[key] = True
        ev = (sem, 16)

        def fn(e, out_ap=out_ap, in_ap=in_ap, kw=kw, sem=sem, reuse=reuse):
            if reuse:
                e.sem_clear(sem)
            return e.dma_start(out=out_ap, in_=in_ap, **kw)
        self.q[eng].append((waits, fn, (sem, 16)))
        self._commit(ev, reads, writes)
        if is_output:
            self.out_events.append(ev)
        return ev

    def barrier(self):
        waits = []
        for key, sem in self.swsem.items():
            if self.swused[key] and self.known["pool"].get(id(sem), 0) < 16:
                waits.append((sem, 16))
                self.known["pool"][id(sem)] = 16
        assert not waits
        for e in ENGS:
            waits = []
            for f in ENGS:
                if f != e and self.cnt[f] > self.known[e].get(id(self.prog[f]), 0):
                    waits.append((self.prog[f], self.cnt[f]))
                    self.known[e][id(self.prog[f])] = self.cnt[f]
            for i in range(N_DMA_SEMS):
                if self.dval[i] > self.known[e].get(id(self.dsem[i]), 0):
                    waits.append((self.dsem[i], self.dval[i]))
                    self.known[e][id(self.dsem[i])] = self.dval[i]
            if waits:
                self.q[e].append((waits, None, None))

    def scope(self):
        return _Scope(self)

    def emit(self):
        nc = self.nc
        seen = {}
        for (sem, val) in self.out_events:
            if seen.get(id(sem), (None, 0))[1] < val:
                seen[id(sem)] = (sem, val)
        fin = list(seen.values())
        engmap = {"pe": "tensor", "dve": "vector", "act": "scalar", "pool": "gpsimd", "sp": "sync"}
        with nc.Block() as block:
            for e in ENGS:
                q = self.q[e]
                is_sp = (e == "sp")

                def body(eng, q=q, is_sp=is_sp):
                    for waits, fn, inc in q:
                        for (sem, val) in waits:
                            eng.wait_ge(sem, val)
                        if fn is None:
                            continue
                        ins = fn(eng)
                        ins.then_inc(inc[0], inc[1])
                    if is_sp:
                        for (sem, val) in fin:
                            eng.wait_ge(sem, val)
                getattr(block, engmap[e])(body)
        self.es.close()


class _Scope:
    def __init__(self, s):
        self.s = s

    def __enter__(self):
        self.saved = self.s.aoff
        return self

    def __exit__(self, *a):
        self.s.barrier()
        self.s.aoff = self.saved
        return False


class KX:
    def __init__(self, nc, npsum=8, nwbuf=3, wbuf_elems=4096):
        self.nc = nc
        self.s = Sched(nc)
        s = self.s
        self.ones = s.tile([128, 128], BF16, "ones")
        self.eps = s.tile([128, 1], F32, "eps")
        s.op("dve", lambda e: e.memset(self.ones[:], 1.0), writes=[self.ones])
        s.op("dve", lambda e: e.memset(self.eps[:], EPS), writes=[self.eps])
        self.ps = [s.ptile(name="psb%d" % i) for i in range(npsum)]
        self.psi = 0
        self.wb = [s.tile([128, wbuf_elems], BF16, "wbuf%d" % i) for i in range(nwbuf)]
        self.wbi = 0
        self.sqs = [s.tile([128, 512], BF16, "sqbuf%d" % i) for i in range(2)]
        self.stg = [s.tile([128, 2048], F32, "stage%d" % i) for i in range(2)]
        self.stgi = 0
        self.dmaq = ("sp", "act")
        self.dqi = 0
        self.cast_engs = CAST_ENGS
        self.cei = 0
        self.rs = [s.tile([128, 512], F32, "rstd%d" % i) for i in range(2)]
        self.rsi = 0
        self.tc = [s.tile([128, 512], F32, "tmpc%d" % i) for i in range(3)]
        self.tci = 0

    def nps(self):
        t = self.ps[self.psi]
        self.psi = (self.psi + 1) % len(self.ps)
        return t

    def ntc(self):
        t = self.tc[self.tci]
        self.tci = (self.tci + 1) % len(self.tc)
        return t

    def nwb(self):
        t = self.wb[self.wbi]
        self.wbi = (self.wbi + 1) % len(self.wb)
        return t

    def stage_cast(self, dst_T, dst_ap, src_ap, n):
        s = self.s
        st = self.stg[self.stgi]
        self.stgi = (self.stgi + 1) % len(self.stg)
        q = "sp"
        shp = list(src_ap.shape)
        sv = st.ap[:, 0:n]
        if len(shp) == 3:
            sv = sv.rearrange("p (a b) -> p a b", a=shp[1])
        s.dma(q, sv, src_ap, writes=[st])
        ce = self.cast_engs[self.cei]
        self.cei = (self.cei + 1) % len(self.cast_engs)
        if ce == "act":
            s.op("act", lambda e: e.activation(out=dst_ap, in_=sv, func=AF.Identity), reads=[st], writes=[dst_T])
        else:
            s.op(ce, lambda e: e.tensor_copy(out=dst_ap, in_=sv), reads=[st], writes=[dst_T])

    def wload(self, w_ap, kc, ncols):
        t = self.nwb()
        view = t.ap[:, 0:kc * ncols].rearrange("p (c n) -> p c n", c=kc)
        src = w_ap.rearrange("(c p) n -> p c n", p=128)
        per = max(1, 2048 // ncols)
        for c0 in range(0, kc, per):
            c1 = min(kc, c0 + per)
            self.stage_cast(t, view[:, c0:c1, :], src[:, c0:c1, :], (c1 - c0) * ncols)
        return t, view

    def rstd(self, src_T, src_ap, n, nchunks=8, dim=D):
        s = self.s
        ps = self.nps()
        for c in range(nchunks):
            sq = self.sqs[c % 2]
            s.op("act", lambda e, c=c, sq=sq: e.activation(out=sq[:, 0:n], in_=src_ap[:, c, :], func=AF.Square), reads=[src_T], writes=[sq])
            s.op("pe", lambda e, c=c, sq=sq: e.matmul(ps[:, 0:n], lhsT=self.ones[:], rhs=sq[:, 0:n], start=(c == 0), stop=(c == nchunks - 1)),
                 reads=[self.ones, sq], writes=[ps])
        r = self.rs[self.rsi]
        self.rsi = (self.rsi + 1) % len(self.rs)
        s.op("act", lambda e: e.activation(out=r[:, 0:n], in_=ps[:, 0:n], func=AF.Ln, bias=self.eps[:], scale=1.0 / dim),
             reads=[ps, self.eps], writes=[r])
        s.op("act", lambda e: e.activation(out=r[:, 0:n], in_=r[:, 0:n], func=AF.Exp, scale=-0.5), reads=[r], writes=[r])
        return r

    def norm_mod(self, src_T, src_ap, n, ms, which, dst_T, dst_ap, tmp_T):
        s = self.s
        A, S = (ms.A0, ms.S0) if which == 0 else (ms.A2, ms.S2)
        r = self.rstd(src_T, src_ap, n)
        for c in range(8):
            tc = self.ntc()
            s.op("dve", lambda e, c=c, tc=tc: e.tensor_tensor(out=tc[:, 0:n], in0=src_ap[:, c, :], in1=r[:, 0:n], op=ALU.mult),
                 reads=[src_T, r], writes=[tc])
            s.op("act", lambda e, c=c, tc=tc: e.activation(out=dst_ap[:, c, :], in_=tc[:, 0:n], func=AF.Identity,
                                                     bias=S[:, c:c + 1], scale=A[:, c:c + 1]),
                 reads=[tc, ms.mv], writes=[dst_T])

    def resid_add(self, y_T, y_ap, n, ms, which, x_T, x_ap, tmp_T):
        s = self.s
        G = ms.G1 if which == 1 else ms.G2
        r = self.rstd(y_T, y_ap, n)
        for c in range(8):
            tc = self.ntc()
            s.op("dve", lambda e, c=c, tc=tc: e.tensor_tensor(out=tc[:, 0:n], in0=y_ap[:, c, :], in1=r[:, 0:n], op=ALU.mult),
                 reads=[y_T, r], writes=[tc])
            s.op("dve", lambda e, c=c, tc=tc: e.scalar_tensor_tensor(out=x_ap[:, c, :], in0=tc[:, 0:n], scalar=G[:, c:c + 1],
                                                               in1=x_ap[:, c, :], op0=ALU.mult, op1=ALU.add),
                 reads=[tc, x_T, ms.mv], writes=[x_T])

    def prep_mod(self, mvec_ap, gvec_ap, name="mv"):
        s = self.s
        mv = s.tile([128, 8, 16], F32, name)
        s.dma("sp", mv[:, :, 0:6], mvec_ap.rearrange("(c p) n -> p c n", p=128), writes=[mv])
        s.dma("sp", mv[:, :, 6:10], gvec_ap.rearrange("(c p) n -> p c n", p=128), writes=[mv])
        s.op("dve", lambda e: e.scalar_tensor_tensor(out=mv[:, :, 10], in0=mv[:, :, 1], scalar=1.0, in1=mv[:, :, 6], op0=ALU.add, op1=ALU.mult),
             reads=[mv], writes=[mv])
        s.op("dve", lambda e: e.tensor_tensor(out=mv[:, :, 11], in0=mv[:, :, 2], in1=mv[:, :, 7], op=ALU.mult), reads=[mv], writes=[mv])
        s.op("dve", lambda e: e.scalar_tensor_tensor(out=mv[:, :, 12], in0=mv[:, :, 4], scalar=1.0, in1=mv[:, :, 8], op0=ALU.add, op1=ALU.mult),
             reads=[mv], writes=[mv])
        s.op("dve", lambda e: e.tensor_tensor(out=mv[:, :, 13], in0=mv[:, :, 5], in1=mv[:, :, 9], op=ALU.mult), reads=[mv], writes=[mv])
        ms = ModSet()
        ms.mv = mv
        ms.A0, ms.S0, ms.G1, ms.A2, ms.S2, ms.G2 = mv[:, :, 10], mv[:, :, 0], mv[:, :, 11], mv[:, :, 12], mv[:, :, 3], mv[:, :, 13]
        return ms

    def wstream(self, loaders, computes, L=2, pre=None):
        n = len(loaders)
        h = list(pre) if pre else []
        for i in range(n + L):
            if len(h) <= i < n:
                h.append(loaders[i]())
            j = i - L
            if 0 <= j < n:
                computes[j](*h[j])

    def mlp_loaders(self, w1_ap, w2_ap):
        ld = [(lambda jb=jb: self.wload(w1_ap[:, jb * 512:(jb + 1) * 512], 8, 512)) for jb in range(HID // 512)]
        ld += [(lambda db=db: self.wload(w2_ap[:, db * 128:(db + 1) * 128], 32, 128)) for db in range(D // 128)]
        return ld

    def mlp(self, blocks, h2_tiles, w1_ap, w2_ap, hid_T, out_fn, pre=None):
        s = self.s
        offs = [0]
        for b in blocks:
            offs.append(offs[-1] + b.w)

        def c1(jb):
            def f(wt, wv):
                for sub in range(4):
                    hc = jb * 4 + sub
                    for i, b in enumerate(blocks):
                        ps = self.nps()
                        for c in range(8):
                            s.op("pe", lambda e, c=c, ps=ps, i=i, b=b, sub=sub, wv=wv: e.matmul(
                                ps[:, 0:b.w], lhsT=wv[:, c, sub * 128:(sub + 1) * 128], rhs=h2_tiles[i][:, c, 0:b.w], start=(c == 0), stop=(c == 7)),
                                reads=[wt, h2_tiles[i]], writes=[ps])
                        dst = hid_T[:, hc, offs[i]:offs[i + 1]]
                        s.op("act", lambda e, ps=ps, dst=dst, b=b: e.activation(out=dst, in_=ps[:, 0:b.w], func=AF.Relu), reads=[ps], writes=[hid_T])
                        s.op("dve", lambda e, dst=dst: e.tensor_tensor(out=dst, in0=dst, in1=dst, op=ALU.mult), reads=[hid_T], writes=[hid_T])
            return f

        def c2(db):
            def f(wt, wv):
                for i, b in enumerate(blocks):
                    ps = self.nps()
                    for c in range(32):
                        s.op("pe", lambda e, c=c, ps=ps, i=i, b=b, wv=wv: e.matmul(
                            ps[:, 0:b.w], lhsT=wv[:, c, :], rhs=hid_T[:, c, offs[i]:offs[i + 1]], start=(c == 0), stop=(c == 31)),
                            reads=[wt, hid_T], writes=[ps])
                    out_fn(i, db, ps)
            return f
        computes = [c1(jb) for jb in range(HID // 512)] + [c2(db) for db in range(D // 128)]
        self.wstream(self.mlp_loaders(w1_ap, w2_ap), computes, L=len(self.wb) - 1, pre=pre)


class ModSet:
    pass


class Blk:
    def __init__(self, w, ms, x_T, x_view, y_T):
        self.w, self.ms, self.x_T, self.x_view, self.y_T = w, ms, x_T, x_view, y_T


def _mk(nc, name, shape, dtype=F32, kind="ExternalInput"):
    return nc.dram_tensor(name, list(shape), dtype, kind=kind).ap()


def emit_finish(kx, blocks, w1_ap, w2_ap, tmp_T):
    s = kx.s
    lds = kx.mlp_loaders(w1_ap, w2_ap)
    pre = [lds[i]() for i in range(len(kx.wb) - 1)]
    for b in blocks:
        kx.resid_add(b.y_T, b.y_T[:, :, 0:b.w], b.w, b.ms, 1, b.x_T, b.x_view, tmp_T)
    with s.scope():
        h2_tiles = [s.tile([128, 8, b.w], BF16, "h2_%d" % i) for i, b in enumerate(blocks)]
        hid_T = s.tile([128, 32, sum(b.w for b in blocks)], BF16, "hid")
        for i, b in enumerate(blocks):
            kx.norm_mod(b.x_T, b.x_view, b.w, b.ms, 2, h2_tiles[i], h2_tiles[i][:, :, 0:b.w], tmp_T)

        def out_fn(i, dc, ps):
            b = blocks[i]
            s.op("act", lambda e: e.activation(out=b.y_T[:, dc, 0:b.w], in_=ps[:, 0:b.w], func=AF.Identity), reads=[ps], writes=[b.y_T])
        kx.mlp(blocks, h2_tiles, w1_ap, w2_ap, hid_T, out_fn, pre=pre)
    for b in blocks:
        kx.resid_add(b.y_T, b.y_T[:, :, 0:b.w], b.w, b.ms, 2, b.x_T, b.x_view, tmp_T)


def build_sc(NT=2048):
    nc = bass.Bass("TRN2", target_bir_lowering=False)
    xT = _mk(nc, "xT", [D, NT + 2])
    mvec = _mk(nc, "mvec", [D, 6])
    gvec = _mk(nc, "gvec", [D, 4])
    hmask = _mk(nc, "hmask", [128, 2])
    w_in = _mk(nc, "w_in", [D, 3 * D])
    convw = _mk(nc, "convw", [D, 3])
    w_out = _mk(nc, "w_out", [D, D])
    w1 = _mk(nc, "w1", [D, HID])
    w2 = _mk(nc, "w2", [HID, D])
    mvec3 = _mk(nc, "mvec3", [D, 6])
    gvec3 = _mk(nc, "gvec3", [D, 4])
    outT = _mk(nc, "outT", [D, NT], kind="ExternalOutput")
    h3T = _mk(nc, "h3T", [D, NT], kind="ExternalOutput")
    kx = KX(nc)
    s = kx.s
    ms = kx.prep_mod(mvec, gvec)
    ms3 = kx.prep_mod(mvec3, gvec3, "mv3")
    HWD = NT // 2
    bw = 512
    nblk = HWD // bw
    hm = s.tile([128, 2], F32, "hm")
    s.dma("sp", hm[:], hmask, writes=[hm])
    cw = s.tile([128, 8, 3], F32, "cw")
    s.dma("sp", cw[:], convw.rearrange("(c p) k -> p c k", p=128), writes=[cw])
    xh = s.tile([128, 8, HWD + 2], F32, "xh")
    tmp_T = None
    y_tiles = [s.tile([128, 8, bw], F32, "y%d" % i) for i in range(nblk)]
    w_in4 = w_in.rearrange("(c p) (t j n) -> p c t j n", p=128, t=3, j=8)

    class XV:
        pass
    for half in range(2):
        c0 = half * HWD
        s.dma("sp", xh[:], xT[:, c0:c0 + HWD + 2].rearrange("(c p) n -> p c n", p=128), writes=[xh])
        with s.scope():
            hT = [s.tile([128, 8, 342], BF16, "hT%d" % i) for i in range(3)]
            bcu = [s.tile([128, 3, HWD + 2], F32, "bcu%d" % i) for i in range(2)]
            acc = [s.tile([128, HWD], F32, "acc%d" % i) for i in range(2)]
            gT = [s.tile([128, 8, bw], BF16, "gT%d" % i) for i in range(nblk)]
            for i in range(3):
                kx.norm_mod(xh, xh[:, :, i * 342:(i + 1) * 342], 342, ms, 0, hT[i], hT[i][:, :, 0:342], tmp_T)
            def ld_in(j):
                wt = kx.nwb()
                wv = wt.ap[:, 0:8 * 3 * 128].rearrange("p (c t n) -> p c t n", c=8, t=3)
                for t in range(3):
                    kx.stage_cast(wt, wv[:, :, t, :], w_in4[:, :, t, j, :], 1024)
                return wt, wv

            def cp_in(j, wt, wv, half=half, hT=hT, bcu=bcu, acc=acc, gT=gT):
                bc = bcu[j % 2]
                ac = acc[j % 2]
                for t in range(3):
                    for i in range(3):
                        ps = kx.nps()
                        for c in range(8):
                            s.op("pe", lambda e, c=c, ps=ps, i=i, t=t, wv=wv: e.matmul(
                                ps[:, 0:342], lhsT=wv[:, c, t, :], rhs=hT[i][:, c, :], start=(c == 0), stop=(c == 7)),
                                reads=[wt, hT[i]], writes=[ps])
                        s.op("act", lambda e, ps=ps, t=t, i=i, bc=bc: e.activation(out=bc[:, t, i * 342:(i + 1) * 342], in_=ps[:, 0:342], func=AF.Identity),
                             reads=[ps], writes=[bc])
                s.op("dve", lambda e, bc=bc: e.tensor_tensor(out=bc[:, 1, :], in0=bc[:, 1, :], in1=bc[:, 2, :], op=ALU.mult), reads=[bc], writes=[bc])
                hc_ = 0 if half == 0 else HWD + 1
                s.op("dve", lambda e, bc=bc, hc_=hc_, half=half: e.tensor_scalar(out=bc[:, 1, hc_:hc_ + 1], in0=bc[:, 1, hc_:hc_ + 1], scalar1=hm[:, half:half + 1],
                                                                    scalar2=None, op0=ALU.mult), reads=[bc, hm], writes=[bc])
                s.op("dve", lambda e, bc=bc, ac=ac, j=j: e.tensor_scalar(out=ac[:, :], in0=bc[:, 1, 0:HWD], scalar1=cw[:, j, 0:1], scalar2=None, op0=ALU.mult),
                     reads=[bc, cw], writes=[ac])
                for k in (1, 2):
                    s.op("dve", lambda e, bc=bc, ac=ac, j=j, k=k: e.scalar_tensor_tensor(out=ac[:, :], in0=bc[:, 1, k:k + HWD], scalar=cw[:, j, k:k + 1],
                                                                                      in1=ac[:, :], op0=ALU.mult, op1=ALU.add),
                         reads=[bc, cw, ac], writes=[ac])
                for tb in range(nblk):
                    s.op("dve", lambda e, bc=bc, ac=ac, j=j, tb=tb: e.tensor_tensor(out=gT[tb][:, j, :], in0=ac[:, tb * bw:(tb + 1) * bw],
                                                                                 in1=bc[:, 0, 1 + tb * bw:1 + (tb + 1) * bw], op=ALU.mult),
                         reads=[ac, bc], writes=[gT[tb]])
            kx.wstream([(lambda j=j: ld_in(j)) for j in range(8)], [(lambda wt, wv, j=j: cp_in(j, wt, wv)) for j in range(8)], L=2)
            for db in range(2):
                wt, wv = kx.wload(w_out[:, db * 512:(db + 1) * 512], 8, 512)
                for sub in range(4):
                    dc = db * 4 + sub
                    for tb in range(nblk):
                        ps = kx.nps()
                        for c in range(8):
                            s.op("pe", lambda e, c=c, ps=ps, tb=tb, sub=sub, wv=wv: e.matmul(
                                ps[:, 0:bw], lhsT=wv[:, c, sub * 128:(sub + 1) * 128], rhs=gT[tb][:, c, :], start=(c == 0), stop=(c == 7)),
                                reads=[wt, gT[tb]], writes=[ps])
                        s.op("act", lambda e, ps=ps, dc=dc, tb=tb: e.activation(out=y_tiles[tb][:, dc, :], in_=ps[:, 0:bw], func=AF.Identity),
                             reads=[ps], writes=[y_tiles[tb]])
        x_views = [xh[:, :, 1 + tb * bw:1 + (tb + 1) * bw] for tb in range(nblk)]
        blocks = [Blk(bw, ms, xh, x_views[tb], y_tiles[tb]) for tb in range(nblk)]
        emit_finish(kx, blocks, w1, w2, tmp_T)
        for tb in range(nblk):
            s.dma("act", outT[:, c0 + tb * bw:c0 + (tb + 1) * bw].rearrange("(c p) n -> p c n", p=128), x_views[tb], reads=[xh], is_output=True)
            kx.norm_mod(xh, x_views[tb], bw, ms3, 0, y_tiles[tb], y_tiles[tb][:, :, 0:bw], None)
            s.dma("act", h3T[:, c0 + tb * bw:c0 + (tb + 1) * bw].rearrange("(c p) n -> p c n", p=128), y_tiles[tb][:, :, 0:bw], reads=[y_tiles[tb]], is_output=True)
    s.emit()
    return nc


def _halo_T(xb, t0, n, lo=1, hi=1):
    L = xb.shape[0]
    out = np.zeros((D, lo + n + hi), np.float32)
    a = max(t0 - lo, 0)
    b = min(t0 + n + hi, L)
    out[:, a - (t0 - lo):b - (t0 - lo)] = xb[a:b].T
    return out


def run_sc(x, m_lat, g, w_in, convw, w_out, w1, w2, m_lat3, g3):
    NT = 2048
    nc = build_sc(NT)
    ins = []
    for core in range(NCORES):
        b, k = divmod(core, 4)
        t0 = k * NT
        hm = np.ones((128, 2), np.float32)
        if k == 0:
            hm[:, 0] = 0
        if k == 3:
            hm[:, 1] = 0
        ins.append({
            "xT": _halo_T(x[b], t0, NT), "mvec": np.ascontiguousarray(m_lat[b].reshape(6, D).T), "gvec": np.ascontiguousarray(g.T),
            "hmask": hm, "w_in": w_in, "convw": np.ascontiguousarray(convw.T), "w_out": w_out, "w1": w1, "w2": w2,
            "mvec3": np.ascontiguousarray(m_lat3[b].reshape(6, D).T), "gvec3": np.ascontiguousarray(g3.T)})
    res = run_bass_kernel_spmd(nc, ins, core_ids=list(range(NCORES)))
    out = np.empty_like(x)
    h3 = np.empty_like(x)
    for core in range(NCORES):
        b, k = divmod(core, 4)
        out[b, k * NT:(k + 1) * NT] = res.results[core]["outT"].T
        h3[b, k * NT:(k + 1) * NT] = res.results[core]["h3T"].T
    return out, h3


def build_mod():
    nc = bass.Bass("TRN2", target_bir_lowering=False)
    ccT = _mk(nc, "ccT", [D, 3])
    w = _mk(nc, "w", [D, 3072])
    bvec = _mk(nc, "bvec", [128, 24])
    outT = _mk(nc, "outT", [3072, 3], kind="ExternalOutput")
    s = Sched(nc)
    cc = s.tile([128, 8, 3], F32, "cc")
    bt = s.tile([128, 24], F32, "bt")
    ot = s.tile([128, 24, 3], F32, "ot")
    ps = [s.ptile(name="ps%d" % i) for i in range(4)]
    wb = [s.tile([128, 8, 512], F32, "wb%d" % i) for i in range(3)]
    s.dma("sp", cc[:], ccT.rearrange("(c p) n -> p c n", p=128), writes=[cc])
    s.dma("sp", bt[:], bvec, writes=[bt])
    s.op("act", lambda e: e.activation(out=cc[:], in_=cc[:], func=AF.Silu), reads=[cc], writes=[cc])
    for jb in range(6):
        wt = wb[jb % 3]
        s.dma("sp" if jb % 2 == 0 else "act", wt[:], w[:, jb * 512:(jb + 1) * 512].rearrange("(c p) n -> p c n", p=128), writes=[wt])
        for sub in range(4):
            oc = jb * 4 + sub
            p = ps[oc % 4]
            for c in range(8):
                s.op("pe", lambda e, c=c, p=p, sub=sub, wt=wt: e.matmul(p[:, 0:3], lhsT=wt[:, c, sub * 128:(sub + 1) * 128], rhs=cc[:, c, :],
                                                                    start=(c == 0), stop=(c == 7)), reads=[wt, cc], writes=[p])
            s.op("act", lambda e, p=p, oc=oc: e.activation(out=ot[:, oc, :], in_=p[:, 0:3], func=AF.Identity, bias=bt[:, oc:oc + 1], scale=1.0),
                 reads=[p, bt], writes=[ot])
    s.dma("sp", outT.rearrange("(c p) n -> p c n", p=128), ot[:], reads=[ot], is_output=True)
    s.emit()
    return nc


def run_mod(c, c_ctx, mod_w, mod_b):
    nc = build_mod()
    ccT = np.ascontiguousarray(np.concatenate([c, c_ctx[None]], 0).T)
    ins = []
    for core in range(NCORES):
        i, hf = divmod(core, 2)
        ins.append({"ccT": ccT, "w": np.ascontiguousarray(mod_w[i][:, hf * 3072:(hf + 1) * 3072]),
                    "bvec": np.ascontiguousarray(mod_b[i][hf * 3072:(hf + 1) * 3072].reshape(24, 128).T)})
    res = run_bass_kernel_spmd(nc, ins, core_ids=list(range(NCORES)))
    m = np.zeros((4, 3, 6144), np.float32)
    for core in range(NCORES):
        i, hf = divmod(core, 2)
        m[i, :, hf * 3072:(hf + 1) * 3072] = res.results[core]["outT"].T
    return m[:, 0:2], m[:, 2]


def _fft_consts():
    c = np.arange(128)
    ang = 2 * np.pi * np.outer(c, c) / 128.0
    fc_cos, fc_sin = np.cos(ang), np.sin(ang)
    f1 = np.concatenate([fc_cos, -fc_sin], 1).astype(np.float32)
    f2 = np.concatenate([fc_sin, fc_cos], 1).astype(np.float32)
    t2 = np.arange(64)[:, None, None]
    k1 = np.arange(128)[None, :, None]
    k2 = np.arange(64)[None, None, :]
    ang3 = 2 * np.pi * (k1 * t2 / 8192.0 + k2 * t2 / 64.0)
    g = np.stack([np.cos(ang3), np.sin(ang3)], 2).astype(np.float32)
    return f1, f2, np.ascontiguousarray(g.reshape(64, 128 * 2 * 64))


def build_fft():
    nc = bass.Bass("TRN2", target_bir_lowering=False)
    hg = _mk(nc, "hg", [2, 128, 8192])
    f1 = _mk(nc, "f1", [128, 256])
    f2 = _mk(nc, "f2", [128, 256])
    g3 = _mk(nc, "g3", [64, 128 * 128])
    fo = _mk(nc, "fo", [2, 64, 128 * 128], kind="ExternalOutput")
    s = Sched(nc)
    f1t = s.tile([128, 256], BF16, "f1t")
    f2t = s.tile([128, 256], BF16, "f2t")
    g3t = s.tile([64, 128, 2, 64], BF16, "g3t")
    stg = [s.tile([128, 2048], F32, "stage%d" % i) for i in range(2)]
    stgi = [0]

    def stage_cast(dst_T, dst_ap, src_ap, np_, n):
        st = stg[stgi[0] % 2]
        s.dma("sp" if stgi[0] % 2 == 0 else "act", st[0:np_, 0:n], src_ap, writes=[st])
        stgi[0] += 1
        s.op("pool", lambda e: e.tensor_copy(out=dst_ap, in_=st[0:np_, 0:n]), reads=[st], writes=[dst_T])
    stage_cast(f1t, f1t[:], f1, 128, 256)
    stage_cast(f2t, f2t[:], f2, 128, 256)
    for q in range(8):
        stage_cast(g3t, g3t[:, q * 16:(q + 1) * 16, :, :].rearrange("p a r k -> p (a r k)"), g3[:, q * 2048:(q + 1) * 2048], 64, 2048)
    ps = [s.ptile(name="ps%d" % i) for i in range(8)]
    psi = [0]

    def nps():
        t = ps[psi[0]]
        psi[0] = (psi[0] + 1) % 8
        return t
    xin = s.tile([128, 64, 128], BF16, "xin")
    B = s.tile([128, 64, 2, 128], BF16, "B")
    A = s.tile([64, 2, 128, 128], BF16, "A")
    ob = [s.tile([64, 16, 128], F32, "ob%d" % i) for i in range(2)]
    for g in range(2):
        for q in range(4):
            stage_cast(xin, xin[:, q * 16:(q + 1) * 16, :].rearrange("p a b -> p (a b)"), hg[g][:, q * 2048:(q + 1) * 2048], 128, 2048)
        for tp in range(32):
            p = nps()
            for u in range(2):
                t2 = tp * 2 + u
                s.op("pe", lambda e, p=p, u=u, t2=t2: e.matmul(p[:, u * 256:(u + 1) * 256], lhsT=xin[:, t2, :], rhs=f1t[:], start=True, stop=True),
                     reads=[xin, f1t], writes=[p])
            eng = "act" if tp % 2 == 0 else "dve"
            dst = B[:, tp * 2:tp * 2 + 2, :, :].rearrange("p a r m -> p (a r m)")
            if eng == "act":
                s.op("act", lambda e, p=p, dst=dst: e.activation(out=dst, in_=p[:], func=AF.Identity), reads=[p], writes=[B])
            else:
                s.op("dve", lambda e, p=p, dst=dst: e.tensor_copy(out=dst, in_=p[:]), reads=[p], writes=[B])
        for mp in range(64):
            p = nps()
            for u in range(2):
                m = mp * 2 + u
                s.op("pe", lambda e, p=p, u=u, m=m: e.matmul(p[0:64, u * 256:(u + 1) * 256], lhsT=B[:, :, 0, m], rhs=f1t[:], start=True, stop=False),
                     reads=[B, f1t], writes=[p])
                s.op("pe", lambda e, p=p, u=u, m=m: e.matmul(p[0:64, u * 256:(u + 1) * 256], lhsT=B[:, :, 1, m], rhs=f2t[:], start=False, stop=True),
                     reads=[B, f2t], writes=[p])
            for u in range(2):
                m = mp * 2 + u
                src = p[0:64, u * 256:(u + 1) * 256].rearrange("p (r k) -> p r k", r=2)
                if u == 0:
                    s.op("act", lambda e, src=src, m=m: e.activation(out=A[:, :, :, m], in_=src, func=AF.Identity), reads=[p], writes=[A])
                else:
                    s.op("dve", lambda e, src=src, m=m: e.tensor_copy(out=A[:, :, :, m], in_=src), reads=[p], writes=[A])
        for kb in range(8):
            o = ob[kb % 2]
            for kq in range(4):
                p = nps()
                for u in range(4):
                    k1 = kb * 16 + kq * 4 + u
                    s.op("pe", lambda e, p=p, u=u, k1=k1: e.matmul(p[0:64, u * 128:(u + 1) * 128], lhsT=g3t[:, k1, 0, :], rhs=A[:, 0, k1, :], start=True, stop=False),
                         reads=[g3t, A], writes=[p])
                    s.op("pe", lambda e, p=p, u=u, k1=k1: e.matmul(p[0:64, u * 128:(u + 1) * 128], lhsT=g3t[:, k1, 1, :], rhs=A[:, 1, k1, :], start=False, stop=True),
                         reads=[g3t, A], writes=[p])
                dst = o[:, kq * 4:(kq + 1) * 4, :].rearrange("p a m -> p (a m)")
                if kq % 2 == 0:
                    s.op("act", lambda e, p=p, dst=dst: e.activation(out=dst, in_=p[0:64, :], func=AF.Identity), reads=[p], writes=[o])
                else:
                    s.op("dve", lambda e, p=p, dst=dst: e.tensor_copy(out=dst, in_=p[0:64, :]), reads=[p], writes=[o])
            s.dma("sp", fo[g][:, kb * 16 * 128:(kb + 1) * 16 * 128], o[:].rearrange("p a m -> p (a m)"), reads=[o], is_output=True)
    s.emit()
    return nc


def run_fft(h):
    nc = build_fft()
    f1, f2, g3 = _fft_consts()
    ins = []
    for core in range(NCORES):
        b, gp = divmod(core, 4)
        hb = np.asarray(h[b][:, gp * 256:(gp + 1) * 256]).reshape(128, 64, 2, 128)
        hgc = np.ascontiguousarray(hb.transpose(2, 3, 1, 0)).reshape(2, 128, 8192)
        ins.append({"hg": hgc.astype(np.float32), "f1": f1, "f2": f2, "g3": g3})
    res = run_bass_kernel_spmd(nc, ins, core_ids=list(range(NCORES)))
    out = np.empty((BATCH, SEQ, D), np.float32)
    for core in range(NCORES):
        b, gp = divmod(core, 4)
        fo = res.results[core]["fo"].reshape(2, 64, 128, 128)
        out[b, :, gp * 256:(gp + 1) * 256] = fo.transpose(1, 2, 0, 3).reshape(8192, 256)
    return out


NA_ROWS_LOCAL = 39


def build_na(NT=2048):
    nc = bass.Bass("TRN2", target_bir_lowering=False)
    xhT = _mk(nc, "xhT", [D, NA_ROWS_LOCAL * 64])
    ctxT = _mk(nc, "ctxT", [D, CTX])
    mvec = _mk(nc, "mvec", [D, 6])
    mvec_c = _mk(nc, "mvec_c", [D, 6])
    gvec = _mk(nc, "gvec", [D, 4])
    w_qkv = _mk(nc, "w_qkv", [D, 3 * D])
    w_out = _mk(nc, "w_out", [D, D])
    w1 = _mk(nc, "w1", [D, HID])
    w2 = _mk(nc, "w2", [HID, D])
    btab = _mk(nc, "btab", [8, 2, 128, 3840])
    identd = _mk(nc, "ident", [128, 128])
    outT = _mk(nc, "outT", [D, NT], kind="ExternalOutput")
    kx = KX(nc)
    s = kx.s
    ms = kx.prep_mod(mvec, gvec)
    msc = kx.prep_mod(mvec_c, gvec, "mvc")
    ident = s.tile([128, 128], BF16, "ident")
    kx.stage_cast(ident, ident[:], identd, 128 * 128 // 128)
    HWD, bw, nblk = 1024, 512, 2
    WIN = 23 * 64
    xres = s.tile([128, 8, HWD], F32, "xres")
    y_tiles = [s.tile([128, 8, bw], F32, "y%d" % i) for i in range(nblk)]
    hcT = s.tile([128, 8, CTX], BF16, "hcT")
    s.dma("sp", y_tiles[0][:, :, 0:CTX], ctxT.rearrange("(c p) n -> p c n", p=128), writes=[y_tiles[0]])
    kx.norm_mod(y_tiles[0], y_tiles[0][:, :, 0:CTX], CTX, msc, 0, hcT, hcT[:, :, :], None)
    w_qkv4 = w_qkv.rearrange("(c p) (t j n) -> p c t j n", p=128, t=3, j=8)
    for half in range(2):
        wc0 = half * HWD
        s.dma("sp", xres[:], xhT[:, 256 + wc0:256 + wc0 + HWD].rearrange("(c p) n -> p c n", p=128), writes=[xres])
        with s.scope():
            hT = s.tile([128, 8, WIN], BF16, "hT")
            attT = [s.tile([128, 8, bw], BF16, "attT%d" % i) for i in range(nblk)]
            qT = s.tile([128, HWD], BF16, "qT")
            kT = s.tile([128, WIN], BF16, "kT")
            kcT = s.tile([128, CTX], BF16, "kcT")
            vt = s.tile([128, 12, 2, 65], BF16, "vt")
            vct = s.tile([128, 2, 2, 65], BF16, "vct")
            att_hp = s.tile([128, 8, 128], BF16, "att_hp")
            bt = s.tile([128, 3, 2, 5, 128], F32, "bt")
            stA = [s.tile([128, 512], F32, "stA%d" % i) for i in range(2)]
            stB = [s.tile([128, 128], F32, "stB%d" % i) for i in range(2)]
            pA = [s.tile([128, 512], BF16, "pA%d" % i) for i in range(2)]
            pB = [s.tile([128, 128], BF16, "pB%d" % i) for i in range(2)]
            pC = [s.tile([128, 256], BF16, "pC%d" % i) for i in range(2)]
            rc = [s.tile([128, 1], F32, "rc%d" % i) for i in range(2)]
            s.op("pool", lambda e: e.memset(vt[:, :, :, 64:65], 1.0), writes=[vt])
            s.op("pool", lambda e: e.memset(vct[:, :, :, 64:65], 1.0), writes=[vct])
            for j, (a, w) in enumerate(((0, 512), (512, 512), (1024, WIN - 1024))):
                yt = y_tiles[j % 2]
                s.dma("sp", yt[:, :, 0:w], xhT[:, wc0 + a:wc0 + a + w].rearrange("(c p) n -> p c n", p=128), writes=[yt])
                kx.norm_mod(yt, yt[:, :, 0:w], w, ms, 0, hT, hT[:, :, a:a + w], None)
            itc = [0]

            def ld_hp(hp):
                wt = kx.nwb()
                wv = wt.ap[:, 0:8 * 3 * 128].rearrange("p (c t n) -> p c t n", c=8, t=3)
                for t in range(3):
                    kx.stage_cast(wt, wv[:, :, t, :], w_qkv4[:, :, t, hp, :], 1024)
                return wt, wv

            def cp_hp(hp, wt, wv, half=half, hT=hT, attT=attT, qT=qT, kT=kT, kcT=kcT, vt=vt, vct=vct, att_hp=att_hp, bt=bt,
                      stA=stA, stB=stB, pA=pA, pB=pB, pC=pC, rc=rc):
                it = itc[0]
                s.dma("act", bt[:].rearrange("p a b c q -> p (a b c q)"), btab[hp, half], writes=[bt])
                for blk in range(2):
                    ps = kx.nps()
                    for c in range(8):
                        s.op("pe", lambda e, c=c, ps=ps, blk=blk, wv=wv: e.matmul(
                            ps[:, 0:512], lhsT=wv[:, c, 0, :], rhs=hT[:, c, 256 + blk * 512:256 + (blk + 1) * 512], start=(c == 0), stop=(c == 7)),
                            reads=[wt, hT], writes=[ps])
                    s.op("act", lambda e, ps=ps, blk=blk: e.activation(out=qT[:, blk * 512:(blk + 1) * 512], in_=ps[:, 0:512], func=AF.Identity),
                         reads=[ps], writes=[qT])
                for (a, w) in ((0, 512), (512, 512), (1024, WIN - 1024)):
                    ps = kx.nps()
                    for c in range(8):
                        s.op("pe", lambda e, c=c, ps=ps, a=a, w=w, wv=wv: e.matmul(
                            ps[:, 0:w], lhsT=wv[:, c, 1, :], rhs=hT[:, c, a:a + w], start=(c == 0), stop=(c == 7)),
                            reads=[wt, hT], writes=[ps])
                    s.op("dve", lambda e, ps=ps, a=a, w=w: e.tensor_copy(out=kT[:, a:a + w], in_=ps[:, 0:w]), reads=[ps], writes=[kT])
                ps = kx.nps()
                for c in range(8):
                    s.op("pe", lambda e, c=c, ps=ps, wv=wv: e.matmul(ps[:, 0:CTX], lhsT=wv[:, c, 1, :], rhs=hcT[:, c, :], start=(c == 0), stop=(c == 7)),
                         reads=[wt, hcT], writes=[ps])
                s.op("act", lambda e, ps=ps: e.activation(out=kcT[:, :], in_=ps[:, 0:CTX], func=AF.Identity), reads=[ps], writes=[kcT])
                for tg in range(3):
                    ps = kx.nps()
                    for u in range(4):
                        tcn = tg * 4 + u
                        ntok = 64 if tcn == 11 else 128
                        for c in range(8):
                            s.op("pe", lambda e, c=c, ps=ps, u=u, tcn=tcn, ntok=ntok, wv=wv: e.matmul(
                                ps[0:ntok, u * 128:(u + 1) * 128], lhsT=hT[:, c, tcn * 128:tcn * 128 + ntok], rhs=wv[:, c, 2, :], start=(c == 0), stop=(c == 7)),
                                reads=[wt, hT], writes=[ps])
                    nfull = 4 if tg < 2 else 3
                    s.op("act", lambda e, ps=ps, tg=tg, nfull=nfull: e.activation(
                        out=vt[:, tg * 4:tg * 4 + nfull, :, 0:64], in_=ps[:, 0:nfull * 128].rearrange("p (a b d) -> p a b d", a=nfull, b=2), func=AF.Identity),
                        reads=[ps], writes=[vt])
                    if tg == 2:
                        s.op("act", lambda e, ps=ps: e.activation(out=vt[0:64, 11, :, 0:64], in_=ps[0:64, 384:512].rearrange("p (b d) -> p b d", b=2), func=AF.Identity),
                             reads=[ps], writes=[vt])
                ps = kx.nps()
                for u in range(2):
                    for c in range(8):
                        s.op("pe", lambda e, c=c, ps=ps, u=u, wv=wv: e.matmul(
                            ps[:, u * 128:(u + 1) * 128], lhsT=hcT[:, c, u * 128:(u + 1) * 128], rhs=wv[:, c, 2, :], start=(c == 0), stop=(c == 7)),
                            reads=[wt, hcT], writes=[ps])
                s.op("dve", lambda e, ps=ps: e.tensor_copy(out=vct[:, :, :, 0:64], in_=ps[:, 0:256].rearrange("p (a b d) -> p a b d", a=2, b=2)),
                     reads=[ps], writes=[vct])
                for h2 in range(2):
                    pb = 64 * h2
                    for i in range(8):
                        P = half * 8 + i
                        cls = (0 if i == 0 else 1 if i == 1 else 2) if half == 0 else (1 if i == 6 else 2 if i == 7 else 0)
                        a_, b_, c_ = stA[it % 2], stB[it % 2], rc[it % 2]
                        pa, pb_, pc = pA[it % 2], pB[it % 2], pC[it % 2]
                        it += 1
                        psA = kx.nps()
                        for ch in range(4):
                            s.op("pe", lambda e, psA=psA, ch=ch, i=i, pb=pb: e.matmul(
                                psA[:, ch * 128:(ch + 1) * 128], lhsT=kT[pb:pb + 64, 128 * (i + ch):128 * (i + ch + 1)], rhs=qT[pb:pb + 64, 128 * i:128 * (i + 1)],
                                start=True, stop=True), reads=[kT, qT], writes=[psA])
                        psB = kx.nps()
                        s.op("pe", lambda e, psB=psB, i=i, pb=pb: e.matmul(
                            psB[0:64, 0:128], lhsT=kT[pb:pb + 64, 128 * (i + 4):128 * (i + 4) + 64], rhs=qT[pb:pb + 64, 128 * i:128 * (i + 1)],
                            start=True, stop=True), reads=[kT, qT], writes=[psB])
                        for cc in range(2):
                            s.op("pe", lambda e, psB=psB, cc=cc, i=i, pb=pb: e.matmul(
                                psB[:, 128 + cc * 128:256 + cc * 128], lhsT=kcT[pb:pb + 64, cc * 128:(cc + 1) * 128], rhs=qT[pb:pb + 64, 128 * i:128 * (i + 1)],
                                start=True, stop=True), reads=[kcT, qT], writes=[psB])
                        s.op("dve", lambda e, psA=psA, a_=a_, cls=cls, h2=h2: e.scalar_tensor_tensor(
                            out=a_[:, :], in0=psA[:, :], scalar=0.125, in1=bt[:, cls, h2, 0:4, :].rearrange("p a q -> p (a q)"), op0=ALU.mult, op1=ALU.add),
                            reads=[psA, bt], writes=[a_])
                        s.op("act", lambda e, a_=a_, pa=pa: e.activation(out=pa[:, :], in_=a_[:, :], func=AF.Exp), reads=[a_], writes=[pa])
                        s.op("dve", lambda e, psB=psB, b_=b_, cls=cls, h2=h2: e.scalar_tensor_tensor(
                            out=b_[0:64, :], in0=psB[0:64, 0:128], scalar=0.125, in1=bt[0:64, cls, h2, 4, :], op0=ALU.mult, op1=ALU.add),
                            reads=[psB, bt], writes=[b_])
                        s.op("act", lambda e, b_=b_, pb_=pb_: e.activation(out=pb_[0:64, :], in_=b_[0:64, :], func=AF.Exp), reads=[b_], writes=[pb_])
                        s.op("act", lambda e, psB=psB, pc=pc: e.activation(out=pc[:, :], in_=psB[:, 128:384], func=AF.Exp, scale=0.125), reads=[psB], writes=[pc])
                        psO = kx.nps()
                        for ch in range(4):
                            s.op("pe", lambda e, psO=psO, ch=ch, i=i, h2=h2, pa=pa: e.matmul(
                                psO[:, 0:65], lhsT=pa[:, ch * 128:(ch + 1) * 128], rhs=vt[:, i + ch, h2, :], start=(ch == 0), stop=False),
                                reads=[pa, vt], writes=[psO])
                        s.op("pe", lambda e, psO=psO, i=i, h2=h2, pb_=pb_: e.matmul(
                            psO[:, 0:65], lhsT=pb_[0:64, :], rhs=vt[0:64, i + 4, h2, :], start=False, stop=False), reads=[pb_, vt], writes=[psO])
                        for cc in range(2):
                            s.op("pe", lambda e, psO=psO, cc=cc, h2=h2, pc=pc: e.matmul(
                                psO[:, 0:65], lhsT=pc[:, cc * 128:(cc + 1) * 128], rhs=vct[:, cc, h2, :], start=False, stop=(cc == 1)),
                                reads=[pc, vct], writes=[psO])
                        s.op("dve", lambda e, psO=psO, c_=c_: e.reciprocal(out=c_[:, :], in_=psO[:, 64:65]), reads=[psO], writes=[c_])
                        s.op("dve", lambda e, psO=psO, c_=c_, i=i, h2=h2: e.tensor_scalar(
                            out=att_hp[:, i, h2 * 64:(h2 + 1) * 64], in0=psO[:, 0:64], scalar1=c_[:, 0:1], scalar2=None, op0=ALU.mult),
                            reads=[psO, c_], writes=[att_hp])
                for ig in range(2):
                    pst = kx.nps()
                    pv = pst.ap.bitcast(BF16)
                    for u in range(4):
                        i = ig * 4 + u
                        s.op("pe", lambda e, pv=pv, u=u, i=i: e.transpose(out=pv[:, u * 128:(u + 1) * 128], in_=att_hp[:, i, :], identity=ident[:]),
                             reads=[att_hp, ident], writes=[pst])
                    s.op("dve", lambda e, pv=pv, ig=ig, hp=hp: e.tensor_copy(out=attT[ig][:, hp, :], in_=pv[:, 0:512]), reads=[pst], writes=[attT[ig]])
                itc[0] = it
            kx.wstream([(lambda hp=hp: ld_hp(hp)) for hp in range(8)], [(lambda wt, wv, hp=hp: cp_hp(hp, wt, wv)) for hp in range(8)], L=2)
            for db in range(2):
                wt, wv = kx.wload(w_out[:, db * 512:(db + 1) * 512], 8, 512)
                for sub in range(4):
                    dc = db * 4 + sub
                    for tb in range(nblk):
                        ps = kx.nps()
                        for c in range(8):
                            s.op("pe", lambda e, c=c, ps=ps, tb=tb, sub=sub, wv=wv: e.matmul(
                                ps[:, 0:bw], lhsT=wv[:, c, sub * 128:(sub + 1) * 128], rhs=attT[tb][:, c, :], start=(c == 0), stop=(c == 7)),
                                reads=[wt, attT[tb]], writes=[ps])
                        s.op("act", lambda e, ps=ps, dc=dc, tb=tb: e.activation(out=y_tiles[tb][:, dc, :], in_=ps[:, 0:bw], func=AF.Identity),
                             reads=[ps], writes=[y_tiles[tb]])
        x_views = [xres[:, :, tb * bw:(tb + 1) * bw] for tb in range(nblk)]
        blocks = [Blk(bw, ms, xres, x_views[tb], y_tiles[tb]) for tb in range(nblk)]
        emit_finish(kx, blocks, w1, w2, None)
        for tb in range(nblk):
            s.dma("act", outT[:, half * HWD + tb * bw:half * HWD + (tb + 1) * bw].rearrange("(c p) n -> p c n", p=128), x_views[tb], reads=[xres], is_output=True)
    s.emit()
    return nc


def _na_rowmap(kq):
    if kq == 0:
        return [5, 6, 7, -1] + list(range(0, 35))
    if kq == 3:
        return [92 + j for j in range(36)] + [120, 121, -1]
    return [32 * kq - 4 + j for j in range(NA_ROWS_LOCAL)]


def _na_tables(rpb, kq):
    NEG = -30000.0
    rm = _na_rowmap(kq)
    tabs = {}
    qc = np.arange(64)
    kc = np.arange(64)
    c0 = np.clip(qc - 8, 0, 48)
    colok = (kc[:, None] >= c0[None, :]) & (kc[:, None] < c0[None, :] + 16)
    dc = kc[:, None] - qc[None, :] + 15
    dcc = np.clip(dc, 0, 30)
    for P in (0, 1, 2, 14, 15):
        tab = np.full((16, 640, 128), NEG, np.float32)
        seen = set()
        for j in range(9):
            g = rm[2 * P + j]
            if g < 0 or g in seen:
                continue
            seen.add(g)
            for u in range(2):
                qr = 32 * kq + 2 * P + u
                r0 = min(max(qr - 4, 0), 120)
                if not (r0 <= g < r0 + 8):
                    continue
                dr = g - qr + 7
                vals = rpb[:, dr, :][:, dcc]
                blk = np.where(colok[None], vals, NEG)
                tab[:, j * 64:(j + 1) * 64, u * 64:(u + 1) * 64] = blk
        tabs[P] = tab.reshape(16, 5, 128, 128)
    out = np.empty((8, 2, 128, 3, 2, 5, 128), np.float32)
    for half, Ps in ((0, (0, 1, 2)), (1, (2, 14, 15))):
        for ci, P in enumerate(Ps):
            t = tabs[P].reshape(8, 2, 5, 128, 128)
            out[:, half, :, ci] = t.transpose(0, 3, 1, 2, 4)
    return out.reshape(8, 2, 128, 3840)


def run_na(x, ctx, m_lat, m_ctx, g, w_qkv, rpb, w_out, w1, w2):
    NT = 2048
    nc = build_na(NT)
    ident = np.eye(128, dtype=np.float32)
    tabs = [_na_tables(rpb, kq) for kq in range(4)]
    ins = []
    for core in range(NCORES):
        b, kq = divmod(core, 4)
        rm = _na_rowmap(kq)
        xg = x[b].reshape(128, 64, D)
        xh = np.zeros((NA_ROWS_LOCAL, 64, D), np.float32)
        for j, gr in enumerate(rm):
            if gr >= 0:
                xh[j] = xg[gr]
        ins.append({
            "xhT": np.ascontiguousarray(xh.reshape(-1, D).T), "ctxT": np.ascontiguousarray(ctx[b].T),
            "mvec": np.ascontiguousarray(m_lat[b].reshape(6, D).T), "mvec_c": np.ascontiguousarray(m_ctx.reshape(6, D).T),
            "gvec": np.ascontiguousarray(g.T), "w_qkv": w_qkv, "w_out": w_out, "w1": w1, "w2": w2, "btab": tabs[kq], "ident": ident})
    res = run_bass_kernel_spmd(nc, ins, core_ids=list(range(NCORES)))
    out = np.empty_like(x)
    for core in range(NCORES):
        b, kq = divmod(core, 4)
        out[b, kq * NT:(kq + 1) * NT] = res.results[core]["outT"].T
    return out


SSD_IN = 6208


def build_ssd_a(NT=2048, NC_=64):
    nc = bass.Bass("TRN2", target_bir_lowering=False)
    xT = _mk(nc, "xT", [D, NT + 2])
    cT = _mk(nc, "cT", [D, NC_ + 2])
    mvec = _mk(nc, "mvec", [D, 6])
    mvec_c = _mk(nc, "mvec_c", [D, 6])
    gvec = _mk(nc, "gvec", [D, 4])
    hmask = _mk(nc, "hmask", [128, 4])
    w_in = _mk(nc, "w_in", [D, SSD_IN])
    convw = _mk(nc, "convw", [128, 32, 4])
    dtb = _mk(nc, "dtb", [64, 1])
    NTOT = NT + NC_
    zT = _mk(nc, "zT", [2048, NTOT], kind="ExternalOutput")
    xbcT = _mk(nc, "xbcT", [4096, NTOT], kind="ExternalOutput")
    dtT = _mk(nc, "dtT", [64, NTOT], kind="ExternalOutput")
    kx = KX(nc)
    s = kx.s
    ms = kx.prep_mod(mvec, gvec)
    msc = kx.prep_mod(mvec_c, gvec, "mvc")
    hm = s.tile([128, 4], F32, "hm")
    s.dma("sp", hm[:], hmask, writes=[hm])
    cw = s.tile([128, 32, 4], F32, "cw")
    s.dma("sp", cw[:], convw, writes=[cw])
    db_ = s.tile([64, 1], F32, "dtb")
    s.dma("sp", db_[:], dtb, writes=[db_])
    HWD = NT // 2
    xin = [s.tile([128, 8, 342], F32, "xin%d" % i) for i in range(2)]
    for grp in range(2):
        with s.scope():
            blks = []
            c0 = grp * HWD
            srcs = [(xT[:, c0 + i * 342:c0 + (i + 1) * 342], 342, ms) for i in range(3)]
            if grp == 1:
                srcs.append((cT[:, :], NC_ + 2, msc))
            W = sum(w for _, w, _ in srcs)
            for i, (src, w, m) in enumerate(srcs):
                xt = xin[i % 2]
                s.dma("sp", xt[:, :, 0:w], src.rearrange("(c p) n -> p c n", p=128), writes=[xt])
                ht = s.tile([128, 8, w], BF16, "hT%d" % i)
                kx.norm_mod(xt, xt[:, :, 0:w], w, m, 0, ht, ht[:, :, 0:w], None)
                blks.append((ht, w))
            pre = [s.tile([128, W], F32, "pre%d" % i) for i in range(2)]
            acc = [s.tile([128, W], F32, "acc%d" % i) for i in range(2)]
            segs = [(0, HWD, 0 if grp == 0 else None, 1 if grp == 1 else None, c0)]
            if grp == 1:
                segs.append((HWD + 2, NC_, 2, 3, NT))
            nchunks = 49
            def ld_a(jb):
                ncols = 512 if jb < 12 else 64
                return kx.wload(w_in[:, jb * 512:jb * 512 + ncols], 8, ncols)

            def cp_a(jb, wt, wv, blks=blks, pre=pre, acc=acc, segs=segs):
                ncols = 512 if jb < 12 else 64
                for sub in range(ncols // 128 if ncols >= 128 else 1):
                    j = jb * 4 + sub
                    mrows = 128 if j < 48 else 64
                    pr = pre[j % 2]
                    ac = acc[j % 2]
                    col = 0
                    for (ht, w) in blks:
                        ps = kx.nps()
                        for c in range(8):
                            s.op("pe", lambda e, c=c, ps=ps, ht=ht, w=w, sub=sub, wv=wv, mrows=mrows: e.matmul(
                                ps[0:mrows, 0:w], lhsT=wv[:, c, sub * 128:sub * 128 + mrows], rhs=ht[:, c, 0:w], start=(c == 0), stop=(c == 7)),
                                reads=[wt, ht], writes=[ps])
                        if j < 16:
                            s.op("act", lambda e, ps=ps, pr=pr, col=col, w=w: e.activation(out=pr[:, col:col + w], in_=ps[:, 0:w], func=AF.Identity),
                                 reads=[ps], writes=[pr])
                        elif j < 48:
                            s.op("act", lambda e, ps=ps, pr=pr, col=col, w=w: e.activation(out=pr[:, col:col + w], in_=ps[:, 0:w], func=AF.Identity),
                                 reads=[ps], writes=[pr])
                        else:
                            s.op("act", lambda e, ps=ps, pr=pr, col=col, w=w: e.activation(out=pr[0:64, col:col + w], in_=ps[0:64, 0:w], func=AF.Exp, bias=db_[:, 0:1], scale=1.0),
                                 reads=[ps, db_], writes=[pr])
                        col += w
                    for (st, ow, lm, rm, oc) in segs:
                        if j < 16:
                            s.dma("act", zT[j * 128:(j + 1) * 128, oc:oc + ow], pr[:, st + 1:st + 1 + ow], reads=[pr], is_output=True)
                        elif j < 48:
                            jc = j - 16
                            if lm is not None:
                                s.op("dve", lambda e, pr=pr, st=st, lm=lm: e.tensor_scalar(out=pr[:, st:st + 1], in0=pr[:, st:st + 1], scalar1=hm[:, lm:lm + 1], scalar2=None, op0=ALU.mult),
                                     reads=[pr, hm], writes=[pr])
                            if rm is not None:
                                s.op("dve", lambda e, pr=pr, st=st, ow=ow, rm=rm: e.tensor_scalar(out=pr[:, st + ow + 1:st + ow + 2], in0=pr[:, st + ow + 1:st + ow + 2],
                                                                                                scalar1=hm[:, rm:rm + 1], scalar2=None, op0=ALU.mult),
                                     reads=[pr, hm], writes=[pr])
                            s.op("dve", lambda e, pr=pr, ac=ac, st=st, ow=ow, jc=jc: e.tensor_scalar(out=ac[:, st:st + ow], in0=pr[:, st:st + ow], scalar1=cw[:, jc, 0:1], scalar2=None, op0=ALU.mult),
                                 reads=[pr, cw], writes=[ac])
                            for k in (1, 2):
                                s.op("dve", lambda e, pr=pr, ac=ac, st=st, ow=ow, jc=jc, k=k: e.scalar_tensor_tensor(
                                    out=ac[:, st:st + ow], in0=pr[:, st + k:st + k + ow], scalar=cw[:, jc, k:k + 1], in1=ac[:, st:st + ow], op0=ALU.mult, op1=ALU.add),
                                    reads=[pr, cw, ac], writes=[ac])
                            s.op("act", lambda e, ac=ac, st=st, ow=ow, jc=jc: e.activation(out=ac[:, st:st + ow], in_=ac[:, st:st + ow], func=AF.Silu, bias=cw[:, jc, 3:4], scale=1.0),
                                 reads=[ac, cw], writes=[ac])
                            s.dma("act", xbcT[jc * 128:(jc + 1) * 128, oc:oc + ow], ac[:, st:st + ow], reads=[ac], is_output=True)
                        else:
                            s.op("act", lambda e, pr=pr, ac=ac, st=st, ow=ow: e.activation(out=ac[0:64, st:st + ow], in_=pr[0:64, st + 1:st + 1 + ow], func=AF.Ln, bias=1.0, scale=1.0),
                                 reads=[pr], writes=[ac])
                            s.dma("act", dtT[:, oc:oc + ow], ac[0:64, st:st + ow], reads=[ac], is_output=True)
            kx.wstream([(lambda jb=jb: ld_a(jb)) for jb in range(13)], [(lambda wt, wv, jb=jb: cp_a(jb, wt, wv)) for jb in range(13)], L=2)
    s.emit()
    return nc


def run_ssd_a(x, ctx, m_lat, m_ctx, g, w_in, conv_w, conv_b, dt_bias):
    NT, NC_ = 2048, 64
    nc = build_ssd_a(NT, NC_)
    cw = np.concatenate([conv_w.T, conv_b[:, None]], 1).astype(np.float32)
    cw = np.ascontiguousarray(cw.reshape(32, 128, 4).transpose(1, 0, 2))
    dtb = np.ascontiguousarray(dt_bias.reshape(64, 1).astype(np.float32))
    ins = []
    for core in range(NCORES):
        b, k = divmod(core, 4)
        hm = np.ones((128, 4), np.float32)
        if k == 0:
            hm[:, 0] = 0
            hm[:, 2] = 0
        if k == 3:
            hm[:, 1] = 0
            hm[:, 3] = 0
        ins.append({"xT": _halo_T(x[b], k * NT, NT), "cT": _halo_T(ctx[b], k * NC_, NC_),
                    "mvec": np.ascontiguousarray(m_lat[b].reshape(6, D).T), "mvec_c": np.ascontiguousarray(m_ctx.reshape(6, D).T),
                    "gvec": np.ascontiguousarray(g.T), "hmask": hm, "w_in": w_in, "convw": cw, "dtb": dtb})
    res = run_bass_kernel_spmd(nc, ins, core_ids=list(range(NCORES)))
    z = np.empty((BATCH, SEQ + CTX, 2048), np.float32)
    xbc = np.empty((BATCH, SEQ + CTX, 4096), np.float32)
    dt = np.empty((BATCH, SEQ + CTX, 64), np.float32)
    for core in range(NCORES):
        b, k = divmod(core, 4)
        r = res.results[core]
        for arr, name in ((z, "zT"), (xbc, "xbcT"), (dt, "dtT")):
            arr[b, k * NT:(k + 1) * NT] = r[name][:, 0:NT].T
            arr[b, SEQ + k * NC_:SEQ + (k + 1) * NC_] = r[name][:, NT:NT + NC_].T
    return z, xbc, dt


NCH = 66


def build_ssd_b(nch=NCH, dbg=False):
    nc = bass.Bass("TRN2", target_bir_lowering=False)
    X = _mk(nc, "X", [nch, 128, 512])
    DT = _mk(nc, "DT", [nch, 128, 16])
    BCT = _mk(nc, "BCT", [nch, 128, 4, 128])
    BTOK = _mk(nc, "BTOK", [nch, 128, 2, 128])
    alog = _mk(nc, "alog", [128, 16])
    masks = _mk(nc, "masks", [128, 4, 128])
    Y = _mk(nc, "Y", [nch, 128, 512], kind="ExternalOutput")
    s = Sched(nc)
    ps = [s.ptile(name="ps%d" % i) for i in range(8)]
    psi = [0]

    def nps():
        t = ps[psi[0]]
        psi[0] = (psi[0] + 1) % 8
        return t
    mk = s.tile([128, 4, 128], F32, "mk")
    s.dma("sp", mk[:], masks, writes=[mk])
    onesf = s.tile([128, 128], F32, "onesf")
    s.op("dve", lambda e: e.memset(onesf[:], 1.0), writes=[onesf])
    a_bc = s.tile([128, 16], F32, "a_bc")
    s.dma("sp", a_bc[:], alog, writes=[a_bc])
    s.op("act", lambda e: e.activation(out=a_bc[:], in_=a_bc[:], func=AF.Exp), reads=[a_bc], writes=[a_bc])
    s.op("dve", lambda e: e.tensor_scalar(out=a_bc[:], in0=a_bc[:], scalar1=-1.0, scalar2=None, op0=ALU.mult), reads=[a_bc], writes=[a_bc])
    state = s.tile([128, 8, 64], F32, "state")
    state_bf = s.tile([128, 512], BF16, "state_bf")
    NB = 2
    xt = [s.tile([128, 8, 64], F32, "xt%d" % i) for i in range(NB)]
    dtt = [s.tile([128, 16], F32, "dtt%d" % i) for i in range(NB)]
    bct = [s.tile([128, 2, 128], F32, "bct%d" % i) for i in range(NB)]
    btk = [s.tile([128, 128], F32, "btk%d" % i) for i in range(NB)]
    bcb = [s.tile([128, 2, 128], BF16, "bcb%d" % i) for i in range(NB)]
    btb = [s.tile([128, 128], BF16, "btb%d" % i) for i in range(NB)]
    yin = [s.tile([128, 512], F32, "yin%d" % i) for i in range(NB)]
    dtA = [s.tile([128, 8], F32, "dtA%d" % i) for i in range(NB)]
    ct = [s.tile([128, 16], F32, "ct%d" % i) for i in range(NB)]
    ee = [s.tile([128, 3, 8], F32, "ee%d" % i) for i in range(NB)]
    dtw = [s.tile([128, 8], F32, "dtw%d" % i) for i in range(NB)]
    xdt = [s.tile([128, 8, 64], BF16, "xdt%d" % i) for i in range(NB)]
    xw = [s.tile([128, 8, 64], BF16, "xw%d" % i) for i in range(NB)]
    Lall = [s.tile([128, 8, 128], F32, "Lall%d" % i) for i in range(NB)]
    dec = [s.tile([128, 8, 128], F32, "dec%d" % i) for i in range(NB)]
    cbm = [s.tile([128, 128], F32, "cbm%d" % i) for i in range(NB)]
    sc = [s.tile([128, 8, 128], BF16, "sc%d" % i) for i in range(NB)]
    tt = [s.tile([128, 8, 64], F32, "tt%d" % i) for i in range(NB)]
    yo = [s.tile([128, 8, 64], F32, "yo%d" % i) for i in range(NB)]
    t2 = s.tile([128, 8, 64], F32, "t2")
    Yc = [T(Y[c], "Y%d" % c) for c in range(nch)]
    nctx = 2
    it = 0
    def sweep(d, it):
        order = list(range(nch)) if d == 0 else ([1, 0] + list(range(nch - 1, nctx - 1, -1)))
        m_incl = mk[:, d, :]
        m_str = mk[:, 2 + d, :]
        s.op("dve", lambda e: e.memset(state[:], 0.0), writes=[state])
        s.op("dve", lambda e: e.memset(state_bf[:], 0.0), writes=[state_bf])
        for c in order:
            k = it % NB
            it += 1
            x_, dt_, bc_, bk_, bcb_, btb_, yin_ = xt[k], dtt[k], bct[k], btk[k], bcb[k], btb[k], yin[k]
            dA, ct_, ee_, dtw_, xdt_, xw_, L_, dec_, cbm_, sc_, tt_, yo_ = dtA[k], ct[k], ee[k], dtw[k], xdt[k], xw[k], Lall[k], dec[k], cbm[k], sc[k], tt[k], yo[k]
            s.dma("sp", x_[:].rearrange("p a b -> p (a b)"), X[c], writes=[x_])
            s.dma("act", dt_[:], DT[c], writes=[dt_])
            s.dma("sp", bc_[:], BCT[c][:, 2 * d:2 * d + 2, :], writes=[bc_])
            s.dma("act", bk_[:], BTOK[c][:, d, :], writes=[bk_])
            if d == 1:
                s.dma("sp", yin_[:], Y[c], reads=[Yc[c]], writes=[yin_])
            s.op("pool", lambda e, bc_=bc_, bcb_=bcb_: e.tensor_copy(out=bcb_[:], in_=bc_[:]), reads=[bc_], writes=[bcb_])
            s.op("pool", lambda e, bk_=bk_, btb_=btb_: e.tensor_copy(out=btb_[:], in_=bk_[:]), reads=[bk_], writes=[btb_])
            dts = dt_[:, d * 8:(d + 1) * 8]
            s.op("dve", lambda e, dA=dA, dts=dts: e.tensor_tensor(out=dA[:], in0=dts, in1=a_bc[:, d * 8:(d + 1) * 8], op=ALU.mult), reads=[dt_, a_bc], writes=[dA])
            psc = nps()
            s.op("pe", lambda e, psc=psc, dA=dA: e.matmul(psc[:, 0:8], lhsT=m_incl, rhs=dA[:], start=True, stop=True), reads=[mk, dA], writes=[psc])
            s.op("pe", lambda e, psc=psc, dA=dA: e.matmul(psc[:, 8:16], lhsT=onesf[:], rhs=dA[:], start=True, stop=True), reads=[onesf, dA], writes=[psc])
            s.op("dve", lambda e, psc=psc, ct_=ct_: e.tensor_copy(out=ct_[:], in_=psc[:, 0:16]), reads=[psc], writes=[ct_])
            s.op("dve", lambda e, ct_=ct_, ee_=ee_: e.tensor_tensor(out=ee_[:, 2, :], in0=ct_[:, 8:16], in1=ct_[:, 0:8], op=ALU.subtract), reads=[ct_], writes=[ee_])
            s.op("act", lambda e, ct_=ct_, ee_=ee_: e.activation(out=ee_[:, 0:2, :].rearrange("p a b -> p (a b)"), in_=ct_[:, 0:16], func=AF.Exp), reads=[ct_, ee_], writes=[ee_])
            s.op("act", lambda e, ee_=ee_: e.activation(out=ee_[:, 2, :], in_=ee_[:, 2, :], func=AF.Exp), reads=[ee_], writes=[ee_])
            s.op("dve", lambda e, dtw_=dtw_, dts=dts, ee_=ee_: e.tensor_tensor(out=dtw_[:], in0=dts, in1=ee_[:, 2, :], op=ALU.mult), reads=[dt_, ee_], writes=[dtw_])
            s.op("dve", lambda e, x_=x_, xdt_=xdt_, dts=dts: e.tensor_tensor(out=xdt_[:], in0=x_[:], in1=dts.unsqueeze(2).to_broadcast([128, 8, 64]), op=ALU.mult),
                 reads=[x_, dt_], writes=[xdt_])
            s.op("dve", lambda e, x_=x_, xw_=xw_, dtw_=dtw_: e.tensor_tensor(out=xw_[:], in0=x_[:], in1=dtw_[:].unsqueeze(2).to_broadcast([128, 8, 64]), op=ALU.mult),
                 reads=[x_, dtw_], writes=[xw_])
            s.op("pool", lambda e, L_=L_, dA=dA: e.tensor_tensor(out=L_[:], in0=m_str.unsqueeze(1).to_broadcast([128, 8, 128]),
                                                                in1=dA[:].unsqueeze(2).to_broadcast([128, 8, 128]), op=ALU.mult), reads=[mk, dA], writes=[L_])
            pss = [nps(), nps()]
            for e_ in range(8):
                p_ = pss[e_ // 4]
                s.op("pe", lambda e, p_=p_, e_=e_, L_=L_: e.matmul(p_[:, (e_ % 4) * 128:(e_ % 4 + 1) * 128], lhsT=L_[:, e_, :], rhs=m_incl, start=True, stop=True),
                     reads=[L_, mk], writes=[p_])
            for hh in range(2):
                s.op("act", lambda e, hh=hh, dec_=dec_, p_=pss[hh]: e.activation(out=dec_[:, hh * 4:(hh + 1) * 4, :].rearrange("p a b -> p (a b)"), in_=p_[:, :], func=AF.Exp),
                     reads=[pss[hh]], writes=[dec_])
            pcb = nps()
            s.op("pe", lambda e, pcb=pcb, bcb_=bcb_: e.matmul(pcb[:, 0:128], lhsT=bcb_[:, 0, :], rhs=bcb_[:, 1, :], start=True, stop=True), reads=[bcb_], writes=[pcb])
            s.op("dve", lambda e, pcb=pcb, cbm_=cbm_: e.tensor_tensor(out=cbm_[:], in0=pcb[:, 0:128], in1=m_incl, op=ALU.mult), reads=[pcb, mk], writes=[cbm_])
            s.op("dve", lambda e, sc_=sc_, dec_=dec_, cbm_=cbm_: e.tensor_tensor(out=sc_[:], in0=dec_[:], in1=cbm_[:].unsqueeze(1).to_broadcast([128, 8, 128]), op=ALU.mult),
                 reads=[dec_, cbm_], writes=[sc_])
            if dbg and d == 0 and c == 0:
                for nm, tl, n in (("d_dec", dec_, 1024), ("d_cbm", cbm_, 128), ("d_ct", ct_, 16), ("d_ee", ee_, 24), ("d_L", L_, 1024), ("d_dA", dA, 8)):
                    o = _mk(nc, nm, [128, n], kind="ExternalOutput")
                    v = tl[:] if len(tl.ap.shape) == 2 else tl[:].rearrange("p a b -> p (a b)")
                    s.dma("sp", o, v, reads=[tl], is_output=True)
            psY = nps()
            for e_ in range(8):
                s.op("pe", lambda e, psY=psY, e_=e_, sc_=sc_, xdt_=xdt_: e.matmul(psY[:, e_ * 64:(e_ + 1) * 64], lhsT=sc_[:, e_, :], rhs=xdt_[:, e_, :], start=True, stop=True),
                     reads=[sc_, xdt_], writes=[psY])
            psS = nps()
            s.op("pe", lambda e, psS=psS, bcb_=bcb_: e.matmul(psS[:, 0:512], lhsT=bcb_[:, 1, :], rhs=state_bf[:], start=True, stop=True), reads=[bcb_, state_bf], writes=[psS])
            s.op("dve", lambda e, psS=psS, tt_=tt_, ee_=ee_: e.tensor_tensor(out=tt_[:], in0=psS[:, 0:512].rearrange("p (a b) -> p a b", a=8),
                                                                         in1=ee_[:, 0, :].unsqueeze(2).to_broadcast([128, 8, 64]), op=ALU.mult), reads=[psS, ee_], writes=[tt_])
            s.op("dve", lambda e, psY=psY, tt_=tt_, yo_=yo_: e.tensor_tensor(out=yo_[:], in0=tt_[:], in1=psY[:, 0:512].rearrange("p (a b) -> p a b", a=8), op=ALU.add),
                 reads=[psY, tt_], writes=[yo_])
            if d == 1:
                s.op("pool", lambda e, yo_=yo_, yin_=yin_: e.tensor_tensor(out=yo_[:].rearrange("p a b -> p (a b)"), in0=yo_[:].rearrange("p a b -> p (a b)"), in1=yin_[:], op=ALU.add),
                     reads=[yo_, yin_], writes=[yo_])
            s.dma("sp", Y[c], yo_[:].rearrange("p a b -> p (a b)"), reads=[yo_], writes=[Yc[c]], is_output=True)
            psU = nps()
            s.op("pe", lambda e, psU=psU, btb_=btb_, xw_=xw_: e.matmul(psU[:, 0:512], lhsT=btb_[:], rhs=xw_[:].rearrange("p a b -> p (a b)"), start=True, stop=True),
                 reads=[btb_, xw_], writes=[psU])
            s.op("dve", lambda e, ee_=ee_: e.tensor_tensor(out=t2[:], in0=state[:], in1=ee_[:, 1, :].unsqueeze(2).to_broadcast([128, 8, 64]), op=ALU.mult),
                 reads=[state, ee_], writes=[t2])
            s.op("dve", lambda e, psU=psU: e.tensor_tensor(out=state[:], in0=t2[:], in1=psU[:, 0:512].rearrange("p (a b) -> p a b", a=8), op=ALU.add),
                 reads=[t2, psU], writes=[state])
            s.op("act", lambda e: e.activation(out=state_bf[:], in_=state[:].rearrange("p a b -> p (a b)"), func=AF.Identity), reads=[state], writes=[state_bf])
        return it
    it = sweep(0, it)
    it = sweep(1, it)
    s.emit()
    return nc


def _ssd_masks():
    k = np.arange(128)
    m = np.zeros((128, 4, 128), np.float32)
    m[:, 0, :] = (k[:, None] <= k[None, :])
    m[:, 1, :] = (k[:, None] >= k[None, :])
    m[:, 2, :] = (k[:, None] > k[None, :])
    m[:, 3, :] = (k[:, None] < k[None, :])
    return m


def run_ssd_b(xbc, dt, a_log):
    nc = build_ssd_b()
    masks = _ssd_masks()
    ins = []
    for core in range(NCORES):
        b, g = divmod(core, 4)
        def chunks(a):
            return np.concatenate([a[SEQ:].reshape(2, 128, -1), a[:SEQ].reshape(64, 128, -1)], 0)
        xg = chunks(xbc[b][:, g * 512:(g + 1) * 512])
        bc = xbc[b][:, 2048:].reshape(-1, 2, 2, 4, 128)[:, :, :, g, :]
        bcc = chunks(bc.reshape(-1, 4 * 128)).reshape(NCH, 128, 4, 128)
        bct = np.ascontiguousarray(bcc.transpose(0, 3, 2, 1))
        btok = np.ascontiguousarray(bcc[:, :, [0, 2], :])
        dtg = dt[b].reshape(-1, 2, 4, 8)[:, :, g, :].reshape(-1, 16)
        al = np.ascontiguousarray(np.broadcast_to(a_log.reshape(2, 4, 8)[:, g, :].reshape(1, 16), (128, 16))).astype(np.float32)
        ins.append({"X": np.ascontiguousarray(xg), "DT": np.ascontiguousarray(chunks(dtg)), "BCT": bct, "BTOK": btok, "alog": al, "masks": masks})
    res = run_bass_kernel_spmd(nc, ins, core_ids=list(range(NCORES)))
    y = np.empty((BATCH, SEQ + CTX, 2048), np.float32)
    for core in range(NCORES):
        b, g = divmod(core, 4)
        Yc = res.results[core]["Y"]
        y[b, SEQ:, g * 512:(g + 1) * 512] = Yc[0:2].reshape(256, 512)
        y[b, :SEQ, g * 512:(g + 1) * 512] = Yc[2:].reshape(SEQ, 512)
    return y


def build_ssd_c(NT=2048, NC_=64):
    nc = bass.Bass("TRN2", target_bir_lowering=False)
    NTOT = NT + NC_
    yT = _mk(nc, "yT", [2048, NTOT])
    xsT = _mk(nc, "xsT", [2048, NTOT])
    zT = _mk(nc, "zT", [2048, NTOT])
    xT = _mk(nc, "xT", [D, NTOT])
    mvec = _mk(nc, "mvec", [D, 6])
    mvec_c = _mk(nc, "mvec_c", [D, 6])
    gvec = _mk(nc, "gvec", [D, 4])
    dcol = _mk(nc, "dcol", [128, 16, 2])
    ngd = _mk(nc, "ng", [128, 16])
    w_out = _mk(nc, "w_out", [2048, D])
    w1 = _mk(nc, "w1", [D, HID])
    w2 = _mk(nc, "w2", [HID, D])
    outT = _mk(nc, "outT", [D, NTOT], kind="ExternalOutput")
    kx = KX(nc, nwbuf=2)
    s = kx.s
    ms = kx.prep_mod(mvec, gvec)
    msc = kx.prep_mod(mvec_c, gvec, "mvc")
    dc_ = s.tile([128, 16, 2], F32, "dcol")
    s.dma("sp", dc_[:], dcol, writes=[dc_])
    dsum = s.tile([128, 16], F32, "dsum")
    s.op("dve", lambda e: e.tensor_tensor(out=dsum[:], in0=dc_[:, :, 0], in1=dc_[:, :, 1], op=ALU.add), reads=[dc_], writes=[dsum])
    ng = s.tile([128, 16], F32, "ng")
    s.dma("sp", ng[:], ngd, writes=[ng])
    HWD = NT // 2
    xres = s.tile([128, 8, HWD + NC_], F32, "xres")
    y_tiles = [s.tile([128, 8, 512], F32, "y0"), s.tile([128, 8, 512], F32, "y1"), s.tile([128, 8, NC_], F32, "y2")]
    for half in range(2):
        c0 = half * HWD
        cols = [(c0, 512, ms, 0), (c0 + 512, 512, ms, 512)]
        s.dma("sp", xres[:, :, 0:HWD], xT[:, c0:c0 + HWD].rearrange("(c p) n -> p c n", p=128), writes=[xres])
        if half == 1:
            cols.append((NT, NC_, msc, HWD))
            s.dma("sp", xres[:, :, HWD:HWD + NC_], xT[:, NT:NT + NC_].rearrange("(c p) n -> p c n", p=128), writes=[xres])
        with s.scope():
            gz = s.tile([128, 16, 512], F32, "gz")
            ynT = [s.tile([128, 16, w], BF16, "ynT%d" % i) for i, (_, w, _, _) in enumerate(cols)]
            ld = [[s.tile([128, 512], F32, "ld%d_%d" % (a, b)) for b in range(2)] for a in range(3)]
            cnt = 0
            for bi, (co, w, m, xo) in enumerate(cols):
                for ch in range(16):
                    ly, lx, lz = ld[0][cnt % 2], ld[1][cnt % 2], ld[2][cnt % 2]
                    cnt += 1
                    s.dma("sp", ly[:, 0:w], yT[ch * 128:(ch + 1) * 128, co:co + w], writes=[ly])
                    s.dma("act", lx[:, 0:w], xsT[ch * 128:(ch + 1) * 128, co:co + w], writes=[lx])
                    s.dma("sp", lz[:, 0:w], zT[ch * 128:(ch + 1) * 128, co:co + w], writes=[lz])
                    s.op("dve", lambda e, ly=ly, lx=lx, ch=ch, w=w: e.scalar_tensor_tensor(out=ly[:, 0:w], in0=lx[:, 0:w], scalar=dsum[:, ch:ch + 1], in1=ly[:, 0:w],
                                                                                      op0=ALU.mult, op1=ALU.add), reads=[lx, ly, dsum], writes=[ly])
                    s.op("act", lambda e, lz=lz, w=w: e.activation(out=lz[:, 0:w], in_=lz[:, 0:w], func=AF.Silu), reads=[lz], writes=[lz])
                    s.op("dve", lambda e, ly=ly, lz=lz, ch=ch, w=w: e.tensor_tensor(out=gz[:, ch, 0:w], in0=ly[:, 0:w], in1=lz[:, 0:w], op=ALU.mult),
                         reads=[ly, lz], writes=[gz])
                r = kx.rstd(gz, gz[:, :, 0:w], w, nchunks=16, dim=2048)
                for ch in range(16):
                    tc = kx.ntc()
                    s.op("dve", lambda e, ch=ch, tc=tc, r=r, w=w: e.tensor_tensor(out=tc[:, 0:w], in0=gz[:, ch, 0:w], in1=r[:, 0:w], op=ALU.mult),
                         reads=[gz, r], writes=[tc])
                    s.op("act", lambda e, ch=ch, tc=tc, w=w, yn=ynT[bi]: e.activation(out=yn[:, ch, :], in_=tc[:, 0:w], func=AF.Identity, scale=ng[:, ch:ch + 1]),
                         reads=[tc, ng], writes=[ynT[bi]])
            def cp_o(dc, wt, wv, cols=cols, ynT=ynT):
                for bi, (co, w, m, xo) in enumerate(cols):
                    ps = kx.nps()
                    for c in range(16):
                        s.op("pe", lambda e, c=c, ps=ps, bi=bi, w=w, wv=wv: e.matmul(ps[:, 0:w], lhsT=wv[:, c, :], rhs=ynT[bi][:, c, :], start=(c == 0), stop=(c == 15)),
                             reads=[wt, ynT[bi]], writes=[ps])
                    s.op("act", lambda e, ps=ps, dc=dc, bi=bi, w=w: e.activation(out=y_tiles[bi][:, dc, 0:w], in_=ps[:, 0:w], func=AF.Identity),
                         reads=[ps], writes=[y_tiles[bi]])
            kx.wstream([(lambda dc=dc: kx.wload(w_out[:, dc * 128:(dc + 1) * 128], 16, 128)) for dc in range(8)],
                       [(lambda wt, wv, dc=dc: cp_o(dc, wt, wv)) for dc in range(8)], L=1)
        blocks = [Blk(w, m, xres, xres[:, :, xo:xo + w], y_tiles[bi]) for bi, (co, w, m, xo) in enumerate(cols)]
        emit_finish(kx, blocks, w1, w2, None)
        for bi, (co, w, m, xo) in enumerate(cols):
            s.dma("act", outT[:, co:co + w].rearrange("(c p) n -> p c n", p=128), xres[:, :, xo:xo + w], reads=[xres], is_output=True)
    s.emit()
    return nc


def run_ssd_c(x, ctx, y, xbc, z, m_lat, m_ctx, g, ssd_d, ssd_norm_g, w_out, w1, w2):
    NT, NC_ = 2048, 64
    nc = build_ssd_c(NT, NC_)
    dcol = np.repeat(ssd_d.reshape(2, 32).T, 64, axis=0).astype(np.float32)
    dcol = np.ascontiguousarray(dcol.reshape(16, 128, 2).transpose(1, 0, 2))
    ng = np.ascontiguousarray(ssd_norm_g.reshape(16, 128).T.astype(np.float32))
    ins = []
    for core in range(NCORES):
        b, k = divmod(core, 4)
        def cat(a_main, a_ctx):
            return np.ascontiguousarray(np.concatenate([a_main[k * NT:(k + 1) * NT], a_ctx[k * NC_:(k + 1) * NC_]], 0).T)
        ins.append({"yT": cat(y[b][:SEQ], y[b][SEQ:]), "xsT": cat(xbc[b][:SEQ, :2048], xbc[b][SEQ:, :2048]), "zT": cat(z[b][:SEQ], z[b][SEQ:]),
                    "xT": cat(x[b], ctx[b]), "mvec": np.ascontiguousarray(m_lat[b].reshape(6, D).T), "mvec_c": np.ascontiguousarray(m_ctx.reshape(6, D).T),
                    "gvec": np.ascontiguousarray(g.T), "dcol": dcol, "ng": ng, "w_out": w_out, "w1": w1, "w2": w2})
    res = run_bass_kernel_spmd(nc, ins, core_ids=list(range(NCORES)))
    xo = np.empty_like(x)
    co = np.empty_like(ctx)
    for core in range(NCORES):
        b, k = divmod(core, 4)
        o = res.results[core]["outT"]
        xo[b, k * NT:(k + 1) * NT] = o[:, :NT].T
        co[b, k * NC_:(k + 1) * NC_] = o[:, NT:].T
    return xo, co


def build_projfin(NT=2048):
    nc = bass.Bass("TRN2", target_bir_lowering=False)
    fT = _mk(nc, "fT", [D, NT])
    xT = _mk(nc, "xT", [D, NT])
    mvec = _mk(nc, "mvec", [D, 6])
    gvec = _mk(nc, "gvec", [D, 4])
    w_out = _mk(nc, "w_out", [D, D])
    w1 = _mk(nc, "w1", [D, HID])
    w2 = _mk(nc, "w2", [HID, D])
    outT = _mk(nc, "outT", [D, NT], kind="ExternalOutput")
    kx = KX(nc)
    s = kx.s
    ms = kx.prep_mod(mvec, gvec)
    HWD, bw, nblk = NT // 2, 512, 2
    xres = s.tile([128, 8, HWD], F32, "xres")
    y_tiles = [s.tile([128, 8, bw], F32, "y%d" % i) for i in range(nblk)]
    for half in range(2):
        c0 = half * HWD
        s.dma("sp", xres[:], xT[:, c0:c0 + HWD].rearrange("(c p) n -> p c n", p=128), writes=[xres])
        with s.scope():
            fb = [s.tile([128, 8, bw], BF16, "fb%d" % i) for i in range(nblk)]
            for tb in range(nblk):
                for c in range(8):
                    kx.stage_cast(fb[tb], fb[tb][:, c, :], fT[c * 128:(c + 1) * 128, c0 + tb * bw:c0 + (tb + 1) * bw], bw)
            for db in range(2):
                wt, wv = kx.wload(w_out[:, db * 512:(db + 1) * 512], 8, 512)
                for sub in range(4):
                    dc = db * 4 + sub
                    for tb in range(nblk):
                        ps = kx.nps()
                        for c in range(8):
                            s.op("pe", lambda e, c=c, ps=ps, tb=tb, sub=sub, wv=wv: e.matmul(
                                ps[:, 0:bw], lhsT=wv[:, c, sub * 128:(sub + 1) * 128], rhs=fb[tb][:, c, :], start=(c == 0), stop=(c == 7)),
                                reads=[wt, fb[tb]], writes=[ps])
                        s.op("act", lambda e, ps=ps, dc=dc, tb=tb: e.activation(out=y_tiles[tb][:, dc, :], in_=ps[:, 0:bw], func=AF.Identity),
                             reads=[ps], writes=[y_tiles[tb]])
        x_views = [xres[:, :, tb * bw:(tb + 1) * bw] for tb in range(nblk)]
        blocks = [Blk(bw, ms, xres, x_views[tb], y_tiles[tb]) for tb in range(nblk)]
        emit_finish(kx, blocks, w1, w2, None)
        for tb in range(nblk):
            s.dma("act", outT[:, c0 + tb * bw:c0 + (tb + 1) * bw].rearrange("(c p) n -> p c n", p=128), x_views[tb], reads=[xres], is_output=True)
    s.emit()
    return nc


def run_projfin(x, f, m_lat, g, w_out, w1, w2):
    NT = 2048
    nc = build_projfin(NT)
    ins = []
    for core in range(NCORES):
        b, k = divmod(core, 4)
        ins.append({"fT": np.ascontiguousarray(f[b, k * NT:(k + 1) * NT].T), "xT": np.ascontiguousarray(x[b, k * NT:(k + 1) * NT].T),
                    "mvec": np.ascontiguousarray(m_lat[b].reshape(6, D).T), "gvec": np.ascontiguousarray(g.T), "w_out": w_out, "w1": w1, "w2": w2})
    res = run_bass_kernel_spmd(nc, ins, core_ids=list(range(NCORES)))
    out = np.empty_like(x)
    for core in range(NCORES):
        b, k = divmod(core, 4)
        out[b, k * NT:(k + 1) * NT] = res.results[core]["outT"].T
    return out


def kernel(x, c, ctx, c_ctx, mod_w, mod_b, norm_g, mlp_w1, mlp_w2, ssd_w_in, ssd_conv_w, ssd_conv_b,
           ssd_dt_bias, ssd_a_log, ssd_d, ssd_norm_g, ssd_w_out, na_w_qkv, na_rpb, na_w_out,
           sc_w_in, sc_conv_w, sc_w_out, fn_w_out):
    f32 = lambda a: np.ascontiguousarray(np.asarray(a), dtype=np.float32)
    x, c, ctx, c_ctx, mod_w, mod_b, norm_g, mlp_w1, mlp_w2 = map(f32, (x, c, ctx, c_ctx, mod_w, mod_b, norm_g, mlp_w1, mlp_w2))
    m_lat, m_ctx = run_mod(c, c_ctx, mod_w, mod_b)
    z, xbc, dt = run_ssd_a(x, ctx, m_lat[0], m_ctx[0], norm_g[0], f32(ssd_w_in)[0], f32(ssd_conv_w)[0], f32(ssd_conv_b)[0], f32(ssd_dt_bias)[0])
    y = run_ssd_b(xbc, dt, f32(ssd_a_log)[0])
    x, ctx = run_ssd_c(x, ctx, y, xbc, z, m_lat[0], m_ctx[0], norm_g[0], f32(ssd_d)[0], f32(ssd_norm_g)[0], f32(ssd_w_out)[0], mlp_w1[0], mlp_w2[0])
    x = run_na(x, ctx, m_lat[1], m_ctx[1], norm_g[1], f32(na_w_qkv)[0], f32(na_rpb)[0], f32(na_w_out)[0], mlp_w1[1], mlp_w2[1])
    x, h3 = run_sc(x, m_lat[2], norm_g[2], f32(sc_w_in)[0], f32(sc_conv_w)[0], f32(sc_w_out)[0], mlp_w1[2], mlp_w2[2], m_lat[3], norm_g[3])
    f = run_fft(h3)
    x = run_projfin(x, f, m_lat[3], norm_g[3], f32(fn_w_out)[0], mlp_w1[3], mlp_w2[3])
    return x.astype(np.float32)
```

```python
import numpy as np
from contextlib import ExitStack
import concourse.bass as bass
import concourse.mybir as mybir
from concourse.bass_utils import run_bass_kernel_spmd

F32 = mybir.dt.float32
BF16 = mybir.dt.bfloat16
AF = mybir.ActivationFunctionType
ALU = mybir.AluOpType
AX = mybir.AxisListType

ENGS = ("pe", "dve", "act", "pool", "sp")
N_DMA_SEMS = 40
NCORES = 8

D = 1024
SEQ = 8192
BATCH = 2
CTX = 256
HID = 4096
EPS = 1e-6
ARENA_WORDS = 52736
CAST_ENGS = ("pool",)


class T:
    __slots__ = ("ap", "w", "r", "name")

    def __init__(self, ap, name=""):
        self.ap = ap
        self.w = None
        self.r = []
        self.name = name

    def __getitem__(self, k):
        return self.ap[k]


class Sched:
    def __init__(self, nc, same_engine_sync=True):
        self.nc = nc
        self.es = ExitStack()
        self.q = {e: [] for e in ENGS}
        self.cnt = {e: 0 for e in ENGS}
        self.prog = {e: self.es.enter_context(nc.semaphore("prog_" + e)) for e in ENGS}
        self.dsem = [self.es.enter_context(nc.semaphore("dma%d" % i)) for i in range(N_DMA_SEMS)]
        self.dval = [0] * N_DMA_SEMS
        self.dnext = 0
        self.known = {e: {} for e in ENGS}
        self.same_engine_sync = same_engine_sync
        self.sem_owner = {id(self.prog[e]): e for e in ENGS}
        self.out_events = []
        self.n_sb = 0
        self.arena = None
        self.aoff = 0
        self.amax = 0
        self.swsem = {}
        self.swused = {}

    def tile(self, shape, dtype, name=None):
        if self.arena is None:
            self.arena = self.es.enter_context(self.nc.sbuf_tensor("arena", [128, ARENA_WORDS], F32))
            self.aoff = 0
        esz = 2 if dtype == BF16 else 4
        n = 1
        for d in shape[1:]:
            n *= d
        words = (n * esz + 3) // 4
        words = (words + 7) // 8 * 8
        if self.aoff + words > ARENA_WORDS:
            raise RuntimeError("SBUF arena overflow: need %d words at %d" % (words, self.aoff))
        ap = self.arena[:, self.aoff:self.aoff + (n * esz + 3) // 4]
        self.aoff += words
        self.amax = max(self.amax, self.aoff)
        if dtype != F32:
            ap = ap.bitcast(dtype)
        ap = ap[0:shape[0], 0:n]
        if len(shape) >= 3:
            names = ["d%d" % i for i in range(len(shape) - 1)]
            kw = {names[i]: shape[1 + i] for i in range(len(shape) - 2)}
            ap = ap.rearrange("p (%s) -> p %s" % (" ".join(names), " ".join(names)), **kw)
        return T(ap, name or "")

    def ptile(self, shape=(128, 512), dtype=F32, name=None):
        self.n_sb += 1
        return T(self.es.enter_context(self.nc.psum_tensor(name or ("ps%d" % self.n_sb), list(shape), dtype)), name or "")

    def _deps(self, eng, reads, writes):
        evs = []
        for t in reads:
            if t.w is not None:
                evs.append(t.w)
        for t in writes:
            if t.w is not None:
                evs.append(t.w)
            evs.extend(t.r)
        need = {}
        for (sem, val) in evs:
            owner = self.sem_owner.get(id(sem))
            if owner == eng and (eng == "pe" or not self.same_engine_sync):
                continue
            k = id(sem)
            if self.known[eng].get(k, 0) >= val:
                continue
            if k not in need or need[k][1] < val:
                need[k] = (sem, val)
        for k, (sem, val) in need.items():
            self.known[eng][k] = val
        return list(need.values())

    def _commit(self, ev, reads, writes):
        for t in reads:
            t.r.append(ev)
            if len(t.r) > 64:
                best = {}
                for (sem, val) in t.r:
                    if id(sem) not in best or best[id(sem)][1] < val:
                        best[id(sem)] = (sem, val)
                t.r = list(best.values())
        for t in writes:
            t.w = ev
            t.r = []

    def op(self, eng, fn, reads=(), writes=()):
        waits = self._deps(eng, reads, writes)
        self.cnt[eng] += 1
        ev = (self.prog[eng], self.cnt[eng])
        self.q[eng].append((waits, fn, (self.prog[eng], 1)))
        self._commit(ev, reads, writes)
        return ev

    def dma(self, eng, out_ap, in_ap, reads=(), writes=(), is_output=False, sub=0, **kw):
        if eng == "pool":
            return self._dma_sw(out_ap, in_ap, reads, writes, is_output, sub, kw)
        i = self.dnext
        self.dnext = (self.dnext + 1) % N_DMA_SEMS
        sem = self.dsem[i]
        waits = self._deps(eng, reads, writes)
        if self.dval[i] > 0 and self.known[eng].get(id(sem), 0) < self.dval[i]:
            waits.append((sem, self.dval[i]))
            self.known[eng][id(sem)] = self.dval[i]
        self.dval[i] += 16
        ev = (sem, self.dval[i])

        def fn(e, out_ap=out_ap, in_ap=in_ap, kw=kw):
            return e.dma_start(out=out_ap, in_=in_ap, **kw)
        self.q[eng].append((waits, fn, (sem, 16)))
        self._commit(ev, reads, writes)
        if is_output:
            self.out_events.append(ev)
        return ev

    def _dma_sw(self, out_ap, in_ap, reads, writes, is_output, sub, kw):
        eng = "pool"
        slot = writes[0]
        key = (id(slot), sub)
        if key not in self.swsem:
            self.swsem[key] = self.es.enter_context(self.nc.semaphore("sw%d" % len(self.swsem)))
            self.swused[key] = False
        sem = self.swsem[key]
        waits = self._deps(eng, reads, writes)
        reuse = self.swused[key]
        if reuse and self.known[eng].get(id(sem), 0) < 16:
            waits.append((sem, 16))
        for e in ENGS:
            self.known[e].pop(id(sem), None)
        self.swused[key] = True
        ev = (sem, 16)

        def fn(e, out_ap=out_ap, in_ap=in_ap, kw=kw, sem=sem, reuse=reuse):
            if reuse:
                e.sem_clear(sem)
            return e.dma_start(out=out_ap, in_=in_ap, **kw)
        self.q[eng].append((waits, fn, (sem, 16)))
        self._commit(ev, reads, writes)
        if is_output:
            self.out_events.append(ev)
        return ev

    def barrier(self):
        waits = []
        for key, sem in self.swsem.items():
            if self.swused[key] and self.known["pool"].get(id(sem), 0) < 16:
                waits.append((sem, 16))
                self.known["pool"][id(sem)] = 16
        assert not waits
        for e in ENGS:
            waits = []
            for f in ENGS:
                if f != e and self.cnt[f] > self.known[e].get(id(self.prog[f]), 0):
                    waits.append((self.prog[f], self.cnt[f]))
                    self.known[e][id(self.prog[f])] = self.cnt[f]
            for i in range(N_DMA_SEMS):
                if self.dval[i] > self.known[e].get(id(self.dsem[i]), 0):
                    waits.append((self.dsem[i], self.dval[i]))
                    self.known[e][id(self.dsem[i])] = self.dval[i]
            if waits:
                self.q[e].append((waits, None, None))

    def scope(self):
        return _Scope(self)

    def emit(self):
        nc = self.nc
        seen = {}
        for (sem, val) in self.out_events:
            if seen.get(id(sem), (None, 0))[1] < val:
                seen[id(sem)] = (sem, val)
        fin = list(seen.values())
        engmap = {"pe": "tensor", "dve": "vector", "act": "scalar", "pool": "gpsimd", "sp": "sync"}
        with nc.Block() as block:
            for e in ENGS:
                q = self.q[e]
                is_sp = (e == "sp")

                def body(eng, q=q, is_sp=is_sp):
                    for waits, fn, inc in q:
                        for (sem, val) in waits:
                            eng.wait_ge(sem, val)
                        if fn is None:
                            continue
                        ins = fn(eng)
                        ins.then_inc(inc[0], inc[1])
                    if is_sp:
                        for (sem, val) in fin:
                            eng.wait_ge(sem, val)
                getattr(block, engmap[e])(body)
        self.es.close()


class _Scope:
    def __init__(self, s):
        self.s = s

    def __enter__(self):
        self.saved = self.s.aoff
        return self

    def __exit__(self, *a):
        self.s.barrier()
        self.s.aoff = self.saved
        return False


class KX:
    def __init__(self, nc, npsum=8, nwbuf=3, wbuf_elems=4096):
        self.nc = nc
        self.s = Sched(nc)
        s = self.s
        self.ones = s.tile([128, 128], BF16, "ones")
        self.eps = s.tile([128, 1], F32, "eps")
        s.op("dve", lambda e: e.memset(self.ones[:], 1.0), writes=[self.ones])
        s.op("dve", lambda e: e.memset(self.eps[:], EPS), writes=[self.eps])
        self.ps = [s.ptile(name="psb%d" % i) for i in range(npsum)]
        self.psi = 0
        self.wb = [s.tile([128, wbuf_elems], BF16, "wbuf%d" % i) for i in range(nwbuf)]
        self.wbi = 0
        self.sqs = [s.tile([128, 512], BF16, "sqbuf%d" % i) for i in range(2)]
        self.stg = [s.tile([128, 2048], F32, "stage%d" % i) for i in range(2)]
        self.stgi = 0
        self.dmaq = ("sp", "act")
        self.dqi = 0
        self.cast_engs = CAST_ENGS
        self.cei = 0
        self.rs = [s.tile([128, 512], F32, "rstd%d" % i) for i in range(2)]
        self.rsi = 0
        self.tc = [s.tile([128, 512], F32, "tmpc%d" % i) for i in range(3)]
        self.tci = 0

    def nps(self):
        t = self.ps[self.psi]
        self.psi = (self.psi + 1) % len(self.ps)
        return t

    def ntc(self):
        t = self.tc[self.tci]
        self.tci = (self.tci + 1) % len(self.tc)
        return t

    def nwb(self):
        t = self.wb[self.wbi]
        self.wbi = (self.wbi + 1) % len(self.wb)
        return t

    def stage_cast(self, dst_T, dst_ap, src_ap, n):
        s = self.s
        st = self.stg[self.stgi]
        self.stgi = (self.stgi + 1) % len(self.stg)
        q = "sp"
        shp = list(src_ap.shape)
        sv = st.ap[:, 0:n]
        if len(shp) == 3:
            sv = sv.rearrange("p (a b) -> p a b", a=shp[1])
        s.dma(q, sv, src_ap, writes=[st])
        ce = self.cast_engs[self.cei]
        self.cei = (self.cei + 1) % len(self.cast_engs)
        if ce == "act":
            s.op("act", lambda e: e.activation(out=dst_ap, in_=sv, func=AF.Identity), reads=[st], writes=[dst_T])
        else:
            s.op(ce, lambda e: e.tensor_copy(out=dst_ap, in_=sv), reads=[st], writes=[dst_T])

    def wload(self, w_ap, kc, ncols):
        t = self.nwb()
        view = t.ap[:, 0:kc * ncols].rearrange("p (c n) -> p c n", c=kc)
        src = w_ap.rearrange("(c p) n -> p c n", p=128)
        per = max(1, 2048 // ncols)
        for c0 in range(0, kc, per):
            c1 = min(kc, c0 + per)
            self.stage_cast(t, view[:, c0:c1, :], src[:, c0:c1, :], (c1 - c0) * ncols)
        return t, view

    def rstd(self, src_T, src_ap, n, nchunks=8, dim=D):
        s = self.s
        ps = self.nps()
        for c in range(nchunks):
            sq = self.sqs[c % 2]
            s.op("act", lambda e, c=c, sq=sq: e.activation(out=sq[:, 0:n], in_=src_ap[:, c, :], func=AF.Square), reads=[src_T], writes=[sq])
            s.op("pe", lambda e, c=c, sq=sq: e.matmul(ps[:, 0:n], lhsT=self.ones[:], rhs=sq[:, 0:n], start=(c == 0), stop=(c == nchunks - 1)),
                 reads=[self.ones, sq], writes=[ps])
        r = self.rs[self.rsi]
        self.rsi = (self.rsi + 1) % len(self.rs)
        s.op("act", lambda e: e.activation(out=r[:, 0:n], in_=ps[:, 0:n], func=AF.Ln, bias=self.eps[:], scale=1.0 / dim),
             reads=[ps, self.eps], writes=[r])
        s.op("act", lambda e: e.activation(out=r[:, 0:n], in_=r[:, 0:n], func=AF.Exp, scale=-0.5), reads=[r], writes=[r])
        return r

    def norm_mod(self, src_T, src_ap, n, ms, which, dst_T, dst_ap, tmp_T):
        s = self.s
        A, S = (ms.A0, ms.S0) if which == 0 else (ms.A2, ms.S2)
        r = self.rstd(src_T, src_ap, n)
        for c in range(8):
            tc = self.ntc()
            s.op("dve", lambda e, c=c, tc=tc: e.tensor_tensor(out=tc[:, 0:n], in0=src_ap[:, c, :], in1=r[:, 0:n], op=ALU.mult),
                 reads=[src_T, r], writes=[tc])
            s.op("act", lambda e, c=c, tc=tc: e.activation(out=dst_ap[:, c, :], in_=tc[:, 0:n], func=AF.Identity,
                                                     bias=S[:, c:c + 1], scale=A[:, c:c + 1]),
                 reads=[tc, ms.mv], writes=[dst_T])

    def resid_add(self, y_T, y_ap, n, ms, which, x_T, x_ap, tmp_T):
        s = self.s
        G = ms.G1 if which == 1 else ms.G2
        r = self.rstd(y_T, y_ap, n)
        for c in range(8):
            tc = self.ntc()
            s.op("dve", lambda e, c=c, tc=tc: e.tensor_tensor(out=tc[:, 0:n], in0=y_ap[:, c, :], in1=r[:, 0:n], op=ALU.mult),
                 reads=[y_T, r], writes=[tc])
            s.op("dve", lambda e, c=c, tc=tc: e.scalar_tensor_tensor(out=x_ap[:, c, :], in0=tc[:, 0:n], scalar=G[:, c:c + 1],
                                                               in1=x_ap[:, c, :], op0=ALU.mult, op1=ALU.add),
                 reads=[tc, x_T, ms.mv], writes=[x_T])

    def prep_mod(self, mvec_ap, gvec_ap, name="mv"):
        s = self.s
        mv = s.tile([128, 8, 16], F32, name)
        s.dma("sp", mv[:, :, 0:6], mvec_ap.rearrange("(c p) n -> p c n", p=128), writes=[mv])
        s.dma("sp", mv[:, :, 6:10], gvec_ap.rearrange("(c p) n -> p c n", p=128), writes=[mv])
        s.op("dve", lambda e: e.scalar_tensor_tensor(out=mv[:, :, 10], in0=mv[:, :, 1], scalar=1.0, in1=mv[:, :, 6], op0=ALU.add, op1=ALU.mult),
             reads=[mv], writes=[mv])
        s.op("dve", lambda e: e.tensor_tensor(out=mv[:, :, 11], in0=mv[:, :, 2], in1=mv[:, :, 7], op=ALU.mult), reads=[mv], writes=[mv])
        s.op("dve", lambda e: e.scalar_tensor_tensor(out=mv[:, :, 12], in0=mv[:, :, 4], scalar=1.0, in1=mv[:, :, 8], op0=ALU.add, op1=ALU.mult),
             reads=[mv], writes=[mv])
        s.op("dve", lambda e: e.tensor_tensor(out=mv[:, :, 13], in0=mv[:, :, 5], in1=mv[:, :, 9], op=ALU.mult), reads=[mv], writes=[mv])
        ms = ModSet()
        ms.mv = mv
        ms.A0, ms.S0, ms.G1, ms.A2, ms.S2, ms.G2 = mv[:, :, 10], mv[:, :, 0], mv[:, :, 11], mv[:, :, 12], mv[:, :, 3], mv[:, :, 13]
        return ms

    def wstream(self, loaders, computes, L=2, pre=None):
        n = len(loaders)
        h = list(pre) if pre else []
        for i in range(n + L):
            if len(h) <= i < n:
                h.append(loaders[i]())
            j = i - L
            if 0 <= j < n:
                computes[j](*h[j])

    def mlp_loaders(self, w1_ap, w2_ap):
        ld = [(lambda jb=jb: self.wload(w1_ap[:, jb * 512:(jb + 1) * 512], 8, 512)) for jb in range(HID // 512)]
        ld += [(lambda db=db: self.wload(w2_ap[:, db * 128:(db + 1) * 128], 32, 128)) for db in range(D // 128)]
        return ld

    def mlp(self, blocks, h2_tiles, w1_ap, w2_ap, hid_T, out_fn, pre=None):
        s = self.s
        offs = [0]
        for b in blocks:
            offs.append(offs[-1] + b.w)

        def c1(jb):
            def f(wt, wv):
                for sub in range(4):
                    hc = jb * 4 + sub
                    for i, b in enumerate(blocks):
                        ps = self.nps()
                        for c in range(8):
                            s.op("pe", lambda e, c=c, ps=ps, i=i, b=b, sub=sub, wv=wv: e.matmul(
                                ps[:, 0:b.w], lhsT=wv[:, c, sub * 128:(sub + 1) * 128], rhs=h2_tiles[i][:, c, 0:b.w], start=(c == 0), stop=(c == 7)),
                                reads=[wt, h2_tiles[i]], writes=[ps])
                        dst = hid_T[:, hc, offs[i]:offs[i + 1]]
                        s.op("act", lambda e, ps=ps, dst=dst, b=b: e.activation(out=dst, in_=ps[:, 0:b.w], func=AF.Relu), reads=[ps], writes=[hid_T])
                        s.op("act", lambda e, dst=dst: e.activation(out=dst, in_=dst, func=AF.Square), reads=[hid_T], writes=[hid_T])
            return f

        def c2(db):
            def f(wt, wv):
                for i, b in enumerate(blocks):
                    ps = self.nps()
                    for c in range(32):
                        s.op("pe", lambda e, c=c, ps=ps, i=i, b=b, wv=wv: e.matmul(
                            ps[:, 0:b.w], lhsT=wv[:, c, :], rhs=hid_T[:, c, offs[i]:offs[i + 1]], start=(c == 0), stop=(c == 31)),
                            reads=[wt, hid_T], writes=[ps])
                    out_fn(i, db, ps)
            return f
        computes = [c1(jb) for jb in range(HID // 512)] + [c2(db) for db in range(D // 128)]
        self.wstream(self.mlp_loaders(w1_ap, w2_ap), computes, L=len(self.wb) - 1, pre=pre)


class ModSet:
    pass


class Blk:
    def __init__(self, w, ms, x_T, x_view, y_T):
        self.w, self.ms, self.x_T, self.x_view, self.y_T = w, ms, x_T, x_view, y_T


def _mk(nc, name, shape, dtype=F32, kind="ExternalInput"):
    return nc.dram_tensor(name, list(shape), dtype, kind=kind).ap()


def emit_finish(kx, blocks, w1_ap, w2_ap, tmp_T):
    s = kx.s
    lds = kx.mlp_loaders(w1_ap, w2_ap)
    pre = [lds[i]() for i in range(len(kx.wb) - 1)]
    for b in blocks:
        kx.resid_add(b.y_T, b.y_T[:, :, 0:b.w], b.w, b.ms, 1, b.x_T, b.x_view, tmp_T)
    with s.scope():
        h2_tiles = [s.tile([128, 8, b.w], BF16, "h2_%d" % i) for i, b in enumerate(blocks)]
        hid_T = s.tile([128, 32, sum(b.w for b in blocks)], BF16, "hid")
        for i, b in enumerate(blocks):
            kx.norm_mod(b.x_T, b.x_view, b.w, b.ms, 2, h2_tiles[i], h2_tiles[i][:, :, 0:b.w], tmp_T)

        def out_fn(i, dc, ps):
            b = blocks[i]
            s.op("act", lambda e: e.activation(out=b.y_T[:, dc, 0:b.w], in_=ps[:, 0:b.w], func=AF.Identity), reads=[ps], writes=[b.y_T])
        kx.mlp(blocks, h2_tiles, w1_ap, w2_ap, hid_T, out_fn, pre=pre)
    for b in blocks:
        kx.resid_add(b.y_T, b.y_T[:, :, 0:b.w], b.w, b.ms, 2, b.x_T, b.x_view, tmp_T)


def build_sc(NT=2048):
    nc = bass.Bass("TRN2", target_bir_lowering=False)
    xT = _mk(nc, "xT", [D, NT + 2])
    mvec = _mk(nc, "mvec", [D, 6])
    gvec = _mk(nc, "gvec", [D, 4])
    hmask = _mk(nc, "hmask", [128, 2])
    w_in = _mk(nc, "w_in", [D, 3 * D])
    convw = _mk(nc, "convw", [D, 3])
    w_out = _mk(nc, "w_out", [D, D])
    w1 = _mk(nc, "w1", [D, HID])
    w2 = _mk(nc, "w2", [HID, D])
    mvec3 = _mk(nc, "mvec3", [D, 6])
    gvec3 = _mk(nc, "gvec3", [D, 4])
    outT = _mk(nc, "outT", [D, NT], kind="ExternalOutput")
    h3T = _mk(nc, "h3T", [D, NT], kind="ExternalOutput")
    kx = KX(nc)
    s = kx.s
    ms = kx.prep_mod(mvec, gvec)
    ms3 = kx.prep_mod(mvec3, gvec3, "mv3")
    HWD = NT // 2
    bw = 512
    nblk = HWD // bw
    hm = s.tile([128, 2], F32, "hm")
    s.dma("sp", hm[:], hmask, writes=[hm])
    cw = s.tile([128, 8, 3], F32, "cw")
    s.dma("sp", cw[:], convw.rearrange("(c p) k -> p c k", p=128), writes=[cw])
    xh = s.tile([128, 8, HWD + 2], F32, "xh")
    tmp_T = None
    y_tiles = [s.tile([128, 8, bw], F32, "y%d" % i) for i in range(nblk)]
    w_in4 = w_in.rearrange("(c p) (t j n) -> p c t j n", p=128, t=3, j=8)

    class XV:
        pass
    for half in range(2):
        c0 = half * HWD
        s.dma("sp", xh[:], xT[:, c0:c0 + HWD + 2].rearrange("(c p) n -> p c n", p=128), writes=[xh])
        with s.scope():
            hT = [s.tile([128, 8, 342], BF16, "hT%d" % i) for i in range(3)]
            bcu = [s.tile([128, 3, HWD + 2], F32, "bcu%d" % i) for i in range(2)]
            acc = [s.tile([128, HWD], F32, "acc%d" % i) for i in range(2)]
            gT = [s.tile([128, 8, bw], BF16, "gT%d" % i) for i in range(nblk)]
            for i in range(3):
                kx.norm_mod(xh, xh[:, :, i * 342:(i + 1) * 342], 342, ms, 0, hT[i], hT[i][:, :, 0:342], tmp_T)
            def ld_in(j):
                wt = kx.nwb()
                wv = wt.ap[:, 0:8 * 3 * 128].rearrange("p (c t n) -> p c t n", c=8, t=3)
                for t in range(3):
                    kx.stage_cast(wt, wv[:, :, t, :], w_in4[:, :, t, j, :], 1024)
                return wt, wv

            def cp_in(j, wt, wv, half=half, hT=hT, bcu=bcu, acc=acc, gT=gT):
                bc = bcu[j % 2]
                ac = acc[j % 2]
                for t in range(3):
                    for i in range(3):
                        ps = kx.nps()
                        for c in range(8):
                            s.op("pe", lambda e, c=c, ps=ps, i=i, t=t, wv=wv: e.matmul(
                                ps[:, 0:342], lhsT=wv[:, c, t, :], rhs=hT[i][:, c, :], start=(c == 0), stop=(c == 7)),
                                reads=[wt, hT[i]], writes=[ps])
                        s.op("act", lambda e, ps=ps, t=t, i=i, bc=bc: e.activation(out=bc[:, t, i * 342:(i + 1) * 342], in_=ps[:, 0:342], func=AF.Identity),
                             reads=[ps], writes=[bc])
                s.op("dve", lambda e, bc=bc: e.tensor_tensor(out=bc[:, 1, :], in0=bc[:, 1, :], in1=bc[:, 2, :], op=ALU.mult), reads=[bc], writes=[bc])
                hc_ = 0 if half == 0 else HWD + 1
                s.op("dve", lambda e, bc=bc, hc_=hc_, half=half: e.tensor_scalar(out=bc[:, 1, hc_:hc_ + 1], in0=bc[:, 1, hc_:hc_ + 1], scalar1=hm[:, half:half + 1],
                                                                    scalar2=None, op0=ALU.mult), reads=[bc, hm], writes=[bc])
                s.op("dve", lambda e, bc=bc, ac=ac, j=j: e.tensor_scalar(out=ac[:, :], in0=bc[:, 1, 0:HWD], scalar1=cw[:, j, 0:1], scalar2=None, op0=ALU.mult),
                     reads=[bc, cw], writes=[ac])
                for k in (1, 2):
                    s.op("dve", lambda e, bc=bc, ac=ac, j=j, k=k: e.scalar_tensor_tensor(out=ac[:, :], in0=bc[:, 1, k:k + HWD], scalar=cw[:, j, k:k + 1],
                                                                                      in1=ac[:, :], op0=ALU.mult, op1=ALU.add),
                         reads=[bc, cw, ac], writes=[ac])
                for tb in range(nblk):
                    s.op("dve", lambda e, bc=bc, ac=ac, j=j, tb=tb: e.tensor_tensor(out=gT[tb][:, j, :], in0=ac[:, tb * bw:(tb + 1) * bw],
                                                                                 in1=bc[:, 0, 1 + tb * bw:1 + (tb + 1) * bw], op=ALU.mult),
                         reads=[ac, bc], writes=[gT[tb]])
            kx.wstream([(lambda j=j: ld_in(j)) for j in range(8)], [(lambda wt, wv, j=j: cp_in(j, wt, wv)) for j in range(8)], L=2)
            for db in range(2):
                wt, wv = kx.wload(w_out[:, db * 512:(db + 1) * 512], 8, 512)
                for sub in range(4):
                    dc = db * 4 + sub
                    for tb in range(nblk):
                        ps = kx.nps()
                        for c in range(8):
                            s.op("pe", lambda e, c=c, ps=ps, tb=tb, sub=sub, wv=wv: e.matmul(
                                ps[:, 0:bw], lhsT=wv[:, c, sub * 128:(sub + 1) * 128], rhs=gT[tb][:, c, :], start=(c == 0), stop=(c == 7)),
                                reads=[wt, gT[tb]], writes=[ps])
                        s.op("act", lambda e, ps=ps, dc=dc, tb=tb: e.activation(out=y_tiles[tb][:, dc, :], in_=ps[:, 0:bw], func=AF.Identity),
                             reads=[ps], writes=[y_tiles[tb]])
        x_views = [xh[:, :, 1 + tb * bw:1 + (tb + 1) * bw] for tb in range(nblk)]
        blocks = [Blk(bw, ms, xh, x_views[tb], y_tiles[tb]) for tb in range(nblk)]
        emit_finish(kx, blocks, w1, w2, tmp_T)
        for tb in range(nblk):
            s.dma("act", outT[:, c0 + tb * bw:c0 + (tb + 1) * bw].rearrange("(c p) n -> p c n", p=128), x_views[tb], reads=[xh], is_output=True)
            kx.norm_mod(xh, x_views[tb], bw, ms3, 0, y_tiles[tb], y_tiles[tb][:, :, 0:bw], None)
            s.dma("act", h3T[:, c0 + tb * bw:c0 + (tb + 1) * bw].rearrange("(c p) n -> p c n", p=128), y_tiles[tb][:, :, 0:bw], reads=[y_tiles[tb]], is_output=True)
    s.emit()
    return nc


def _halo_T(xb, t0, n, lo=1, hi=1):
    L = xb.shape[0]
    out = np.zeros((D, lo + n + hi), np.float32)
    a = max(t0 - lo, 0)
    b = min(t0 + n + hi, L)
    out[:, a - (t0 - lo):b - (t0 - lo)] = xb[a:b].T
    return out


def run_sc(x, m_lat, g, w_in, convw, w_out, w1, w2, m_lat3, g3):
    NT = 2048
    nc = build_sc(NT)
    ins = []
    for core in range(NCORES):
        b, k = divmod(core, 4)
        t0 = k * NT
        hm = np.ones((128, 2), np.float32)
        if k == 0:
            hm[:, 0] = 0
        if k == 3:
            hm[:, 1] = 0
        ins.append({
            "xT": _halo_T(x[b], t0, NT), "mvec": np.ascontiguousarray(m_lat[b].reshape(6, D).T), "gvec": np.ascontiguousarray(g.T),
            "hmask": hm, "w_in": w_in, "convw": np.ascontiguousarray(convw.T), "w_out": w_out, "w1": w1, "w2": w2,
            "mvec3": np.ascontiguousarray(m_lat3[b].reshape(6, D).T), "gvec3": np.ascontiguousarray(g3.T)})
    res = run_bass_kernel_spmd(nc, ins, core_ids=list(range(NCORES)))
    out = np.empty_like(x)
    h3 = np.empty_like(x)
    for core in range(NCORES):
        b, k = divmod(core, 4)
        out[b, k * NT:(k + 1) * NT] = res.results[core]["outT"].T
        h3[b, k * NT:(k + 1) * NT] = res.results[core]["h3T"].T
    return out, h3


def build_mod():
    nc = bass.Bass("TRN2", target_bir_lowering=False)
    ccT = _mk(nc, "ccT", [D, 3])
    w = _mk(nc, "w", [D, 3072])
    bvec = _mk(nc, "bvec", [128, 24])
    outT = _mk(nc, "outT", [3072, 3], kind="ExternalOutput")
    s = Sched(nc)
    cc = s.tile([128, 8, 3], F32, "cc")
    bt = s.tile([128, 24], F32, "bt")
    ot = s.tile([128, 24, 3], F32, "ot")
    ps = [s.ptile(name="ps%d" % i) for i in range(4)]
    wb = [s.tile([128, 8, 512], F32, "wb%d" % i) for i in range(3)]
    s.dma("sp", cc[:], ccT.rearrange("(c p) n -> p c n", p=128), writes=[cc])
    s.dma("sp", bt[:], bvec, writes=[bt])
    s.op("act", lambda e: e.activation(out=cc[:], in_=cc[:], func=AF.Silu), reads=[cc], writes=[cc])
    for jb in range(6):
        wt = wb[jb % 3]
        s.dma("sp" if jb % 2 == 0 else "act", wt[:], w[:, jb * 512:(jb + 1) * 512].rearrange("(c p) n -> p c n", p=128), writes=[wt])
        for sub in range(4):
            oc = jb * 4 + sub
            p = ps[oc % 4]
            for c in range(8):
                s.op("pe", lambda e, c=c, p=p, sub=sub, wt=wt: e.matmul(p[:, 0:3], lhsT=wt[:, c, sub * 128:(sub + 1) * 128], rhs=cc[:, c, :],
                                                                    start=(c == 0), stop=(c == 7)), reads=[wt, cc], writes=[p])
            s.op("act", lambda e, p=p, oc=oc: e.activation(out=ot[:, oc, :], in_=p[:, 0:3], func=AF.Identity, bias=bt[:, oc:oc + 1], scale=1.0),
                 reads=[p, bt], writes=[ot])
    s.dma("sp", outT.rearrange("(c p) n -> p c n", p=128), ot[:], reads=[ot], is_output=True)
    s.emit()
    return nc


def run_mod(c, c_ctx, mod_w, mod_b):
    nc = build_mod()
    ccT = np.ascontiguousarray(np.concatenate([c, c_ctx[None]], 0).T)
    ins = []
    for core in range(NCORES):
        i, hf = divmod(core, 2)
        ins.append({"ccT": ccT, "w": np.ascontiguousarray(mod_w[i][:, hf * 3072:(hf + 1) * 3072]),
                    "bvec": np.ascontiguousarray(mod_b[i][hf * 3072:(hf + 1) * 3072].reshape(24, 128).T)})
    res = run_bass_kernel_spmd(nc, ins, core_ids=list(range(NCORES)))
    m = np.zeros((4, 3, 6144), np.float32)
    for core in range(NCORES):
        i, hf = divmod(core, 2)
        m[i, :, hf * 3072:(hf + 1) * 3072] = res.results[core]["outT"].T
    return m[:, 0:2], m[:, 2]


def _fft_consts():
    c = np.arange(128)
    ang = 2 * np.pi * np.outer(c, c) / 128.0
    fc_cos, fc_sin = np.cos(ang), np.sin(ang)
    f1 = np.concatenate([fc_cos, -fc_sin], 1).astype(np.float32)
    f2 = np.concatenate([fc_sin, fc_cos], 1).astype(np.float32)
    t2 = np.arange(64)[:, None, None]
    k1 = np.arange(128)[None, :, None]
    k2 = np.arange(64)[None, None, :]
    ang3 = 2 * np.pi * (k1 * t2 / 8192.0 + k2 * t2 / 64.0)
    g = np.stack([np.cos(ang3), np.sin(ang3)], 2).astype(np.float32)
    return f1, f2, np.ascontiguousarray(g.reshape(64, 128 * 2 * 64))


def build_fft():
    nc = bass.Bass("TRN2", target_bir_lowering=False)
    hg = _mk(nc, "hg", [2, 128, 8192])
    f1 = _mk(nc, "f1", [128, 256])
    f2 = _mk(nc, "f2", [128, 256])
    g3 = _mk(nc, "g3", [64, 128 * 128])
    fo = _mk(nc, "fo", [2, 64, 128 * 128], kind="ExternalOutput")
    s = Sched(nc)
    f1t = s.tile([128, 256], BF16, "f1t")
    f2t = s.tile([128, 256], BF16, "f2t")
    g3t = s.tile([64, 128, 2, 64], BF16, "g3t")
    stg = [s.tile([128, 2048], F32, "stage%d" % i) for i in range(2)]
    stgi = [0]

    def stage_cast(dst_T, dst_ap, src_ap, np_, n):
        st = stg[stgi[0] % 2]
        s.dma("sp" if stgi[0] % 2 == 0 else "act", st[0:np_, 0:n], src_ap, writes=[st])
        stgi[0] += 1
        s.op("pool", lambda e: e.tensor_copy(out=dst_ap, in_=st[0:np_, 0:n]), reads=[st], writes=[dst_T])
    stage_cast(f1t, f1t[:], f1, 128, 256)
    stage_cast(f2t, f2t[:], f2, 128, 256)
    for q in range(8):
        stage_cast(g3t, g3t[:, q * 16:(q + 1) * 16, :, :].rearrange("p a r k -> p (a r k)"), g3[:, q * 2048:(q + 1) * 2048], 64, 2048)
    ps = [s.ptile(name="ps%d" % i) for i in range(8)]
    psi = [0]

    def nps():
        t = ps[psi[0]]
        psi[0] = (psi[0] + 1) % 8
        return t
    xin = s.tile([128, 64, 128], BF16, "xin")
    B = s.tile([128, 64, 2, 128], BF16, "B")
    A = s.tile([64, 2, 128, 128], BF16, "A")
    ob = [s.tile([64, 16, 128], F32, "ob%d" % i) for i in range(2)]
    for g in range(2):
        for q in range(4):
            stage_cast(xin, xin[:, q * 16:(q + 1) * 16, :].rearrange("p a b -> p (a b)"), hg[g][:, q * 2048:(q + 1) * 2048], 128, 2048)
        for tp in range(32):
            p = nps()
            for u in range(2):
                t2 = tp * 2 + u
                s.op("pe", lambda e, p=p, u=u, t2=t2: e.matmul(p[:, u * 256:(u + 1) * 256], lhsT=xin[:, t2, :], rhs=f1t[:], start=True, stop=True),
                     reads=[xin, f1t], writes=[p])
            eng = "act" if tp % 2 == 0 else "dve"
            dst = B[:, tp * 2:tp * 2 + 2, :, :].rearrange("p a r m -> p (a r m)")
            if eng == "act":
                s.op("act", lambda e, p=p, dst=dst: e.activation(out=dst, in_=p[:], func=AF.Identity), reads=[p], writes=[B])
            else:
                s.op("dve", lambda e, p=p, dst=dst: e.tensor_copy(out=dst, in_=p[:]), reads=[p], writes=[B])
        for mp in range(64):
            p = nps()
            for u in range(2):
                m = mp * 2 + u
                s.op("pe", lambda e, p=p, u=u, m=m: e.matmul(p[0:64, u * 256:(u + 1) * 256], lhsT=B[:, :, 0, m], rhs=f1t[:], start=True, stop=False),
                     reads=[B, f1t], writes=[p])
                s.op("pe", lambda e, p=p, u=u, m=m: e.matmul(p[0:64, u * 256:(u + 1) * 256], lhsT=B[:, :, 1, m], rhs=f2t[:], start=False, stop=True),
                     reads=[B, f2t], writes=[p])
            for u in range(2):
                m = mp * 2 + u
                src = p[0:64, u * 256:(u + 1) * 256].rearrange("p (r k) -> p r k", r=2)
                if u == 0:
                    s.op("act", lambda e, src=src, m=m: e.activation(out=A[:, :, :, m], in_=src, func=AF.Identity), reads=[p], writes=[A])
                else:
                    s.op("dve", lambda e, src=src, m=m: e.tensor_copy(out=A[:, :, :, m], in_=src), reads=[p], writes=[A])
        for kb in range(8):
            o = ob[kb % 2]
            for kq in range(4):
                p = nps()
                for u in range(4):
                    k1 = kb * 16 + kq * 4 + u
                    s.op("pe", lambda e, p=p, u=u, k1=k1: e.matmul(p[0:64, u * 128:(u + 1) * 128], lhsT=g3t[:, k1, 0, :], rhs=A[:, 0, k1, :], start=True, stop=False),
                         reads=[g3t, A], writes=[p])
                    s.op("pe", lambda e, p=p, u=u, k1=k1: e.matmul(p[0:64, u * 128:(u + 1) * 128], lhsT=g3t[:, k1, 1, :], rhs=A[:, 1, k1, :], start=False, stop=True),
                         reads=[g3t, A], writes=[p])
                dst = o[:, kq * 4:(kq + 1) * 4, :].rearrange("p a m -> p (a m)")
                if kq % 2 == 0:
                    s.op("act", lambda e, p=p, dst=dst: e.activation(out=dst, in_=p[0:64, :], func=AF.Identity), reads=[p], writes=[o])
                else:
                    s.op("dve", lambda e, p=p, dst=dst: e.tensor_copy(out=dst, in_=p[0:64, :]), reads=[p], writes=[o])
            s.dma("sp", fo[g][:, kb * 16 * 128:(kb + 1) * 16 * 128], o[:].rearrange("p a m -> p (a m)"), reads=[o], is_output=True)
    s.emit()
    return nc


def run_fft(h):
    nc = build_fft()
    f1, f2, g3 = _fft_consts()
    ins = []
    for core in range(NCORES):
        b, gp = divmod(core, 4)
        hb = np.asarray(h[b][:, gp * 256:(gp + 1) * 256]).reshape(128, 64, 2, 128)
        hgc = np.ascontiguousarray(hb.transpose(2, 3, 1, 0)).reshape(2, 128, 8192)
        ins.append({"hg": hgc.astype(np.float32), "f1": f1, "f2": f2, "g3": g3})
    res = run_bass_kernel_spmd(nc, ins, core_ids=list(range(NCORES)))
    out = np.empty((BATCH, SEQ, D), np.float32)
    for core in range(NCORES):
        b, gp = divmod(core, 4)
        fo = res.results[core]["fo"].reshape(2, 64, 128, 128)
        out[b, :, gp * 256:(gp + 1) * 256] = fo.transpose(1, 2, 0, 3).reshape(8192, 256)
    return out


NA_ROWS_LOCAL = 39


def build_na(NT=2048):
    nc = bass.Bass("TRN2", target_bir_lowering=False)
    xhT = _mk(nc, "xhT", [D, NA_ROWS_LOCAL * 64])
    ctxT = _mk(nc, "ctxT", [D, CTX])
    mvec = _mk(nc, "mvec", [D, 6])
    mvec_c = _mk(nc, "mvec_c", [D, 6])
    gvec = _mk(nc, "gvec", [D, 4])
    w_qkv = _mk(nc, "w_qkv", [D, 3 * D])
    w_out = _mk(nc, "w_out", [D, D])
    w1 = _mk(nc, "w1", [D, HID])
    w2 = _mk(nc, "w2", [HID, D])
    btab = _mk(nc, "btab", [8, 2, 128, 3840])
    identd = _mk(nc, "ident", [128, 128])
    outT = _mk(nc, "outT", [D, NT], kind="ExternalOutput")
    kx = KX(nc)
    s = kx.s
    ms = kx.prep_mod(mvec, gvec)
    msc = kx.prep_mod(mvec_c, gvec, "mvc")
    ident = s.tile([128, 128], BF16, "ident")
    kx.stage_cast(ident, ident[:], identd, 128 * 128 // 128)
    HWD, bw, nblk = 1024, 512, 2
    WIN = 23 * 64
    xres = s.tile([128, 8, HWD], F32, "xres")
    y_tiles = [s.tile([128, 8, bw], F32, "y%d" % i) for i in range(nblk)]
    hcT = s.tile([128, 8, CTX], BF16, "hcT")
    s.dma("sp", y_tiles[0][:, :, 0:CTX], ctxT.rearrange("(c p) n -> p c n", p=128), writes=[y_tiles[0]])
    kx.norm_mod(y_tiles[0], y_tiles[0][:, :, 0:CTX], CTX, msc, 0, hcT, hcT[:, :, :], None)
    w_qkv4 = w_qkv.rearrange("(c p) (t j n) -> p c t j n", p=128, t=3, j=8)
    for half in range(2):
        wc0 = half * HWD
        s.dma("sp", xres[:], xhT[:, 256 + wc0:256 + wc0 + HWD].rearrange("(c p) n -> p c n", p=128), writes=[xres])
        with s.scope():
            hT = s.tile([128, 8, WIN], BF16, "hT")
            attT = [s.tile([128, 8, bw], BF16, "attT%d" % i) for i in range(nblk)]
            qT = s.tile([128, HWD], BF16, "qT")
            kT = s.tile([128, WIN], BF16, "kT")
            kcT = s.tile([128, CTX], BF16, "kcT")
            vt = s.tile([128, 12, 2, 65], BF16, "vt")
            vct = s.tile([128, 2, 2, 65], BF16, "vct")
            att_hp = s.tile([128, 8, 128], BF16, "att_hp")
            bt = s.tile([128, 3, 2, 5, 128], F32, "bt")
            stA = [s.tile([128, 512], F32, "stA%d" % i) for i in range(2)]
            stB = [s.tile([128, 128], F32, "stB%d" % i) for i in range(2)]
            pA = [s.tile([128, 512], BF16, "pA%d" % i) for i in range(2)]
            pB = [s.tile([128, 128], BF16, "pB%d" % i) for i in range(2)]
            pC = [s.tile([128, 256], BF16, "pC%d" % i) for i in range(2)]
            rc = [s.tile([128, 1], F32, "rc%d" % i) for i in range(2)]
            s.op("pool", lambda e: e.memset(vt[:, :, :, 64:65], 1.0), writes=[vt])
            s.op("pool", lambda e: e.memset(vct[:, :, :, 64:65], 1.0), writes=[vct])
            for j, (a, w) in enumerate(((0, 512), (512, 512), (1024, WIN - 1024))):
                yt = y_tiles[j % 2]
                s.dma("sp", yt[:, :, 0:w], xhT[:, wc0 + a:wc0 + a + w].rearrange("(c p) n -> p c n", p=128), writes=[yt])
                kx.norm_mod(yt, yt[:, :, 0:w], w, ms, 0, hT, hT[:, :, a:a + w], None)
            itc = [0]

            def ld_hp(hp):
                wt = kx.nwb()
                wv = wt.ap[:, 0:8 * 3 * 128].rearrange("p (c t n) -> p c t n", c=8, t=3)
                for t in range(3):
                    kx.stage_cast(wt, wv[:, :, t, :], w_qkv4[:, :, t, hp, :], 1024)
                return wt, wv

            def cp_hp(hp, wt, wv, half=half, hT=hT, attT=attT, qT=qT, kT=kT, kcT=kcT, vt=vt, vct=vct, att_hp=att_hp, bt=bt,
                      stA=stA, stB=stB, pA=pA, pB=pB, pC=pC, rc=rc):
                it = itc[0]
                s.dma("act", bt[:].rearrange("p a b c q -> p (a b c q)"), btab[hp, half], writes=[bt])
                for blk in range(2):
                    ps = kx.nps()
                    for c in range(8):
                        s.op("pe", lambda e, c=c, ps=ps, blk=blk, wv=wv: e.matmul(
                            ps[:, 0:512], lhsT=wv[:, c, 0, :], rhs=hT[:, c, 256 + blk * 512:256 + (blk + 1) * 512], start=(c == 0), stop=(c == 7)),
                            reads=[wt, hT], writes=[ps])
                    s.op("act", lambda e, ps=ps, blk=blk: e.activation(out=qT[:, blk * 512:(blk + 1) * 512], in_=ps[:, 0:512], func=AF.Identity),
                         reads=[ps], writes=[qT])
                for (a, w) in ((0, 512), (512, 512), (1024, WIN - 1024)):
                    ps = kx.nps()
                    for c in range(8):
                        s.op("pe", lambda e, c=c, ps=ps, a=a, w=w, wv=wv: e.matmul(
                            ps[:, 0:w], lhsT=wv[:, c, 1, :], rhs=hT[:, c, a:a + w], start=(c == 0), stop=(c == 7)),
                            reads=[wt, hT], writes=[ps])
                    s.op("dve", lambda e, ps=ps, a=a, w=w: e.tensor_copy(out=kT[:, a:a + w], in_=ps[:, 0:w]), reads=[ps], writes=[kT])
                ps = kx.nps()
                for c in range(8):
                    s.op("pe", lambda e, c=c, ps=ps, wv=wv: e.matmul(ps[:, 0:CTX], lhsT=wv[:, c, 1, :], rhs=hcT[:, c, :], start=(c == 0), stop=(c == 7)),
                         reads=[wt, hcT], writes=[ps])
                s.op("act", lambda e, ps=ps: e.activation(out=kcT[:, :], in_=ps[:, 0:CTX], func=AF.Identity), reads=[ps], writes=[kcT])
                for tg in range(3):
                    ps = kx.nps()
                    for u in range(4):
                        tcn = tg * 4 + u
                        ntok = 64 if tcn == 11 else 128
                        for c in range(8):
                            s.op("pe", lambda e, c=c, ps=ps, u=u, tcn=tcn, ntok=ntok, wv=wv: e.matmul(
                                ps[0:ntok, u * 128:(u + 1) * 128], lhsT=hT[:, c, tcn * 128:tcn * 128 + ntok], rhs=wv[:, c, 2, :], start=(c == 0), stop=(c == 7)),
                                reads=[wt, hT], writes=[ps])
                    nfull = 4 if tg < 2 else 3
                    s.op("act", lambda e, ps=ps, tg=tg, nfull=nfull: e.activation(
                        out=vt[:, tg * 4:tg * 4 + nfull, :, 0:64], in_=ps[:, 0:nfull * 128].rearrange("p (a b d) -> p a b d", a=nfull, b=2), func=AF.Identity),
                        reads=[ps], writes=[vt])
                    if tg == 2:
                        s.op("act", lambda e, ps=ps: e.activation(out=vt[0:64, 11, :, 0:64], in_=ps[0:64, 384:512].rearrange("p (b d) -> p b d", b=2), func=AF.Identity),
                             reads=[ps], writes=[vt])
                ps = kx.nps()
                for u in range(2):
                    for c in range(8):
                        s.op("pe", lambda e, c=c, ps=ps, u=u, wv=wv: e.matmul(
                            ps[:, u * 128:(u + 1) * 128], lhsT=hcT[:, c, u * 128:(u + 1) * 128], rhs=wv[:, c, 2, :], start=(c == 0), stop=(c == 7)),
                            reads=[wt, hcT], writes=[ps])
                s.op("dve", lambda e, ps=ps: e.tensor_copy(out=vct[:, :, :, 0:64], in_=ps[:, 0:256].rearrange("p (a b d) -> p a b d", a=2, b=2)),
                     reads=[ps], writes=[vct])
                units = [(h2, i) for h2 in range(2) for i in range(8)]
                pend = []

                def front(h2, i, it):
                    pb = 64 * h2
                    if True:
                        cls = (0 if i == 0 else 1 if i == 1 else 2) if half == 0 else (1 if i == 6 else 2 if i == 7 else 0)
                        a_, b_, c_ = stA[it % 2], stB[it % 2], rc[it % 2]
                        pa, pb_, pc = pA[it % 2], pB[it % 2], pC[it % 2]
                        psA = kx.nps()
                        for ch in range(4):
                            s.op("pe", lambda e, psA=psA, ch=ch, i=i, pb=pb: e.matmul(
                                psA[:, ch * 128:(ch + 1) * 128], lhsT=kT[pb:pb + 64, 128 * (i + ch):128 * (i + ch + 1)], rhs=qT[pb:pb + 64, 128 * i:128 * (i + 1)],
                                start=True, stop=True), reads=[kT, qT], writes=[psA])
                        psB = kx.nps()
                        s.op("pe", lambda e, psB=psB, i=i, pb=pb: e.matmul(
                            psB[0:64, 0:128], lhsT=kT[pb:pb + 64, 128 * (i + 4):128 * (i + 4) + 64], rhs=qT[pb:pb + 64, 128 * i:128 * (i + 1)],
                            start=True, stop=True), reads=[kT, qT], writes=[psB])
                        for cc in range(2):
                            s.op("pe", lambda e, psB=psB, cc=cc, i=i, pb=pb: e.matmul(
                                psB[:, 128 + cc * 128:256 + cc * 128], lhsT=kcT[pb:pb + 64, cc * 128:(cc + 1) * 128], rhs=qT[pb:pb + 64, 128 * i:128 * (i + 1)],
                                start=True, stop=True), reads=[kcT, qT], writes=[psB])
                        s.op("dve", lambda e, psA=psA, a_=a_, cls=cls, h2=h2: e.scalar_tensor_tensor(
                            out=a_[:, :], in0=psA[:, :], scalar=0.125, in1=bt[:, cls, h2, 0:4, :].rearrange("p a q -> p (a q)"), op0=ALU.mult, op1=ALU.add),
                            reads=[psA, bt], writes=[a_])
                        s.op("act", lambda e, a_=a_, pa=pa: e.activation(out=pa[:, :], in_=a_[:, :], func=AF.Exp), reads=[a_], writes=[pa])
                        s.op("dve", lambda e, psB=psB, b_=b_, cls=cls, h2=h2: e.scalar_tensor_tensor(
                            out=b_[0:64, :], in0=psB[0:64, 0:128], scalar=0.125, in1=bt[0:64, cls, h2, 4, :], op0=ALU.mult, op1=ALU.add),
                            reads=[psB, bt], writes=[b_])
                        s.op("act", lambda e, b_=b_, pb_=pb_: e.activation(out=pb_[0:64, :], in_=b_[0:64, :], func=AF.Exp), reads=[b_], writes=[pb_])
                        s.op("act", lambda e, psB=psB, pc=pc: e.activation(out=pc[:, :], in_=psB[:, 128:384], func=AF.Exp, scale=0.125), reads=[psB], writes=[pc])
                    return (h2, i, pa, pb_, pc, c_)

                def back(h2, i, pa, pb_, pc, c_):
                    if True:
                        psO = kx.nps()
                        for ch in range(4):
                            s.op("pe", lambda e, psO=psO, ch=ch, i=i, h2=h2, pa=pa: e.matmul(
                                psO[:, 0:65], lhsT=pa[:, ch * 128:(ch + 1) * 128], rhs=vt[:, i + ch, h2, :], start=(ch == 0), stop=False),
                                reads=[pa, vt], writes=[psO])
                        s.op("pe", lambda e, psO=psO, i=i, h2=h2, pb_=pb_: e.matmul(
                            psO[:, 0:65], lhsT=pb_[0:64, :], rhs=vt[0:64, i + 4, h2, :], start=False, stop=False), reads=[pb_, vt], writes=[psO])
                        for cc in range(2):
                            s.op("pe", lambda e, psO=psO, cc=cc, h2=h2, pc=pc: e.matmul(
                                psO[:, 0:65], lhsT=pc[:, cc * 128:(cc + 1) * 128], rhs=vct[:, cc, h2, :], start=False, stop=(cc == 1)),
                                reads=[pc, vct], writes=[psO])
                        s.op("dve", lambda e, psO=psO, c_=c_: e.reciprocal(out=c_[:, :], in_=psO[:, 64:65]), reads=[psO], writes=[c_])
                        s.op("dve", lambda e, psO=psO, c_=c_, i=i, h2=h2: e.tensor_scalar(
                            out=att_hp[:, i, h2 * 64:(h2 + 1) * 64], in0=psO[:, 0:64], scalar1=c_[:, 0:1], scalar2=None, op0=ALU.mult),
                            reads=[psO, c_], writes=[att_hp])
                for (h2, i) in units:
                    pend.append(front(h2, i, it))
                    it += 1
                    if len(pend) > 1:
                        back(*pend.pop(0))
                while pend:
                    back(*pend.pop(0))
                for ig in range(2):
                    pst = kx.nps()
                    pv = pst.ap.bitcast(BF16)
                    for u in range(4):
                        i = ig * 4 + u
                        s.op("pe", lambda e, pv=pv, u=u, i=i: e.transpose(out=pv[:, u * 128:(u + 1) * 128], in_=att_hp[:, i, :], identity=ident[:]),
                             reads=[att_hp, ident], writes=[pst])
                    s.op("dve", lambda e, pv=pv, ig=ig, hp=hp: e.tensor_copy(out=attT[ig][:, hp, :], in_=pv[:, 0:512]), reads=[pst], writes=[attT[ig]])
                itc[0] = it
            kx.wstream([(lambda hp=hp: ld_hp(hp)) for hp in range(8)], [(lambda wt, wv, hp=hp: cp_hp(hp, wt, wv)) for hp in range(8)], L=2)
            for db in range(2):
                wt, wv = kx.wload(w_out[:, db * 512:(db + 1) * 512], 8, 512)
                for sub in range(4):
                    dc = db * 4 + sub
                    for tb in range(nblk):
                        ps = kx.nps()
                        for c in range(8):
                            s.op("pe", lambda e, c=c, ps=ps, tb=tb, sub=sub, wv=wv: e.matmul(
                                ps[:, 0:bw], lhsT=wv[:, c, sub * 128:(sub + 1) * 128], rhs=attT[tb][:, c, :], start=(c == 0), stop=(c == 7)),
                                reads=[wt, attT[tb]], writes=[ps])
                        s.op("act", lambda e, ps=ps, dc=dc, tb=tb: e.activation(out=y_tiles[tb][:, dc, :], in_=ps[:, 0:bw], func=AF.Identity),
                             reads=[ps], writes=[y_tiles[tb]])
        x_views = [xres[:, :, tb * bw:(tb + 1) * bw] for tb in range(nblk)]
        blocks = [Blk(bw, ms, xres, x_views[tb], y_tiles[tb]) for tb in range(nblk)]
        emit_finish(kx, blocks, w1, w2, None)
        for tb in range(nblk):
            s.dma("act", outT[:, half * HWD + tb * bw:half * HWD + (tb + 1) * bw].rearrange("(c p) n -> p c n", p=128), x_views[tb], reads=[xres], is_output=True)
    s.emit()
    return nc


def _na_rowmap(kq):
    if kq == 0:
        return [5, 6, 7, -1] + list(range(0, 35))
    if kq == 3:
        return [92 + j for j in range(36)] + [120, 121, -1]
    return [32 * kq - 4 + j for j in range(NA_ROWS_LOCAL)]


def _na_tables(rpb, kq):
    NEG = -30000.0
    rm = _na_rowmap(kq)
    tabs = {}
    qc = np.arange(64)
    kc = np.arange(64)
    c0 = np.clip(qc - 8, 0, 48)
    colok = (kc[:, None] >= c0[None, :]) & (kc[:, None] < c0[None, :] + 16)
    dc = kc[:, None] - qc[None, :] + 15
    dcc = np.clip(dc, 0, 30)
    for P in (0, 1, 2, 14, 15):
        tab = np.full((16, 640, 128), NEG, np.float32)
        seen = set()
        for j in range(9):
            g = rm[2 * P + j]
            if g < 0 or g in seen:
                continue
            seen.add(g)
            for u in range(2):
                qr = 32 * kq + 2 * P + u
                r0 = min(max(qr - 4, 0), 120)
                if not (r0 <= g < r0 + 8):
                    continue
                dr = g - qr + 7
                vals = rpb[:, dr, :][:, dcc]
                blk = np.where(colok[None], vals, NEG)
                tab[:, j * 64:(j + 1) * 64, u * 64:(u + 1) * 64] = blk
        tabs[P] = tab.reshape(16, 5, 128, 128)
    out = np.empty((8, 2, 128, 3, 2, 5, 128), np.float32)
    for half, Ps in ((0, (0, 1, 2)), (1, (2, 14, 15))):
        for ci, P in enumerate(Ps):
            t = tabs[P].reshape(8, 2, 5, 128, 128)
            out[:, half, :, ci] = t.transpose(0, 3, 1, 2, 4)
    return out.reshape(8, 2, 128, 3840)


def run_na(x, ctx, m_lat, m_ctx, g, w_qkv, rpb, w_out, w1, w2):
    NT = 2048
    nc = build_na(NT)
    ident = np.eye(128, dtype=np.float32)
    tabs = [_na_tables(rpb, kq) for kq in range(4)]
    ins = []
    for core in range(NCORES):
        b, kq = divmod(core, 4)
        rm = _na_rowmap(kq)
        xg = x[b].reshape(128, 64, D)
        xh = np.zeros((NA_ROWS_LOCAL, 64, D), np.float32)
        for j, gr in enumerate(rm):
            if gr >= 0:
                xh[j] = xg[gr]
        ins.append({
            "xhT": np.ascontiguousarray(xh.reshape(-1, D).T), "ctxT": np.ascontiguousarray(ctx[b].T),
            "mvec": np.ascontiguousarray(m_lat[b].reshape(6, D).T), "mvec_c": np.ascontiguousarray(m_ctx.reshape(6, D).T),
            "gvec": np.ascontiguousarray(g.T), "w_qkv": w_qkv, "w_out": w_out, "w1": w1, "w2": w2, "btab": tabs[kq], "ident": ident})
    res = run_bass_kernel_spmd(nc, ins, core_ids=list(range(NCORES)))
    out = np.empty_like(x)
    for core in range(NCORES):
        b, kq = divmod(core, 4)
        out[b, kq * NT:(kq + 1) * NT] = res.results[core]["outT"].T
    return out


SSD_IN = 6208


def build_ssd_a(NT=2048, NC_=64):
    nc = bass.Bass("TRN2", target_bir_lowering=False)
    xT = _mk(nc, "xT", [D, NT + 2])
    cT = _mk(nc, "cT", [D, NC_ + 2])
    mvec = _mk(nc, "mvec", [D, 6])
    mvec_c = _mk(nc, "mvec_c", [D, 6])
    gvec = _mk(nc, "gvec", [D, 4])
    hmask = _mk(nc, "hmask", [128, 4])
    w_in = _mk(nc, "w_in", [D, SSD_IN])
    convw = _mk(nc, "convw", [128, 32, 4])
    dtb = _mk(nc, "dtb", [64, 1])
    NTOT = NT + NC_
    zT = _mk(nc, "zT", [2048, NTOT], kind="ExternalOutput")
    xbcT = _mk(nc, "xbcT", [4096, NTOT], kind="ExternalOutput")
    dtT = _mk(nc, "dtT", [64, NTOT], kind="ExternalOutput")
    kx = KX(nc)
    s = kx.s
    ms = kx.prep_mod(mvec, gvec)
    msc = kx.prep_mod(mvec_c, gvec, "mvc")
    hm = s.tile([128, 4], F32, "hm")
    s.dma("sp", hm[:], hmask, writes=[hm])
    cw = s.tile([128, 32, 4], F32, "cw")
    s.dma("sp", cw[:], convw, writes=[cw])
    db_ = s.tile([64, 1], F32, "dtb")
    s.dma("sp", db_[:], dtb, writes=[db_])
    HWD = NT // 2
    xin = [s.tile([128, 8, 342], F32, "xin%d" % i) for i in range(2)]
    for grp in range(2):
        with s.scope():
            blks = []
            c0 = grp * HWD
            srcs = [(xT[:, c0 + i * 342:c0 + (i + 1) * 342], 342, ms) for i in range(3)]
            if grp == 1:
                srcs.append((cT[:, :], NC_ + 2, msc))
            W = sum(w for _, w, _ in srcs)
            for i, (src, w, m) in enumerate(srcs):
                xt = xin[i % 2]
                s.dma("sp", xt[:, :, 0:w], src.rearrange("(c p) n -> p c n", p=128), writes=[xt])
                ht = s.tile([128, 8, w], BF16, "hT%d" % i)
                kx.norm_mod(xt, xt[:, :, 0:w], w, m, 0, ht, ht[:, :, 0:w], None)
                blks.append((ht, w))
            pre = [s.tile([128, W], F32, "pre%d" % i) for i in range(2)]
            acc = [s.tile([128, W], F32, "acc%d" % i) for i in range(2)]
            segs = [(0, HWD, 0 if grp == 0 else None, 1 if grp == 1 else None, c0)]
            if grp == 1:
                segs.append((HWD + 2, NC_, 2, 3, NT))
            nchunks = 49
            def ld_a(jb):
                ncols = 512 if jb < 12 else 64
                return kx.wload(w_in[:, jb * 512:jb * 512 + ncols], 8, ncols)

            def cp_a(jb, wt, wv, blks=blks, pre=pre, acc=acc, segs=segs):
                ncols = 512 if jb < 12 else 64
                for sub in range(ncols // 128 if ncols >= 128 else 1):
                    j = jb * 4 + sub
                    mrows = 128 if j < 48 else 64
                    pr = pre[j % 2]
                    ac = acc[j % 2]
                    col = 0
                    for (ht, w) in blks:
                        ps = kx.nps()
                        for c in range(8):
                            s.op("pe", lambda e, c=c, ps=ps, ht=ht, w=w, sub=sub, wv=wv, mrows=mrows: e.matmul(
                                ps[0:mrows, 0:w], lhsT=wv[:, c, sub * 128:sub * 128 + mrows], rhs=ht[:, c, 0:w], start=(c == 0), stop=(c == 7)),
                                reads=[wt, ht], writes=[ps])
                        if j < 16:
                            s.op("act", lambda e, ps=ps, pr=pr, col=col, w=w: e.activation(out=pr[:, col:col + w], in_=ps[:, 0:w], func=AF.Identity),
                                 reads=[ps], writes=[pr])
                        elif j < 48:
                            s.op("act", lambda e, ps=ps, pr=pr, col=col, w=w: e.activation(out=pr[:, col:col + w], in_=ps[:, 0:w], func=AF.Identity),
                                 reads=[ps], writes=[pr])
                        else:
                            s.op("act", lambda e, ps=ps, pr=pr, col=col, w=w: e.activation(out=pr[0:64, col:col + w], in_=ps[0:64, 0:w], func=AF.Exp, bias=db_[:, 0:1], scale=1.0),
                                 reads=[ps, db_], writes=[pr])
                        col += w
                    for (st, ow, lm, rm, oc) in segs:
                        if j < 16:
                            s.dma("act", zT[j * 128:(j + 1) * 128, oc:oc + ow], pr[:, st + 1:st + 1 + ow], reads=[pr], is_output=True)
                        elif j < 48:
                            jc = j - 16
                            if lm is not None:
                                s.op("dve", lambda e, pr=pr, st=st, lm=lm: e.tensor_scalar(out=pr[:, st:st + 1], in0=pr[:, st:st + 1], scalar1=hm[:, lm:lm + 1], scalar2=None, op0=ALU.mult),
                                     reads=[pr, hm], writes=[pr])
                            if rm is not None:
                                s.op("dve", lambda e, pr=pr, st=st, ow=ow, rm=rm: e.tensor_scalar(out=pr[:, st + ow + 1:st + ow + 2], in0=pr[:, st + ow + 1:st + ow + 2],
                                                                                                scalar1=hm[:, rm:rm + 1], scalar2=None, op0=ALU.mult),
                                     reads=[pr, hm], writes=[pr])
                            s.op("dve", lambda e, pr=pr, ac=ac, st=st, ow=ow, jc=jc: e.tensor_scalar(out=ac[:, st:st + ow], in0=pr[:, st:st + ow], scalar1=cw[:, jc, 0:1], scalar2=None, op0=ALU.mult),
                                 reads=[pr, cw], writes=[ac])
                            for k in (1, 2):
                                s.op("dve", lambda e, pr=pr, ac=ac, st=st, ow=ow, jc=jc, k=k: e.scalar_tensor_tensor(
                                    out=ac[:, st:st + ow], in0=pr[:, st + k:st + k + ow], scalar=cw[:, jc, k:k + 1], in1=ac[:, st:st + ow], op0=ALU.mult, op1=ALU.add),
                                    reads=[pr, cw, ac], writes=[ac])
                            s.op("act", lambda e, ac=ac, st=st, ow=ow, jc=jc: e.activation(out=ac[:, st:st + ow], in_=ac[:, st:st + ow], func=AF.Silu, bias=cw[:, jc, 3:4], scale=1.0),
                                 reads=[ac, cw], writes=[ac])
                            s.dma("act", xbcT[jc * 128:(jc + 1) * 128, oc:oc + ow], ac[:, st:st + ow], reads=[ac], is_output=True)
                        else:
                            s.op("act", lambda e, pr=pr, ac=ac, st=st, ow=ow: e.activation(out=ac[0:64, st:st + ow], in_=pr[0:64, st + 1:st + 1 + ow], func=AF.Ln, bias=1.0, scale=1.0),
                                 reads=[pr], writes=[ac])
                            s.dma("act", dtT[:, oc:oc + ow], ac[0:64, st:st + ow], reads=[ac], is_output=True)
            kx.wstream([(lambda jb=jb: ld_a(jb)) for jb in range(13)], [(lambda wt, wv, jb=jb: cp_a(jb, wt, wv)) for jb in range(13)], L=2)
    s.emit()
    return nc


def run_ssd_a(x, ctx, m_lat, m_ctx, g, w_in, conv_w, conv_b, dt_bias):
    NT, NC_ = 2048, 64
    nc = build_ssd_a(NT, NC_)
    cw = np.concatenate([conv_w.T, conv_b[:, None]], 1).astype(np.float32)
    cw = np.ascontiguousarray(cw.reshape(32, 128, 4).transpose(1, 0, 2))
    dtb = np.ascontiguousarray(dt_bias.reshape(64, 1).astype(np.float32))
    ins = []
    for core in range(NCORES):
        b, k = divmod(core, 4)
        hm = np.ones((128, 4), np.float32)
        if k == 0:
            hm[:, 0] = 0
            hm[:, 2] = 0
        if k == 3:
            hm[:, 1] = 0
            hm[:, 3] = 0
        ins.append({"xT": _halo_T(x[b], k * NT, NT), "cT": _halo_T(ctx[b], k * NC_, NC_),
                    "mvec": np.ascontiguousarray(m_lat[b].reshape(6, D).T), "mvec_c": np.ascontiguousarray(m_ctx.reshape(6, D).T),
                    "gvec": np.ascontiguousarray(g.T), "hmask": hm, "w_in": w_in, "convw": cw, "dtb": dtb})
    res = run_bass_kernel_spmd(nc, ins, core_ids=list(range(NCORES)))
    z = np.empty((BATCH, SEQ + CTX, 2048), np.float32)
    xbc = np.empty((BATCH, SEQ + CTX, 4096), np.float32)
    dt = np.empty((BATCH, SEQ + CTX, 64), np.float32)
    for core in range(NCORES):
        b, k = divmod(core, 4)
        r = res.results[core]
        for arr, name in ((z, "zT"), (xbc, "xbcT"), (dt, "dtT")):
            arr[b, k * NT:(k + 1) * NT] = r[name][:, 0:NT].T
            arr[b, SEQ + k * NC_:SEQ + (k + 1) * NC_] = r[name][:, NT:NT + NC_].T
    return z, xbc, dt


NCH = 66


def build_ssd_b(nch=NCH, dbg=False):
    nc = bass.Bass("TRN2", target_bir_lowering=False)
    X = _mk(nc, "X", [nch, 128, 512])
    DT = _mk(nc, "DT", [nch, 128, 16])
    BCT = _mk(nc, "BCT", [nch, 128, 4, 128])
    BTOK = _mk(nc, "BTOK", [nch, 128, 2, 128])
    alog = _mk(nc, "alog", [128, 16])
    masks = _mk(nc, "masks", [128, 4, 128])
    Y = _mk(nc, "Y", [nch, 128, 512], kind="ExternalOutput")
    s = Sched(nc)
    ps = [s.ptile(name="ps%d" % i) for i in range(8)]
    psi = [0]

    def nps():
        t = ps[psi[0]]
        psi[0] = (psi[0] + 1) % 8
        return t
    mk = s.tile([128, 4, 128], F32, "mk")
    s.dma("sp", mk[:], masks, writes=[mk])
    onesf = s.tile([128, 128], F32, "onesf")
    s.op("dve", lambda e: e.memset(onesf[:], 1.0), writes=[onesf])
    a_bc = s.tile([128, 16], F32, "a_bc")
    s.dma("sp", a_bc[:], alog, writes=[a_bc])
    s.op("act", lambda e: e.activation(out=a_bc[:], in_=a_bc[:], func=AF.Exp), reads=[a_bc], writes=[a_bc])
    s.op("dve", lambda e: e.tensor_scalar(out=a_bc[:], in0=a_bc[:], scalar1=-1.0, scalar2=None, op0=ALU.mult), reads=[a_bc], writes=[a_bc])
    state = s.tile([128, 8, 64], F32, "state")
    state_bf = s.tile([128, 512], BF16, "state_bf")
    NB = 2
    xt = [s.tile([128, 8, 64], F32, "xt%d" % i) for i in range(NB)]
    dtt = [s.tile([128, 16], F32, "dtt%d" % i) for i in range(NB)]
    bct = [s.tile([128, 2, 128], F32, "bct%d" % i) for i in range(NB)]
    btk = [s.tile([128, 128], F32, "btk%d" % i) for i in range(NB)]
    bcb = [s.tile([128, 2, 128], BF16, "bcb%d" % i) for i in range(NB)]
    btb = [s.tile([128, 128], BF16, "btb%d" % i) for i in range(NB)]
    yin = [s.tile([128, 512], F32, "yin%d" % i) for i in range(NB)]
    dtA = [s.tile([128, 8], F32, "dtA%d" % i) for i in range(NB)]
    ct = [s.tile([128, 16], F32, "ct%d" % i) for i in range(NB)]
    ee = [s.tile([128, 3, 8], F32, "ee%d" % i) for i in range(NB)]
    dtw = [s.tile([128, 8], F32, "dtw%d" % i) for i in range(NB)]
    xdt = [s.tile([128, 8, 64], BF16, "xdt%d" % i) for i in range(NB)]
    xw = [s.tile([128, 8, 64], BF16, "xw%d" % i) for i in range(NB)]
    Lall = [s.tile([128, 8, 128], F32, "Lall%d" % i) for i in range(NB)]
    dec = [s.tile([128, 8, 128], F32, "dec%d" % i) for i in range(NB)]
    cbm = [s.tile([128, 128], F32, "cbm%d" % i) for i in range(NB)]
    sc = [s.tile([128, 8, 128], BF16, "sc%d" % i) for i in range(NB)]
    tt = [s.tile([128, 8, 64], F32, "tt%d" % i) for i in range(NB)]
    yo = [s.tile([128, 8, 64], F32, "yo%d" % i) for i in range(NB)]
    t2 = s.tile([128, 8, 64], F32, "t2")
    Yc = [T(Y[c], "Y%d" % c) for c in range(nch)]
    nctx = 2
    it = 0
    def sweep(d, it):
        order = list(range(nch)) if d == 0 else ([1, 0] + list(range(nch - 1, nctx - 1, -1)))
        m_incl = mk[:, d, :]
        m_str = mk[:, 2 + d, :]
        s.op("dve", lambda e: e.memset(state[:], 0.0), writes=[state])
        s.op("dve", lambda e: e.memset(state_bf[:], 0.0), writes=[state_bf])

        def phase_a(c, k):
            x_, dt_, bc_, bk_, bcb_, btb_, yin_ = xt[k], dtt[k], bct[k], btk[k], bcb[k], btb[k], yin[k]
            dA, ct_, ee_, dtw_, xdt_, xw_, L_, dec_, cbm_, sc_ = dtA[k], ct[k], ee[k], dtw[k], xdt[k], xw[k], Lall[k], dec[k], cbm[k], sc[k]
            s.dma("sp", x_[:].rearrange("p a b -> p (a b)"), X[c], writes=[x_])
            s.dma("act", dt_[:], DT[c], writes=[dt_])
            s.dma("sp", bc_[:], BCT[c][:, 2 * d:2 * d + 2, :], writes=[bc_])
            s.dma("act", bk_[:], BTOK[c][:, d, :], writes=[bk_])
            if d == 1:
                s.dma("sp", yin_[:], Y[c], reads=[Yc[c]], writes=[yin_])
            s.op("pool", lambda e: e.tensor_copy(out=bcb_[:], in_=bc_[:]), reads=[bc_], writes=[bcb_])
            s.op("pool", lambda e: e.tensor_copy(out=btb_[:], in_=bk_[:]), reads=[bk_], writes=[btb_])
            dts = dt_[:, d * 8:(d + 1) * 8]
            s.op("dve", lambda e: e.tensor_tensor(out=dA[:], in0=dts, in1=a_bc[:, d * 8:(d + 1) * 8], op=ALU.mult), reads=[dt_, a_bc], writes=[dA])
            psc = ps[0]
            s.op("pe", lambda e: e.matmul(psc[:, 0:8], lhsT=m_incl, rhs=dA[:], start=True, stop=True), reads=[mk, dA], writes=[psc])
            s.op("pe", lambda e: e.matmul(psc[:, 8:16], lhsT=onesf[:], rhs=dA[:], start=True, stop=True), reads=[onesf, dA], writes=[psc])
            s.op("dve", lambda e: e.tensor_copy(out=ct_[:], in_=psc[:, 0:16]), reads=[psc], writes=[ct_])
            s.op("dve", lambda e: e.tensor_tensor(out=ee_[:, 2, :], in0=ct_[:, 8:16], in1=ct_[:, 0:8], op=ALU.subtract), reads=[ct_], writes=[ee_])
            s.op("act", lambda e: e.activation(out=ee_[:, 0:2, :].rearrange("p a b -> p (a b)"), in_=ct_[:, 0:16], func=AF.Exp), reads=[ct_, ee_], writes=[ee_])
            s.op("act", lambda e: e.activation(out=ee_[:, 2, :], in_=ee_[:, 2, :], func=AF.Exp), reads=[ee_], writes=[ee_])
            s.op("dve", lambda e: e.tensor_tensor(out=dtw_[:], in0=dts, in1=ee_[:, 2, :], op=ALU.mult), reads=[dt_, ee_], writes=[dtw_])
            s.op("dve", lambda e: e.tensor_tensor(out=xdt_[:], in0=x_[:], in1=dts.unsqueeze(2).to_broadcast([128, 8, 64]), op=ALU.mult), reads=[x_, dt_], writes=[xdt_])
            s.op("dve", lambda e: e.tensor_tensor(out=xw_[:], in0=x_[:], in1=dtw_[:].unsqueeze(2).to_broadcast([128, 8, 64]), op=ALU.mult), reads=[x_, dtw_], writes=[xw_])
            s.op("pool", lambda e: e.tensor_tensor(out=L_[:], in0=m_str.unsqueeze(1).to_broadcast([128, 8, 128]),
                                                   in1=dA[:].unsqueeze(2).to_broadcast([128, 8, 128]), op=ALU.mult), reads=[mk, dA], writes=[L_])
            pss = [ps[1], ps[2]]
            for e_ in range(8):
                p_ = pss[e_ // 4]
                s.op("pe", lambda e, p_=p_, e_=e_: e.matmul(p_[:, (e_ % 4) * 128:(e_ % 4 + 1) * 128], lhsT=L_[:, e_, :], rhs=m_incl, start=True, stop=True),
                     reads=[L_, mk], writes=[p_])
            for hh in range(2):
                s.op("act", lambda e, hh=hh, p_=pss[hh]: e.activation(out=dec_[:, hh * 4:(hh + 1) * 4, :].rearrange("p a b -> p (a b)"), in_=p_[:, :], func=AF.Exp),
                     reads=[pss[hh]], writes=[dec_])
            pcb = ps[3]
            s.op("pe", lambda e: e.matmul(pcb[:, 0:128], lhsT=bcb_[:, 0, :], rhs=bcb_[:, 1, :], start=True, stop=True), reads=[bcb_], writes=[pcb])
            s.op("dve", lambda e: e.tensor_tensor(out=cbm_[:], in0=pcb[:, 0:128], in1=m_incl, op=ALU.mult), reads=[pcb, mk], writes=[cbm_])
            s.op("dve", lambda e: e.tensor_tensor(out=sc_[:], in0=dec_[:], in1=cbm_[:].unsqueeze(1).to_broadcast([128, 8, 128]), op=ALU.mult),
                 reads=[dec_, cbm_], writes=[sc_])
            psY = ps[4 + k]
            for e_ in range(8):
                s.op("pe", lambda e, e_=e_: e.matmul(psY[:, e_ * 64:(e_ + 1) * 64], lhsT=sc_[:, e_, :], rhs=xdt_[:, e_, :], start=True, stop=True),
                     reads=[sc_, xdt_], writes=[psY])

        def phase_b(c, k):
            bcb_, btb_, yin_, ee_, xw_, tt_, yo_ = bcb[k], btb[k], yin[k], ee[k], xw[k], tt[k], yo[k]
            psY = ps[4 + k]
            psS = ps[6]
            s.op("pe", lambda e: e.matmul(psS[:, 0:512], lhsT=bcb_[:, 1, :], rhs=state_bf[:], start=True, stop=True), reads=[bcb_, state_bf], writes=[psS])
            psU = ps[7]
            s.op("pe", lambda e: e.matmul(psU[:, 0:512], lhsT=btb_[:], rhs=xw_[:].rearrange("p a b -> p (a b)"), start=True, stop=True),
                 reads=[btb_, xw_], writes=[psU])
            s.op("dve", lambda e: e.tensor_tensor(out=t2[:], in0=state[:], in1=ee_[:, 1, :].unsqueeze(2).to_broadcast([128, 8, 64]), op=ALU.mult),
                 reads=[state, ee_], writes=[t2])
            s.op("dve", lambda e: e.tensor_tensor(out=tt_[:], in0=psS[:, 0:512].rearrange("p (a b) -> p a b", a=8),
                                                  in1=ee_[:, 0, :].unsqueeze(2).to_broadcast([128, 8, 64]), op=ALU.mult), reads=[psS, ee_], writes=[tt_])
            s.op("dve", lambda e: e.tensor_tensor(out=state[:], in0=t2[:], in1=psU[:, 0:512].rearrange("p (a b) -> p a b", a=8), op=ALU.add),
                 reads=[t2, psU], writes=[state])
            s.op("act", lambda e: e.activation(out=state_bf[:], in_=state[:].rearrange("p a b -> p (a b)"), func=AF.Identity), reads=[state], writes=[state_bf])
            s.op("dve", lambda e: e.tensor_tensor(out=yo_[:], in0=tt_[:], in1=psY[:, 0:512].rearrange("p (a b) -> p a b", a=8), op=ALU.add),
                 reads=[psY, tt_], writes=[yo_])
            if d == 1:
                s.op("pool", lambda e: e.tensor_tensor(out=yo_[:].rearrange("p a b -> p (a b)"), in0=yo_[:].rearrange("p a b -> p (a b)"), in1=yin_[:], op=ALU.add),
                     reads=[yo_, yin_], writes=[yo_])
            s.dma("sp", Y[c], yo_[:].rearrange("p a b -> p (a b)"), reads=[yo_], writes=[Yc[c]], is_output=True)

        pend = []
        for c in order:
            k = it % NB
            it += 1
            phase_a(c, k)
            pend.append((c, k))
            if len(pend) > 1:
                phase_b(*pend.pop(0))
        while pend:
            phase_b(*pend.pop(0))
        return it
    it = sweep(0, it)
    it = sweep(1, it)
    s.emit()
    return nc


def _ssd_masks():
    k = np.arange(128)
    m = np.zeros((128, 4, 128), np.float32)
    m[:, 0, :] = (k[:, None] <= k[None, :])
    m[:, 1, :] = (k[:, None] >= k[None, :])
    m[:, 2, :] = (k[:, None] > k[None, :])
    m[:, 3, :] = (k[:, None] < k[None, :])
    return m


def run_ssd_b(xbc, dt, a_log):
    nc = build_ssd_b()
    masks = _ssd_masks()
    ins = []
    for core in range(NCORES):
        b, g = divmod(core, 4)
        def chunks(a):
            return np.concatenate([a[SEQ:].reshape(2, 128, -1), a[:SEQ].reshape(64, 128, -1)], 0)
        xg = chunks(xbc[b][:, g * 512:(g + 1) * 512])
        bc = xbc[b][:, 2048:].reshape(-1, 2, 2, 4, 128)[:, :, :, g, :]
        bcc = chunks(bc.reshape(-1, 4 * 128)).reshape(NCH, 128, 4, 128)
        bct = np.ascontiguousarray(bcc.transpose(0, 3, 2, 1))
        btok = np.ascontiguousarray(bcc[:, :, [0, 2], :])
        dtg = dt[b].reshape(-1, 2, 4, 8)[:, :, g, :].reshape(-1, 16)
        al = np.ascontiguousarray(np.broadcast_to(a_log.reshape(2, 4, 8)[:, g, :].reshape(1, 16), (128, 16))).astype(np.float32)
        ins.append({"X": np.ascontiguousarray(xg), "DT": np.ascontiguousarray(chunks(dtg)), "BCT": bct, "BTOK": btok, "alog": al, "masks": masks})
    res = run_bass_kernel_spmd(nc, ins, core_ids=list(range(NCORES)))
    y = np.empty((BATCH, SEQ + CTX, 2048), np.float32)
    for core in range(NCORES):
        b, g = divmod(core, 4)
        Yc = res.results[core]["Y"]
        y[b, SEQ:, g * 512:(g + 1) * 512] = Yc[0:2].reshape(256, 512)
        y[b, :SEQ, g * 512:(g + 1) * 512] = Yc[2:].reshape(SEQ, 512)
    return y


def build_ssd_c(NT=2048, NC_=64):
    nc = bass.Bass("TRN2", target_bir_lowering=False)
    NTOT = NT + NC_
    yT = _mk(nc, "yT", [2048, NTOT])
    xsT = _mk(nc, "xsT", [2048, NTOT])
    zT = _mk(nc, "zT", [2048, NTOT])
    xT = _mk(nc, "xT", [D, NTOT])
    mvec = _mk(nc, "mvec", [D, 6])
    mvec_c = _mk(nc, "mvec_c", [D, 6])
    gvec = _mk(nc, "gvec", [D, 4])
    dcol = _mk(nc, "dcol", [128, 16, 2])
    ngd = _mk(nc, "ng", [128, 16])
    w_out = _mk(nc, "w_out", [2048, D])
    w1 = _mk(nc, "w1", [D, HID])
    w2 = _mk(nc, "w2", [HID, D])
    outT = _mk(nc, "outT", [D, NTOT], kind="ExternalOutput")
    kx = KX(nc, nwbuf=2)
    s = kx.s
    ms = kx.prep_mod(mvec, gvec)
    msc = kx.prep_mod(mvec_c, gvec, "mvc")
    dc_ = s.tile([128, 16, 2], F32, "dcol")
    s.dma("sp", dc_[:], dcol, writes=[dc_])
    dsum = s.tile([128, 16], F32, "dsum")
    s.op("dve", lambda e: e.tensor_tensor(out=dsum[:], in0=dc_[:, :, 0], in1=dc_[:, :, 1], op=ALU.add), reads=[dc_], writes=[dsum])
    ng = s.tile([128, 16], F32, "ng")
    s.dma("sp", ng[:], ngd, writes=[ng])
    HWD = NT // 2
    xres = s.tile([128, 8, HWD + NC_], F32, "xres")
    y_tiles = [s.tile([128, 8, 512], F32, "y0"), s.tile([128, 8, 512], F32, "y1"), s.tile([128, 8, NC_], F32, "y2")]
    for half in range(2):
        c0 = half * HWD
        cols = [(c0, 512, ms, 0), (c0 + 512, 512, ms, 512)]
        s.dma("sp", xres[:, :, 0:HWD], xT[:, c0:c0 + HWD].rearrange("(c p) n -> p c n", p=128), writes=[xres])
        if half == 1:
            cols.append((NT, NC_, msc, HWD))
            s.dma("sp", xres[:, :, HWD:HWD + NC_], xT[:, NT:NT + NC_].rearrange("(c p) n -> p c n", p=128), writes=[xres])
        with s.scope():
            gz = s.tile([128, 16, 512], F32, "gz")
            ynT = [s.tile([128, 16, w], BF16, "ynT%d" % i) for i, (_, w, _, _) in enumerate(cols)]
            ld = [[s.tile([128, 512], F32, "ld%d_%d" % (a, b)) for b in range(2)] for a in range(3)]
            cnt = 0
            for bi, (co, w, m, xo) in enumerate(cols):
                for ch in range(16):
                    ly, lx, lz = ld[0][cnt % 2], ld[1][cnt % 2], ld[2][cnt % 2]
                    cnt += 1
                    s.dma("sp", ly[:, 0:w], yT[ch * 128:(ch + 1) * 128, co:co + w], writes=[ly])
                    s.dma("act", lx[:, 0:w], xsT[ch * 128:(ch + 1) * 128, co:co + w], writes=[lx])
                    s.dma("sp", lz[:, 0:w], zT[ch * 128:(ch + 1) * 128, co:co + w], writes=[lz])
                    s.op("dve", lambda e, ly=ly, lx=lx, ch=ch, w=w: e.scalar_tensor_tensor(out=ly[:, 0:w], in0=lx[:, 0:w], scalar=dsum[:, ch:ch + 1], in1=ly[:, 0:w],
                                                                                      op0=ALU.mult, op1=ALU.add), reads=[lx, ly, dsum], writes=[ly])
                    s.op("act", lambda e, lz=lz, w=w: e.activation(out=lz[:, 0:w], in_=lz[:, 0:w], func=AF.Silu), reads=[lz], writes=[lz])
                    s.op("dve", lambda e, ly=ly, lz=lz, ch=ch, w=w: e.tensor_tensor(out=gz[:, ch, 0:w], in0=ly[:, 0:w], in1=lz[:, 0:w], op=ALU.mult),
                         reads=[ly, lz], writes=[gz])
                r = kx.rstd(gz, gz[:, :, 0:w], w, nchunks=16, dim=2048)
                for ch in range(16):
                    tc = kx.ntc()
                    s.op("dve", lambda e, ch=ch, tc=tc, r=r, w=w: e.tensor_tensor(out=tc[:, 0:w], in0=gz[:, ch, 0:w], in1=r[:, 0:w], op=ALU.mult),
                         reads=[gz, r], writes=[tc])
                    s.op("act", lambda e, ch=ch, tc=tc, w=w, yn=ynT[bi]: e.activation(out=yn[:, ch, :], in_=tc[:, 0:w], func=AF.Identity, scale=ng[:, ch:ch + 1]),
                         reads=[tc, ng], writes=[ynT[bi]])
            def cp_o(dc, wt, wv, cols=cols, ynT=ynT):
                for bi, (co, w, m, xo) in enumerate(cols):
                    ps = kx.nps()
                    for c in range(16):
                        s.op("pe", lambda e, c=c, ps=ps, bi=bi, w=w, wv=wv: e.matmul(ps[:, 0:w], lhsT=wv[:, c, :], rhs=ynT[bi][:, c, :], start=(c == 0), stop=(c == 15)),
                             reads=[wt, ynT[bi]], writes=[ps])
                    s.op("act", lambda e, ps=ps, dc=dc, bi=bi, w=w: e.activation(out=y_tiles[bi][:, dc, 0:w], in_=ps[:, 0:w], func=AF.Identity),
                         reads=[ps], writes=[y_tiles[bi]])
            kx.wstream([(lambda dc=dc: kx.wload(w_out[:, dc * 128:(dc + 1) * 128], 16, 128)) for dc in range(8)],
                       [(lambda wt, wv, dc=dc: cp_o(dc, wt, wv)) for dc in range(8)], L=1)
        blocks = [Blk(w, m, xres, xres[:, :, xo:xo + w], y_tiles[bi]) for bi, (co, w, m, xo) in enumerate(cols)]
        emit_finish(kx, blocks, w1, w2, None)
        for bi, (co, w, m, xo) in enumerate(cols):
            s.dma("act", outT[:, co:co + w].rearrange("(c p) n -> p c n", p=128), xres[:, :, xo:xo + w], reads=[xres], is_output=True)
    s.emit()
    return nc


def run_ssd_c(x, ctx, y, xbc, z, m_lat, m_ctx, g, ssd_d, ssd_norm_g, w_out, w1, w2):
    NT, NC_ = 2048, 64
    nc = build_ssd_c(NT, NC_)
    dcol = np.repeat(ssd_d.reshape(2, 32).T, 64, axis=0).astype(np.float32)
    dcol = np.ascontiguousarray(dcol.reshape(16, 128, 2).transpose(1, 0, 2))
    ng = np.ascontiguousarray(ssd_norm_g.reshape(16, 128).T.astype(np.float32))
    ins = []
    for core in range(NCORES):
        b, k = divmod(core, 4)
        def cat(a_main, a_ctx):
            return np.ascontiguousarray(np.concatenate([a_main[k * NT:(k + 1) * NT], a_ctx[k * NC_:(k + 1) * NC_]], 0).T)
        ins.append({"yT": cat(y[b][:SEQ], y[b][SEQ:]), "xsT": cat(xbc[b][:SEQ, :2048], xbc[b][SEQ:, :2048]), "zT": cat(z[b][:SEQ], z[b][SEQ:]),
                    "xT": cat(x[b], ctx[b]), "mvec": np.ascontiguousarray(m_lat[b].reshape(6, D).T), "mvec_c": np.ascontiguousarray(m_ctx.reshape(6, D).T),
                    "gvec": np.ascontiguousarray(g.T), "dcol": dcol, "ng": ng, "w_out": w_out, "w1": w1, "w2": w2})
    res = run_bass_kernel_spmd(nc, ins, core_ids=list(range(NCORES)))
    xo = np.empty_like(x)
    co = np.empty_like(ctx)
    for core in range(NCORES):
        b, k = divmod(core, 4)
        o = res.results[core]["outT"]
        xo[b, k * NT:(k + 1) * NT] = o[:, :NT].T
        co[b, k * NC_:(k + 1) * NC_] = o[:, NT:].T
    return xo, co


def build_projfin(NT=2048):
    nc = bass.Bass("TRN2", target_bir_lowering=False)
    fT = _mk(nc, "fT", [D, NT])
    xT = _mk(nc, "xT", [D, NT])
    mvec = _mk(nc, "mvec", [D, 6])
    gvec = _mk(nc, "gvec", [D, 4])
    w_out = _mk(nc, "w_out", [D, D])
    w1 = _mk(nc, "w1", [D, HID])
    w2 = _mk(nc, "w2", [HID, D])
    outT = _mk(nc, "outT", [D, NT], kind="ExternalOutput")
    kx = KX(nc)
    s = kx.s
    ms = kx.prep_mod(mvec, gvec)
    HWD, bw, nblk = NT // 2, 512, 2
    xres = s.tile([128, 8, HWD], F32, "xres")
    y_tiles = [s.tile([128, 8, bw], F32, "y%d" % i) for i in range(nblk)]
    for half in range(2):
        c0 = half * HWD
        s.dma("sp", xres[:], xT[:, c0:c0 + HWD].rearrange("(c p) n -> p c n", p=128), writes=[xres])
        with s.scope():
            fb = [s.tile([128, 8, bw], BF16, "fb%d" % i) for i in range(nblk)]
            for tb in range(nblk):
                for c in range(8):
                    kx.stage_cast(fb[tb], fb[tb][:, c, :], fT[c * 128:(c + 1) * 128, c0 + tb * bw:c0 + (tb + 1) * bw], bw)
            for db in range(2):
                wt, wv = kx.wload(w_out[:, db * 512:(db + 1) * 512], 8, 512)
                for sub in range(4):
                    dc = db * 4 + sub
                    for tb in range(nblk):
                        ps = kx.nps()
                        for c in range(8):
                            s.op("pe", lambda e, c=c, ps=ps, tb=tb, sub=sub, wv=wv: e.matmul(
                                ps[:, 0:bw], lhsT=wv[:, c, sub * 128:(sub + 1) * 128], rhs=fb[tb][:, c, :], start=(c == 0), stop=(c == 7)),
                                reads=[wt, fb[tb]], writes=[ps])
                        s.op("act", lambda e, ps=ps, dc=dc, tb=tb: e.activation(out=y_tiles[tb][:, dc, :], in_=ps[:, 0:bw], func=AF.Identity),
                             reads=[ps], writes=[y_tiles[tb]])
        x_views = [xres[:, :, tb * bw:(tb + 1) * bw] for tb in range(nblk)]
        blocks = [Blk(bw, ms, xres, x_views[tb], y_tiles[tb]) for tb in range(nblk)]
        emit_finish(kx, blocks, w1, w2, None)
        for tb in range(nblk):
            s.dma("act", outT[:, c0 + tb * bw:c0 + (tb + 1) * bw].rearrange("(c p) n -> p c n", p=128), x_views[tb], reads=[xres], is_output=True)
    s.emit()
    return nc


def run_projfin(x, f, m_lat, g, w_out, w1, w2):
    NT = 2048
    nc = build_projfin(NT)
    ins = []
    for core in range(NCORES):
        b, k = divmod(core, 4)
        ins.append({"fT": np.ascontiguousarray(f[b, k * NT:(k + 1) * NT].T), "xT": np.ascontiguousarray(x[b, k * NT:(k + 1) * NT].T),
                    "mvec": np.ascontiguousarray(m_lat[b].reshape(6, D).T), "gvec": np.ascontiguousarray(g.T), "w_out": w_out, "w1": w1, "w2": w2})
    res = run_bass_kernel_spmd(nc, ins, core_ids=list(range(NCORES)))
    out = np.empty_like(x)
    for core in range(NCORES):
        b, k = divmod(core, 4)
        out[b, k * NT:(k + 1) * NT] = res.results[core]["outT"].T
    return out


def kernel(x, c, ctx, c_ctx, mod_w, mod_b, norm_g, mlp_w1, mlp_w2, ssd_w_in, ssd_conv_w, ssd_conv_b,
           ssd_dt_bias, ssd_a_log, ssd_d, ssd_norm_g, ssd_w_out, na_w_qkv, na_rpb, na_w_out,
           sc_w_in, sc_conv_w, sc_w_out, fn_w_out):
    f32 = lambda a: np.ascontiguousarray(np.asarray(a), dtype=np.float32)
    x, c, ctx, c_ctx, mod_w, mod_b, norm_g, mlp_w1, mlp_w2 = map(f32, (x, c, ctx, c_ctx, mod_w, mod_b, norm_g, mlp_w1, mlp_w2))
    m_lat, m_ctx = run_mod(c, c_ctx, mod_w, mod_b)
    z, xbc, dt = run_ssd_a(x, ctx, m_lat[0], m_ctx[0], norm_g[0], f32(ssd_w_in)[0], f32(ssd_conv_w)[0], f32(ssd_conv_b)[0], f32(ssd_dt_bias)[0])
    y = run_ssd_b(xbc, dt, f32(ssd_a_log)[0])
    x, ctx = run_ssd_c(x, ctx, y, xbc, z, m_lat[0], m_ctx[0], norm_g[0], f32(ssd_d)[0], f32(ssd_norm_g)[0], f32(ssd_w_out)[0], mlp_w1[0], mlp_w2[0])
    x = run_na(x, ctx, m_lat[1], m_ctx[1], norm_g[1], f32(na_w_qkv)[0], f32(na_rpb)[0], f32(na_w_out)[0], mlp_w1[1], mlp_w2[1])
    x, h3 = run_sc(x, m_lat[2], norm_g[2], f32(sc_w_in)[0], f32(sc_conv_w)[0], f32(sc_w_out)[0], mlp_w1[2], mlp_w2[2], m_lat[3], norm_g[3])
    f = run_fft(h3)
    x = run_projfin(x, f, m_lat[3], norm_g[3], f32(fn_w_out)[0], mlp_w1[3], mlp_w2[3])
    return x.astype(np.float32)
```

```python
import numpy as np
from contextlib import ExitStack
import concourse.bass as bass
import concourse.mybir as mybir
from concourse.bass_utils import run_bass_kernel_spmd

F32 = mybir.dt.float32
BF16 = mybir.dt.bfloat16
AF = mybir.ActivationFunctionType
ALU = mybir.AluOpType
AX = mybir.AxisListType

ENGS = ("pe", "dve", "act", "pool", "sp")
N_DMA_SEMS = 40
NCORES = 8

D = 1024
SEQ = 8192
BATCH = 2
CTX = 256
HID = 4096
EPS = 1e-6
ARENA_WORDS = 52736
CAST_ENGS = ("pool",)


class T:
    __slots__ = ("ap", "w", "r", "name")

    def __init__(self, ap, name=""):
        self.ap = ap
        self.w = None
        self.r = []
        self.name = name

    def __getitem__(self, k):
        return self.ap[k]


class Sched:
    def __init__(self, nc, same_engine_sync=True):
        self.nc = nc
        self.es = ExitStack()
        self.q = {e: [] for e in ENGS}
        self.cnt = {e: 0 for e in ENGS}
        self.prog = {e: self.es.enter_context(nc.semaphore("prog_" + e)) for e in ENGS}
        self.dsem = [self.es.enter_context(nc.semaphore("dma%d" % i)) for i in range(N_DMA_SEMS)]
        self.dval = [0] * N_DMA_SEMS
        self.dnext = 0
        self.known = {e: {} for e in ENGS}
        self.same_engine_sync = same_engine_sync
        self.sem_owner = {id(self.prog[e]): e for e in ENGS}
        self.out_events = []
        self.n_sb = 0
        self.arena = None
        self.aoff = 0
        self.amax = 0
        self.swsem = {}
        self.swused = {}

    def tile(self, shape, dtype, name=None):
        if self.arena is None:
            self.arena = self.es.enter_context(self.nc.sbuf_tensor("arena", [128, ARENA_WORDS], F32))
            self.aoff = 0
        esz = 2 if dtype == BF16 else 4
        n = 1
        for d in shape[1:]:
            n *= d
        words = (n * esz + 3) // 4
        words = (words + 7) // 8 * 8
        if self.aoff + words > ARENA_WORDS:
            raise RuntimeError("SBUF arena overflow: need %d words at %d" % (words, self.aoff))
        ap = self.arena[:, self.aoff:self.aoff + (n * esz + 3) // 4]
        self.aoff += words
        self.amax = max(self.amax, self.aoff)
        if dtype != F32:
            ap = ap.bitcast(dtype)
        ap = ap[0:shape[0], 0:n]
        if len(shape) >= 3:
            names = ["d%d" % i for i in range(len(shape) - 1)]
            kw = {names[i]: shape[1 + i] for i in range(len(shape) - 2)}
            ap = ap.rearrange("p (%s) -> p %s" % (" ".join(names), " ".join(names)), **kw)
        return T(ap, name or "")

    def ptile(self, shape=(128, 512), dtype=F32, name=None):
        self.n_sb += 1
        return T(self.es.enter_context(self.nc.psum_tensor(name or ("ps%d" % self.n_sb), list(shape), dtype)), name or "")

    def _deps(self, eng, reads, writes):
        evs = []
        for t in reads:
            if t.w is not None:
                evs.append(t.w)
        for t in writes:
            if t.w is not None:
                evs.append(t.w)
            evs.extend(t.r)
        need = {}
        for (sem, val) in evs:
            owner = self.sem_owner.get(id(sem))
            if owner == eng and (eng == "pe" or not self.same_engine_sync):
                continue
            k = id(sem)
            if self.known[eng].get(k, 0) >= val:
                continue
            if k not in need or need[k][1] < val:
                need[k] = (sem, val)
        for k, (sem, val) in need.items():
            self.known[eng][k] = val
        return list(need.values())

    def _commit(self, ev, reads, writes):
        for t in reads:
            t.r.append(ev)
            if len(t.r) > 64:
                best = {}
                for (sem, val) in t.r:
                    if id(sem) not in best or best[id(sem)][1] < val:
                        best[id(sem)] = (sem, val)
                t.r = list(best.values())
        for t in writes:
            t.w = ev
            t.r = []

    def op(self, eng, fn, reads=(), writes=()):
        waits = self._deps(eng, reads, writes)
        self.cnt[eng] += 1
        ev = (self.prog[eng], self.cnt[eng])
        self.q[eng].append((waits, fn, (self.prog[eng], 1)))
        self._commit(ev, reads, writes)
        return ev

    def dma(self, eng, out_ap, in_ap, reads=(), writes=(), is_output=False, sub=0, **kw):
        if eng == "pool":
            return self._dma_sw(out_ap, in_ap, reads, writes, is_output, sub, kw)
        i = self.dnext
        self.dnext = (self.dnext + 1) % N_DMA_SEMS
        sem = self.dsem[i]
        waits = self._deps(eng, reads, writes)
        if self.dval[i] > 0 and self.known[eng].get(id(sem), 0) < self.dval[i]:
            waits.append((sem, self.dval[i]))
            self.known[eng][id(sem)] = self.dval[i]
        self.dval[i] += 16
        ev = (sem, self.dval[i])

        def fn(e, out_ap=out_ap, in_ap=in_ap, kw=kw):
            return e.dma_start(out=out_ap, in_=in_ap, **kw)
        self.q[eng].append((waits, fn, (sem, 16)))
        self._commit(ev, reads, writes)
        if is_output:
            self.out_events.append(ev)
        return ev

    def _dma_sw(self, out_ap, in_ap, reads, writes, is_output, sub, kw):
        eng = "pool"
        slot = writes[0]
        key = (id(slot), sub)
        if key not in self.swsem:
            self.swsem[key] = self.es.enter_context(self.nc.semaphore("sw%d" % len(self.swsem)))
            self.swused[key] = False
        sem = self.swsem[key]
        waits = self._deps(eng, reads, writes)
        reuse = self.swused[key]
        if reuse and self.known[eng].get(id(sem), 0) < 16:
            waits.append((sem, 16))
        for e in ENGS:
            self.known[e].pop(id(sem), None)
        self.swused[key] = True
        ev = (sem, 16)

        def fn(e, out_ap=out_ap, in_ap=in_ap, kw=kw, sem=sem, reuse=reuse):
            if reuse:
                e.sem_clear(sem)
            return e.dma_start(out=out_ap, in_=in_ap, **kw)
        self.q[eng].append((waits, fn, (sem, 16)))
        self._commit(ev, reads, writes)
        if is_output:
            self.out_events.append(ev)
        return ev

    def barrier(self):
        waits = []
        for key, sem in self.swsem.items():
            if self.swused[key] and self.known["pool"].get(id(sem), 0) < 16:
                waits.append((sem, 16))
                self.known["pool"][id(sem)] = 16
        assert not waits
        for e in ENGS:
            waits = []
            for f in ENGS:
                if f != e and self.cnt[f] > self.known[e].get(id(self.prog[f]), 0):
                    waits.append((self.prog[f], self.cnt[f]))
                    self.known[e][id(self.prog[f])] = self.cnt[f]
            for i in range(N_DMA_SEMS):
                if self.dval[i] > self.known[e].get(id(self.dsem[i]), 0):
                    waits.append((self.dsem[i], self.dval[i]))
                    self.known[e][id(self.dsem[i])] = self.dval[i]
            if waits:
                self.q[e].append((waits, None, None))

    def scope(self):
        return _Scope(self)

    def emit(self):
        nc = self.nc
        seen = {}
        for (sem, val) in self.out_events:
            if seen.get(id(sem), (None, 0))[1] < val:
                seen[id(sem)] = (sem, val)
        fin = list(seen.values())
        engmap = {"pe": "tensor", "dve": "vector", "act": "scalar", "pool": "gpsimd", "sp": "sync"}
        with nc.Block() as block:
            for e in ENGS:
                q = self.q[e]
                is_sp = (e == "sp")

                def body(eng, q=q, is_sp=is_sp):
                    for waits, fn, inc in q:
                        for (sem, val) in waits:
                            eng.wait_ge(sem, val)
                        if fn is None:
                            continue
                        ins = fn(eng)
                        ins.then_inc(inc[0], inc[1])
                    if is_sp:
                        for (sem, val) in fin:
                            eng.wait_ge(sem, val)
                getattr(block, engmap[e])(body)
        self.es.close()


class _Scope:
    def __init__(self, s):
        self.s = s

    def __enter__(self):
        self.saved = self.s.aoff
        return self

    def __exit__(self, *a):
        self.s.barrier()
        self.s.aoff = self.saved
        return False


class KX:
    def __init__(self, nc, npsum=8, nwbuf=3, wbuf_elems=4096):
        self.nc = nc
        self.s = Sched(nc)
        s = self.s
        self.ones = s.tile([128, 128], BF16, "ones")
        self.eps = s.tile([128, 1], F32, "eps")
        s.op("dve", lambda e: e.memset(self.ones[:], 1.0), writes=[self.ones])
        s.op("dve", lambda e: e.memset(self.eps[:], EPS), writes=[self.eps])
        self.ps = [s.ptile(name="psb%d" % i) for i in range(npsum)]
        self.psi = 0
        self.wb = [s.tile([128, wbuf_elems], BF16, "wbuf%d" % i) for i in range(nwbuf)]
        self.wbi = 0
        self.sqs = [s.tile([128, 512], BF16, "sqbuf%d" % i) for i in range(2)]
        self.stg = [s.tile([128, 2048], F32, "stage%d" % i) for i in range(2)]
        self.stgi = 0
        self.dmaq = ("sp", "act")
        self.dqi = 0
        self.cast_engs = CAST_ENGS
        self.cei = 0
        self.rs = [s.tile([128, 512], F32, "rstd%d" % i) for i in range(2)]
        self.rsi = 0
        self.tc = [s.tile([128, 512], F32, "tmpc%d" % i) for i in range(3)]
        self.tci = 0

    def nps(self):
        t = self.ps[self.psi]
        self.psi = (self.psi + 1) % len(self.ps)
        return t

    def ntc(self):
        t = self.tc[self.tci]
        self.tci = (self.tci + 1) % len(self.tc)
        return t

    def nwb(self):
        t = self.wb[self.wbi]
        self.wbi = (self.wbi + 1) % len(self.wb)
        return t

    def stage_cast(self, dst_T, dst_ap, src_ap, n):
        s = self.s
        st = self.stg[self.stgi]
        self.stgi = (self.stgi + 1) % len(self.stg)
        q = "sp"
        shp = list(src_ap.shape)
        sv = st.ap[:, 0:n]
        if len(shp) == 3:
            sv = sv.rearrange("p (a b) -> p a b", a=shp[1])
        s.dma(q, sv, src_ap, writes=[st])
        ce = self.cast_engs[self.cei]
        self.cei = (self.cei + 1) % len(self.cast_engs)
        if ce == "act":
            s.op("act", lambda e: e.activation(out=dst_ap, in_=sv, func=AF.Identity), reads=[st], writes=[dst_T])
        else:
            s.op(ce, lambda e: e.tensor_copy(out=dst_ap, in_=sv), reads=[st], writes=[dst_T])

    def wload(self, w_ap, kc, ncols):
        t = self.nwb()
        view = t.ap[:, 0:kc * ncols].rearrange("p (c n) -> p c n", c=kc)
        src = w_ap.rearrange("(c p) n -> p c n", p=128)
        per = max(1, 2048 // ncols)
        for c0 in range(0, kc, per):
            c1 = min(kc, c0 + per)
            self.stage_cast(t, view[:, c0:c1, :], src[:, c0:c1, :], (c1 - c0) * ncols)
        return t, view

    def rstd(self, src_T, src_ap, n, nchunks=8, dim=D):
        s = self.s
        ps = self.nps()
        for c in range(nchunks):
            sq = self.sqs[c % 2]
            s.op("act", lambda e, c=c, sq=sq: e.activation(out=sq[:, 0:n], in_=src_ap[:, c, :], func=AF.Square), reads=[src_T], writes=[sq])
            s.op("pe", lambda e, c=c, sq=sq: e.matmul(ps[:, 0:n], lhsT=self.ones[:], rhs=sq[:, 0:n], start=(c == 0), stop=(c == nchunks - 1)),
                 reads=[self.ones, sq], writes=[ps])
        r = self.rs[self.rsi]
        self.rsi = (self.rsi + 1) % len(self.rs)
        s.op("act", lambda e: e.activation(out=r[:, 0:n], in_=ps[:, 0:n], func=AF.Ln, bias=self.eps[:], scale=1.0 / dim),
             reads=[ps, self.eps], writes=[r])
        s.op("act", lambda e: e.activation(out=r[:, 0:n], in_=r[:, 0:n], func=AF.Exp, scale=-0.5), reads=[r], writes=[r])
        return r

    def norm_mod(self, src_T, src_ap, n, ms, which, dst_T, dst_ap, tmp_T):
        s = self.s
        A, S = (ms.A0, ms.S0) if which == 0 else (ms.A2, ms.S2)
        r = self.rstd(src_T, src_ap, n)
        for c in range(8):
            tc = self.ntc()
            s.op("dve", lambda e, c=c, tc=tc: e.tensor_tensor(out=tc[:, 0:n], in0=src_ap[:, c, :], in1=r[:, 0:n], op=ALU.mult),
                 reads=[src_T, r], writes=[tc])
            s.op("act", lambda e, c=c, tc=tc: e.activation(out=dst_ap[:, c, :], in_=tc[:, 0:n], func=AF.Identity,
                                                     bias=S[:, c:c + 1], scale=A[:, c:c + 1]),
                 reads=[tc, ms.mv], writes=[dst_T])

    def resid_add(self, y_T, y_ap, n, ms, which, x_T, x_ap, tmp_T):
        s = self.s
        G = ms.G1 if which == 1 else ms.G2
        r = self.rstd(y_T, y_ap, n)
        for c in range(8):
            tc = self.ntc()
            s.op("dve", lambda e, c=c, tc=tc: e.tensor_tensor(out=tc[:, 0:n], in0=y_ap[:, c, :], in1=r[:, 0:n], op=ALU.mult),
                 reads=[y_T, r], writes=[tc])
            s.op("dve", lambda e, c=c, tc=tc: e.scalar_tensor_tensor(out=x_ap[:, c, :], in0=tc[:, 0:n], scalar=G[:, c:c + 1],
                                                               in1=x_ap[:, c, :], op0=ALU.mult, op1=ALU.add),
                 reads=[tc, x_T, ms.mv], writes=[x_T])

    def prep_mod(self, mvec_ap, gvec_ap, name="mv"):
        s = self.s
        mv = s.tile([128, 8, 16], F32, name)
        s.dma("sp", mv[:, :, 0:6], mvec_ap.rearrange("(c p) n -> p c n", p=128), writes=[mv])
        s.dma("sp", mv[:, :, 6:10], gvec_ap.rearrange("(c p) n -> p c n", p=128), writes=[mv])
        s.op("dve", lambda e: e.scalar_tensor_tensor(out=mv[:, :, 10], in0=mv[:, :, 1], scalar=1.0, in1=mv[:, :, 6], op0=ALU.add, op1=ALU.mult),
             reads=[mv], writes=[mv])
        s.op("dve", lambda e: e.tensor_tensor(out=mv[:, :, 11], in0=mv[:, :, 2], in1=mv[:, :, 7], op=ALU.mult), reads=[mv], writes=[mv])
        s.op("dve", lambda e: e.scalar_tensor_tensor(out=mv[:, :, 12], in0=mv[:, :, 4], scalar=1.0, in1=mv[:, :, 8], op0=ALU.add, op1=ALU.mult),
             reads=[mv], writes=[mv])
        s.op("dve", lambda e: e.tensor_tensor(out=mv[:, :, 13], in0=mv[:, :, 5], in1=mv[:, :, 9], op=ALU.mult), reads=[mv], writes=[mv])
        ms = ModSet()
        ms.mv = mv
        ms.A0, ms.S0, ms.G1, ms.A2, ms.S2, ms.G2 = mv[:, :, 10], mv[:, :, 0], mv[:, :, 11], mv[:, :, 12], mv[:, :, 3], mv[:, :, 13]
        return ms

    def wstream(self, loaders, computes, L=2, pre=None):
        n = len(loaders)
        h = list(pre) if pre else []
        for i in range(n + L):
            if len(h) <= i < n:
                h.append(loaders[i]())
            j = i - L
            if 0 <= j < n:
                computes[j](*h[j])

    def mlp_loaders(self, w1_ap, w2_ap):
        ld = [(lambda jb=jb: self.wload(w1_ap[:, jb * 512:(jb + 1) * 512], 8, 512)) for jb in range(HID // 512)]
        ld += [(lambda db=db: self.wload(w2_ap[:, db * 128:(db + 1) * 128], 32, 128)) for db in range(D // 128)]
        return ld

    def mlp(self, blocks, h2_tiles, w1_ap, w2_ap, hid_T, out_fn, pre=None):
        s = self.s
        offs = [0]
        for b in blocks:
            offs.append(offs[-1] + b.w)

        def c1(jb):
            def f(wt, wv):
                for sub in range(4):
                    hc = jb * 4 + sub
                    for i, b in enumerate(blocks):
                        ps = self.nps()
                        for c in range(8):
                            s.op("pe", lambda e, c=c, ps=ps, i=i, b=b, sub=sub, wv=wv: e.matmul(
                                ps[:, 0:b.w], lhsT=wv[:, c, sub * 128:(sub + 1) * 128], rhs=h2_tiles[i][:, c, 0:b.w], start=(c == 0), stop=(c == 7)),
                                reads=[wt, h2_tiles[i]], writes=[ps])
                        dst = hid_T[:, hc, offs[i]:offs[i + 1]]
                        s.op("act", lambda e, ps=ps, dst=dst, b=b: e.activation(out=dst, in_=ps[:, 0:b.w], func=AF.Relu), reads=[ps], writes=[hid_T])
                        s.op("act", lambda e, dst=dst: e.activation(out=dst, in_=dst, func=AF.Square), reads=[hid_T], writes=[hid_T])
            return f

        def c2(db):
            def f(wt, wv):
                for i, b in enumerate(blocks):
                    ps = self.nps()
                    for c in range(32):
                        s.op("pe", lambda e, c=c, ps=ps, i=i, b=b, wv=wv: e.matmul(
                            ps[:, 0:b.w], lhsT=wv[:, c, :], rhs=hid_T[:, c, offs[i]:offs[i + 1]], start=(c == 0), stop=(c == 31)),
                            reads=[wt, hid_T], writes=[ps])
                    out_fn(i, db, ps)
            return f
        computes = [c1(jb) for jb in range(HID // 512)] + [c2(db) for db in range(D // 128)]
        self.wstream(self.mlp_loaders(w1_ap, w2_ap), computes, L=len(self.wb) - 1, pre=pre)


class ModSet:
    pass


class Blk:
    def __init__(self, w, ms, x_T, x_view, y_T):
        self.w, self.ms, self.x_T, self.x_view, self.y_T = w, ms, x_T, x_view, y_T


def _mk(nc, name, shape, dtype=F32, kind="ExternalInput"):
    return nc.dram_tensor(name, list(shape), dtype, kind=kind).ap()


def emit_finish(kx, blocks, w1_ap, w2_ap, tmp_T):
    s = kx.s
    lds = kx.mlp_loaders(w1_ap, w2_ap)
    pre = [lds[i]() for i in range(len(kx.wb) - 1)]
    for b in blocks:
        kx.resid_add(b.y_T, b.y_T[:, :, 0:b.w], b.w, b.ms, 1, b.x_T, b.x_view, tmp_T)
    with s.scope():
        h2_tiles = [s.tile([128, 8, b.w], BF16, "h2_%d" % i) for i, b in enumerate(blocks)]
        hid_T = s.tile([128, 32, sum(b.w for b in blocks)], BF16, "hid")
        for i, b in enumerate(blocks):
            kx.norm_mod(b.x_T, b.x_view, b.w, b.ms, 2, h2_tiles[i], h2_tiles[i][:, :, 0:b.w], tmp_T)

        def out_fn(i, dc, ps):
            b = blocks[i]
            s.op("act", lambda e: e.activation(out=b.y_T[:, dc, 0:b.w], in_=ps[:, 0:b.w], func=AF.Identity), reads=[ps], writes=[b.y_T])
        kx.mlp(blocks, h2_tiles, w1_ap, w2_ap, hid_T, out_fn, pre=pre)
    for b in blocks:
        kx.resid_add(b.y_T, b.y_T[:, :, 0:b.w], b.w, b.ms, 2, b.x_T, b.x_view, tmp_T)


def build_sc(NT=2048):
    nc = bass.Bass("TRN2", target_bir_lowering=False)
    xT = _mk(nc, "xT", [D, NT + 2])
    mvec = _mk(nc, "mvec", [D, 6])
    gvec = _mk(nc, "gvec", [D, 4])
    hmask = _mk(nc, "hmask", [128, 2])
    w_in = _mk(nc, "w_in", [D, 3 * D])
    convw = _mk(nc, "convw", [D, 3])
    w_out = _mk(nc, "w_out", [D, D])
    w1 = _mk(nc, "w1", [D, HID])
    w2 = _mk(nc, "w2", [HID, D])
    mvec3 = _mk(nc, "mvec3", [D, 6])
    gvec3 = _mk(nc, "gvec3", [D, 4])
    outT = _mk(nc, "outT", [D, NT], kind="ExternalOutput")
    h3T = _mk(nc, "h3T", [D, NT], kind="ExternalOutput")
    kx = KX(nc)
    s = kx.s
    ms = kx.prep_mod(mvec, gvec)
    ms3 = kx.prep_mod(mvec3, gvec3, "mv3")
    HWD = NT // 2
    bw = 512
    nblk = HWD // bw
    hm = s.tile([128, 2], F32, "hm")
    s.dma("sp", hm[:], hmask, writes=[hm])
    cw = s.tile([128, 8, 3], F32, "cw")
    s.dma("sp", cw[:], convw.rearrange("(c p) k -> p c k", p=128), writes=[cw])
    xh = s.tile([128, 8, HWD + 2], F32, "xh")
    tmp_T = None
    y_tiles = [s.tile([128, 8, bw], F32, "y%d" % i) for i in range(nblk)]
    w_in4 = w_in.rearrange("(c p) (t j n) -> p c t j n", p=128, t=3, j=8)

    class XV:
        pass
    for half in range(2):
        c0 = half * HWD
        s.dma("sp", xh[:], xT[:, c0:c0 + HWD + 2].rearrange("(c p) n -> p c n", p=128), writes=[xh])
        with s.scope():
            hT = [s.tile([128, 8, 342], BF16, "hT%d" % i) for i in range(3)]
            bcu = [s.tile([128, 3, HWD + 2], F32, "bcu%d" % i) for i in range(2)]
            acc = [s.tile([128, HWD], F32, "acc%d" % i) for i in range(2)]
            gT = [s.tile([128, 8, bw], BF16, "gT%d" % i) for i in range(nblk)]
            for i in range(3):
                kx.norm_mod(xh, xh[:, :, i * 342:(i + 1) * 342], 342, ms, 0, hT[i], hT[i][:, :, 0:342], tmp_T)
            def ld_in(j):
                wt = kx.nwb()
                wv = wt.ap[:, 0:8 * 3 * 128].rearrange("p (c t n) -> p c t n", c=8, t=3)
                for t in range(3):
                    kx.stage_cast(wt, wv[:, :, t, :], w_in4[:, :, t, j, :], 1024)
                return wt, wv

            def cp_in(j, wt, wv, half=half, hT=hT, bcu=bcu, acc=acc, gT=gT):
                bc = bcu[j % 2]
                ac = acc[j % 2]
                for t in range(3):
                    for i in range(3):
                        ps = kx.nps()
                        for c in range(8):
                            s.op("pe", lambda e, c=c, ps=ps, i=i, t=t, wv=wv: e.matmul(
                                ps[:, 0:342], lhsT=wv[:, c, t, :], rhs=hT[i][:, c, :], start=(c == 0), stop=(c == 7)),
                                reads=[wt, hT[i]], writes=[ps])
                        s.op("act", lambda e, ps=ps, t=t, i=i, bc=bc: e.activation(out=bc[:, t, i * 342:(i + 1) * 342], in_=ps[:, 0:342], func=AF.Identity),
                             reads=[ps], writes=[bc])
                s.op("dve", lambda e, bc=bc: e.tensor_tensor(out=bc[:, 1, :], in0=bc[:, 1, :], in1=bc[:, 2, :], op=ALU.mult), reads=[bc], writes=[bc])
                hc_ = 0 if half == 0 else HWD + 1
                s.op("dve", lambda e, bc=bc, hc_=hc_, half=half: e.tensor_scalar(out=bc[:, 1, hc_:hc_ + 1], in0=bc[:, 1, hc_:hc_ + 1], scalar1=hm[:, half:half + 1],
                                                                    scalar2=None, op0=ALU.mult), reads=[bc, hm], writes=[bc])
                s.op("dve", lambda e, bc=bc, ac=ac, j=j: e.tensor_scalar(out=ac[:, :], in0=bc[:, 1, 0:HWD], scalar1=cw[:, j, 0:1], scalar2=None, op0=ALU.mult),
                     reads=[bc, cw], writes=[ac])
                for k in (1, 2):
                    s.op("dve", lambda e, bc=bc, ac=ac, j=j, k=k: e.scalar_tensor_tensor(out=ac[:, :], in0=bc[:, 1, k:k + HWD], scalar=cw[:, j, k:k + 1],
                                                                                      in1=ac[:, :], op0=ALU.mult, op1=ALU.add),
                         reads=[bc, cw, ac], writes=[ac])
                for tb in range(nblk):
                    s.op("dve", lambda e, bc=bc, ac=ac, j=j, tb=tb: e.tensor_tensor(out=gT[tb][:, j, :], in0=ac[:, tb * bw:(tb + 1) * bw],
                                                                                 in1=bc[:, 0, 1 + tb * bw:1 + (tb + 1) * bw], op=ALU.mult),
                         reads=[ac, bc], writes=[gT[tb]])
            kx.wstream([(lambda j=j: ld_in(j)) for j in range(8)], [(lambda wt, wv, j=j: cp_in(j, wt, wv)) for j in range(8)], L=2)
            for db in range(2):
                wt, wv = kx.wload(w_out[:, db * 512:(db + 1) * 512], 8, 512)
                for sub in range(4):
                    dc = db * 4 + sub
                    for tb in range(nblk):
                        ps = kx.nps()
                        for c in range(8):
                            s.op("pe", lambda e, c=c, ps=ps, tb=tb, sub=sub, wv=wv: e.matmul(
                                ps[:, 0:bw], lhsT=wv[:, c, sub * 128:(sub + 1) * 128], rhs=gT[tb][:, c, :], start=(c == 0), stop=(c == 7)),
                                reads=[wt, gT[tb]], writes=[ps])
                        s.op("act", lambda e, ps=ps, dc=dc, tb=tb: e.activation(out=y_tiles[tb][:, dc, :], in_=ps[:, 0:bw], func=AF.Identity),
                             reads=[ps], writes=[y_tiles[tb]])
        x_views = [xh[:, :, 1 + tb * bw:1 + (tb + 1) * bw] for tb in range(nblk)]
        blocks = [Blk(bw, ms, xh, x_views[tb], y_tiles[tb]) for tb in range(nblk)]
        emit_finish(kx, blocks, w1, w2, tmp_T)
        for tb in range(nblk):
            s.dma("act", outT[:, c0 + tb * bw:c0 + (tb + 1) * bw].rearrange("(c p) n -> p c n", p=128), x_views[tb], reads=[xh], is_output=True)
            kx.norm_mod(xh, x_views[tb], bw, ms3, 0, y_tiles[tb], y_tiles[tb][:, :, 0:bw], None)
            s.dma("act", h3T[:, c0 + tb * bw:c0 + (tb + 1) * bw].rearrange("(c p) n -> p c n", p=128), y_tiles[tb][:, :, 0:bw], reads=[y_tiles[tb]], is_output=True)
    s.emit()
    return nc


def _halo_T(xb, t0, n, lo=1, hi=1):
    L = xb.shape[0]
    out = np.zeros((D, lo + n + hi), np.float32)
    a = max(t0 - lo, 0)
    b = min(t0 + n + hi, L)
    out[:, a - (t0 - lo):b - (t0 - lo)] = xb[a:b].T
    return out


def run_sc(x, m_lat, g, w_in, convw, w_out, w1, w2, m_lat3, g3):
    NT = 2048
    nc = build_sc(NT)
    ins = []
    for core in range(NCORES):
        b, k = divmod(core, 4)
        t0 = k * NT
        hm = np.ones((128, 2), np.float32)
        if k == 0:
            hm[:, 0] = 0
        if k == 3:
            hm[:, 1] = 0
        ins.append({
            "xT": _halo_T(x[b], t0, NT), "mvec": np.ascontiguousarray(m_lat[b].reshape(6, D).T), "gvec": np.ascontiguousarray(g.T),
            "hmask": hm, "w_in": w_in, "convw": np.ascontiguousarray(convw.T), "w_out": w_out, "w1": w1, "w2": w2,
            "mvec3": np.ascontiguousarray(m_lat3[b].reshape(6, D).T), "gvec3": np.ascontiguousarray(g3.T)})
    res = run_bass_kernel_spmd(nc, ins, core_ids=list(range(NCORES)))
    out = np.empty_like(x)
    h3 = np.empty_like(x)
    for core in range(NCORES):
        b, k = divmod(core, 4)
        out[b, k * NT:(k + 1) * NT] = res.results[core]["outT"].T
        h3[b, k * NT:(k + 1) * NT] = res.results[core]["h3T"].T
    return out, h3


def build_mod():
    nc = bass.Bass("TRN2", target_bir_lowering=False)
    ccT = _mk(nc, "ccT", [D, 3])
    w = _mk(nc, "w", [D, 3072])
    bvec = _mk(nc, "bvec", [128, 24])
    outT = _mk(nc, "outT", [3072, 3], kind="ExternalOutput")
    s = Sched(nc)
    cc = s.tile([128, 8, 3], F32, "cc")
    bt = s.tile([128, 24], F32, "bt")
    ot = s.tile([128, 24, 3], F32, "ot")
    ps = [s.ptile(name="ps%d" % i) for i in range(4)]
    wb = [s.tile([128, 8, 512], F32, "wb%d" % i) for i in range(3)]
    s.dma("sp", cc[:], ccT.rearrange("(c p) n -> p c n", p=128), writes=[cc])
    s.dma("sp", bt[:], bvec, writes=[bt])
    s.op("act", lambda e: e.activation(out=cc[:], in_=cc[:], func=AF.Silu), reads=[cc], writes=[cc])
    for jb in range(6):
        wt = wb[jb % 3]
        s.dma("sp" if jb % 2 == 0 else "act", wt[:], w[:, jb * 512:(jb + 1) * 512].rearrange("(c p) n -> p c n", p=128), writes=[wt])
        for sub in range(4):
            oc = jb * 4 + sub
            p = ps[oc % 4]
            for c in range(8):
                s.op("pe", lambda e, c=c, p=p, sub=sub, wt=wt: e.matmul(p[:, 0:3], lhsT=wt[:, c, sub * 128:(sub + 1) * 128], rhs=cc[:, c, :],
                                                                    start=(c == 0), stop=(c == 7)), reads=[wt, cc], writes=[p])
            s.op("act", lambda e, p=p, oc=oc: e.activation(out=ot[:, oc, :], in_=p[:, 0:3], func=AF.Identity, bias=bt[:, oc:oc + 1], scale=1.0),
                 reads=[p, bt], writes=[ot])
    s.dma("sp", outT.rearrange("(c p) n -> p c n", p=128), ot[:], reads=[ot], is_output=True)
    s.emit()
    return nc


def run_mod(c, c_ctx, mod_w, mod_b):
    nc = build_mod()
    ccT = np.ascontiguousarray(np.concatenate([c, c_ctx[None]], 0).T)
    ins = []
    for core in range(NCORES):
        i, hf = divmod(core, 2)
        ins.append({"ccT": ccT, "w": np.ascontiguousarray(mod_w[i][:, hf * 3072:(hf + 1) * 3072]),
                    "bvec": np.ascontiguousarray(mod_b[i][hf * 3072:(hf + 1) * 3072].reshape(24, 128).T)})
    res = run_bass_kernel_spmd(nc, ins, core_ids=list(range(NCORES)))
    m = np.zeros((4, 3, 6144), np.float32)
    for core in range(NCORES):
        i, hf = divmod(core, 2)
        m[i, :, hf * 3072:(hf + 1) * 3072] = res.results[core]["outT"].T
    return m[:, 0:2], m[:, 2]


def _fft_consts():
    c = np.arange(128)
    ang = 2 * np.pi * np.outer(c, c) / 128.0
    fc_cos, fc_sin = np.cos(ang), np.sin(ang)
    f1 = np.concatenate([fc_cos, -fc_sin], 1).astype(np.float32)
    f2 = np.concatenate([fc_sin, fc_cos], 1).astype(np.float32)
    t2 = np.arange(64)[:, None, None]
    k1 = np.arange(128)[None, :, None]
    k2 = np.arange(64)[None, None, :]
    ang3 = 2 * np.pi * (k1 * t2 / 8192.0 + k2 * t2 / 64.0)
    g = np.stack([np.cos(ang3), np.sin(ang3)], 2).astype(np.float32)
    return f1, f2, np.ascontiguousarray(g.reshape(64, 128 * 2 * 64))


def build_fft():
    nc = bass.Bass("TRN2", target_bir_lowering=False)
    hg = _mk(nc, "hg", [2, 128, 8192])
    f1 = _mk(nc, "f1", [128, 256])
    f2 = _mk(nc, "f2", [128, 256])
    g3 = _mk(nc, "g3", [64, 128 * 128])
    fo = _mk(nc, "fo", [2, 64, 128 * 128], kind="ExternalOutput")
    s = Sched(nc)
    f1t = s.tile([128, 256], BF16, "f1t")
    f2t = s.tile([128, 256], BF16, "f2t")
    g3t = s.tile([64, 128, 2, 64], BF16, "g3t")
    stg = [s.tile([128, 2048], F32, "stage%d" % i) for i in range(2)]
    stgi = [0]

    def stage_cast(dst_T, dst_ap, src_ap, np_, n):
        st = stg[stgi[0] % 2]
        s.dma("sp" if stgi[0] % 2 == 0 else "act", st[0:np_, 0:n], src_ap, writes=[st])
        stgi[0] += 1
        s.op("pool", lambda e: e.tensor_copy(out=dst_ap, in_=st[0:np_, 0:n]), reads=[st], writes=[dst_T])
    stage_cast(f1t, f1t[:], f1, 128, 256)
    stage_cast(f2t, f2t[:], f2, 128, 256)
    for q in range(8):
        stage_cast(g3t, g3t[:, q * 16:(q + 1) * 16, :, :].rearrange("p a r k -> p (a r k)"), g3[:, q * 2048:(q + 1) * 2048], 64, 2048)
    ps = [s.ptile(name="ps%d" % i) for i in range(8)]
    psi = [0]

    def nps():
        t = ps[psi[0]]
        psi[0] = (psi[0] + 1) % 8
        return t
    xin = s.tile([128, 64, 128], BF16, "xin")
    B = s.tile([128, 64, 2, 128], BF16, "B")
    A = s.tile([64, 2, 128, 128], BF16, "A")
    ob = [s.tile([64, 16, 128], F32, "ob%d" % i) for i in range(2)]
    for g in range(2):
        for q in range(4):
            stage_cast(xin, xin[:, q * 16:(q + 1) * 16, :].rearrange("p a b -> p (a b)"), hg[g][:, q * 2048:(q + 1) * 2048], 128, 2048)
        for tp in range(32):
            p = nps()
            for u in range(2):
                t2 = tp * 2 + u
                s.op("pe", lambda e, p=p, u=u, t2=t2: e.matmul(p[:, u * 256:(u + 1) * 256], lhsT=xin[:, t2, :], rhs=f1t[:], start=True, stop=True),
                     reads=[xin, f1t], writes=[p])
            eng = "act" if tp % 2 == 0 else "dve"
            dst = B[:, tp * 2:tp * 2 + 2, :, :].rearrange("p a r m -> p (a r m)")
            if eng == "act":
                s.op("act", lambda e, p=p, dst=dst: e.activation(out=dst, in_=p[:], func=AF.Identity), reads=[p], writes=[B])
            else:
                s.op("dve", lambda e, p=p, dst=dst: e.tensor_copy(out=dst, in_=p[:]), reads=[p], writes=[B])
        for mp in range(64):
            p = nps()
            for u in range(2):
                m = mp * 2 + u
                s.op("pe", lambda e, p=p, u=u, m=m: e.matmul(p[0:64, u * 256:(u + 1) * 256], lhsT=B[:, :, 0, m], rhs=f1t[:], start=True, stop=False),
                     reads=[B, f1t], writes=[p])
                s.op("pe", lambda e, p=p, u=u, m=m: e.matmul(p[0:64, u * 256:(u + 1) * 256], lhsT=B[:, :, 1, m], rhs=f2t[:], start=False, stop=True),
                     reads=[B, f2t], writes=[p])
            for u in range(2):
                m = mp * 2 + u
                src = p[0:64, u * 256:(u + 1) * 256].rearrange("p (r k) -> p r k", r=2)
                if u == 0:
                    s.op("act", lambda e, src=src, m=m: e.activation(out=A[:, :, :, m], in_=src, func=AF.Identity), reads=[p], writes=[A])
                else:
                    s.op("dve", lambda e, src=src, m=m: e.tensor_copy(out=A[:, :, :, m], in_=src), reads=[p], writes=[A])
        for kb in range(8):
            o = ob[kb % 2]
            for kq in range(4):
                p = nps()
                for u in range(4):
                    k1 = kb * 16 + kq * 4 + u
                    s.op("pe", lambda e, p=p, u=u, k1=k1: e.matmul(p[0:64, u * 128:(u + 1) * 128], lhsT=g3t[:, k1, 0, :], rhs=A[:, 0, k1, :], start=True, stop=False),
                         reads=[g3t, A], writes=[p])
                    s.op("pe", lambda e, p=p, u=u, k1=k1: e.matmul(p[0:64, u * 128:(u + 1) * 128], lhsT=g3t[:, k1, 1, :], rhs=A[:, 1, k1, :], start=False, stop=True),
                         reads=[g3t, A], writes=[p])
                dst = o[:, kq * 4:(kq + 1) * 4, :].rearrange("p a m -> p (a m)")
                if kq % 2 == 0:
                    s.op("act", lambda e, p=p, dst=dst: e.activation(out=dst, in_=p[0:64, :], func=AF.Identity), reads=[p], writes=[o])
                else:
                    s.op("dve", lambda e, p=p, dst=dst: e.tensor_copy(out=dst, in_=p[0:64, :]), reads=[p], writes=[o])
            s.dma("sp", fo[g][:, kb * 16 * 128:(kb + 1) * 16 * 128], o[:].rearrange("p a m -> p (a m)"), reads=[o], is_output=True)
    s.emit()
    return nc


def run_fft(h):
    nc = build_fft()
    f1, f2, g3 = _fft_consts()
    ins = []
    for core in range(NCORES):
        b, gp = divmod(core, 4)
        hb = np.asarray(h[b][:, gp * 256:(gp + 1) * 256]).reshape(128, 64, 2, 128)
        hgc = np.ascontiguousarray(hb.transpose(2, 3, 1, 0)).reshape(2, 128, 8192)
        ins.append({"hg": hgc.astype(np.float32), "f1": f1, "f2": f2, "g3": g3})
    res = run_bass_kernel_spmd(nc, ins, core_ids=list(range(NCORES)))
    out = np.empty((BATCH, SEQ, D), np.float32)
    for core in range(NCORES):
        b, gp = divmod(core, 4)
        fo = res.results[core]["fo"].reshape(2, 64, 128, 128)
        out[b, :, gp * 256:(gp + 1) * 256] = fo.transpose(1, 2, 0, 3).reshape(8192, 256)
    return out


NA_ROWS_LOCAL = 39


def build_na(NT=2048):
    nc = bass.Bass("TRN2", target_bir_lowering=False)
    xhT = _mk(nc, "xhT", [D, NA_ROWS_LOCAL * 64])
    ctxT = _mk(nc, "ctxT", [D, CTX])
    mvec = _mk(nc, "mvec", [D, 6])
    mvec_c = _mk(nc, "mvec_c", [D, 6])
    gvec = _mk(nc, "gvec", [D, 4])
    w_qkv = _mk(nc, "w_qkv", [D, 3 * D])
    w_out = _mk(nc, "w_out", [D, D])
    w1 = _mk(nc, "w1", [D, HID])
    w2 = _mk(nc, "w2", [HID, D])
    btab = _mk(nc, "btab", [8, 2, 128, 3840])
    identd = _mk(nc, "ident", [128, 128])
    outT = _mk(nc, "outT", [D, NT], kind="ExternalOutput")
    kx = KX(nc)
    s = kx.s
    ms = kx.prep_mod(mvec, gvec)
    msc = kx.prep_mod(mvec_c, gvec, "mvc")
    ident = s.tile([128, 128], BF16, "ident")
    kx.stage_cast(ident, ident[:], identd, 128 * 128 // 128)
    HWD, bw, nblk = 1024, 512, 2
    WIN = 23 * 64
    xres = s.tile([128, 8, HWD], F32, "xres")
    y_tiles = [s.tile([128, 8, bw], F32, "y%d" % i) for i in range(nblk)]
    hcT = s.tile([128, 8, CTX], BF16, "hcT")
    s.dma("sp", y_tiles[0][:, :, 0:CTX], ctxT.rearrange("(c p) n -> p c n", p=128), writes=[y_tiles[0]])
    kx.norm_mod(y_tiles[0], y_tiles[0][:, :, 0:CTX], CTX, msc, 0, hcT, hcT[:, :, :], None)
    w_qkv4 = w_qkv.rearrange("(c p) (t j n) -> p c t j n", p=128, t=3, j=8)
    for half in range(2):
        wc0 = half * HWD
        s.dma("sp", xres[:], xhT[:, 256 + wc0:256 + wc0 + HWD].rearrange("(c p) n -> p c n", p=128), writes=[xres])
        with s.scope():
            hT = s.tile([128, 8, WIN], BF16, "hT")
            attT = [s.tile([128, 8, bw], BF16, "attT%d" % i) for i in range(nblk)]
            qT = s.tile([128, HWD], BF16, "qT")
            kT = s.tile([128, WIN], BF16, "kT")
            kcT = s.tile([128, CTX], BF16, "kcT")
            vt = s.tile([128, 12, 2, 65], BF16, "vt")
            vct = s.tile([128, 2, 2, 65], BF16, "vct")
            att_hp = s.tile([128, 8, 128], BF16, "att_hp")
            bt = s.tile([128, 3, 2, 5, 128], F32, "bt")
            stA = [s.tile([128, 512], F32, "stA%d" % i) for i in range(2)]
            stB = [s.tile([128, 128], F32, "stB%d" % i) for i in range(2)]
            pA = [s.tile([128, 512], BF16, "pA%d" % i) for i in range(2)]
            pB = [s.tile([128, 128], BF16, "pB%d" % i) for i in range(2)]
            pC = [s.tile([128, 256], BF16, "pC%d" % i) for i in range(2)]
            rc = [s.tile([128, 1], F32, "rc%d" % i) for i in range(2)]
            s.op("pool", lambda e: e.memset(vt[:, :, :, 64:65], 1.0), writes=[vt])
            s.op("pool", lambda e: e.memset(vct[:, :, :, 64:65], 1.0), writes=[vct])
            for j, (a, w) in enumerate(((0, 512), (512, 512), (1024, WIN - 1024))):
                yt = y_tiles[j % 2]
                s.dma("sp", yt[:, :, 0:w], xhT[:, wc0 + a:wc0 + a + w].rearrange("(c p) n -> p c n", p=128), writes=[yt])
                kx.norm_mod(yt, yt[:, :, 0:w], w, ms, 0, hT, hT[:, :, a:a + w], None)
            itc = [0]

            def ld_hp(hp):
                wt = kx.nwb()
                wv = wt.ap[:, 0:8 * 3 * 128].rearrange("p (c t n) -> p c t n", c=8, t=3)
                for t in range(3):
                    kx.stage_cast(wt, wv[:, :, t, :], w_qkv4[:, :, t, hp, :], 1024)
                return wt, wv

            def cp_hp(hp, wt, wv, half=half, hT=hT, attT=attT, qT=qT, kT=kT, kcT=kcT, vt=vt, vct=vct, att_hp=att_hp, bt=bt,
                      stA=stA, stB=stB, pA=pA, pB=pB, pC=pC, rc=rc):
                it = itc[0]
                s.dma("act", bt[:].rearrange("p a b c q -> p (a b c q)"), btab[hp, half], writes=[bt])
                for blk in range(2):
                    ps = kx.nps()
                    for c in range(8):
                        s.op("pe", lambda e, c=c, ps=ps, blk=blk, wv=wv: e.matmul(
                            ps[:, 0:512], lhsT=wv[:, c, 0, :], rhs=hT[:, c, 256 + blk * 512:256 + (blk + 1) * 512], start=(c == 0), stop=(c == 7)),
                            reads=[wt, hT], writes=[ps])
                    s.op("act", lambda e, ps=ps, blk=blk: e.activation(out=qT[:, blk * 512:(blk + 1) * 512], in_=ps[:, 0:512], func=AF.Identity),
                         reads=[ps], writes=[qT])
                for (a, w) in ((0, 512), (512, 512), (1024, WIN - 1024)):
                    ps = kx.nps()
                    for c in range(8):
                        s.op("pe", lambda e, c=c, ps=ps, a=a, w=w, wv=wv: e.matmul(
                            ps[:, 0:w], lhsT=wv[:, c, 1, :], rhs=hT[:, c, a:a + w], start=(c == 0), stop=(c == 7)),
                            reads=[wt, hT], writes=[ps])
                    s.op("dve", lambda e, ps=ps, a=a, w=w: e.tensor_copy(out=kT[:, a:a + w], in_=ps[:, 0:w]), reads=[ps], writes=[kT])
                ps = kx.nps()
                for c in range(8):
                    s.op("pe", lambda e, c=c, ps=ps, wv=wv: e.matmul(ps[:, 0:CTX], lhsT=wv[:, c, 1, :], rhs=hcT[:, c, :], start=(c == 0), stop=(c == 7)),
                         reads=[wt, hcT], writes=[ps])
                s.op("act", lambda e, ps=ps: e.activation(out=kcT[:, :], in_=ps[:, 0:CTX], func=AF.Identity), reads=[ps], writes=[kcT])
                for tg in range(3):
                    ps = kx.nps()
                    for u in range(4):
                        tcn = tg * 4 + u
                        ntok = 64 if tcn == 11 else 128
                        for c in range(8):
                            s.op("pe", lambda e, c=c, ps=ps, u=u, tcn=tcn, ntok=ntok, wv=wv: e.matmul(
                                ps[0:ntok, u * 128:(u + 1) * 128], lhsT=hT[:, c, tcn * 128:tcn * 128 + ntok], rhs=wv[:, c, 2, :], start=(c == 0), stop=(c == 7)),
                                reads=[wt, hT], writes=[ps])
                    nfull = 4 if tg < 2 else 3
                    s.op("act", lambda e, ps=ps, tg=tg, nfull=nfull: e.activation(
                        out=vt[:, tg * 4:tg * 4 + nfull, :, 0:64], in_=ps[:, 0:nfull * 128].rearrange("p (a b d) -> p a b d", a=nfull, b=2), func=AF.Identity),
                        reads=[ps], writes=[vt])
                    if tg == 2:
                        s.op("act", lambda e, ps=ps: e.activation(out=vt[0:64, 11, :, 0:64], in_=ps[0:64, 384:512].rearrange("p (b d) -> p b d", b=2), func=AF.Identity),
                             reads=[ps], writes=[vt])
                ps = kx.nps()
                for u in range(2):
                    for c in range(8):
                        s.op("pe", lambda e, c=c, ps=ps, u=u, wv=wv: e.matmul(
                            ps[:, u * 128:(u + 1) * 128], lhsT=hcT[:, c, u * 128:(u + 1) * 128], rhs=wv[:, c, 2, :], start=(c == 0), stop=(c == 7)),
                            reads=[wt, hcT], writes=[ps])
                s.op("dve", lambda e, ps=ps: e.tensor_copy(out=vct[:, :, :, 0:64], in_=ps[:, 0:256].rearrange("p (a b d) -> p a b d", a=2, b=2)),
                     reads=[ps], writes=[vct])
                units = [(h2, i) for h2 in range(2) for i in range(8)]
                pend = []

                def front(h2, i, it):
                    pb = 64 * h2
                    if True:
                        cls = (0 if i == 0 else 1 if i == 1 else 2) if half == 0 else (1 if i == 6 else 2 if i == 7 else 0)
                        a_, b_, c_ = stA[it % 2], stB[it % 2], rc[it % 2]
                        pa, pb_, pc = pA[it % 2], pB[it % 2], pC[it % 2]
                        psA = kx.nps()
                        for ch in range(4):
                            s.op("pe", lambda e, psA=psA, ch=ch, i=i, pb=pb: e.matmul(
                                psA[:, ch * 128:(ch + 1) * 128], lhsT=kT[pb:pb + 64, 128 * (i + ch):128 * (i + ch + 1)], rhs=qT[pb:pb + 64, 128 * i:128 * (i + 1)],
                                start=True, stop=True), reads=[kT, qT], writes=[psA])
                        psB = kx.nps()
                        s.op("pe", lambda e, psB=psB, i=i, pb=pb: e.matmul(
                            psB[0:64, 0:128], lhsT=kT[pb:pb + 64, 128 * (i + 4):128 * (i + 4) + 64], rhs=qT[pb:pb + 64, 128 * i:128 * (i + 1)],
                            start=True, stop=True), reads=[kT, qT], writes=[psB])
                        for cc in range(2):
                            s.op("pe", lambda e, psB=psB, cc=cc, i=i, pb=pb: e.matmul(
                                psB[:, 128 + cc * 128:256 + cc * 128], lhsT=kcT[pb:pb + 64, cc * 128:(cc + 1) * 128], rhs=qT[pb:pb + 64, 128 * i:128 * (i + 1)],
                                start=True, stop=True), reads=[kcT, qT], writes=[psB])
                        s.op("dve", lambda e, psA=psA, a_=a_, cls=cls, h2=h2: e.scalar_tensor_tensor(
                            out=a_[:, :], in0=psA[:, :], scalar=0.125, in1=bt[:, cls, h2, 0:4, :].rearrange("p a q -> p (a q)"), op0=ALU.mult, op1=ALU.add),
                            reads=[psA, bt], writes=[a_])
                        s.op("act", lambda e, a_=a_, pa=pa: e.activation(out=pa[:, :], in_=a_[:, :], func=AF.Exp), reads=[a_], writes=[pa])
                        s.op("dve", lambda e, psB=psB, b_=b_, cls=cls, h2=h2: e.scalar_tensor_tensor(
                            out=b_[0:64, :], in0=psB[0:64, 0:128], scalar=0.125, in1=bt[0:64, cls, h2, 4, :], op0=ALU.mult, op1=ALU.add),
                            reads=[psB, bt], writes=[b_])
                        s.op("act", lambda e, b_=b_, pb_=pb_: e.activation(out=pb_[0:64, :], in_=b_[0:64, :], func=AF.Exp), reads=[b_], writes=[pb_])
                        s.op("act", lambda e, psB=psB, pc=pc: e.activation(out=pc[:, :], in_=psB[:, 128:384], func=AF.Exp, scale=0.125), reads=[psB], writes=[pc])
                    return (h2, i, pa, pb_, pc, c_)

                def back(h2, i, pa, pb_, pc, c_):
                    if True:
                        psO = kx.nps()
                        for ch in range(4):
                            s.op("pe", lambda e, psO=psO, ch=ch, i=i, h2=h2, pa=pa: e.matmul(
                                psO[:, 0:65], lhsT=pa[:, ch * 128:(ch + 1) * 128], rhs=vt[:, i + ch, h2, :], start=(ch == 0), stop=False),
                                reads=[pa, vt], writes=[psO])
                        s.op("pe", lambda e, psO=psO, i=i, h2=h2, pb_=pb_: e.matmul(
                            psO[:, 0:65], lhsT=pb_[0:64, :], rhs=vt[0:64, i + 4, h2, :], start=False, stop=False), reads=[pb_, vt], writes=[psO])
                        for cc in range(2):
                            s.op("pe", lambda e, psO=psO, cc=cc, h2=h2, pc=pc: e.matmul(
                                psO[:, 0:65], lhsT=pc[:, cc * 128:(cc + 1) * 128], rhs=vct[:, cc, h2, :], start=False, stop=(cc == 1)),
                                reads=[pc, vct], writes=[psO])
                        s.op("dve", lambda e, psO=psO, c_=c_: e.reciprocal(out=c_[:, :], in_=psO[:, 64:65]), reads=[psO], writes=[c_])
                        s.op("dve", lambda e, psO=psO, c_=c_, i=i, h2=h2: e.tensor_scalar(
                            out=att_hp[:, i, h2 * 64:(h2 + 1) * 64], in0=psO[:, 0:64], scalar1=c_[:, 0:1], scalar2=None, op0=ALU.mult),
                            reads=[psO, c_], writes=[att_hp])
                for (h2, i) in units:
                    pend.append(front(h2, i, it))
                    it += 1
                    if len(pend) > 1:
                        back(*pend.pop(0))
                while pend:
                    back(*pend.pop(0))
                for ig in range(2):
                    pst = kx.nps()
                    pv = pst.ap.bitcast(BF16)
                    for u in range(4):
                        i = ig * 4 + u
                        s.op("pe", lambda e, pv=pv, u=u, i=i: e.transpose(out=pv[:, u * 128:(u + 1) * 128], in_=att_hp[:, i, :], identity=ident[:]),
                             reads=[att_hp, ident], writes=[pst])
                    s.op("dve", lambda e, pv=pv, ig=ig, hp=hp: e.tensor_copy(out=attT[ig][:, hp, :], in_=pv[:, 0:512]), reads=[pst], writes=[attT[ig]])
                itc[0] = it
            kx.wstream([(lambda hp=hp: ld_hp(hp)) for hp in range(8)], [(lambda wt, wv, hp=hp: cp_hp(hp, wt, wv)) for hp in range(8)], L=2)
            for db in range(2):
                wt, wv = kx.wload(w_out[:, db * 512:(db + 1) * 512], 8, 512)
                for sub in range(4):
                    dc = db * 4 + sub
                    for tb in range(nblk):
                        ps = kx.nps()
                        for c in range(8):
                            s.op("pe", lambda e, c=c, ps=ps, tb=tb, sub=sub, wv=wv: e.matmul(
                                ps[:, 0:bw], lhsT=wv[:, c, sub * 128:(sub + 1) * 128], rhs=attT[tb][:, c, :], start=(c == 0), stop=(c == 7)),
                                reads=[wt, attT[tb]], writes=[ps])
                        s.op("act", lambda e, ps=ps, dc=dc, tb=tb: e.activation(out=y_tiles[tb][:, dc, :], in_=ps[:, 0:bw], func=AF.Identity),
                             reads=[ps], writes=[y_tiles[tb]])
        x_views = [xres[:, :, tb * bw:(tb + 1) * bw] for tb in range(nblk)]
        blocks = [Blk(bw, ms, xres, x_views[tb], y_tiles[tb]) for tb in range(nblk)]
        emit_finish(kx, blocks, w1, w2, None)
        for tb in range(nblk):
            s.dma("act", outT[:, half * HWD + tb * bw:half * HWD + (tb + 1) * bw].rearrange("(c p) n -> p c n", p=128), x_views[tb], reads=[xres], is_output=True)
    s.emit()
    return nc


def _na_rowmap(kq):
    if kq == 0:
        return [5, 6, 7, -1] + list(range(0, 35))
    if kq == 3:
        return [92 + j for j in range(36)] + [120, 121, -1]
    return [32 * kq - 4 + j for j in range(NA_ROWS_LOCAL)]


def _na_tables(rpb, kq):
    NEG = -30000.0
    rm = _na_rowmap(kq)
    tabs = {}
    qc = np.arange(64)
    kc = np.arange(64)
    c0 = np.clip(qc - 8, 0, 48)
    colok = (kc[:, None] >= c0[None, :]) & (kc[:, None] < c0[None, :] + 16)
    dc = kc[:, None] - qc[None, :] + 15
    dcc = np.clip(dc, 0, 30)
    for P in (0, 1, 2, 14, 15):
        tab = np.full((16, 640, 128), NEG, np.float32)
        seen = set()
        for j in range(9):
            g = rm[2 * P + j]
            if g < 0 or g in seen:
                continue
            seen.add(g)
            for u in range(2):
                qr = 32 * kq + 2 * P + u
                r0 = min(max(qr - 4, 0), 120)
                if not (r0 <= g < r0 + 8):
                    continue
                dr = g - qr + 7
                vals = rpb[:, dr, :][:, dcc]
                blk = np.where(colok[None], vals, NEG)
                tab[:, j * 64:(j + 1) * 64, u * 64:(u + 1) * 64] = blk
        tabs[P] = tab.reshape(16, 5, 128, 128)
    out = np.empty((8, 2, 128, 3, 2, 5, 128), np.float32)
    for half, Ps in ((0, (0, 1, 2)), (1, (2, 14, 15))):
        for ci, P in enumerate(Ps):
            t = tabs[P].reshape(8, 2, 5, 128, 128)
            out[:, half, :, ci] = t.transpose(0, 3, 1, 2, 4)
    return out.reshape(8, 2, 128, 3840)


def run_na(x, ctx, m_lat, m_ctx, g, w_qkv, rpb, w_out, w1, w2):
    NT = 2048
    nc = build_na(NT)
    ident = np.eye(128, dtype=np.float32)
    tabs = [_na_tables(rpb, kq) for kq in range(4)]
    ins = []
    for core in range(NCORES):
        b, kq = divmod(core, 4)
        rm = _na_rowmap(kq)
        xg = x[b].reshape(128, 64, D)
        xh = np.zeros((NA_ROWS_LOCAL, 64, D), np.float32)
        for j, gr in enumerate(rm):
            if gr >= 0:
                xh[j] = xg[gr]
        ins.append({
            "xhT": np.ascontiguousarray(xh.reshape(-1, D).T), "ctxT": np.ascontiguousarray(ctx[b].T),
            "mvec": np.ascontiguousarray(m_lat[b].reshape(6, D).T), "mvec_c": np.ascontiguousarray(m_ctx.reshape(6, D).T),
            "gvec": np.ascontiguousarray(g.T), "w_qkv": w_qkv, "w_out": w_out, "w1": w1, "w2": w2, "btab": tabs[kq], "ident": ident})
    res = run_bass_kernel_spmd(nc, ins, core_ids=list(range(NCORES)))
    out = np.empty_like(x)
    for core in range(NCORES):
        b, kq = divmod(core, 4)
        out[b, kq * NT:(kq + 1) * NT] = res.results[core]["outT"].T
    return out


SSD_IN = 6208


def build_ssd_a(NT=2048, NC_=64):
    nc = bass.Bass("TRN2", target_bir_lowering=False)
    xT = _mk(nc, "xT", [D, NT + 2])
    cT = _mk(nc, "cT", [D, NC_ + 2])
    mvec = _mk(nc, "mvec", [D, 6])
    mvec_c = _mk(nc, "mvec_c", [D, 6])
    gvec = _mk(nc, "gvec", [D, 4])
    hmask = _mk(nc, "hmask", [128, 4])
    w_in = _mk(nc, "w_in", [D, SSD_IN])
    convw = _mk(nc, "convw", [128, 32, 4])
    dtb = _mk(nc, "dtb", [64, 1])
    NTOT = NT + NC_
    zT = _mk(nc, "zT", [2048, NTOT], kind="ExternalOutput")
    xbcT = _mk(nc, "xbcT", [4096, NTOT], kind="ExternalOutput")
    dtT = _mk(nc, "dtT", [64, NTOT], kind="ExternalOutput")
    kx = KX(nc)
    s = kx.s
    ms = kx.prep_mod(mvec, gvec)
    msc = kx.prep_mod(mvec_c, gvec, "mvc")
    hm = s.tile([128, 4], F32, "hm")
    s.dma("sp", hm[:], hmask, writes=[hm])
    cw = s.tile([128, 32, 4], F32, "cw")
    s.dma("sp", cw[:], convw, writes=[cw])
    db_ = s.tile([64, 1], F32, "dtb")
    s.dma("sp", db_[:], dtb, writes=[db_])
    HWD = NT // 2
    xin = [s.tile([128, 8, 342], F32, "xin%d" % i) for i in range(2)]
    for grp in range(2):
        with s.scope():
            blks = []
            c0 = grp * HWD
            srcs = [(xT[:, c0 + i * 342:c0 + (i + 1) * 342], 342, ms) for i in range(3)]
            if grp == 1:
                srcs.append((cT[:, :], NC_ + 2, msc))
            W = sum(w for _, w, _ in srcs)
            for i, (src, w, m) in enumerate(srcs):
                xt = xin[i % 2]
                s.dma("sp", xt[:, :, 0:w], src.rearrange("(c p) n -> p c n", p=128), writes=[xt])
                ht = s.tile([128, 8, w], BF16, "hT%d" % i)
                kx.norm_mod(xt, xt[:, :, 0:w], w, m, 0, ht, ht[:, :, 0:w], None)
                blks.append((ht, w))
            pre = [s.tile([128, W], F32, "pre%d" % i) for i in range(2)]
            acc = [s.tile([128, W], F32, "acc%d" % i) for i in range(2)]
            segs = [(0, HWD, 0 if grp == 0 else None, 1 if grp == 1 else None, c0)]
            if grp == 1:
                segs.append((HWD + 2, NC_, 2, 3, NT))
            nchunks = 49
            def ld_a(jb):
                ncols = 512 if jb < 12 else 64
                return kx.wload(w_in[:, jb * 512:jb * 512 + ncols], 8, ncols)

            def cp_a(jb, wt, wv, blks=blks, pre=pre, acc=acc, segs=segs):
                ncols = 512 if jb < 12 else 64
                for sub in range(ncols // 128 if ncols >= 128 else 1):
                    j = jb * 4 + sub
                    mrows = 128 if j < 48 else 64
                    pr = pre[j % 2]
                    ac = acc[j % 2]
                    col = 0
                    for (ht, w) in blks:
                        ps = kx.nps()
                        for c in range(8):
                            s.op("pe", lambda e, c=c, ps=ps, ht=ht, w=w, sub=sub, wv=wv, mrows=mrows: e.matmul(
                                ps[0:mrows, 0:w], lhsT=wv[:, c, sub * 128:sub * 128 + mrows], rhs=ht[:, c, 0:w], start=(c == 0), stop=(c == 7)),
                                reads=[wt, ht], writes=[ps])
                        if j < 16:
                            s.op("act", lambda e, ps=ps, pr=pr, col=col, w=w: e.activation(out=pr[:, col:col + w], in_=ps[:, 0:w], func=AF.Identity),
                                 reads=[ps], writes=[pr])
                        elif j < 48:
                            s.op("act", lambda e, ps=ps, pr=pr, col=col, w=w: e.activation(out=pr[:, col:col + w], in_=ps[:, 0:w], func=AF.Identity),
                                 reads=[ps], writes=[pr])
                        else:
                            s.op("act", lambda e, ps=ps, pr=pr, col=col, w=w: e.activation(out=pr[0:64, col:col + w], in_=ps[0:64, 0:w], func=AF.Exp, bias=db_[:, 0:1], scale=1.0),
                                 reads=[ps, db_], writes=[pr])
                        col += w
                    for (st, ow, lm, rm, oc) in segs:
                        if j < 16:
                            s.dma("act", zT[j * 128:(j + 1) * 128, oc:oc + ow], pr[:, st + 1:st + 1 + ow], reads=[pr], is_output=True)
                        elif j < 48:
                            jc = j - 16
                            if lm is not None:
                                s.op("dve", lambda e, pr=pr, st=st, lm=lm: e.tensor_scalar(out=pr[:, st:st + 1], in0=pr[:, st:st + 1], scalar1=hm[:, lm:lm + 1], scalar2=None, op0=ALU.mult),
                                     reads=[pr, hm], writes=[pr])
                            if rm is not None:
                                s.op("dve", lambda e, pr=pr, st=st, ow=ow, rm=rm: e.tensor_scalar(out=pr[:, st + ow + 1:st + ow + 2], in0=pr[:, st + ow + 1:st + ow + 2],
                                                                                                scalar1=hm[:, rm:rm + 1], scalar2=None, op0=ALU.mult),
                                     reads=[pr, hm], writes=[pr])
                            s.op("dve", lambda e, pr=pr, ac=ac, st=st, ow=ow, jc=jc: e.tensor_scalar(out=ac[:, st:st + ow], in0=pr[:, st:st + ow], scalar1=cw[:, jc, 0:1], scalar2=None, op0=ALU.mult),
                                 reads=[pr, cw], writes=[ac])
                            for k in (1, 2):
                                s.op("dve", lambda e, pr=pr, ac=ac, st=st, ow=ow, jc=jc, k=k: e.scalar_tensor_tensor(
                                    out=ac[:, st:st + ow], in0=pr[:, st + k:st + k + ow], scalar=cw[:, jc, k:k + 1], in1=ac[:, st:st + ow], op0=ALU.mult, op1=ALU.add),
                                    reads=[pr, cw, ac], writes=[ac])
                            s.op("act", lambda e, ac=ac, st=st, ow=ow, jc=jc: e.activation(out=ac[:, st:st + ow], in_=ac[:, st:st + ow], func=AF.Silu, bias=cw[:, jc, 3:4], scale=1.0),
                                 reads=[ac, cw], writes=[ac])
                            s.dma("act", xbcT[jc * 128:(jc + 1) * 128, oc:oc + ow], ac[:, st:st + ow], reads=[ac], is_output=True)
                        else:
                            s.op("act", lambda e, pr=pr, ac=ac, st=st, ow=ow: e.activation(out=ac[0:64, st:st + ow], in_=pr[0:64, st + 1:st + 1 + ow], func=AF.Ln, bias=1.0, scale=1.0),
                                 reads=[pr], writes=[ac])
                            s.dma("act", dtT[:, oc:oc + ow], ac[0:64, st:st + ow], reads=[ac], is_output=True)
            kx.wstream([(lambda jb=jb: ld_a(jb)) for jb in range(13)], [(lambda wt, wv, jb=jb: cp_a(jb, wt, wv)) for jb in range(13)], L=2)
    s.emit()
    return nc


def run_ssd_a(x, ctx, m_lat, m_ctx, g, w_in, conv_w, conv_b, dt_bias):
    NT, NC_ = 2048, 64
    nc = build_ssd_a(NT, NC_)
    cw = np.concatenate([conv_w.T, conv_b[:, None]], 1).astype(np.float32)
    cw = np.ascontiguousarray(cw.reshape(32, 128, 4).transpose(1, 0, 2))
    dtb = np.ascontiguousarray(dt_bias.reshape(64, 1).astype(np.float32))
    ins = []
    for core in range(NCORES):
        b, k = divmod(core, 4)
        hm = np.ones((128, 4), np.float32)
        if k == 0:
            hm[:, 0] = 0
            hm[:, 2] = 0
        if k == 3:
            hm[:, 1] = 0
            hm[:, 3] = 0
        ins.append({"xT": _halo_T(x[b], k * NT, NT), "cT": _halo_T(ctx[b], k * NC_, NC_),
                    "mvec": np.ascontiguousarray(m_lat[b].reshape(6, D).T), "mvec_c": np.ascontiguousarray(m_ctx.reshape(6, D).T),
                    "gvec": np.ascontiguousarray(g.T), "hmask": hm, "w_in": w_in, "convw": cw, "dtb": dtb})
    res = run_bass_kernel_spmd(nc, ins, core_ids=list(range(NCORES)))
    z = np.empty((BATCH, SEQ + CTX, 2048), np.float32)
    xbc = np.empty((BATCH, SEQ + CTX, 4096), np.float32)
    dt = np.empty((BATCH, SEQ + CTX, 64), np.float32)
    for core in range(NCORES):
        b, k = divmod(core, 4)
        r = res.results[core]
        for arr, name in ((z, "zT"), (xbc, "xbcT"), (dt, "dtT")):
            arr[b, k * NT:(k + 1) * NT] = r[name][:, 0:NT].T
            arr[b, SEQ + k * NC_:SEQ + (k + 1) * NC_] = r[name][:, NT:NT + NC_].T
    return z, xbc, dt


NCH = 66


def build_ssd_b(nch=NCH, dbg=False):
    nc = bass.Bass("TRN2", target_bir_lowering=False)
    X = _mk(nc, "X", [nch, 128, 512])
    DT = _mk(nc, "DT", [nch, 128, 16])
    BCT = _mk(nc, "BCT", [nch, 128, 4, 128])
    BTOK = _mk(nc, "BTOK", [nch, 128, 2, 128])
    alog = _mk(nc, "alog", [128, 16])
    masks = _mk(nc, "masks", [128, 4, 128])
    Y = _mk(nc, "Y", [nch, 128, 512], kind="ExternalOutput")
    s = Sched(nc)
    ps = [s.ptile(name="ps%d" % i) for i in range(8)]
    psi = [0]

    def nps():
        t = ps[psi[0]]
        psi[0] = (psi[0] + 1) % 8
        return t
    mk = s.tile([128, 4, 128], F32, "mk")
    s.dma("sp", mk[:], masks, writes=[mk])
    onesf = s.tile([128, 128], F32, "onesf")
    s.op("dve", lambda e: e.memset(onesf[:], 1.0), writes=[onesf])
    a_bc = s.tile([128, 16], F32, "a_bc")
    s.dma("sp", a_bc[:], alog, writes=[a_bc])
    s.op("act", lambda e: e.activation(out=a_bc[:], in_=a_bc[:], func=AF.Exp), reads=[a_bc], writes=[a_bc])
    s.op("dve", lambda e: e.tensor_scalar(out=a_bc[:], in0=a_bc[:], scalar1=-1.0, scalar2=None, op0=ALU.mult), reads=[a_bc], writes=[a_bc])
    state = s.tile([128, 8, 64], F32, "state")
    state_bf = s.tile([128, 512], BF16, "state_bf")
    NB = 4
    xt = [s.tile([128, 8, 64], F32, "xt%d" % i) for i in range(NB)]
    dtt = [s.tile([128, 16], F32, "dtt%d" % i) for i in range(NB)]
    bct = [s.tile([128, 2, 128], F32, "bct%d" % i) for i in range(NB)]
    btk = [s.tile([128, 128], F32, "btk%d" % i) for i in range(NB)]
    bcb = [s.tile([128, 2, 128], BF16, "bcb%d" % i) for i in range(NB)]
    btb = [s.tile([128, 128], BF16, "btb%d" % i) for i in range(NB)]
    yin = [s.tile([128, 512], F32, "yin%d" % i) for i in range(NB)]
    dtA = [s.tile([128, 8], F32, "dtA%d" % i) for i in range(NB)]
    ct = [s.tile([128, 16], F32, "ct%d" % i) for i in range(NB)]
    ee = [s.tile([128, 3, 8], F32, "ee%d" % i) for i in range(NB)]
    dtw = [s.tile([128, 8], F32, "dtw%d" % i) for i in range(NB)]
    xdt = [s.tile([128, 8, 64], BF16, "xdt%d" % i) for i in range(NB)]
    xw = [s.tile([128, 8, 64], BF16, "xw%d" % i) for i in range(NB)]
    Lall = [s.tile([128, 8, 128], F32, "Lall%d" % i) for i in range(NB)]
    dec = [s.tile([128, 8, 128], F32, "dec%d" % i) for i in range(NB)]
    cbm = [s.tile([128, 128], F32, "cbm%d" % i) for i in range(NB)]
    sc = [s.tile([128, 8, 128], BF16, "sc%d" % i) for i in range(NB)]
    tt = [s.tile([128, 8, 64], F32, "tt%d" % i) for i in range(NB)]
    yo = [s.tile([128, 8, 64], F32, "yo%d" % i) for i in range(NB)]
    t2 = s.tile([128, 8, 64], F32, "t2")
    Yc = [T(Y[c], "Y%d" % c) for c in range(nch)]
    nctx = 2
    it = 0
    ysb = [s.tile([128, 8, 64], F32, "ysb%d" % i) for i in range(NB)]

    def sweep(d, it):
        order = list(range(nch)) if d == 0 else ([1, 0] + list(range(nch - 1, nctx - 1, -1)))
        m_incl = mk[:, d, :]
        m_str = mk[:, 2 + d, :]
        s.op("dve", lambda e: e.memset(state[:], 0.0), writes=[state])
        s.op("dve", lambda e: e.memset(state_bf[:], 0.0), writes=[state_bf])

        def phase_a(c, k):
            x_, dt_, bc_, bk_, bcb_, btb_, yin_ = xt[k], dtt[k], bct[k], btk[k], bcb[k], btb[k], yin[k]
            dA, ct_, ee_, dtw_, xdt_, xw_, L_, dec_, cbm_, sc_, ysb_ = dtA[k], ct[k], ee[k], dtw[k], xdt[k], xw[k], Lall[k], dec[k], cbm[k], sc[k], ysb[k]
            pb0 = 3 * (k % 2)
            pscb, pss, psY = ps[pb0], ps[pb0 + 1], ps[pb0 + 2]
            s.dma("sp", x_[:].rearrange("p a b -> p (a b)"), X[c], writes=[x_])
            s.dma("act", dt_[:], DT[c], writes=[dt_])
            s.dma("sp", bc_[:], BCT[c][:, 2 * d:2 * d + 2, :], writes=[bc_])
            s.dma("act", bk_[:], BTOK[c][:, d, :], writes=[bk_])
            if d == 1:
                s.dma("sp", yin_[:], Y[c], reads=[Yc[c]], writes=[yin_])
            yield
            dts = dt_[:, d * 8:(d + 1) * 8]
            s.op("dve", lambda e: e.tensor_tensor(out=dA[:], in0=dts, in1=a_bc[:, d * 8:(d + 1) * 8], op=ALU.mult), reads=[dt_, a_bc], writes=[dA])
            yield
            s.op("pool", lambda e: e.tensor_copy(out=bcb_[:], in_=bc_[:]), reads=[bc_], writes=[bcb_])
            s.op("pool", lambda e: e.tensor_copy(out=btb_[:], in_=bk_[:]), reads=[bk_], writes=[btb_])
            yield
            s.op("pe", lambda e: e.matmul(pscb[:, 0:8], lhsT=m_incl, rhs=dA[:], start=True, stop=True), reads=[mk, dA], writes=[pscb])
            s.op("pe", lambda e: e.matmul(pscb[:, 8:16], lhsT=onesf[:], rhs=dA[:], start=True, stop=True), reads=[onesf, dA], writes=[pscb])
            yield
            s.op("pool", lambda e: e.tensor_tensor(out=L_[:], in0=m_str.unsqueeze(1).to_broadcast([128, 8, 128]),
                                                   in1=dA[:].unsqueeze(2).to_broadcast([128, 8, 128]), op=ALU.mult), reads=[mk, dA], writes=[L_])
            yield
            s.op("pe", lambda e: e.matmul(pscb[:, 128:256], lhsT=bcb_[:, 0, :], rhs=bcb_[:, 1, :], start=True, stop=True), reads=[bcb_], writes=[pscb])
            yield
            s.op("dve", lambda e: e.tensor_copy(out=ct_[:], in_=pscb[:, 0:16]), reads=[pscb], writes=[ct_])
            yield
            s.op("dve", lambda e: e.tensor_tensor(out=cbm_[:], in0=pscb[:, 128:256], in1=m_incl, op=ALU.mult), reads=[pscb, mk], writes=[cbm_])
            yield
            s.op("dve", lambda e: e.tensor_tensor(out=ee_[:, 2, :], in0=ct_[:, 8:16], in1=ct_[:, 0:8], op=ALU.subtract), reads=[ct_], writes=[ee_])
            yield
            s.op("act", lambda e: e.activation(out=ee_[:, 0:2, :].rearrange("p a b -> p (a b)"), in_=ct_[:, 0:16], func=AF.Exp), reads=[ct_, ee_], writes=[ee_])
            yield
            s.op("act", lambda e: e.activation(out=ee_[:, 2, :], in_=ee_[:, 2, :], func=AF.Exp), reads=[ee_], writes=[ee_])
            yield
            s.op("dve", lambda e: e.tensor_tensor(out=xdt_[:], in0=x_[:], in1=dts.unsqueeze(2).to_broadcast([128, 8, 64]), op=ALU.mult), reads=[x_, dt_], writes=[xdt_])
            yield
            for hh in range(2):
                for e_ in range(hh * 4, hh * 4 + 4):
                    s.op("pe", lambda e, e_=e_: e.matmul(pss[:, (e_ % 4) * 128:(e_ % 4 + 1) * 128], lhsT=L_[:, e_, :], rhs=m_incl, start=True, stop=True),
                         reads=[L_, mk], writes=[pss])
                yield
                s.op("act", lambda e, hh=hh: e.activation(out=dec_[:, hh * 4:(hh + 1) * 4, :].rearrange("p a b -> p (a b)"), in_=pss[:, :], func=AF.Exp),
                     reads=[pss], writes=[dec_])
                yield
                if hh == 0:
                    s.op("dve", lambda e: e.tensor_tensor(out=dtw_[:], in0=dts, in1=ee_[:, 2, :], op=ALU.mult), reads=[dt_, ee_], writes=[dtw_])
                    yield
                    s.op("dve", lambda e: e.tensor_tensor(out=xw_[:], in0=x_[:], in1=dtw_[:].unsqueeze(2).to_broadcast([128, 8, 64]), op=ALU.mult), reads=[x_, dtw_], writes=[xw_])
                    yield
            s.op("dve", lambda e: e.tensor_tensor(out=sc_[:], in0=dec_[:], in1=cbm_[:].unsqueeze(1).to_broadcast([128, 8, 128]), op=ALU.mult),
                 reads=[dec_, cbm_], writes=[sc_])
            yield
            for e_ in range(8):
                s.op("pe", lambda e, e_=e_: e.matmul(psY[:, e_ * 64:(e_ + 1) * 64], lhsT=sc_[:, e_, :], rhs=xdt_[:, e_, :], start=True, stop=True),
                     reads=[sc_, xdt_], writes=[psY])
            yield
            s.op("act", lambda e: e.activation(out=ysb_[:].rearrange("p a b -> p (a b)"), in_=psY[:, 0:512], func=AF.Identity), reads=[psY], writes=[ysb_])
            yield

        def phase_b(c, k):
            bcb_, btb_, yin_, ee_, xw_, tt_, yo_, ysb_ = bcb[k], btb[k], yin[k], ee[k], xw[k], tt[k % 2], yo[k % 2], ysb[k]
            psS, psU = ps[6], ps[7]
            s.op("pe", lambda e: e.matmul(psS[:, 0:512], lhsT=bcb_[:, 1, :], rhs=state_bf[:], start=True, stop=True), reads=[bcb_, state_bf], writes=[psS])
            s.op("pe", lambda e: e.matmul(psU[:, 0:512], lhsT=btb_[:], rhs=xw_[:].rearrange("p a b -> p (a b)"), start=True, stop=True),
                 reads=[btb_, xw_], writes=[psU])
            yield
            s.op("dve", lambda e: e.tensor_tensor(out=t2[:], in0=state[:], in1=ee_[:, 1, :].unsqueeze(2).to_broadcast([128, 8, 64]), op=ALU.mult),
                 reads=[state, ee_], writes=[t2])
            yield
            s.op("dve", lambda e: e.tensor_tensor(out=state[:], in0=t2[:], in1=psU[:, 0:512].rearrange("p (a b) -> p a b", a=8), op=ALU.add),
                 reads=[t2, psU], writes=[state])
            yield
            s.op("act", lambda e: e.activation(out=state_bf[:], in_=state[:].rearrange("p a b -> p (a b)"), func=AF.Identity), reads=[state], writes=[state_bf])
            yield
            s.op("dve", lambda e: e.tensor_tensor(out=tt_[:], in0=psS[:, 0:512].rearrange("p (a b) -> p a b", a=8),
                                                  in1=ee_[:, 0, :].unsqueeze(2).to_broadcast([128, 8, 64]), op=ALU.mult), reads=[psS, ee_], writes=[tt_])
            yield
            if d == 1:
                s.op("pool", lambda e: e.tensor_tensor(out=ysb_[:].rearrange("p a b -> p (a b)"), in0=ysb_[:].rearrange("p a b -> p (a b)"), in1=yin_[:], op=ALU.add),
                     reads=[ysb_, yin_], writes=[ysb_])
                yield
            s.op("dve", lambda e: e.tensor_tensor(out=yo_[:], in0=tt_[:], in1=ysb_[:], op=ALU.add), reads=[ysb_, tt_], writes=[yo_])
            yield
            s.dma("sp", Y[c], yo_[:].rearrange("p a b -> p (a b)"), reads=[yo_], writes=[Yc[c]], is_output=True)
            yield

        def chain(gens):
            for g in gens:
                yield from g

        def merge(gens):
            gens = list(gens)
            while gens:
                for g in list(gens):
                    try:
                        next(g)
                    except StopIteration:
                        gens.remove(g)

        slots = [(c, (it + i) % NB) for i, c in enumerate(order)]
        it += len(order)
        n = len(slots)
        merge([phase_a(*slots[0]), phase_a(*slots[1])])
        for t in range(0, n, 2):
            gens = []
            for j in (t + 2, t + 3):
                if j < n:
                    gens.append(phase_a(*slots[j]))
            gens.append(chain([phase_b(*slots[j]) for j in (t, t + 1) if j < n]))
            merge(gens)
        return it
    it = sweep(0, it)
    it = sweep(1, it)
    s.emit()
    return nc


def _ssd_masks():
    k = np.arange(128)
    m = np.zeros((128, 4, 128), np.float32)
    m[:, 0, :] = (k[:, None] <= k[None, :])
    m[:, 1, :] = (k[:, None] >= k[None, :])
    m[:, 2, :] = (k[:, None] > k[None, :])
    m[:, 3, :] = (k[:, None] < k[None, :])
    return m


def run_ssd_b(xbc, dt, a_log):
    nc = build_ssd_b()
    masks = _ssd_masks()
    ins = []
    for core in range(NCORES):
        b, g = divmod(core, 4)
        def chunks(a):
            return np.concatenate([a[SEQ:].reshape(2, 128, -1), a[:SEQ].reshape(64, 128, -1)], 0)
        xg = chunks(xbc[b][:, g * 512:(g + 1) * 512])
        bc = xbc[b][:, 2048:].reshape(-1, 2, 2, 4, 128)[:, :, :, g, :]
        bcc = chunks(bc.reshape(-1, 4 * 128)).reshape(NCH, 128, 4, 128)
        bct = np.ascontiguousarray(bcc.transpose(0, 3, 2, 1))
        btok = np.ascontiguousarray(bcc[:, :, [0, 2], :])
        dtg = dt[b].reshape(-1, 2, 4, 8)[:, :, g, :].reshape(-1, 16)
        al = np.ascontiguousarray(np.broadcast_to(a_log.reshape(2, 4, 8)[:, g, :].reshape(1, 16), (128, 16))).astype(np.float32)
        ins.append({"X": np.ascontiguousarray(xg), "DT": np.ascontiguousarray(chunks(dtg)), "BCT": bct, "BTOK": btok, "alog": al, "masks": masks})
    res = run_bass_kernel_spmd(nc, ins, core_ids=list(range(NCORES)))
    y = np.empty((BATCH, SEQ + CTX, 2048), np.float32)
    for core in range(NCORES):
        b, g = divmod(core, 4)
        Yc = res.results[core]["Y"]
        y[b, SEQ:, g * 512:(g + 1) * 512] = Yc[0:2].reshape(256, 512)
        y[b, :SEQ, g * 512:(g + 1) * 512] = Yc[2:].reshape(SEQ, 512)
    return y


def build_ssd_c(NT=2048, NC_=64):
    nc = bass.Bass("TRN2", target_bir_lowering=False)
    NTOT = NT + NC_
    yT = _mk(nc, "yT", [2048, NTOT])
    xsT = _mk(nc, "xsT", [2048, NTOT])
    zT = _mk(nc, "zT", [2048, NTOT])
    xT = _mk(nc, "xT", [D, NTOT])
    mvec = _mk(nc, "mvec", [D, 6])
    mvec_c = _mk(nc, "mvec_c", [D, 6])
    gvec = _mk(nc, "gvec", [D, 4])
    dcol = _mk(nc, "dcol", [128, 16, 2])
    ngd = _mk(nc, "ng", [128, 16])
    w_out = _mk(nc, "w_out", [2048, D])
    w1 = _mk(nc, "w1", [D, HID])
    w2 = _mk(nc, "w2", [HID, D])
    outT = _mk(nc, "outT", [D, NTOT], kind="ExternalOutput")
    kx = KX(nc, nwbuf=2)
    s = kx.s
    ms = kx.prep_mod(mvec, gvec)
    msc = kx.prep_mod(mvec_c, gvec, "mvc")
    dc_ = s.tile([128, 16, 2], F32, "dcol")
    s.dma("sp", dc_[:], dcol, writes=[dc_])
    dsum = s.tile([128, 16], F32, "dsum")
    s.op("dve", lambda e: e.tensor_tensor(out=dsum[:], in0=dc_[:, :, 0], in1=dc_[:, :, 1], op=ALU.add), reads=[dc_], writes=[dsum])
    ng = s.tile([128, 16], F32, "ng")
    s.dma("sp", ng[:], ngd, writes=[ng])
    HWD = NT // 2
    xres = s.tile([128, 8, HWD + NC_], F32, "xres")
    y_tiles = [s.tile([128, 8, 512], F32, "y0"), s.tile([128, 8, 512], F32, "y1"), s.tile([128, 8, NC_], F32, "y2")]
    for half in range(2):
        c0 = half * HWD
        cols = [(c0, 512, ms, 0), (c0 + 512, 512, ms, 512)]
        s.dma("sp", xres[:, :, 0:HWD], xT[:, c0:c0 + HWD].rearrange("(c p) n -> p c n", p=128), writes=[xres])
        if half == 1:
            cols.append((NT, NC_, msc, HWD))
            s.dma("sp", xres[:, :, HWD:HWD + NC_], xT[:, NT:NT + NC_].rearrange("(c p) n -> p c n", p=128), writes=[xres])
        with s.scope():
            gz = s.tile([128, 16, 512], F32, "gz")
            ynT = [s.tile([128, 16, w], BF16, "ynT%d" % i) for i, (_, w, _, _) in enumerate(cols)]
            ld = [[s.tile([128, 512], F32, "ld%d_%d" % (a, b)) for b in range(2)] for a in range(3)]
            cnt = 0
            for bi, (co, w, m, xo) in enumerate(cols):
                for ch in range(16):
                    ly, lx, lz = ld[0][cnt % 2], ld[1][cnt % 2], ld[2][cnt % 2]
                    cnt += 1
                    s.dma("sp", ly[:, 0:w], yT[ch * 128:(ch + 1) * 128, co:co + w], writes=[ly])
                    s.dma("act", lx[:, 0:w], xsT[ch * 128:(ch + 1) * 128, co:co + w], writes=[lx])
                    s.dma("sp", lz[:, 0:w], zT[ch * 128:(ch + 1) * 128, co:co + w], writes=[lz])
                    s.op("dve", lambda e, ly=ly, lx=lx, ch=ch, w=w: e.scalar_tensor_tensor(out=ly[:, 0:w], in0=lx[:, 0:w], scalar=dsum[:, ch:ch + 1], in1=ly[:, 0:w],
                                                                                      op0=ALU.mult, op1=ALU.add), reads=[lx, ly, dsum], writes=[ly])
                    s.op("act", lambda e, lz=lz, w=w: e.activation(out=lz[:, 0:w], in_=lz[:, 0:w], func=AF.Silu), reads=[lz], writes=[lz])
                    s.op("dve", lambda e, ly=ly, lz=lz, ch=ch, w=w: e.tensor_tensor(out=gz[:, ch, 0:w], in0=ly[:, 0:w], in1=lz[:, 0:w], op=ALU.mult),
                         reads=[ly, lz], writes=[gz])
                r = kx.rstd(gz, gz[:, :, 0:w], w, nchunks=16, dim=2048)
                for ch in range(16):
                    tc = kx.ntc()
                    s.op("dve", lambda e, ch=ch, tc=tc, r=r, w=w: e.tensor_tensor(out=tc[:, 0:w], in0=gz[:, ch, 0:w], in1=r[:, 0:w], op=ALU.mult),
                         reads=[gz, r], writes=[tc])
                    s.op("act", lambda e, ch=ch, tc=tc, w=w, yn=ynT[bi]: e.activation(out=yn[:, ch, :], in_=tc[:, 0:w], func=AF.Identity, scale=ng[:, ch:ch + 1]),
                         reads=[tc, ng], writes=[ynT[bi]])
            def cp_o(dc, wt, wv, cols=cols, ynT=ynT):
                for bi, (co, w, m, xo) in enumerate(cols):
                    ps = kx.nps()
                    for c in range(16):
                        s.op("pe", lambda e, c=c, ps=ps, bi=bi, w=w, wv=wv: e.matmul(ps[:, 0:w], lhsT=wv[:, c, :], rhs=ynT[bi][:, c, :], start=(c == 0), stop=(c == 15)),
                             reads=[wt, ynT[bi]], writes=[ps])
                    s.op("act", lambda e, ps=ps, dc=dc, bi=bi, w=w: e.activation(out=y_tiles[bi][:, dc, 0:w], in_=ps[:, 0:w], func=AF.Identity),
                         reads=[ps], writes=[y_tiles[bi]])
            kx.wstream([(lambda dc=dc: kx.wload(w_out[:, dc * 128:(dc + 1) * 128], 16, 128)) for dc in range(8)],
                       [(lambda wt, wv, dc=dc: cp_o(dc, wt, wv)) for dc in range(8)], L=1)
        blocks = [Blk(w, m, xres, xres[:, :, xo:xo + w], y_tiles[bi]) for bi, (co, w, m, xo) in enumerate(cols)]
        emit_finish(kx, blocks, w1, w2, None)
        for bi, (co, w, m, xo) in enumerate(cols):
            s.dma("act", outT[:, co:co + w].rearrange("(c p) n -> p c n", p=128), xres[:, :, xo:xo + w], reads=[xres], is_output=True)
    s.emit()
    return nc


def run_ssd_c(x, ctx, y, xbc, z, m_lat, m_ctx, g, ssd_d, ssd_norm_g, w_out, w1, w2):
    NT, NC_ = 2048, 64
    nc = build_ssd_c(NT, NC_)
    dcol = np.repeat(ssd_d.reshape(2, 32).T, 64, axis=0).astype(np.float32)
    dcol = np.ascontiguousarray(dcol.reshape(16, 128, 2).transpose(1, 0, 2))
    ng = np.ascontiguousarray(ssd_norm_g.reshape(16, 128).T.astype(np.float32))
    ins = []
    for core in range(NCORES):
        b, k = divmod(core, 4)
        def cat(a_main, a_ctx):
            return np.ascontiguousarray(np.concatenate([a_main[k * NT:(k + 1) * NT], a_ctx[k * NC_:(k + 1) * NC_]], 0).T)
        ins.append({"yT": cat(y[b][:SEQ], y[b][SEQ:]), "xsT": cat(xbc[b][:SEQ, :2048], xbc[b][SEQ:, :2048]), "zT": cat(z[b][:SEQ], z[b][SEQ:]),
                    "xT": cat(x[b], ctx[b]), "mvec": np.ascontiguousarray(m_lat[b].reshape(6, D).T), "mvec_c": np.ascontiguousarray(m_ctx.reshape(6, D).T),
                    "gvec": np.ascontiguousarray(g.T), "dcol": dcol, "ng": ng, "w_out": w_out, "w1": w1, "w2": w2})
    res = run_bass_kernel_spmd(nc, ins, core_ids=list(range(NCORES)))
    xo = np.empty_like(x)
    co = np.empty_like(ctx)
    for core in range(NCORES):
        b, k = divmod(core, 4)
        o = res.results[core]["outT"]
        xo[b, k * NT:(k + 1) * NT] = o[:, :NT].T
        co[b, k * NC_:(k + 1) * NC_] = o[:, NT:].T
    return xo, co


def build_projfin(NT=2048):
    nc = bass.Bass("TRN2", target_bir_lowering=False)
    fT = _mk(nc, "fT", [D, NT])
    xT = _mk(nc, "xT", [D, NT])
    mvec = _mk(nc, "mvec", [D, 6])
    gvec = _mk(nc, "gvec", [D, 4])
    w_out = _mk(nc, "w_out", [D, D])
    w1 = _mk(nc, "w1", [D, HID])
    w2 = _mk(nc, "w2", [HID, D])
    outT = _mk(nc, "outT", [D, NT], kind="ExternalOutput")
    kx = KX(nc)
    s = kx.s
    ms = kx.prep_mod(mvec, gvec)
    HWD, bw, nblk = NT // 2, 512, 2
    xres = s.tile([128, 8, HWD], F32, "xres")
    y_tiles = [s.tile([128, 8, bw], F32, "y%d" % i) for i in range(nblk)]
    for half in range(2):
        c0 = half * HWD
        s.dma("sp", xres[:], xT[:, c0:c0 + HWD].rearrange("(c p) n -> p c n", p=128), writes=[xres])
        with s.scope():
            fb = [s.tile([128, 8, bw], BF16, "fb%d" % i) for i in range(nblk)]
            for tb in range(nblk):
                for c in range(8):
                    kx.stage_cast(fb[tb], fb[tb][:, c, :], fT[c * 128:(c + 1) * 128, c0 + tb * bw:c0 + (tb + 1) * bw], bw)
            for db in range(2):
                wt, wv = kx.wload(w_out[:, db * 512:(db + 1) * 512], 8, 512)
                for sub in range(4):
                    dc = db * 4 + sub
                    for tb in range(nblk):
                        ps = kx.nps()
                        for c in range(8):
                            s.op("pe", lambda e, c=c, ps=ps, tb=tb, sub=sub, wv=wv: e.matmul(
                                ps[:, 0:bw], lhsT=wv[:, c, sub * 128:(sub + 1) * 128], rhs=fb[tb][:, c, :], start=(c == 0), stop=(c == 7)),
                                reads=[wt, fb[tb]], writes=[ps])
                        s.op("act", lambda e, ps=ps, dc=dc, tb=tb: e.activation(out=y_tiles[tb][:, dc, :], in_=ps[:, 0:bw], func=AF.Identity),
                             reads=[ps], writes=[y_tiles[tb]])
        x_views = [xres[:, :, tb * bw:(tb + 1) * bw] for tb in range(nblk)]
        blocks = [Blk(bw, ms, xres, x_views[tb], y_tiles[tb]) for tb in range(nblk)]
        emit_finish(kx, blocks, w1, w2, None)
        for tb in range(nblk):
            s.dma("act", outT[:, c0 + tb * bw:c0 + (tb + 1) * bw].rearrange("(c p) n -> p c n", p=128), x_views[tb], reads=[xres], is_output=True)
    s.emit()
    return nc


def run_projfin(x, f, m_lat, g, w_out, w1, w2):
    NT = 2048
    nc = build_projfin(NT)
    ins = []
    for core in range(NCORES):
        b, k = divmod(core, 4)
        ins.append({"fT": np.ascontiguousarray(f[b, k * NT:(k + 1) * NT].T), "xT": np.ascontiguousarray(x[b, k * NT:(k + 1) * NT].T),
                    "mvec": np.ascontiguousarray(m_lat[b].reshape(6, D).T), "gvec": np.ascontiguousarray(g.T), "w_out": w_out, "w1": w1, "w2": w2})
    res = run_bass_kernel_spmd(nc, ins, core_ids=list(range(NCORES)))
    out = np.empty_like(x)
    for core in range(NCORES):
        b, k = divmod(core, 4)
        out[b, k * NT:(k + 1) * NT] = res.results[core]["outT"].T
    return out


def kernel(x, c, ctx, c_ctx, mod_w, mod_b, norm_g, mlp_w1, mlp_w2, ssd_w_in, ssd_conv_w, ssd_conv_b,
           ssd_dt_bias, ssd_a_log, ssd_d, ssd_norm_g, ssd_w_out, na_w_qkv, na_rpb, na_w_out,
           sc_w_in, sc_conv_w, sc_w_out, fn_w_out):
    f32 = lambda a: np.ascontiguousarray(np.asarray(a), dtype=np.float32)
    x, c, ctx, c_ctx, mod_w, mod_b, norm_g, mlp_w1, mlp_w2 = map(f32, (x, c, ctx, c_ctx, mod_w, mod_b, norm_g, mlp_w1, mlp_w2))
    m_lat, m_ctx = run_mod(c, c_ctx, mod_w, mod_b)
    z, xbc, dt = run_ssd_a(x, ctx, m_lat[0], m_ctx[0], norm_g[0], f32(ssd_w_in)[0], f32(ssd_conv_w)[0], f32(ssd_conv_b)[0], f32(ssd_dt_bias)[0])
    y = run_ssd_b(xbc, dt, f32(ssd_a_log)[0])
    x, ctx = run_ssd_c(x, ctx, y, xbc, z, m_lat[0], m_ctx[0], norm_g[0], f32(ssd_d)[0], f32(ssd_norm_g)[0], f32(ssd_w_out)[0], mlp_w1[0], mlp_w2[0])
    x = run_na(x, ctx, m_lat[1], m_ctx[1], norm_g[1], f32(na_w_qkv)[0], f32(na_rpb)[0], f32(na_w_out)[0], mlp_w1[1], mlp_w2[1])
    x, h3 = run_sc(x, m_lat[2], norm_g[2], f32(sc_w_in)[0], f32(sc_conv_w)[0], f32(sc_w_out)[0], mlp_w1[2], mlp_w2[2], m_lat[3], norm_g[3])
    f = run_fft(h3)
    x = run_projfin(x, f, m_lat[3], norm_g[3], f32(fn_w_out)[0], mlp_w1[3], mlp_w2[3])
    return x.astype(np.float32)
```

```python
import numpy as np
from contextlib import ExitStack
import concourse.bass as bass
import concourse.mybir as mybir
from concourse.bass_utils import run_bass_kernel_spmd

F32 = mybir.dt.float32
BF16 = mybir.dt.bfloat16
AF = mybir.ActivationFunctionType
ALU = mybir.AluOpType
AX = mybir.AxisListType

ENGS = ("pe", "dve", "act", "pool", "sp")
N_DMA_SEMS = 40
NCORES = 8

D = 1024
SEQ = 8192
BATCH = 2
CTX = 256
HID = 4096
EPS = 1e-6
ARENA_WORDS = 52736
CAST_ENGS = ("pool",)


class T:
    __slots__ = ("ap", "w", "r", "name")

    def __init__(self, ap, name=""):
        self.ap = ap
        self.w = None
        self.r = []
        self.name = name

    def __getitem__(self, k):
        return self.ap[k]


class Sched:
    def __init__(self, nc, same_engine_sync=True):
        self.nc = nc
        self.es = ExitStack()
        self.q = {e: [] for e in ENGS}
        self.cnt = {e: 0 for e in ENGS}
        self.prog = {e: self.es.enter_context(nc.semaphore("prog_" + e)) for e in ENGS}
        self.dsem = [self.es.enter_context(nc.semaphore("dma%d" % i)) for i in range(N_DMA_SEMS)]
        self.dval = [0] * N_DMA_SEMS
        self.dnext = 0
        self.known = {e: {} for e in ENGS}
        self.same_engine_sync = same_engine_sync
        self.sem_owner = {id(self.prog[e]): e for e in ENGS}
        self.out_events = []
        self.n_sb = 0
        self.arena = None
        self.aoff = 0
        self.amax = 0
        self.swsem = {}
        self.swused = {}

    def tile(self, shape, dtype, name=None):
        if self.arena is None:
            self.arena = self.es.enter_context(self.nc.sbuf_tensor("arena", [128, ARENA_WORDS], F32))
            self.aoff = 0
        esz = 2 if dtype == BF16 else 4
        n = 1
        for d in shape[1:]:
            n *= d
        words = (n * esz + 3) // 4
        words = (words + 7) // 8 * 8
        if self.aoff + words > ARENA_WORDS:
            raise RuntimeError("SBUF arena overflow: need %d words at %d" % (words, self.aoff))
        ap = self.arena[:, self.aoff:self.aoff + (n * esz + 3) // 4]
        self.aoff += words
        self.amax = max(self.amax, self.aoff)
        if dtype != F32:
            ap = ap.bitcast(dtype)
        ap = ap[0:shape[0], 0:n]
        if len(shape) >= 3:
            names = ["d%d" % i for i in range(len(shape) - 1)]
            kw = {names[i]: shape[1 + i] for i in range(len(shape) - 2)}
            ap = ap.rearrange("p (%s) -> p %s" % (" ".join(names), " ".join(names)), **kw)
        return T(ap, name or "")

    def ptile(self, shape=(128, 512), dtype=F32, name=None):
        self.n_sb += 1
        return T(self.es.enter_context(self.nc.psum_tensor(name or ("ps%d" % self.n_sb), list(shape), dtype)), name or "")

    def _deps(self, eng, reads, writes):
        evs = []
        for t in reads:
            if t.w is not None:
                evs.append(t.w)
        for t in writes:
            if t.w is not None:
                evs.append(t.w)
            evs.extend(t.r)
        need = {}
        for (sem, val) in evs:
            owner = self.sem_owner.get(id(sem))
            if owner == eng and (eng == "pe" or not self.same_engine_sync):
                continue
            k = id(sem)
            if self.known[eng].get(k, 0) >= val:
                continue
            if k not in need or need[k][1] < val:
                need[k] = (sem, val)
        for k, (sem, val) in need.items():
            self.known[eng][k] = val
        return list(need.values())

    def _commit(self, ev, reads, writes):
        for t in reads:
            t.r.append(ev)
            if len(t.r) > 64:
                best = {}
                for (sem, val) in t.r:
                    if id(sem) not in best or best[id(sem)][1] < val:
                        best[id(sem)] = (sem, val)
                t.r = list(best.values())
        for t in writes:
            t.w = ev
            t.r = []

    def op(self, eng, fn, reads=(), writes=()):
        waits = self._deps(eng, reads, writes)
        self.cnt[eng] += 1
        ev = (self.prog[eng], self.cnt[eng])
        self.q[eng].append((waits, fn, (self.prog[eng], 1)))
        self._commit(ev, reads, writes)
        return ev

    def dma(self, eng, out_ap, in_ap, reads=(), writes=(), is_output=False, sub=0, **kw):
        if eng == "pool":
            return self._dma_sw(out_ap, in_ap, reads, writes, is_output, sub, kw)
        i = self.dnext
        self.dnext = (self.dnext + 1) % N_DMA_SEMS
        sem = self.dsem[i]
        waits = self._deps(eng, reads, writes)
        if self.dval[i] > 0 and self.known[eng].get(id(sem), 0) < self.dval[i]:
            waits.append((sem, self.dval[i]))
            self.known[eng][id(sem)] = self.dval[i]
        self.dval[i] += 16
        ev = (sem, self.dval[i])

        def fn(e, out_ap=out_ap, in_ap=in_ap, kw=kw):
            return e.dma_start(out=out_ap, in_=in_ap, **kw)
        self.q[eng].append((waits, fn, (sem, 16)))
        self._commit(ev, reads, writes)
        if is_output:
            self.out_events.append(ev)
        return ev

    def _dma_sw(self, out_ap, in_ap, reads, writes, is_output, sub, kw):
        eng = "pool"
        slot = writes[0]
        key = (id(slot), sub)
        if key not in self.swsem:
            self.swsem[key] = self.es.enter_context(self.nc.semaphore("sw%d" % len(self.swsem)))
            self.swused[key] = False
        sem = self.swsem[key]
        waits = self._deps(eng, reads, writes)
        reuse = self.swused[key]
        if reuse and self.known[eng].get(id(sem), 0) < 16:
            waits.append((sem, 16))
        for e in ENGS:
            self.known[e].pop(id(sem), None)
        self.swused[key] = True
        ev = (sem, 16)

        def fn(e, out_ap=out_ap, in_ap=in_ap, kw=kw, sem=sem, reuse=reuse):
            if reuse:
                e.sem_clear(sem)
            return e.dma_start(out=out_ap, in_=in_ap, **kw)
        self.q[eng].append((waits, fn, (sem, 16)))
        self._commit(ev, reads, writes)
        if is_output:
            self.out_events.append(ev)
        return ev

    def barrier(self):
        waits = []
        for key, sem in self.swsem.items():
            if self.swused[key] and self.known["pool"].get(id(sem), 0) < 16:
                waits.append((sem, 16))
                self.known["pool"][id(sem)] = 16
        assert not waits
        for e in ENGS:
            waits = []
            for f in ENGS:
                if f != e and self.cnt[f] > self.known[e].get(id(self.prog[f]), 0):
                    waits.append((self.prog[f], self.cnt[f]))
                    self.known[e][id(self.prog[f])] = self.cnt[f]
            for i in range(N_DMA_SEMS):
                if self.dval[i] > self.known[e].get(id(self.dsem[i]), 0):
                    waits.append((self.dsem[i], self.dval[i]))
                    self.known[e][id(self.dsem[i])] = self.dval[i]
            if waits:
                self.q[e].append((waits, None, None))

    def scope(self):
        return _Scope(self)

    def emit(self):
        nc = self.nc
        seen = {}
        for (sem, val) in self.out_events:
            if seen.get(id(sem), (None, 0))[1] < val:
                seen[id(sem)] = (sem, val)
        fin = list(seen.values())
        engmap = {"pe": "tensor", "dve": "vector", "act": "scalar", "pool": "gpsimd", "sp": "sync"}
        with nc.Block() as block:
            for e in ENGS:
                q = self.q[e]
                is_sp = (e == "sp")

                def body(eng, q=q, is_sp=is_sp):
                    for waits, fn, inc in q:
                        for (sem, val) in waits:
                            eng.wait_ge(sem, val)
                        if fn is None:
                            continue
                        ins = fn(eng)
                        ins.then_inc(inc[0], inc[1])
                    if is_sp:
                        for (sem, val) in fin:
                            eng.wait_ge(sem, val)
                getattr(block, engmap[e])(body)
        self.es.close()


class _Scope:
    def __init__(self, s):
        self.s = s

    def __enter__(self):
        self.saved = self.s.aoff
        return self

    def __exit__(self, *a):
        self.s.barrier()
        self.s.aoff = self.saved
        return False


class KX:
    def __init__(self, nc, npsum=8, nwbuf=3, wbuf_elems=4096):
        self.nc = nc
        self.s = Sched(nc)
        s = self.s
        self.ones = s.tile([128, 128], BF16, "ones")
        self.eps = s.tile([128, 1], F32, "eps")
        s.op("dve", lambda e: e.memset(self.ones[:], 1.0), writes=[self.ones])
        s.op("dve", lambda e: e.memset(self.eps[:], EPS), writes=[self.eps])
        self.ps = [s.ptile(name="psb%d" % i) for i in range(npsum)]
        self.psi = 0
        self.wb = [s.tile([128, wbuf_elems], BF16, "wbuf%d" % i) for i in range(nwbuf)]
        self.wbi = 0
        self.sqs = [s.tile([128, 512], BF16, "sqbuf%d" % i) for i in range(2)]
        self.stg = [s.tile([128, 2048], F32, "stage%d" % i) for i in range(2)]
        self.stgi = 0
        self.dmaq = ("sp", "act")
        self.dqi = 0
        self.cast_engs = CAST_ENGS
        self.cei = 0
        self.rs = [s.tile([128, 512], F32, "rstd%d" % i) for i in range(2)]
        self.rsi = 0
        self.tc = [s.tile([128, 512], F32, "tmpc%d" % i) for i in range(3)]
        self.tci = 0

    def nps(self):
        t = self.ps[self.psi]
        self.psi = (self.psi + 1) % len(self.ps)
        return t

    def ntc(self):
        t = self.tc[self.tci]
        self.tci = (self.tci + 1) % len(self.tc)
        return t

    def nwb(self):
        t = self.wb[self.wbi]
        self.wbi = (self.wbi + 1) % len(self.wb)
        return t

    def stage_cast(self, dst_T, dst_ap, src_ap, n):
        s = self.s
        st = self.stg[self.stgi]
        self.stgi = (self.stgi + 1) % len(self.stg)
        q = "sp"
        shp = list(src_ap.shape)
        sv = st.ap[:, 0:n]
        if len(shp) == 3:
            sv = sv.rearrange("p (a b) -> p a b", a=shp[1])
        s.dma(q, sv, src_ap, writes=[st])
        ce = self.cast_engs[self.cei]
        self.cei = (self.cei + 1) % len(self.cast_engs)
        if ce == "act":
            s.op("act", lambda e: e.activation(out=dst_ap, in_=sv, func=AF.Identity), reads=[st], writes=[dst_T])
        else:
            s.op(ce, lambda e: e.tensor_copy(out=dst_ap, in_=sv), reads=[st], writes=[dst_T])

    def wload(self, w_ap, kc, ncols):
        t = self.nwb()
        view = t.ap[:, 0:kc * ncols].rearrange("p (c n) -> p c n", c=kc)
        src = w_ap.rearrange("(c p) n -> p c n", p=128)
        per = max(1, 2048 // ncols)
        for c0 in range(0, kc, per):
            c1 = min(kc, c0 + per)
            self.stage_cast(t, view[:, c0:c1, :], src[:, c0:c1, :], (c1 - c0) * ncols)
        return t, view

    def rstd(self, src_T, src_ap, n, nchunks=8, dim=D):
        s = self.s
        ps = self.nps()
        for c in range(nchunks):
            sq = self.sqs[c % 2]
            s.op("act", lambda e, c=c, sq=sq: e.activation(out=sq[:, 0:n], in_=src_ap[:, c, :], func=AF.Square), reads=[src_T], writes=[sq])
            s.op("pe", lambda e, c=c, sq=sq: e.matmul(ps[:, 0:n], lhsT=self.ones[:], rhs=sq[:, 0:n], start=(c == 0), stop=(c == nchunks - 1)),
                 reads=[self.ones, sq], writes=[ps])
        r = self.rs[self.rsi]
        self.rsi = (self.rsi + 1) % len(self.rs)
        s.op("act", lambda e: e.activation(out=r[:, 0:n], in_=ps[:, 0:n], func=AF.Ln, bias=self.eps[:], scale=1.0 / dim),
             reads=[ps, self.eps], writes=[r])
        s.op("act", lambda e: e.activation(out=r[:, 0:n], in_=r[:, 0:n], func=AF.Exp, scale=-0.5), reads=[r], writes=[r])
        return r

    def norm_mod(self, src_T, src_ap, n, ms, which, dst_T, dst_ap, tmp_T):
        s = self.s
        A, S = (ms.A0, ms.S0) if which == 0 else (ms.A2, ms.S2)
        r = self.rstd(src_T, src_ap, n)
        for c in range(8):
            tc = self.ntc()
            s.op("dve", lambda e, c=c, tc=tc: e.tensor_tensor(out=tc[:, 0:n], in0=src_ap[:, c, :], in1=r[:, 0:n], op=ALU.mult),
                 reads=[src_T, r], writes=[tc])
            s.op("act", lambda e, c=c, tc=tc: e.activation(out=dst_ap[:, c, :], in_=tc[:, 0:n], func=AF.Identity,
                                                     bias=S[:, c:c + 1], scale=A[:, c:c + 1]),
                 reads=[tc, ms.mv], writes=[dst_T])

    def resid_add(self, y_T, y_ap, n, ms, which, x_T, x_ap, tmp_T):
        s = self.s
        G = ms.G1 if which == 1 else ms.G2
        r = self.rstd(y_T, y_ap, n)
        for c in range(8):
            tc = self.ntc()
            s.op("dve", lambda e, c=c, tc=tc: e.tensor_tensor(out=tc[:, 0:n], in0=y_ap[:, c, :], in1=r[:, 0:n], op=ALU.mult),
                 reads=[y_T, r], writes=[tc])
            s.op("dve", lambda e, c=c, tc=tc: e.scalar_tensor_tensor(out=x_ap[:, c, :], in0=tc[:, 0:n], scalar=G[:, c:c + 1],
                                                               in1=x_ap[:, c, :], op0=ALU.mult, op1=ALU.add),
                 reads=[tc, x_T, ms.mv], writes=[x_T])

    def prep_mod(self, mvec_ap, gvec_ap, name="mv"):
        s = self.s
        mv = s.tile([128, 8, 16], F32, name)
        s.dma("sp", mv[:, :, 0:6], mvec_ap.rearrange("(c p) n -> p c n", p=128), writes=[mv])
        s.dma("sp", mv[:, :, 6:10], gvec_ap.rearrange("(c p) n -> p c n", p=128), writes=[mv])
        s.op("dve", lambda e: e.scalar_tensor_tensor(out=mv[:, :, 10], in0=mv[:, :, 1], scalar=1.0, in1=mv[:, :, 6], op0=ALU.add, op1=ALU.mult),
             reads=[mv], writes=[mv])
        s.op("dve", lambda e: e.tensor_tensor(out=mv[:, :, 11], in0=mv[:, :, 2], in1=mv[:, :, 7], op=ALU.mult), reads=[mv], writes=[mv])
        s.op("dve", lambda e: e.scalar_tensor_tensor(out=mv[:, :, 12], in0=mv[:, :, 4], scalar=1.0, in1=mv[:, :, 8], op0=ALU.add, op1=ALU.mult),
             reads=[mv], writes=[mv])
        s.op("dve", lambda e: e.tensor_tensor(out=mv[:, :, 13], in0=mv[:, :, 5], in1=mv[:, :, 9], op=ALU.mult), reads=[mv], writes=[mv])
        ms = ModSet()
        ms.mv = mv
        ms.A0, ms.S0, ms.G1, ms.A2, ms.S2, ms.G2 = mv[:, :, 10], mv[:, :, 0], mv[:, :, 11], mv[:, :, 12], mv[:, :, 3], mv[:, :, 13]
        return ms

    def wstream(self, loaders, computes, L=2, pre=None):
        n = len(loaders)
        h = list(pre) if pre else []
        for i in range(n + L):
            if len(h) <= i < n:
                h.append(loaders[i]())
            j = i - L
            if 0 <= j < n:
                computes[j](*h[j])

    def mlp_loaders(self, w1_ap, w2_ap):
        ld = [(lambda jb=jb: self.wload(w1_ap[:, jb * 512:(jb + 1) * 512], 8, 512)) for jb in range(HID // 512)]
        ld += [(lambda db=db: self.wload(w2_ap[:, db * 128:(db + 1) * 128], 32, 128)) for db in range(D // 128)]
        return ld

    def mlp(self, blocks, h2_tiles, w1_ap, w2_ap, hid_T, out_fn, pre=None):
        s = self.s
        offs = [0]
        for b in blocks:
            offs.append(offs[-1] + b.w)

        def c1(jb):
            def f(wt, wv):
                for sub in range(4):
                    hc = jb * 4 + sub
                    for i, b in enumerate(blocks):
                        ps = self.nps()
                        for c in range(8):
                            s.op("pe", lambda e, c=c, ps=ps, i=i, b=b, sub=sub, wv=wv: e.matmul(
                                ps[:, 0:b.w], lhsT=wv[:, c, sub * 128:(sub + 1) * 128], rhs=h2_tiles[i][:, c, 0:b.w], start=(c == 0), stop=(c == 7)),
                                reads=[wt, h2_tiles[i]], writes=[ps])
                        dst = hid_T[:, hc, offs[i]:offs[i + 1]]
                        s.op("act", lambda e, ps=ps, dst=dst, b=b: e.activation(out=dst, in_=ps[:, 0:b.w], func=AF.Relu), reads=[ps], writes=[hid_T])
                        s.op("act", lambda e, dst=dst: e.activation(out=dst, in_=dst, func=AF.Square), reads=[hid_T], writes=[hid_T])
            return f

        def c2(db):
            def f(wt, wv):
                for i, b in enumerate(blocks):
                    ps = self.nps()
                    for c in range(32):
                        s.op("pe", lambda e, c=c, ps=ps, i=i, b=b, wv=wv: e.matmul(
                            ps[:, 0:b.w], lhsT=wv[:, c, :], rhs=hid_T[:, c, offs[i]:offs[i + 1]], start=(c == 0), stop=(c == 31)),
                            reads=[wt, hid_T], writes=[ps])
                    out_fn(i, db, ps)
            return f
        computes = [c1(jb) for jb in range(HID // 512)] + [c2(db) for db in range(D // 128)]
        self.wstream(self.mlp_loaders(w1_ap, w2_ap), computes, L=len(self.wb) - 1, pre=pre)


class ModSet:
    pass


class Blk:
    def __init__(self, w, ms, x_T, x_view, y_T):
        self.w, self.ms, self.x_T, self.x_view, self.y_T = w, ms, x_T, x_view, y_T


def _mk(nc, name, shape, dtype=F32, kind="ExternalInput"):
    return nc.dram_tensor(name, list(shape), dtype, kind=kind).ap()


def emit_finish(kx, blocks, w1_ap, w2_ap, tmp_T):
    s = kx.s
    lds = kx.mlp_loaders(w1_ap, w2_ap)
    pre = [lds[i]() for i in range(len(kx.wb) - 1)]
    for b in blocks:
        kx.resid_add(b.y_T, b.y_T[:, :, 0:b.w], b.w, b.ms, 1, b.x_T, b.x_view, tmp_T)
    with s.scope():
        h2_tiles = [s.tile([128, 8, b.w], BF16, "h2_%d" % i) for i, b in enumerate(blocks)]
        hid_T = s.tile([128, 32, sum(b.w for b in blocks)], BF16, "hid")
        for i, b in enumerate(blocks):
            kx.norm_mod(b.x_T, b.x_view, b.w, b.ms, 2, h2_tiles[i], h2_tiles[i][:, :, 0:b.w], tmp_T)

        def out_fn(i, dc, ps):
            b = blocks[i]
            s.op("act", lambda e: e.activation(out=b.y_T[:, dc, 0:b.w], in_=ps[:, 0:b.w], func=AF.Identity), reads=[ps], writes=[b.y_T])
        kx.mlp(blocks, h2_tiles, w1_ap, w2_ap, hid_T, out_fn, pre=pre)
    for b in blocks:
        kx.resid_add(b.y_T, b.y_T[:, :, 0:b.w], b.w, b.ms, 2, b.x_T, b.x_view, tmp_T)


def build_sc(NT=2048):
    nc = bass.Bass("TRN2", target_bir_lowering=False)
    xT = _mk(nc, "xT", [D, NT + 2])
    mvec = _mk(nc, "mvec", [D, 6])
    gvec = _mk(nc, "gvec", [D, 4])
    hmask = _mk(nc, "hmask", [128, 2])
    w_in = _mk(nc, "w_in", [D, 3 * D])
    convw = _mk(nc, "convw", [D, 3])
    w_out = _mk(nc, "w_out", [D, D])
    w1 = _mk(nc, "w1", [D, HID])
    w2 = _mk(nc, "w2", [HID, D])
    mvec3 = _mk(nc, "mvec3", [D, 6])
    gvec3 = _mk(nc, "gvec3", [D, 4])
    outT = _mk(nc, "outT", [D, NT], kind="ExternalOutput")
    h3T = _mk(nc, "h3T", [D, NT], kind="ExternalOutput")
    kx = KX(nc)
    s = kx.s
    ms = kx.prep_mod(mvec, gvec)
    ms3 = kx.prep_mod(mvec3, gvec3, "mv3")
    HWD = NT // 2
    bw = 512
    nblk = HWD // bw
    hm = s.tile([128, 2], F32, "hm")
    s.dma("sp", hm[:], hmask, writes=[hm])
    cw = s.tile([128, 8, 3], F32, "cw")
    s.dma("sp", cw[:], convw.rearrange("(c p) k -> p c k", p=128), writes=[cw])
    xh = s.tile([128, 8, HWD + 2], F32, "xh")
    tmp_T = None
    y_tiles = [s.tile([128, 8, bw], F32, "y%d" % i) for i in range(nblk)]
    w_in4 = w_in.rearrange("(c p) (t j n) -> p c t j n", p=128, t=3, j=8)

    class XV:
        pass
    for half in range(2):
        c0 = half * HWD
        s.dma("sp", xh[:], xT[:, c0:c0 + HWD + 2].rearrange("(c p) n -> p c n", p=128), writes=[xh])
        with s.scope():
            hT = [s.tile([128, 8, 342], BF16, "hT%d" % i) for i in range(3)]
            bcu = [s.tile([128, 3, HWD + 2], F32, "bcu%d" % i) for i in range(2)]
            acc = [s.tile([128, HWD], F32, "acc%d" % i) for i in range(2)]
            gT = [s.tile([128, 8, bw], BF16, "gT%d" % i) for i in range(nblk)]
            def ld_in(j):
                wt = kx.nwb()
                wv = wt.ap[:, 0:8 * 3 * 128].rearrange("p (c t n) -> p c t n", c=8, t=3)
                for t in range(3):
                    kx.stage_cast(wt, wv[:, :, t, :], w_in4[:, :, t, j, :], 1024)
                return wt, wv
            pre_in = [ld_in(0), ld_in(1)]
            for i in range(3):
                kx.norm_mod(xh, xh[:, :, i * 342:(i + 1) * 342], 342, ms, 0, hT[i], hT[i][:, :, 0:342], tmp_T)

            def cp_in(j, wt, wv, half=half, hT=hT, bcu=bcu, acc=acc, gT=gT):
                bc = bcu[j % 2]
                ac = acc[j % 2]
                for t in range(3):
                    for i in range(3):
                        ps = kx.nps()
                        for c in range(8):
                            s.op("pe", lambda e, c=c, ps=ps, i=i, t=t, wv=wv: e.matmul(
                                ps[:, 0:342], lhsT=wv[:, c, t, :], rhs=hT[i][:, c, :], start=(c == 0), stop=(c == 7)),
                                reads=[wt, hT[i]], writes=[ps])
                        s.op("act", lambda e, ps=ps, t=t, i=i, bc=bc: e.activation(out=bc[:, t, i * 342:(i + 1) * 342], in_=ps[:, 0:342], func=AF.Identity),
                             reads=[ps], writes=[bc])
                s.op("dve", lambda e, bc=bc: e.tensor_tensor(out=bc[:, 1, :], in0=bc[:, 1, :], in1=bc[:, 2, :], op=ALU.mult), reads=[bc], writes=[bc])
                hc_ = 0 if half == 0 else HWD + 1
                s.op("dve", lambda e, bc=bc, hc_=hc_, half=half: e.tensor_scalar(out=bc[:, 1, hc_:hc_ + 1], in0=bc[:, 1, hc_:hc_ + 1], scalar1=hm[:, half:half + 1],
                                                                    scalar2=None, op0=ALU.mult), reads=[bc, hm], writes=[bc])
                s.op("dve", lambda e, bc=bc, ac=ac, j=j: e.tensor_scalar(out=ac[:, :], in0=bc[:, 1, 0:HWD], scalar1=cw[:, j, 0:1], scalar2=None, op0=ALU.mult),
                     reads=[bc, cw], writes=[ac])
                for k in (1, 2):
                    s.op("dve", lambda e, bc=bc, ac=ac, j=j, k=k: e.scalar_tensor_tensor(out=ac[:, :], in0=bc[:, 1, k:k + HWD], scalar=cw[:, j, k:k + 1],
                                                                                      in1=ac[:, :], op0=ALU.mult, op1=ALU.add),
                         reads=[bc, cw, ac], writes=[ac])
                for tb in range(nblk):
                    s.op("dve", lambda e, bc=bc, ac=ac, j=j, tb=tb: e.tensor_tensor(out=gT[tb][:, j, :], in0=ac[:, tb * bw:(tb + 1) * bw],
                                                                                 in1=bc[:, 0, 1 + tb * bw:1 + (tb + 1) * bw], op=ALU.mult),
                         reads=[ac, bc], writes=[gT[tb]])
            kx.wstream([(lambda j=j: ld_in(j)) for j in range(8)], [(lambda wt, wv, j=j: cp_in(j, wt, wv)) for j in range(8)], L=2, pre=pre_in)
            for db in range(2):
                wt, wv = kx.wload(w_out[:, db * 512:(db + 1) * 512], 8, 512)
                for sub in range(4):
                    dc = db * 4 + sub
                    for tb in range(nblk):
                        ps = kx.nps()
                        for c in range(8):
                            s.op("pe", lambda e, c=c, ps=ps, tb=tb, sub=sub, wv=wv: e.matmul(
                                ps[:, 0:bw], lhsT=wv[:, c, sub * 128:(sub + 1) * 128], rhs=gT[tb][:, c, :], start=(c == 0), stop=(c == 7)),
                                reads=[wt, gT[tb]], writes=[ps])
                        s.op("act", lambda e, ps=ps, dc=dc, tb=tb: e.activation(out=y_tiles[tb][:, dc, :], in_=ps[:, 0:bw], func=AF.Identity),
                             reads=[ps], writes=[y_tiles[tb]])
        x_views = [xh[:, :, 1 + tb * bw:1 + (tb + 1) * bw] for tb in range(nblk)]
        blocks = [Blk(bw, ms, xh, x_views[tb], y_tiles[tb]) for tb in range(nblk)]
        emit_finish(kx, blocks, w1, w2, tmp_T)
        for tb in range(nblk):
            s.dma("act", outT[:, c0 + tb * bw:c0 + (tb + 1) * bw].rearrange("(c p) n -> p c n", p=128), x_views[tb], reads=[xh], is_output=True)
            kx.norm_mod(xh, x_views[tb], bw, ms3, 0, y_tiles[tb], y_tiles[tb][:, :, 0:bw], None)
            s.dma("act", h3T[:, c0 + tb * bw:c0 + (tb + 1) * bw].rearrange("(c p) n -> p c n", p=128), y_tiles[tb][:, :, 0:bw], reads=[y_tiles[tb]], is_output=True)
    s.emit()
    return nc


def _halo_T(xb, t0, n, lo=1, hi=1):
    L = xb.shape[0]
    out = np.zeros((D, lo + n + hi), np.float32)
    a = max(t0 - lo, 0)
    b = min(t0 + n + hi, L)
    out[:, a - (t0 - lo):b - (t0 - lo)] = xb[a:b].T
    return out


def run_sc(x, m_lat, g, w_in, convw, w_out, w1, w2, m_lat3, g3):
    NT = 2048
    nc = build_sc(NT)
    ins = []
    for core in range(NCORES):
        b, k = divmod(core, 4)
        t0 = k * NT
        hm = np.ones((128, 2), np.float32)
        if k == 0:
            hm[:, 0] = 0
        if k == 3:
            hm[:, 1] = 0
        ins.append({
            "xT": _halo_T(x[b], t0, NT), "mvec": np.ascontiguousarray(m_lat[b].reshape(6, D).T), "gvec": np.ascontiguousarray(g.T),
            "hmask": hm, "w_in": w_in, "convw": np.ascontiguousarray(convw.T), "w_out": w_out, "w1": w1, "w2": w2,
            "mvec3": np.ascontiguousarray(m_lat3[b].reshape(6, D).T), "gvec3": np.ascontiguousarray(g3.T)})
    res = run_bass_kernel_spmd(nc, ins, core_ids=list(range(NCORES)))
    out = np.empty_like(x)
    h3 = np.empty_like(x)
    for core in range(NCORES):
        b, k = divmod(core, 4)
        out[b, k * NT:(k + 1) * NT] = res.results[core]["outT"].T
        h3[b, k * NT:(k + 1) * NT] = res.results[core]["h3T"].T
    return out, h3


def build_mod():
    nc = bass.Bass("TRN2", target_bir_lowering=False)
    ccT = _mk(nc, "ccT", [D, 3])
    w = _mk(nc, "w", [D, 3072])
    bvec = _mk(nc, "bvec", [128, 24])
    outT = _mk(nc, "outT", [3072, 3], kind="ExternalOutput")
    s = Sched(nc)
    cc = s.tile([128, 8, 3], F32, "cc")
    bt = s.tile([128, 24], F32, "bt")
    ot = s.tile([128, 24, 3], F32, "ot")
    ps = [s.ptile(name="ps%d" % i) for i in range(4)]
    wb = [s.tile([128, 8, 512], F32, "wb%d" % i) for i in range(3)]
    s.dma("sp", cc[:], ccT.rearrange("(c p) n -> p c n", p=128), writes=[cc])
    s.dma("sp", bt[:], bvec, writes=[bt])
    s.op("act", lambda e: e.activation(out=cc[:], in_=cc[:], func=AF.Silu), reads=[cc], writes=[cc])
    for jb in range(6):
        wt = wb[jb % 3]
        s.dma("sp" if jb % 2 == 0 else "act", wt[:], w[:, jb * 512:(jb + 1) * 512].rearrange("(c p) n -> p c n", p=128), writes=[wt])
        for sub in range(4):
            oc = jb * 4 + sub
            p = ps[oc % 4]
            for c in range(8):
                s.op("pe", lambda e, c=c, p=p, sub=sub, wt=wt: e.matmul(p[:, 0:3], lhsT=wt[:, c, sub * 128:(sub + 1) * 128], rhs=cc[:, c, :],
                                                                    start=(c == 0), stop=(c == 7)), reads=[wt, cc], writes=[p])
            s.op("act", lambda e, p=p, oc=oc: e.activation(out=ot[:, oc, :], in_=p[:, 0:3], func=AF.Identity, bias=bt[:, oc:oc + 1], scale=1.0),
                 reads=[p, bt], writes=[ot])
    s.dma("sp", outT.rearrange("(c p) n -> p c n", p=128), ot[:], reads=[ot], is_output=True)
    s.emit()
    return nc


def run_mod(c, c_ctx, mod_w, mod_b):
    nc = build_mod()
    ccT = np.ascontiguousarray(np.concatenate([c, c_ctx[None]], 0).T)
    ins = []
    for core in range(NCORES):
        i, hf = divmod(core, 2)
        ins.append({"ccT": ccT, "w": np.ascontiguousarray(mod_w[i][:, hf * 3072:(hf + 1) * 3072]),
                    "bvec": np.ascontiguousarray(mod_b[i][hf * 3072:(hf + 1) * 3072].reshape(24, 128).T)})
    res = run_bass_kernel_spmd(nc, ins, core_ids=list(range(NCORES)))
    m = np.zeros((4, 3, 6144), np.float32)
    for core in range(NCORES):
        i, hf = divmod(core, 2)
        m[i, :, hf * 3072:(hf + 1) * 3072] = res.results[core]["outT"].T
    return m[:, 0:2], m[:, 2]


def _fft_consts():
    c = np.arange(128)
    ang = 2 * np.pi * np.outer(c, c) / 128.0
    fc_cos, fc_sin = np.cos(ang), np.sin(ang)
    f1 = np.concatenate([fc_cos, -fc_sin], 1).astype(np.float32)
    f2 = np.concatenate([fc_sin, fc_cos], 1).astype(np.float32)
    t2 = np.arange(64)[:, None, None]
    k1 = np.arange(128)[None, :, None]
    k2 = np.arange(64)[None, None, :]
    ang3 = 2 * np.pi * (k1 * t2 / 8192.0 + k2 * t2 / 64.0)
    g = np.stack([np.cos(ang3), np.sin(ang3)], 2).astype(np.float32)
    return f1, f2, np.ascontiguousarray(g.reshape(64, 128 * 2 * 64))


def build_fft():
    nc = bass.Bass("TRN2", target_bir_lowering=False)
    hg = _mk(nc, "hg", [2, 128, 8192])
    f1 = _mk(nc, "f1", [128, 256])
    f2 = _mk(nc, "f2", [128, 256])
    g3 = _mk(nc, "g3", [64, 128 * 128])
    fo = _mk(nc, "fo", [2, 64, 128 * 128], kind="ExternalOutput")
    s = Sched(nc)
    f1t = s.tile([128, 256], BF16, "f1t")
    f2t = s.tile([128, 256], BF16, "f2t")
    g3t = s.tile([64, 128, 2, 64], BF16, "g3t")
    stg = [s.tile([128, 2048], F32, "stage%d" % i) for i in range(2)]
    stgi = [0]

    def stage_cast(dst_T, dst_ap, src_ap, np_, n):
        st = stg[stgi[0] % 2]
        s.dma("sp" if stgi[0] % 2 == 0 else "act", st[0:np_, 0:n], src_ap, writes=[st])
        stgi[0] += 1
        s.op("pool", lambda e: e.tensor_copy(out=dst_ap, in_=st[0:np_, 0:n]), reads=[st], writes=[dst_T])
    stage_cast(f1t, f1t[:], f1, 128, 256)
    stage_cast(f2t, f2t[:], f2, 128, 256)
    for q in range(8):
        stage_cast(g3t, g3t[:, q * 16:(q + 1) * 16, :, :].rearrange("p a r k -> p (a r k)"), g3[:, q * 2048:(q + 1) * 2048], 64, 2048)
    ps = [s.ptile(name="ps%d" % i) for i in range(8)]
    psi = [0]

    def nps():
        t = ps[psi[0]]
        psi[0] = (psi[0] + 1) % 8
        return t
    xin = s.tile([128, 64, 128], BF16, "xin")
    B = s.tile([128, 64, 2, 128], BF16, "B")
    A = s.tile([64, 2, 128, 128], BF16, "A")
    ob = [s.tile([64, 16, 128], F32, "ob%d" % i) for i in range(2)]
    for g in range(2):
        for q in range(4):
            stage_cast(xin, xin[:, q * 16:(q + 1) * 16, :].rearrange("p a b -> p (a b)"), hg[g][:, q * 2048:(q + 1) * 2048], 128, 2048)
        for tp in range(32):
            p = nps()
            for u in range(2):
                t2 = tp * 2 + u
                s.op("pe", lambda e, p=p, u=u, t2=t2: e.matmul(p[:, u * 256:(u + 1) * 256], lhsT=xin[:, t2, :], rhs=f1t[:], start=True, stop=True),
                     reads=[xin, f1t], writes=[p])
            eng = "act" if tp % 2 == 0 else "dve"
            dst = B[:, tp * 2:tp * 2 + 2, :, :].rearrange("p a r m -> p (a r m)")
            if eng == "act":
                s.op("act", lambda e, p=p, dst=dst: e.activation(out=dst, in_=p[:], func=AF.Identity), reads=[p], writes=[B])
            else:
                s.op("dve", lambda e, p=p, dst=dst: e.tensor_copy(out=dst, in_=p[:]), reads=[p], writes=[B])
        for mp in range(64):
            p = nps()
            for u in range(2):
                m = mp * 2 + u
                s.op("pe", lambda e, p=p, u=u, m=m: e.matmul(p[0:64, u * 256:(u + 1) * 256], lhsT=B[:, :, 0, m], rhs=f1t[:], start=True, stop=False),
                     reads=[B, f1t], writes=[p])
                s.op("pe", lambda e, p=p, u=u, m=m: e.matmul(p[0:64, u * 256:(u + 1) * 256], lhsT=B[:, :, 1, m], rhs=f2t[:], start=False, stop=True),
                     reads=[B, f2t], writes=[p])
            for u in range(2):
                m = mp * 2 + u
                src = p[0:64, u * 256:(u + 1) * 256].rearrange("p (r k) -> p r k", r=2)
                if u == 0:
                    s.op("act", lambda e, src=src, m=m: e.activation(out=A[:, :, :, m], in_=src, func=AF.Identity), reads=[p], writes=[A])
                else:
                    s.op("dve", lambda e, src=src, m=m: e.tensor_copy(out=A[:, :, :, m], in_=src), reads=[p], writes=[A])
        for kb in range(8):
            o = ob[kb % 2]
            for kq in range(4):
                p = nps()
                for u in range(4):
                    k1 = kb * 16 + kq * 4 + u
                    s.op("pe", lambda e, p=p, u=u, k1=k1: e.matmul(p[0:64, u * 128:(u + 1) * 128], lhsT=g3t[:, k1, 0, :], rhs=A[:, 0, k1, :], start=True, stop=False),
                         reads=[g3t, A], writes=[p])
                    s.op("pe", lambda e, p=p, u=u, k1=k1: e.matmul(p[0:64, u * 128:(u + 1) * 128], lhsT=g3t[:, k1, 1, :], rhs=A[:, 1, k1, :], start=False, stop=True),
                         reads=[g3t, A], writes=[p])
                dst = o[:, kq * 4:(kq + 1) * 4, :].rearrange("p a m -> p (a m)")
                if kq % 2 == 0:
                    s.op("act", lambda e, p=p, dst=dst: e.activation(out=dst, in_=p[0:64, :], func=AF.Identity), reads=[p], writes=[o])
                else:
                    s.op("dve", lambda e, p=p, dst=dst: e.tensor_copy(out=dst, in_=p[0:64, :]), reads=[p], writes=[o])
            s.dma("sp", fo[g][:, kb * 16 * 128:(kb + 1) * 16 * 128], o[:].rearrange("p a m -> p (a m)"), reads=[o], is_output=True)
    s.emit()
    return nc


def run_fft(h):
    nc = build_fft()
    f1, f2, g3 = _fft_consts()
    ins = []
    for core in range(NCORES):
        b, gp = divmod(core, 4)
        hb = np.asarray(h[b][:, gp * 256:(gp + 1) * 256]).reshape(128, 64, 2, 128)
        hgc = np.ascontiguousarray(hb.transpose(2, 3, 1, 0)).reshape(2, 128, 8192)
        ins.append({"hg": hgc.astype(np.float32), "f1": f1, "f2": f2, "g3": g3})
    res = run_bass_kernel_spmd(nc, ins, core_ids=list(range(NCORES)))
    out = np.empty((BATCH, SEQ, D), np.float32)
    for core in range(NCORES):
        b, gp = divmod(core, 4)
        fo = res.results[core]["fo"].reshape(2, 64, 128, 128)
        out[b, :, gp * 256:(gp + 1) * 256] = fo.transpose(1, 2, 0, 3).reshape(8192, 256)
    return out


NA_ROWS_LOCAL = 39


def build_na(NT=2048):
    nc = bass.Bass("TRN2", target_bir_lowering=False)
    xhT = _mk(nc, "xhT", [D, NA_ROWS_LOCAL * 64])
    ctxT = _mk(nc, "ctxT", [D, CTX])
    mvec = _mk(nc, "mvec", [D, 6])
    mvec_c = _mk(nc, "mvec_c", [D, 6])
    gvec = _mk(nc, "gvec", [D, 4])
    w_qkv = _mk(nc, "w_qkv", [D, 3 * D])
    w_out = _mk(nc, "w_out", [D, D])
    w1 = _mk(nc, "w1", [D, HID])
    w2 = _mk(nc, "w2", [HID, D])
    btab = _mk(nc, "btab", [8, 2, 128, 3840])
    identd = _mk(nc, "ident", [128, 128])
    outT = _mk(nc, "outT", [D, NT], kind="ExternalOutput")
    kx = KX(nc)
    s = kx.s
    ms = kx.prep_mod(mvec, gvec)
    msc = kx.prep_mod(mvec_c, gvec, "mvc")
    ident = s.tile([128, 128], BF16, "ident")
    kx.stage_cast(ident, ident[:], identd, 128 * 128 // 128)
    HWD, bw, nblk = 1024, 512, 2
    WIN = 23 * 64
    xres = s.tile([128, 8, HWD], F32, "xres")
    y_tiles = [s.tile([128, 8, bw], F32, "y%d" % i) for i in range(nblk)]
    hcT = s.tile([128, 8, CTX], BF16, "hcT")
    s.dma("sp", y_tiles[0][:, :, 0:CTX], ctxT.rearrange("(c p) n -> p c n", p=128), writes=[y_tiles[0]])
    kx.norm_mod(y_tiles[0], y_tiles[0][:, :, 0:CTX], CTX, msc, 0, hcT, hcT[:, :, :], None)
    w_qkv4 = w_qkv.rearrange("(c p) (t j n) -> p c t j n", p=128, t=3, j=8)
    for half in range(2):
        wc0 = half * HWD
        s.dma("sp", xres[:], xhT[:, 256 + wc0:256 + wc0 + HWD].rearrange("(c p) n -> p c n", p=128), writes=[xres])
        with s.scope():
            hT = s.tile([128, 8, WIN], BF16, "hT")
            attT = [s.tile([128, 8, bw], BF16, "attT%d" % i) for i in range(nblk)]
            qT = s.tile([128, HWD], BF16, "qT")
            kT = s.tile([128, WIN], BF16, "kT")
            kcT = s.tile([128, CTX], BF16, "kcT")
            vt = s.tile([128, 12, 2, 65], BF16, "vt")
            vct = s.tile([128, 2, 2, 65], BF16, "vct")
            att_hp = s.tile([128, 8, 128], BF16, "att_hp")
            bt = s.tile([128, 3, 2, 5, 128], F32, "bt")
            stA = [s.tile([128, 512], F32, "stA%d" % i) for i in range(2)]
            stB = [s.tile([128, 128], F32, "stB%d" % i) for i in range(2)]
            pA = [s.tile([128, 512], BF16, "pA%d" % i) for i in range(2)]
            pB = [s.tile([128, 128], BF16, "pB%d" % i) for i in range(2)]
            pC = [s.tile([128, 256], BF16, "pC%d" % i) for i in range(2)]
            rc = [s.tile([128, 1], F32, "rc%d" % i) for i in range(2)]
            s.op("pool", lambda e: e.memset(vt[:, :, :, 64:65], 1.0), writes=[vt])
            s.op("pool", lambda e: e.memset(vct[:, :, :, 64:65], 1.0), writes=[vct])
            for j, (a, w) in enumerate(((0, 512), (512, 512), (1024, WIN - 1024))):
                yt = y_tiles[j % 2]
                s.dma("sp", yt[:, :, 0:w], xhT[:, wc0 + a:wc0 + a + w].rearrange("(c p) n -> p c n", p=128), writes=[yt])
                kx.norm_mod(yt, yt[:, :, 0:w], w, ms, 0, hT, hT[:, :, a:a + w], None)
            itc = [0]

            def ld_hp(hp):
                wt = kx.nwb()
                wv = wt.ap[:, 0:8 * 3 * 128].rearrange("p (c t n) -> p c t n", c=8, t=3)
                for t in range(3):
                    kx.stage_cast(wt, wv[:, :, t, :], w_qkv4[:, :, t, hp, :], 1024)
                return wt, wv

            def cp_hp(hp, wt, wv, half=half, hT=hT, attT=attT, qT=qT, kT=kT, kcT=kcT, vt=vt, vct=vct, att_hp=att_hp, bt=bt,
                      stA=stA, stB=stB, pA=pA, pB=pB, pC=pC, rc=rc):
                it = itc[0]
                s.dma("act", bt[:].rearrange("p a b c q -> p (a b c q)"), btab[hp, half], writes=[bt])
                for blk in range(2):
                    ps = kx.nps()
                    for c in range(8):
                        s.op("pe", lambda e, c=c, ps=ps, blk=blk, wv=wv: e.matmul(
                            ps[:, 0:512], lhsT=wv[:, c, 0, :], rhs=hT[:, c, 256 + blk * 512:256 + (blk + 1) * 512], start=(c == 0), stop=(c == 7)),
                            reads=[wt, hT], writes=[ps])
                    s.op("act", lambda e, ps=ps, blk=blk: e.activation(out=qT[:, blk * 512:(blk + 1) * 512], in_=ps[:, 0:512], func=AF.Identity),
                         reads=[ps], writes=[qT])
                for (a, w) in ((0, 512), (512, 512), (1024, WIN - 1024)):
                    ps = kx.nps()
                    for c in range(8):
                        s.op("pe", lambda e, c=c, ps=ps, a=a, w=w, wv=wv: e.matmul(
                            ps[:, 0:w], lhsT=wv[:, c, 1, :], rhs=hT[:, c, a:a + w], start=(c == 0), stop=(c == 7)),
                            reads=[wt, hT], writes=[ps])
                    s.op("dve", lambda e, ps=ps, a=a, w=w: e.tensor_copy(out=kT[:, a:a + w], in_=ps[:, 0:w]), reads=[ps], writes=[kT])
                ps = kx.nps()
                for c in range(8):
                    s.op("pe", lambda e, c=c, ps=ps, wv=wv: e.matmul(ps[:, 0:CTX], lhsT=wv[:, c, 1, :], rhs=hcT[:, c, :], start=(c == 0), stop=(c == 7)),
                         reads=[wt, hcT], writes=[ps])
                s.op("act", lambda e, ps=ps: e.activation(out=kcT[:, :], in_=ps[:, 0:CTX], func=AF.Identity), reads=[ps], writes=[kcT])
                for tg in range(3):
                    ps = kx.nps()
                    for u in range(4):
                        tcn = tg * 4 + u
                        ntok = 64 if tcn == 11 else 128
                        for c in range(8):
                            s.op("pe", lambda e, c=c, ps=ps, u=u, tcn=tcn, ntok=ntok, wv=wv: e.matmul(
                                ps[0:ntok, u * 128:(u + 1) * 128], lhsT=hT[:, c, tcn * 128:tcn * 128 + ntok], rhs=wv[:, c, 2, :], start=(c == 0), stop=(c == 7)),
                                reads=[wt, hT], writes=[ps])
                    nfull = 4 if tg < 2 else 3
                    s.op("act", lambda e, ps=ps, tg=tg, nfull=nfull: e.activation(
                        out=vt[:, tg * 4:tg * 4 + nfull, :, 0:64], in_=ps[:, 0:nfull * 128].rearrange("p (a b d) -> p a b d", a=nfull, b=2), func=AF.Identity),
                        reads=[ps], writes=[vt])
                    if tg == 2:
                        s.op("act", lambda e, ps=ps: e.activation(out=vt[0:64, 11, :, 0:64], in_=ps[0:64, 384:512].rearrange("p (b d) -> p b d", b=2), func=AF.Identity),
                             reads=[ps], writes=[vt])
                ps = kx.nps()
                for u in range(2):
                    for c in range(8):
                        s.op("pe", lambda e, c=c, ps=ps, u=u, wv=wv: e.matmul(
                            ps[:, u * 128:(u + 1) * 128], lhsT=hcT[:, c, u * 128:(u + 1) * 128], rhs=wv[:, c, 2, :], start=(c == 0), stop=(c == 7)),
                            reads=[wt, hcT], writes=[ps])
                s.op("dve", lambda e, ps=ps: e.tensor_copy(out=vct[:, :, :, 0:64], in_=ps[:, 0:256].rearrange("p (a b d) -> p a b d", a=2, b=2)),
                     reads=[ps], writes=[vct])
                units = [(h2, i) for h2 in range(2) for i in range(8)]
                pend = []

                def front(h2, i, it):
                    pb = 64 * h2
                    if True:
                        cls = (0 if i == 0 else 1 if i == 1 else 2) if half == 0 else (1 if i == 6 else 2 if i == 7 else 0)
                        a_, b_, c_ = stA[it % 2], stB[it % 2], rc[it % 2]
                        pa, pb_, pc = pA[it % 2], pB[it % 2], pC[it % 2]
                        psA = kx.nps()
                        for ch in range(4):
                            s.op("pe", lambda e, psA=psA, ch=ch, i=i, pb=pb: e.matmul(
                                psA[:, ch * 128:(ch + 1) * 128], lhsT=kT[pb:pb + 64, 128 * (i + ch):128 * (i + ch + 1)], rhs=qT[pb:pb + 64, 128 * i:128 * (i + 1)],
                                start=True, stop=True), reads=[kT, qT], writes=[psA])
                        psB = kx.nps()
                        s.op("pe", lambda e, psB=psB, i=i, pb=pb: e.matmul(
                            psB[0:64, 0:128], lhsT=kT[pb:pb + 64, 128 * (i + 4):128 * (i + 4) + 64], rhs=qT[pb:pb + 64, 128 * i:128 * (i + 1)],
                            start=True, stop=True), reads=[kT, qT], writes=[psB])
                        for cc in range(2):
                            s.op("pe", lambda e, psB=psB, cc=cc, i=i, pb=pb: e.matmul(
                                psB[:, 128 + cc * 128:256 + cc * 128], lhsT=kcT[pb:pb + 64, cc * 128:(cc + 1) * 128], rhs=qT[pb:pb + 64, 128 * i:128 * (i + 1)],
                                start=True, stop=True), reads=[kcT, qT], writes=[psB])
                        s.op("dve", lambda e, psA=psA, a_=a_, cls=cls, h2=h2: e.scalar_tensor_tensor(
                            out=a_[:, :], in0=psA[:, :], scalar=0.125, in1=bt[:, cls, h2, 0:4, :].rearrange("p a q -> p (a q)"), op0=ALU.mult, op1=ALU.add),
                            reads=[psA, bt], writes=[a_])
                        s.op("act", lambda e, a_=a_, pa=pa: e.activation(out=pa[:, :], in_=a_[:, :], func=AF.Exp), reads=[a_], writes=[pa])
                        s.op("dve", lambda e, psB=psB, b_=b_, cls=cls, h2=h2: e.scalar_tensor_tensor(
                            out=b_[0:64, :], in0=psB[0:64, 0:128], scalar=0.125, in1=bt[0:64, cls, h2, 4, :], op0=ALU.mult, op1=ALU.add),
                            reads=[psB, bt], writes=[b_])
                        s.op("act", lambda e, b_=b_, pb_=pb_: e.activation(out=pb_[0:64, :], in_=b_[0:64, :], func=AF.Exp), reads=[b_], writes=[pb_])
                        s.op("act", lambda e, psB=psB, pc=pc: e.activation(out=pc[:, :], in_=psB[:, 128:384], func=AF.Exp, scale=0.125), reads=[psB], writes=[pc])
                    return (h2, i, pa, pb_, pc, c_)

                def back(h2, i, pa, pb_, pc, c_):
                    if True:
                        psO = kx.nps()
                        for ch in range(4):
                            s.op("pe", lambda e, psO=psO, ch=ch, i=i, h2=h2, pa=pa: e.matmul(
                                psO[:, 0:65], lhsT=pa[:, ch * 128:(ch + 1) * 128], rhs=vt[:, i + ch, h2, :], start=(ch == 0), stop=False),
                                reads=[pa, vt], writes=[psO])
                        s.op("pe", lambda e, psO=psO, i=i, h2=h2, pb_=pb_: e.matmul(
                            psO[:, 0:65], lhsT=pb_[0:64, :], rhs=vt[0:64, i + 4, h2, :], start=False, stop=False), reads=[pb_, vt], writes=[psO])
                        for cc in range(2):
                            s.op("pe", lambda e, psO=psO, cc=cc, h2=h2, pc=pc: e.matmul(
                                psO[:, 0:65], lhsT=pc[:, cc * 128:(cc + 1) * 128], rhs=vct[:, cc, h2, :], start=False, stop=(cc == 1)),
                                reads=[pc, vct], writes=[psO])
                        s.op("dve", lambda e, psO=psO, c_=c_: e.reciprocal(out=c_[:, :], in_=psO[:, 64:65]), reads=[psO], writes=[c_])
                        s.op("dve", lambda e, psO=psO, c_=c_, i=i, h2=h2: e.tensor_scalar(
                            out=att_hp[:, i, h2 * 64:(h2 + 1) * 64], in0=psO[:, 0:64], scalar1=c_[:, 0:1], scalar2=None, op0=ALU.mult),
                            reads=[psO, c_], writes=[att_hp])
                for (h2, i) in units:
                    pend.append(front(h2, i, it))
                    it += 1
                    if len(pend) > 1:
                        back(*pend.pop(0))
                while pend:
                    back(*pend.pop(0))
                for ig in range(2):
                    pst = kx.nps()
                    pv = pst.ap.bitcast(BF16)
                    for u in range(4):
                        i = ig * 4 + u
                        s.op("pe", lambda e, pv=pv, u=u, i=i: e.transpose(out=pv[:, u * 128:(u + 1) * 128], in_=att_hp[:, i, :], identity=ident[:]),
                             reads=[att_hp, ident], writes=[pst])
                    s.op("dve", lambda e, pv=pv, ig=ig, hp=hp: e.tensor_copy(out=attT[ig][:, hp, :], in_=pv[:, 0:512]), reads=[pst], writes=[attT[ig]])
                itc[0] = it
            kx.wstream([(lambda hp=hp: ld_hp(hp)) for hp in range(8)], [(lambda wt, wv, hp=hp: cp_hp(hp, wt, wv)) for hp in range(8)], L=2)
            for db in range(2):
                wt, wv = kx.wload(w_out[:, db * 512:(db + 1) * 512], 8, 512)
                for sub in range(4):
                    dc = db * 4 + sub
                    for tb in range(nblk):
                        ps = kx.nps()
                        for c in range(8):
                            s.op("pe", lambda e, c=c, ps=ps, tb=tb, sub=sub, wv=wv: e.matmul(
                                ps[:, 0:bw], lhsT=wv[:, c, sub * 128:(sub + 1) * 128], rhs=attT[tb][:, c, :], start=(c == 0), stop=(c == 7)),
                                reads=[wt, attT[tb]], writes=[ps])
                        s.op("act", lambda e, ps=ps, dc=dc, tb=tb: e.activation(out=y_tiles[tb][:, dc, :], in_=ps[:, 0:bw], func=AF.Identity),
                             reads=[ps], writes=[y_tiles[tb]])
        x_views = [xres[:, :, tb * bw:(tb + 1) * bw] for tb in range(nblk)]
        blocks = [Blk(bw, ms, xres, x_views[tb], y_tiles[tb]) for tb in range(nblk)]
        emit_finish(kx, blocks, w1, w2, None)
        for tb in range(nblk):
            s.dma("act", outT[:, half * HWD + tb * bw:half * HWD + (tb + 1) * bw].rearrange("(c p) n -> p c n", p=128), x_views[tb], reads=[xres], is_output=True)
    s.emit()
    return nc


def _na_rowmap(kq):
    if kq == 0:
        return [5, 6, 7, -1] + list(range(0, 35))
    if kq == 3:
        return [92 + j for j in range(36)] + [120, 121, -1]
    return [32 * kq - 4 + j for j in range(NA_ROWS_LOCAL)]


def _na_tables(rpb, kq):
    NEG = -30000.0
    rm = _na_rowmap(kq)
    tabs = {}
    qc = np.arange(64)
    kc = np.arange(64)
    c0 = np.clip(qc - 8, 0, 48)
    colok = (kc[:, None] >= c0[None, :]) & (kc[:, None] < c0[None, :] + 16)
    dc = kc[:, None] - qc[None, :] + 15
    dcc = np.clip(dc, 0, 30)
    for P in (0, 1, 2, 14, 15):
        tab = np.full((16, 640, 128), NEG, np.float32)
        seen = set()
        for j in range(9):
            g = rm[2 * P + j]
            if g < 0 or g in seen:
                continue
            seen.add(g)
            for u in range(2):
                qr = 32 * kq + 2 * P + u
                r0 = min(max(qr - 4, 0), 120)
                if not (r0 <= g < r0 + 8):
                    continue
                dr = g - qr + 7
                vals = rpb[:, dr, :][:, dcc]
                blk = np.where(colok[None], vals, NEG)
                tab[:, j * 64:(j + 1) * 64, u * 64:(u + 1) * 64] = blk
        tabs[P] = tab.reshape(16, 5, 128, 128)
    out = np.empty((8, 2, 128, 3, 2, 5, 128), np.float32)
    for half, Ps in ((0, (0, 1, 2)), (1, (2, 14, 15))):
        for ci, P in enumerate(Ps):
            t = tabs[P].reshape(8, 2, 5, 128, 128)
            out[:, half, :, ci] = t.transpose(0, 3, 1, 2, 4)
    return out.reshape(8, 2, 128, 3840)


def run_na(x, ctx, m_lat, m_ctx, g, w_qkv, rpb, w_out, w1, w2):
    NT = 2048
    nc = build_na(NT)
    ident = np.eye(128, dtype=np.float32)
    tabs = [_na_tables(rpb, kq) for kq in range(4)]
    ins = []
    for core in range(NCORES):
        b, kq = divmod(core, 4)
        rm = _na_rowmap(kq)
        xg = x[b].reshape(128, 64, D)
        xh = np.zeros((NA_ROWS_LOCAL, 64, D), np.float32)
        for j, gr in enumerate(rm):
            if gr >= 0:
                xh[j] = xg[gr]
        ins.append({
            "xhT": np.ascontiguousarray(xh.reshape(-1, D).T), "ctxT": np.ascontiguousarray(ctx[b].T),
            "mvec": np.ascontiguousarray(m_lat[b].reshape(6, D).T), "mvec_c": np.ascontiguousarray(m_ctx.reshape(6, D).T),
            "gvec": np.ascontiguousarray(g.T), "w_qkv": w_qkv, "w_out": w_out, "w1": w1, "w2": w2, "btab": tabs[kq], "ident": ident})
    res = run_bass_kernel_spmd(nc, ins, core_ids=list(range(NCORES)))
    out = np.empty_like(x)
    for core in range(NCORES):
        b, kq = divmod(core, 4)
        out[b, kq * NT:(kq + 1) * NT] = res.results[core]["outT"].T
    return out


SSD_IN = 6208


def build_ssd_a(NT=2048, NC_=64):
    nc = bass.Bass("TRN2", target_bir_lowering=False)
    xT = _mk(nc, "xT", [D, NT + 2])
    cT = _mk(nc, "cT", [D, NC_ + 2])
    mvec = _mk(nc, "mvec", [D, 6])
    mvec_c = _mk(nc, "mvec_c", [D, 6])
    gvec = _mk(nc, "gvec", [D, 4])
    hmask = _mk(nc, "hmask", [128, 4])
    w_in = _mk(nc, "w_in", [D, SSD_IN])
    convw = _mk(nc, "convw", [128, 32, 4])
    dtb = _mk(nc, "dtb", [64, 1])
    NTOT = NT + NC_
    zT = _mk(nc, "zT", [2048, NTOT], kind="ExternalOutput")
    xbcT = _mk(nc, "xbcT", [4096, NTOT], kind="ExternalOutput")
    dtT = _mk(nc, "dtT", [64, NTOT], kind="ExternalOutput")
    kx = KX(nc)
    s = kx.s
    ms = kx.prep_mod(mvec, gvec)
    msc = kx.prep_mod(mvec_c, gvec, "mvc")
    hm = s.tile([128, 4], F32, "hm")
    s.dma("sp", hm[:], hmask, writes=[hm])
    cw = s.tile([128, 32, 4], F32, "cw")
    s.dma("sp", cw[:], convw, writes=[cw])
    db_ = s.tile([64, 1], F32, "dtb")
    s.dma("sp", db_[:], dtb, writes=[db_])
    HWD = NT // 2
    xin = [s.tile([128, 8, 342], F32, "xin%d" % i) for i in range(2)]
    for grp in range(2):
        with s.scope():
            blks = []
            c0 = grp * HWD
            srcs = [(xT[:, c0 + i * 342:c0 + (i + 1) * 342], 342, ms) for i in range(3)]
            if grp == 1:
                srcs.append((cT[:, :], NC_ + 2, msc))
            W = sum(w for _, w, _ in srcs)
            for i, (src, w, m) in enumerate(srcs):
                xt = xin[i % 2]
                s.dma("sp", xt[:, :, 0:w], src.rearrange("(c p) n -> p c n", p=128), writes=[xt])
                ht = s.tile([128, 8, w], BF16, "hT%d" % i)
                kx.norm_mod(xt, xt[:, :, 0:w], w, m, 0, ht, ht[:, :, 0:w], None)
                blks.append((ht, w))
            pre = [s.tile([128, W], F32, "pre%d" % i) for i in range(2)]
            acc = [s.tile([128, W], F32, "acc%d" % i) for i in range(2)]
            segs = [(0, HWD, 0 if grp == 0 else None, 1 if grp == 1 else None, c0)]
            if grp == 1:
                segs.append((HWD + 2, NC_, 2, 3, NT))
            nchunks = 49
            def ld_a(jb):
                ncols = 512 if jb < 12 else 64
                return kx.wload(w_in[:, jb * 512:jb * 512 + ncols], 8, ncols)

            def cp_a(jb, wt, wv, blks=blks, pre=pre, acc=acc, segs=segs):
                ncols = 512 if jb < 12 else 64
                for sub in range(ncols // 128 if ncols >= 128 else 1):
                    j = jb * 4 + sub
                    mrows = 128 if j < 48 else 64
                    pr = pre[j % 2]
                    ac = acc[j % 2]
                    col = 0
                    for (ht, w) in blks:
                        ps = kx.nps()
                        for c in range(8):
                            s.op("pe", lambda e, c=c, ps=ps, ht=ht, w=w, sub=sub, wv=wv, mrows=mrows: e.matmul(
                                ps[0:mrows, 0:w], lhsT=wv[:, c, sub * 128:sub * 128 + mrows], rhs=ht[:, c, 0:w], start=(c == 0), stop=(c == 7)),
                                reads=[wt, ht], writes=[ps])
                        if j < 16:
                            s.op("act", lambda e, ps=ps, pr=pr, col=col, w=w: e.activation(out=pr[:, col:col + w], in_=ps[:, 0:w], func=AF.Identity),
                                 reads=[ps], writes=[pr])
                        elif j < 48:
                            s.op("act", lambda e, ps=ps, pr=pr, col=col, w=w: e.activation(out=pr[:, col:col + w], in_=ps[:, 0:w], func=AF.Identity),
                                 reads=[ps], writes=[pr])
                        else:
                            s.op("act", lambda e, ps=ps, pr=pr, col=col, w=w: e.activation(out=pr[0:64, col:col + w], in_=ps[0:64, 0:w], func=AF.Exp, bias=db_[:, 0:1], scale=1.0),
                                 reads=[ps, db_], writes=[pr])
                        col += w
                    for (st, ow, lm, rm, oc) in segs:
                        if j < 16:
                            s.dma("act", zT[j * 128:(j + 1) * 128, oc:oc + ow], pr[:, st + 1:st + 1 + ow], reads=[pr], is_output=True)
                        elif j < 48:
                            jc = j - 16
                            if lm is not None:
                                s.op("dve", lambda e, pr=pr, st=st, lm=lm: e.tensor_scalar(out=pr[:, st:st + 1], in0=pr[:, st:st + 1], scalar1=hm[:, lm:lm + 1], scalar2=None, op0=ALU.mult),
                                     reads=[pr, hm], writes=[pr])
                            if rm is not None:
                                s.op("dve", lambda e, pr=pr, st=st, ow=ow, rm=rm: e.tensor_scalar(out=pr[:, st + ow + 1:st + ow + 2], in0=pr[:, st + ow + 1:st + ow + 2],
                                                                                                scalar1=hm[:, rm:rm + 1], scalar2=None, op0=ALU.mult),
                                     reads=[pr, hm], writes=[pr])
                            s.op("dve", lambda e, pr=pr, ac=ac, st=st, ow=ow, jc=jc: e.tensor_scalar(out=ac[:, st:st + ow], in0=pr[:, st:st + ow], scalar1=cw[:, jc, 0:1], scalar2=None, op0=ALU.mult),
                                 reads=[pr, cw], writes=[ac])
                            for k in (1, 2):
                                s.op("dve", lambda e, pr=pr, ac=ac, st=st, ow=ow, jc=jc, k=k: e.scalar_tensor_tensor(
                                    out=ac[:, st:st + ow], in0=pr[:, st + k:st + k + ow], scalar=cw[:, jc, k:k + 1], in1=ac[:, st:st + ow], op0=ALU.mult, op1=ALU.add),
                                    reads=[pr, cw, ac], writes=[ac])
                            s.op("act", lambda e, ac=ac, st=st, ow=ow, jc=jc: e.activation(out=ac[:, st:st + ow], in_=ac[:, st:st + ow], func=AF.Silu, bias=cw[:, jc, 3:4], scale=1.0),
                                 reads=[ac, cw], writes=[ac])
                            s.dma("act", xbcT[jc * 128:(jc + 1) * 128, oc:oc + ow], ac[:, st:st + ow], reads=[ac], is_output=True)
                        else:
                            s.op("act", lambda e, pr=pr, ac=ac, st=st, ow=ow: e.activation(out=ac[0:64, st:st + ow], in_=pr[0:64, st + 1:st + 1 + ow], func=AF.Ln, bias=1.0, scale=1.0),
                                 reads=[pr], writes=[ac])
                            s.dma("act", dtT[:, oc:oc + ow], ac[0:64, st:st + ow], reads=[ac], is_output=True)
            kx.wstream([(lambda jb=jb: ld_a(jb)) for jb in range(13)], [(lambda wt, wv, jb=jb: cp_a(jb, wt, wv)) for jb in range(13)], L=2)
    s.emit()
    return nc


def run_ssd_a(x, ctx, m_lat, m_ctx, g, w_in, conv_w, conv_b, dt_bias):
    NT, NC_ = 2048, 64
    nc = build_ssd_a(NT, NC_)
    cw = np.concatenate([conv_w.T, conv_b[:, None]], 1).astype(np.float32)
    cw = np.ascontiguousarray(cw.reshape(32, 128, 4).transpose(1, 0, 2))
    dtb = np.ascontiguousarray(dt_bias.reshape(64, 1).astype(np.float32))
    ins = []
    for core in range(NCORES):
        b, k = divmod(core, 4)
        hm = np.ones((128, 4), np.float32)
        if k == 0:
            hm[:, 0] = 0
            hm[:, 2] = 0
        if k == 3:
            hm[:, 1] = 0
            hm[:, 3] = 0
        ins.append({"xT": _halo_T(x[b], k * NT, NT), "cT": _halo_T(ctx[b], k * NC_, NC_),
                    "mvec": np.ascontiguousarray(m_lat[b].reshape(6, D).T), "mvec_c": np.ascontiguousarray(m_ctx.reshape(6, D).T),
                    "gvec": np.ascontiguousarray(g.T), "hmask": hm, "w_in": w_in, "convw": cw, "dtb": dtb})
    res = run_bass_kernel_spmd(nc, ins, core_ids=list(range(NCORES)))
    z = np.empty((BATCH, SEQ + CTX, 2048), np.float32)
    xbc = np.empty((BATCH, SEQ + CTX, 4096), np.float32)
    dt = np.empty((BATCH, SEQ + CTX, 64), np.float32)
    for core in range(NCORES):
        b, k = divmod(core, 4)
        r = res.results[core]
        for arr, name in ((z, "zT"), (xbc, "xbcT"), (dt, "dtT")):
            arr[b, k * NT:(k + 1) * NT] = r[name][:, 0:NT].T
            arr[b, SEQ + k * NC_:SEQ + (k + 1) * NC_] = r[name][:, NT:NT + NC_].T
    return z, xbc, dt


NCH = 66


def build_ssd_b(nch=NCH, dbg=False):
    nc = bass.Bass("TRN2", target_bir_lowering=False)
    X = _mk(nc, "X", [nch, 128, 512])
    DT = _mk(nc, "DT", [nch, 128, 16])
    BCT = _mk(nc, "BCT", [nch, 128, 4, 128])
    BTOK = _mk(nc, "BTOK", [nch, 128, 2, 128])
    alog = _mk(nc, "alog", [128, 16])
    masks = _mk(nc, "masks", [128, 4, 128])
    Y = _mk(nc, "Y", [nch, 128, 512], kind="ExternalOutput")
    s = Sched(nc)
    ps = [s.ptile(name="ps%d" % i) for i in range(8)]
    psi = [0]

    def nps():
        t = ps[psi[0]]
        psi[0] = (psi[0] + 1) % 8
        return t
    mk = s.tile([128, 4, 128], F32, "mk")
    s.dma("sp", mk[:], masks, writes=[mk])
    onesf = s.tile([128, 128], F32, "onesf")
    s.op("dve", lambda e: e.memset(onesf[:], 1.0), writes=[onesf])
    a_bc = s.tile([128, 16], F32, "a_bc")
    s.dma("sp", a_bc[:], alog, writes=[a_bc])
    s.op("act", lambda e: e.activation(out=a_bc[:], in_=a_bc[:], func=AF.Exp), reads=[a_bc], writes=[a_bc])
    s.op("dve", lambda e: e.tensor_scalar(out=a_bc[:], in0=a_bc[:], scalar1=-1.0, scalar2=None, op0=ALU.mult), reads=[a_bc], writes=[a_bc])
    state = s.tile([128, 8, 64], F32, "state")
    state_bf = s.tile([128, 512], BF16, "state_bf")
    NB = 4
    xt = [s.tile([128, 8, 64], F32, "xt%d" % i) for i in range(NB)]
    dtt = [s.tile([128, 16], F32, "dtt%d" % i) for i in range(NB)]
    bct = [s.tile([128, 2, 128], F32, "bct%d" % i) for i in range(NB)]
    btk = [s.tile([128, 128], F32, "btk%d" % i) for i in range(NB)]
    bcb = [s.tile([128, 2, 128], BF16, "bcb%d" % i) for i in range(NB)]
    btb = [s.tile([128, 128], BF16, "btb%d" % i) for i in range(NB)]
    yin = [s.tile([128, 512], F32, "yin%d" % i) for i in range(NB)]
    dtA = [s.tile([128, 8], F32, "dtA%d" % i) for i in range(NB)]
    ct = [s.tile([128, 16], F32, "ct%d" % i) for i in range(NB)]
    ee = [s.tile([128, 3, 8], F32, "ee%d" % i) for i in range(NB)]
    dtw = [s.tile([128, 8], F32, "dtw%d" % i) for i in range(NB)]
    xdt = [s.tile([128, 8, 64], BF16, "xdt%d" % i) for i in range(NB)]
    xw = [s.tile([128, 8, 64], BF16, "xw%d" % i) for i in range(NB)]
    Lall = [s.tile([128, 8, 128], F32, "Lall%d" % i) for i in range(NB)]
    dec = [s.tile([128, 8, 128], F32, "dec%d" % i) for i in range(NB)]
    cbm = [s.tile([128, 128], F32, "cbm%d" % i) for i in range(NB)]
    sc = [s.tile([128, 8, 128], BF16, "sc%d" % i) for i in range(NB)]
    tt = [s.tile([128, 8, 64], F32, "tt%d" % i) for i in range(NB)]
    yo = [s.tile([128, 8, 64], F32, "yo%d" % i) for i in range(NB)]
    t2 = s.tile([128, 8, 64], F32, "t2")
    Yc = [T(Y[c], "Y%d" % c) for c in range(nch)]
    nctx = 2
    it = 0
    ysb = [s.tile([128, 8, 64], F32, "ysb%d" % i) for i in range(NB)]

    def sweep(d, it):
        order = list(range(nch)) if d == 0 else ([1, 0] + list(range(nch - 1, nctx - 1, -1)))
        m_incl = mk[:, d, :]
        m_str = mk[:, 2 + d, :]
        s.op("dve", lambda e: e.memset(state[:], 0.0), writes=[state])
        s.op("dve", lambda e: e.memset(state_bf[:], 0.0), writes=[state_bf])

        def phase_a(c, k):
            x_, dt_, bc_, bk_, bcb_, btb_, yin_ = xt[k], dtt[k], bct[k], btk[k], bcb[k], btb[k], yin[k]
            dA, ct_, ee_, dtw_, xdt_, xw_, L_, dec_, cbm_, sc_, ysb_ = dtA[k], ct[k], ee[k], dtw[k], xdt[k], xw[k], Lall[k], dec[k], cbm[k], sc[k], ysb[k]
            pb0 = 3 * (k % 2)
            pscb, pss, psY = ps[pb0], ps[pb0 + 1], ps[pb0 + 2]
            s.dma("sp", x_[:].rearrange("p a b -> p (a b)"), X[c], writes=[x_])
            s.dma("act", dt_[:], DT[c], writes=[dt_])
            s.dma("sp", bc_[:], BCT[c][:, 2 * d:2 * d + 2, :], writes=[bc_])
            s.dma("act", bk_[:], BTOK[c][:, d, :], writes=[bk_])
            if d == 1:
                s.dma("sp", yin_[:], Y[c], reads=[Yc[c]], writes=[yin_])
            yield
            dts = dt_[:, d * 8:(d + 1) * 8]
            s.op("dve", lambda e: e.tensor_tensor(out=dA[:], in0=dts, in1=a_bc[:, d * 8:(d + 1) * 8], op=ALU.mult), reads=[dt_, a_bc], writes=[dA])
            yield
            s.op("pool", lambda e: e.tensor_copy(out=bcb_[:], in_=bc_[:]), reads=[bc_], writes=[bcb_])
            s.op("pool", lambda e: e.tensor_copy(out=btb_[:], in_=bk_[:]), reads=[bk_], writes=[btb_])
            yield
            s.op("pe", lambda e: e.matmul(pscb[:, 0:8], lhsT=m_incl, rhs=dA[:], start=True, stop=True), reads=[mk, dA], writes=[pscb])
            s.op("pe", lambda e: e.matmul(pscb[:, 8:16], lhsT=onesf[:], rhs=dA[:], start=True, stop=True), reads=[onesf, dA], writes=[pscb])
            yield
            s.op("pool", lambda e: e.tensor_tensor(out=L_[:], in0=m_str.unsqueeze(1).to_broadcast([128, 8, 128]),
                                                   in1=dA[:].unsqueeze(2).to_broadcast([128, 8, 128]), op=ALU.mult), reads=[mk, dA], writes=[L_])
            yield
            s.op("pe", lambda e: e.matmul(pscb[:, 128:256], lhsT=bcb_[:, 0, :], rhs=bcb_[:, 1, :], start=True, stop=True), reads=[bcb_], writes=[pscb])
            yield
            s.op("dve", lambda e: e.tensor_copy(out=ct_[:], in_=pscb[:, 0:16]), reads=[pscb], writes=[ct_])
            yield
            s.op("dve", lambda e: e.tensor_tensor(out=cbm_[:], in0=pscb[:, 128:256], in1=m_incl, op=ALU.mult), reads=[pscb, mk], writes=[cbm_])
            yield
            s.op("dve", lambda e: e.tensor_tensor(out=ee_[:, 2, :], in0=ct_[:, 8:16], in1=ct_[:, 0:8], op=ALU.subtract), reads=[ct_], writes=[ee_])
            yield
            s.op("act", lambda e: e.activation(out=ee_[:, 0:2, :].rearrange("p a b -> p (a b)"), in_=ct_[:, 0:16], func=AF.Exp), reads=[ct_, ee_], writes=[ee_])
            yield
            s.op("act", lambda e: e.activation(out=ee_[:, 2, :], in_=ee_[:, 2, :], func=AF.Exp), reads=[ee_], writes=[ee_])
            yield
            s.op("dve", lambda e: e.tensor_tensor(out=xdt_[:], in0=x_[:], in1=dts.unsqueeze(2).to_broadcast([128, 8, 64]), op=ALU.mult), reads=[x_, dt_], writes=[xdt_])
            yield
            for hh in range(2):
                for e_ in range(hh * 4, hh * 4 + 4):
                    s.op("pe", lambda e, e_=e_: e.matmul(pss[:, (e_ % 4) * 128:(e_ % 4 + 1) * 128], lhsT=L_[:, e_, :], rhs=m_incl, start=True, stop=True),
                         reads=[L_, mk], writes=[pss])
                yield
                s.op("act", lambda e, hh=hh: e.activation(out=dec_[:, hh * 4:(hh + 1) * 4, :].rearrange("p a b -> p (a b)"), in_=pss[:, :], func=AF.Exp),
                     reads=[pss], writes=[dec_])
                yield
                if hh == 0:
                    s.op("dve", lambda e: e.tensor_tensor(out=dtw_[:], in0=dts, in1=ee_[:, 2, :], op=ALU.mult), reads=[dt_, ee_], writes=[dtw_])
                    yield
                    s.op("dve", lambda e: e.tensor_tensor(out=xw_[:], in0=x_[:], in1=dtw_[:].unsqueeze(2).to_broadcast([128, 8, 64]), op=ALU.mult), reads=[x_, dtw_], writes=[xw_])
                    yield
            s.op("dve", lambda e: e.tensor_tensor(out=sc_[:], in0=dec_[:], in1=cbm_[:].unsqueeze(1).to_broadcast([128, 8, 128]), op=ALU.mult),
                 reads=[dec_, cbm_], writes=[sc_])
            yield
            for e_ in range(8):
                s.op("pe", lambda e, e_=e_: e.matmul(psY[:, e_ * 64:(e_ + 1) * 64], lhsT=sc_[:, e_, :], rhs=xdt_[:, e_, :], start=True, stop=True),
                     reads=[sc_, xdt_], writes=[psY])
            yield
            s.op("act", lambda e: e.activation(out=ysb_[:].rearrange("p a b -> p (a b)"), in_=psY[:, 0:512], func=AF.Identity), reads=[psY], writes=[ysb_])
            yield

        def phase_b(c, k):
            bcb_, btb_, yin_, ee_, xw_, tt_, yo_, ysb_ = bcb[k], btb[k], yin[k], ee[k], xw[k], tt[k % 2], yo[k % 2], ysb[k]
            psS, psU = ps[6], ps[7]
            s.op("pe", lambda e: e.matmul(psS[:, 0:512], lhsT=bcb_[:, 1, :], rhs=state_bf[:], start=True, stop=True), reads=[bcb_, state_bf], writes=[psS])
            s.op("pe", lambda e: e.matmul(psU[:, 0:512], lhsT=btb_[:], rhs=xw_[:].rearrange("p a b -> p (a b)"), start=True, stop=True),
                 reads=[btb_, xw_], writes=[psU])
            yield
            s.op("dve", lambda e: e.tensor_tensor(out=t2[:], in0=state[:], in1=ee_[:, 1, :].unsqueeze(2).to_broadcast([128, 8, 64]), op=ALU.mult),
                 reads=[state, ee_], writes=[t2])
            yield
            s.op("dve", lambda e: e.tensor_tensor(out=state[:], in0=t2[:], in1=psU[:, 0:512].rearrange("p (a b) -> p a b", a=8), op=ALU.add),
                 reads=[t2, psU], writes=[state])
            yield
            s.op("act", lambda e: e.activation(out=state_bf[:], in_=state[:].rearrange("p a b -> p (a b)"), func=AF.Identity), reads=[state], writes=[state_bf])
            yield
            s.op("dve", lambda e: e.tensor_tensor(out=tt_[:], in0=psS[:, 0:512].rearrange("p (a b) -> p a b", a=8),
                                                  in1=ee_[:, 0, :].unsqueeze(2).to_broadcast([128, 8, 64]), op=ALU.mult), reads=[psS, ee_], writes=[tt_])
            yield
            if d == 1:
                s.op("pool", lambda e: e.tensor_tensor(out=ysb_[:].rearrange("p a b -> p (a b)"), in0=ysb_[:].rearrange("p a b -> p (a b)"), in1=yin_[:], op=ALU.add),
                     reads=[ysb_, yin_], writes=[ysb_])
                yield
            s.op("dve", lambda e: e.tensor_tensor(out=yo_[:], in0=tt_[:], in1=ysb_[:], op=ALU.add), reads=[ysb_, tt_], writes=[yo_])
            yield
            s.dma("sp", Y[c], yo_[:].rearrange("p a b -> p (a b)"), reads=[yo_], writes=[Yc[c]], is_output=True)
            yield

        def chain(gens):
            for g in gens:
                yield from g

        def merge(gens):
            gens = list(gens)
            while gens:
                for g in list(gens):
                    try:
                        next(g)
                    except StopIteration:
                        gens.remove(g)

        slots = [(c, (it + i) % NB) for i, c in enumerate(order)]
        it += len(order)
        n = len(slots)
        merge([phase_a(*slots[0]), phase_a(*slots[1])])
        for t in range(0, n, 2):
            gens = []
            for j in (t + 2, t + 3):
                if j < n:
                    gens.append(phase_a(*slots[j]))
            gens.append(chain([phase_b(*slots[j]) for j in (t, t + 1) if j < n]))
            merge(gens)
        return it
    it = sweep(0, it)
    it = sweep(1, it)
    s.emit()
    return nc


def _ssd_masks():
    k = np.arange(128)
    m = np.zeros((128, 4, 128), np.float32)
    m[:, 0, :] = (k[:, None] <= k[None, :])
    m[:, 1, :] = (k[:, None] >= k[None, :])
    m[:, 2, :] = (k[:, None] > k[None, :])
    m[:, 3, :] = (k[:, None] < k[None, :])
    return m


def run_ssd_b(xbc, dt, a_log):
    nc = build_ssd_b()
    masks = _ssd_masks()
    ins = []
    for core in range(NCORES):
        b, g = divmod(core, 4)
        def chunks(a):
            return np.concatenate([a[SEQ:].reshape(2, 128, -1), a[:SEQ].reshape(64, 128, -1)], 0)
        xg = chunks(xbc[b][:, g * 512:(g + 1) * 512])
        bc = xbc[b][:, 2048:].reshape(-1, 2, 2, 4, 128)[:, :, :, g, :]
        bcc = chunks(bc.reshape(-1, 4 * 128)).reshape(NCH, 128, 4, 128)
        bct = np.ascontiguousarray(bcc.transpose(0, 3, 2, 1))
        btok = np.ascontiguousarray(bcc[:, :, [0, 2], :])
        dtg = dt[b].reshape(-1, 2, 4, 8)[:, :, g, :].reshape(-1, 16)
        al = np.ascontiguousarray(np.broadcast_to(a_log.reshape(2, 4, 8)[:, g, :].reshape(1, 16), (128, 16))).astype(np.float32)
        ins.append({"X": np.ascontiguousarray(xg), "DT": np.ascontiguousarray(chunks(dtg)), "BCT": bct, "BTOK": btok, "alog": al, "masks": masks})
    res = run_bass_kernel_spmd(nc, ins, core_ids=list(range(NCORES)))
    y = np.empty((BATCH, SEQ + CTX, 2048), np.float32)
    for core in range(NCORES):
        b, g = divmod(core, 4)
        Yc = res.results[core]["Y"]
        y[b, SEQ:, g * 512:(g + 1) * 512] = Yc[0:2].reshape(256, 512)
        y[b, :SEQ, g * 512:(g + 1) * 512] = Yc[2:].reshape(SEQ, 512)
    return y


def build_ssd_c(NT=2048, NC_=64):
    nc = bass.Bass("TRN2", target_bir_lowering=False)
    NTOT = NT + NC_
    yT = _mk(nc, "yT", [2048, NTOT])
    xsT = _mk(nc, "xsT", [2048, NTOT])
    zT = _mk(nc, "zT", [2048, NTOT])
    xT = _mk(nc, "xT", [D, NTOT])
    mvec = _mk(nc, "mvec", [D, 6])
    mvec_c = _mk(nc, "mvec_c", [D, 6])
    gvec = _mk(nc, "gvec", [D, 4])
    dcol = _mk(nc, "dcol", [128, 16, 2])
    ngd = _mk(nc, "ng", [128, 16])
    w_out = _mk(nc, "w_out", [2048, D])
    w1 = _mk(nc, "w1", [D, HID])
    w2 = _mk(nc, "w2", [HID, D])
    outT = _mk(nc, "outT", [D, NTOT], kind="ExternalOutput")
    kx = KX(nc, nwbuf=2)
    s = kx.s
    ms = kx.prep_mod(mvec, gvec)
    msc = kx.prep_mod(mvec_c, gvec, "mvc")
    dc_ = s.tile([128, 16, 2], F32, "dcol")
    s.dma("sp", dc_[:], dcol, writes=[dc_])
    dsum = s.tile([128, 16], F32, "dsum")
    s.op("dve", lambda e: e.tensor_tensor(out=dsum[:], in0=dc_[:, :, 0], in1=dc_[:, :, 1], op=ALU.add), reads=[dc_], writes=[dsum])
    ng = s.tile([128, 16], F32, "ng")
    s.dma("sp", ng[:], ngd, writes=[ng])
    HWD = NT // 2
    xres = s.tile([128, 8, HWD + NC_], F32, "xres")
    y_tiles = [s.tile([128, 8, 512], F32, "y0"), s.tile([128, 8, 512], F32, "y1"), s.tile([128, 8, NC_], F32, "y2")]
    for half in range(2):
        c0 = half * HWD
        cols = [(c0, 512, ms, 0), (c0 + 512, 512, ms, 512)]
        s.dma("sp", xres[:, :, 0:HWD], xT[:, c0:c0 + HWD].rearrange("(c p) n -> p c n", p=128), writes=[xres])
        if half == 1:
            cols.append((NT, NC_, msc, HWD))
            s.dma("sp", xres[:, :, HWD:HWD + NC_], xT[:, NT:NT + NC_].rearrange("(c p) n -> p c n", p=128), writes=[xres])
        with s.scope():
            gz = s.tile([128, 16, 512], F32, "gz")
            ynT = [s.tile([128, 16, w], BF16, "ynT%d" % i) for i, (_, w, _, _) in enumerate(cols)]
            ld = [[s.tile([128, 512], F32, "ld%d_%d" % (a, b)) for b in range(2)] for a in range(3)]
            cnt = 0
            for bi, (co, w, m, xo) in enumerate(cols):
                for ch in range(16):
                    ly, lx, lz = ld[0][cnt % 2], ld[1][cnt % 2], ld[2][cnt % 2]
                    cnt += 1
                    s.dma("sp", ly[:, 0:w], yT[ch * 128:(ch + 1) * 128, co:co + w], writes=[ly])
                    s.dma("act", lx[:, 0:w], xsT[ch * 128:(ch + 1) * 128, co:co + w], writes=[lx])
                    s.dma("sp", lz[:, 0:w], zT[ch * 128:(ch + 1) * 128, co:co + w], writes=[lz])
                    s.op("dve", lambda e, ly=ly, lx=lx, ch=ch, w=w: e.scalar_tensor_tensor(out=ly[:, 0:w], in0=lx[:, 0:w], scalar=dsum[:, ch:ch + 1], in1=ly[:, 0:w],
                                                                                      op0=ALU.mult, op1=ALU.add), reads=[lx, ly, dsum], writes=[ly])
                    s.op("act", lambda e, lz=lz, w=w: e.activation(out=lz[:, 0:w], in_=lz[:, 0:w], func=AF.Silu), reads=[lz], writes=[lz])
                    s.op("dve", lambda e, ly=ly, lz=lz, ch=ch, w=w: e.tensor_tensor(out=gz[:, ch, 0:w], in0=ly[:, 0:w], in1=lz[:, 0:w], op=ALU.mult),
                         reads=[ly, lz], writes=[gz])
                r = kx.rstd(gz, gz[:, :, 0:w], w, nchunks=16, dim=2048)
                for ch in range(16):
                    tc = kx.ntc()
                    s.op("dve", lambda e, ch=ch, tc=tc, r=r, w=w: e.tensor_tensor(out=tc[:, 0:w], in0=gz[:, ch, 0:w], in1=r[:, 0:w], op=ALU.mult),
                         reads=[gz, r], writes=[tc])
                    s.op("act", lambda e, ch=ch, tc=tc, w=w, yn=ynT[bi]: e.activation(out=yn[:, ch, :], in_=tc[:, 0:w], func=AF.Identity, scale=ng[:, ch:ch + 1]),
                         reads=[tc, ng], writes=[ynT[bi]])
            def cp_o(dc, wt, wv, cols=cols, ynT=ynT):
                for bi, (co, w, m, xo) in enumerate(cols):
                    ps = kx.nps()
                    for c in range(16):
                        s.op("pe", lambda e, c=c, ps=ps, bi=bi, w=w, wv=wv: e.matmul(ps[:, 0:w], lhsT=wv[:, c, :], rhs=ynT[bi][:, c, :], start=(c == 0), stop=(c == 15)),
                             reads=[wt, ynT[bi]], writes=[ps])
                    s.op("act", lambda e, ps=ps, dc=dc, bi=bi, w=w: e.activation(out=y_tiles[bi][:, dc, 0:w], in_=ps[:, 0:w], func=AF.Identity),
                         reads=[ps], writes=[y_tiles[bi]])
            kx.wstream([(lambda dc=dc: kx.wload(w_out[:, dc * 128:(dc + 1) * 128], 16, 128)) for dc in range(8)],
                       [(lambda wt, wv, dc=dc: cp_o(dc, wt, wv)) for dc in range(8)], L=1)
        blocks = [Blk(w, m, xres, xres[:, :, xo:xo + w], y_tiles[bi]) for bi, (co, w, m, xo) in enumerate(cols)]
        emit_finish(kx, blocks, w1, w2, None)
        for bi, (co, w, m, xo) in enumerate(cols):
            s.dma("act", outT[:, co:co + w].rearrange("(c p) n -> p c n", p=128), xres[:, :, xo:xo + w], reads=[xres], is_output=True)
    s.emit()
    return nc


def run_ssd_c(x, ctx, y, xbc, z, m_lat, m_ctx, g, ssd_d, ssd_norm_g, w_out, w1, w2):
    NT, NC_ = 2048, 64
    nc = build_ssd_c(NT, NC_)
    dcol = np.repeat(ssd_d.reshape(2, 32).T, 64, axis=0).astype(np.float32)
    dcol = np.ascontiguousarray(dcol.reshape(16, 128, 2).transpose(1, 0, 2))
    ng = np.ascontiguousarray(ssd_norm_g.reshape(16, 128).T.astype(np.float32))
    ins = []
    for core in range(NCORES):
        b, k = divmod(core, 4)
        def cat(a_main, a_ctx):
            return np.ascontiguousarray(np.concatenate([a_main[k * NT:(k + 1) * NT], a_ctx[k * NC_:(k + 1) * NC_]], 0).T)
        ins.append({"yT": cat(y[b][:SEQ], y[b][SEQ:]), "xsT": cat(xbc[b][:SEQ, :2048], xbc[b][SEQ:, :2048]), "zT": cat(z[b][:SEQ], z[b][SEQ:]),
                    "xT": cat(x[b], ctx[b]), "mvec": np.ascontiguousarray(m_lat[b].reshape(6, D).T), "mvec_c": np.ascontiguousarray(m_ctx.reshape(6, D).T),
                    "gvec": np.ascontiguousarray(g.T), "dcol": dcol, "ng": ng, "w_out": w_out, "w1": w1, "w2": w2})
    res = run_bass_kernel_spmd(nc, ins, core_ids=list(range(NCORES)))
    xo = np.empty_like(x)
    co = np.empty_like(ctx)
    for core in range(NCORES):
        b, k = divmod(core, 4)
        o = res.results[core]["outT"]
        xo[b, k * NT:(k + 1) * NT] = o[:, :NT].T
        co[b, k * NC_:(k + 1) * NC_] = o[:, NT:].T
    return xo, co


def build_projfin(NT=2048):
    nc = bass.Bass("TRN2", target_bir_lowering=False)
    fT = _mk(nc, "fT", [D, NT])
    xT = _mk(nc, "xT", [D, NT])
    mvec = _mk(nc, "mvec", [D, 6])
    gvec = _mk(nc, "gvec", [D, 4])
    w_out = _mk(nc, "w_out", [D, D])
    w1 = _mk(nc, "w1", [D, HID])
    w2 = _mk(nc, "w2", [HID, D])
    outT = _mk(nc, "outT", [D, NT], kind="ExternalOutput")
    kx = KX(nc)
    s = kx.s
    ms = kx.prep_mod(mvec, gvec)
    HWD, bw, nblk = NT // 2, 512, 2
    xres = s.tile([128, 8, HWD], F32, "xres")
    y_tiles = [s.tile([128, 8, bw], F32, "y%d" % i) for i in range(nblk)]
    for half in range(2):
        c0 = half * HWD
        s.dma("sp", xres[:], xT[:, c0:c0 + HWD].rearrange("(c p) n -> p c n", p=128), writes=[xres])
        with s.scope():
            fb = [s.tile([128, 8, bw], BF16, "fb%d" % i) for i in range(nblk)]
            wpre = [kx.wload(w_out[:, db * 512:(db + 1) * 512], 8, 512) for db in range(2)]
            for tb in range(nblk):
                s.dma("act", y_tiles[tb][:], fT[:, c0 + tb * bw:c0 + (tb + 1) * bw].rearrange("(c p) n -> p c n", p=128), writes=[y_tiles[tb]])
                for c in range(8):
                    if c % 2 == 0:
                        s.op("act", lambda e, tb=tb, c=c: e.activation(out=fb[tb][:, c, :], in_=y_tiles[tb][:, c, :], func=AF.Identity), reads=[y_tiles[tb]], writes=[fb[tb]])
                    else:
                        s.op("dve", lambda e, tb=tb, c=c: e.tensor_copy(out=fb[tb][:, c, :], in_=y_tiles[tb][:, c, :]), reads=[y_tiles[tb]], writes=[fb[tb]])
            for db in range(2):
                wt, wv = wpre[db]
                for sub in range(4):
                    dc = db * 4 + sub
                    for tb in range(nblk):
                        ps = kx.nps()
                        for c in range(8):
                            s.op("pe", lambda e, c=c, ps=ps, tb=tb, sub=sub, wv=wv: e.matmul(
                                ps[:, 0:bw], lhsT=wv[:, c, sub * 128:(sub + 1) * 128], rhs=fb[tb][:, c, :], start=(c == 0), stop=(c == 7)),
                                reads=[wt, fb[tb]], writes=[ps])
                        s.op("act", lambda e, ps=ps, dc=dc, tb=tb: e.activation(out=y_tiles[tb][:, dc, :], in_=ps[:, 0:bw], func=AF.Identity),
                             reads=[ps], writes=[y_tiles[tb]])
        x_views = [xres[:, :, tb * bw:(tb + 1) * bw] for tb in range(nblk)]
        blocks = [Blk(bw, ms, xres, x_views[tb], y_tiles[tb]) for tb in range(nblk)]
        emit_finish(kx, blocks, w1, w2, None)
        for tb in range(nblk):
            s.dma("act", outT[:, c0 + tb * bw:c0 + (tb + 1) * bw].rearrange("(c p) n -> p c n", p=128), x_views[tb], reads=[xres], is_output=True)
    s.emit()
    return nc


def run_projfin(x, f, m_lat, g, w_out, w1, w2):
    NT = 2048
    nc = build_projfin(NT)
    ins = []
    for core in range(NCORES):
        b, k = divmod(core, 4)
        ins.append({"fT": np.ascontiguousarray(f[b, k * NT:(k + 1) * NT].T), "xT": np.ascontiguousarray(x[b, k * NT:(k + 1) * NT].T),
                    "mvec": np.ascontiguousarray(m_lat[b].reshape(6, D).T), "gvec": np.ascontiguousarray(g.T), "w_out": w_out, "w1": w1, "w2": w2})
    res = run_bass_kernel_spmd(nc, ins, core_ids=list(range(NCORES)))
    out = np.empty_like(x)
    for core in range(NCORES):
        b, k = divmod(core, 4)
        out[b, k * NT:(k + 1) * NT] = res.results[core]["outT"].T
    return out


def kernel(x, c, ctx, c_ctx, mod_w, mod_b, norm_g, mlp_w1, mlp_w2, ssd_w_in, ssd_conv_w, ssd_conv_b,
           ssd_dt_bias, ssd_a_log, ssd_d, ssd_norm_g, ssd_w_out, na_w_qkv, na_rpb, na_w_out,
           sc_w_in, sc_conv_w, sc_w_out, fn_w_out):
    f32 = lambda a: np.ascontiguousarray(np.asarray(a), dtype=np.float32)
    x, c, ctx, c_ctx, mod_w, mod_b, norm_g, mlp_w1, mlp_w2 = map(f32, (x, c, ctx, c_ctx, mod_w, mod_b, norm_g, mlp_w1, mlp_w2))
    m_lat, m_ctx = run_mod(c, c_ctx, mod_w, mod_b)
    z, xbc, dt = run_ssd_a(x, ctx, m_lat[0], m_ctx[0], norm_g[0], f32(ssd_w_in)[0], f32(ssd_conv_w)[0], f32(ssd_conv_b)[0], f32(ssd_dt_bias)[0])
    y = run_ssd_b(xbc, dt, f32(ssd_a_log)[0])
    x, ctx = run_ssd_c(x, ctx, y, xbc, z, m_lat[0], m_ctx[0], norm_g[0], f32(ssd_d)[0], f32(ssd_norm_g)[0], f32(ssd_w_out)[0], mlp_w1[0], mlp_w2[0])
    x = run_na(x, ctx, m_lat[1], m_ctx[1], norm_g[1], f32(na_w_qkv)[0], f32(na_rpb)[0], f32(na_w_out)[0], mlp_w1[1], mlp_w2[1])
    x, h3 = run_sc(x, m_lat[2], norm_g[2], f32(sc_w_in)[0], f32(sc_conv_w)[0], f32(sc_w_out)[0], mlp_w1[2], mlp_w2[2], m_lat[3], norm_g[3])
    f = run_fft(h3)
    x = run_projfin(x, f, m_lat[3], norm_g[3], f32(fn_w_out)[0], mlp_w1[3], mlp_w2[3])
    return x.astype(np.float32)
```
